# Optimizing a Trainium2 kernel written in Bass

```python
import math
import jax
import jax.numpy as jnp
from jax import lax
import numpy as np

D_MODEL = 1024
BATCH = 8
SEQ = 4096
DEPTH = 4

GRID_W = 64
CTX_LEN = 256
D_MIX = D_MODEL
D_S5 = D_MIX // 2
S5_GROUP = 16
S5_GROUPS = D_S5 // S5_GROUP
S5_STATE = 64
D_LRU = D_MIX - D_S5
LRU_HEADS = 8
LRU_HEAD_DIM = D_LRU // LRU_HEADS
CONV_W = 4
CONV_LEFT = CONV_W // 2
LRU_C = 8.0
D_IN = D_S5 + 2 * D_LRU
D_FF = -(-(8 * D_MODEL) // (3 * 256)) * 256
N_DIR = 2
EPS = 1e-6

kernel_name = "hybrid_s5_rglru_dit_block"


def rms_norm(x, g):
    xf = x.astype(jnp.float32)
    var = jnp.mean(xf * xf, axis=-1, keepdims=True)
    return (xf * lax.rsqrt(var + EPS)).astype(x.dtype) * g


def adaln(cond, w, b):
    m = (jax.nn.silu(cond) @ w + b)[..., None, :]
    return jnp.split(m, 6, axis=-1)


def to_col_major(t, rows):
    b, l, ch = t.shape
    return t.reshape(b, rows, GRID_W, ch).transpose(0, 2, 1, 3).reshape(b, l, ch)


def from_col_major(t, rows):
    b, l, ch = t.shape
    return t.reshape(b, GRID_W, rows, ch).transpose(0, 2, 1, 3).reshape(b, l, ch)


def _real_combine(left, right):
    a1, b1 = left
    a2, b2 = right
    return a1 * a2, a2 * b1 + b2


def _cplx_combine(left, right):
    ar1, ai1, br1, bi1 = left
    ar2, ai2, br2, bi2 = right
    return (ar2 * ar1 - ai2 * ai1,
            ar2 * ai1 + ai2 * ar1,
            ar2 * br1 - ai2 * bi1 + br2,
            ar2 * bi1 + ai2 * br1 + bi2)


def linear_scan(a, b, h0, reverse):
    if h0 is not None:
        edge = -1 if reverse else 0
        b = b.at[:, edge].add(a[:, edge] * h0)
    _, h = lax.associative_scan(_real_combine, (a, b), axis=1, reverse=reverse)
    return h


def s5_discretise(a_re, a_im, log_dt, b_re, b_im):
    dt = jnp.exp(log_dt)[:, None]
    mag = jnp.exp(a_re * dt)
    ab_re = mag * jnp.cos(a_im * dt)
    ab_im = mag * jnp.sin(a_im * dt)
    den = a_re * a_re + a_im * a_im
    nr = ab_re - 1.0
    f_re = (nr * a_re + ab_im * a_im) / den
    f_im = (ab_im * a_re - nr * a_im) / den
    bb_re = f_re[..., None] * b_re - f_im[..., None] * b_im
    bb_im = f_re[..., None] * b_im + f_im[..., None] * b_re
    return ab_re, ab_im, bb_re, bb_im


def s5_scan(u, disc, s0, reverse):
    ab_re, ab_im, bb_re, bb_im = disc
    b, l, _ = u.shape
    ug = u.reshape(b, l, S5_GROUPS, S5_GROUP)
    x_re = jnp.einsum('blgp,gnp->blgn', ug, bb_re)
    x_im = jnp.einsum('blgp,gnp->blgn', ug, bb_im)
    if s0 is not None:
        edge = -1 if reverse else 0
        s0_re, s0_im = s0
        x_re = x_re.at[:, edge].add(ab_re * s0_re - ab_im * s0_im)
        x_im = x_im.at[:, edge].add(ab_re * s0_im + ab_im * s0_re)
    a_re = jnp.broadcast_to(ab_re, (1, l) + ab_re.shape)
    a_im = jnp.broadcast_to(ab_im, (1, l) + ab_im.shape)
    _, _, s_re, s_im = lax.associative_scan(_cplx_combine, (a_re, a_im, x_re, x_im), axis=1, reverse=reverse)
    return s_re, s_im


def s5_readout(s_re, s_im, c_re, c_im):
    y = jnp.einsum('blgn,gpn->blgp', s_re, c_re) - jnp.einsum('blgn,gpn->blgp', s_im, c_im)
    b, l = y.shape[:2]
    return y.reshape(b, l, D_S5)


def s5_glu(y, w_glu, b_glu):
    y = jax.nn.gelu(y)
    return y * jax.nn.sigmoid(y @ w_glu + b_glu)


def centred_depthwise_conv(x, w, b):
    l = x.shape[1]
    xp = jnp.pad(x, ((0, 0), (CONV_LEFT, CONV_W - 1 - CONV_LEFT), (0, 0)))
    out = xp[:, 0:l] * w[0]
    for k in range(1, CONV_W):
        out = out + xp[:, k:k + l] * w[k]
    return out + b


def block_diag(x, w, b):
    xh = x.reshape(x.shape[:-1] + (LRU_HEADS, LRU_HEAD_DIM))
    return jnp.einsum('blhi,hij->blhj', xh, w).reshape(x.shape) + b


def rglru_coeffs(x, w_rg, b_rg, w_ig, b_ig, lam):
    r = jax.nn.sigmoid(block_diag(x, w_rg, b_rg))
    i = jax.nn.sigmoid(block_diag(x, w_ig, b_ig))
    log_a = -LRU_C * r * jax.nn.softplus(-lam)
    a = jnp.exp(log_a)
    mult = jnp.sqrt(-jnp.expm1(2.0 * log_a))
    return a, mult * (i * x)


def hybrid_mixer(h_lat, h_ctx, rows, need_ctx, w_in,
                 s5_a_re, s5_a_im, s5_log_dt, s5_b_re, s5_b_im, s5_c_re, s5_c_im, s5_d, s5_w_glu, s5_b_glu,
                 lru_conv_w, lru_conv_b, lru_w_rg, lru_b_rg, lru_w_ig, lru_b_ig, lru_lambda, w_out):
    split = [D_S5, D_S5 + D_LRU]
    u_lat, xr_lat, gr_lat = jnp.split(h_lat @ w_in, split, axis=-1)
    u_ctx, xr_ctx, gr_ctx = jnp.split(h_ctx @ w_in, split, axis=-1)

    ys_lat, ys_ctx = [s5_d * u_lat], [s5_d * u_ctx]
    for d, reverse in enumerate((False, True)):
        disc = s5_discretise(s5_a_re[d], s5_a_im[d], s5_log_dt[d], s5_b_re[d], s5_b_im[d])
        edge = 0 if reverse else -1
        sc_re, sc_im = s5_scan(u_ctx, disc, None, reverse)
        sl_re, sl_im = s5_scan(u_lat, disc, (sc_re[:, edge], sc_im[:, edge]), reverse)
        ys_lat.append(s5_readout(sl_re, sl_im, s5_c_re[d], s5_c_im[d]))
        if need_ctx:
            ys_ctx.append(s5_readout(sc_re, sc_im, s5_c_re[d], s5_c_im[d]))
    y_s5_lat = s5_glu(ys_lat[0] + ys_lat[1] + ys_lat[2], s5_w_glu, s5_b_glu)

    xc_lat = centred_depthwise_conv(to_col_major(xr_lat, rows), lru_conv_w, lru_conv_b)
    xc_ctx = centred_depthwise_conv(xr_ctx, lru_conv_w, lru_conv_b)
    hs_lat, hs_ctx = [], []
    for d, reverse in enumerate((False, True)):
        edge = 0 if reverse else -1
        a_c, b_c = rglru_coeffs(xc_ctx, lru_w_rg[d], lru_b_rg[d], lru_w_ig[d], lru_b_ig[d], lru_lambda[d])
        h_c = linear_scan(a_c, b_c, None, reverse)
        a_l, b_l = rglru_coeffs(xc_lat, lru_w_rg[d], lru_b_rg[d], lru_w_ig[d], lru_b_ig[d], lru_lambda[d])
        hs_lat.append(linear_scan(a_l, b_l, h_c[:, edge], reverse))
        hs_ctx.append(h_c)
    y_lru_lat = from_col_major(hs_lat[0] + hs_lat[1], rows) * jax.nn.gelu(gr_lat)
    out_lat = jnp.concatenate([y_s5_lat, y_lru_lat], axis=-1) @ w_out

    out_ctx = None
    if need_ctx:
        y_s5_ctx = s5_glu(ys_ctx[0] + ys_ctx[1] + ys_ctx[2], s5_w_glu, s5_b_glu)
        y_lru_ctx = (hs_ctx[0] + hs_ctx[1]) * jax.nn.gelu(gr_ctx)
        out_ctx = jnp.concatenate([y_s5_ctx, y_lru_ctx], axis=-1) @ w_out
    return out_lat, out_ctx


def swiglu(h, w_ffn_in, w_ffn_out):
    gate, up = jnp.split(h @ w_ffn_in, 2, axis=-1)
    return (jax.nn.silu(gate) * up) @ w_ffn_out


def setup_inputs(seed: int = 0) -> dict:
    key = jax.random.key(seed)
    ks = jax.random.split(key, 32)
    f32 = jnp.float32

    def nrm(k, shape, scale):
        return scale * jax.random.normal(k, shape, f32)

    L, G, N, P = DEPTH, S5_GROUPS, S5_STATE, S5_GROUP
    n_idx = jnp.arange(N, dtype=f32)
    a_pow_c = jax.random.uniform(ks[24], (L, N_DIR, D_LRU), f32, 0.9, 0.999)
    a0 = a_pow_c ** (1.0 / LRU_C)
    return {
        "x": nrm(ks[0], (BATCH, SEQ, D_MODEL), 1.0),
        "c": nrm(ks[1], (BATCH, D_MODEL), 1.0),
        "ctx": nrm(ks[2], (BATCH, CTX_LEN, D_MODEL), 1.0),
        "c_ctx": nrm(ks[3], (D_MODEL,), 1.0),
        "w_ada": nrm(ks[4], (L, D_MODEL, 6 * D_MODEL), D_MODEL ** -0.5),
        "b_ada": nrm(ks[5], (L, 6 * D_MODEL), 0.02),
        "norm_gains": 1.0 + nrm(ks[6], (L, 4, D_MODEL), 0.05),
        "w_in": nrm(ks[7], (L, D_MODEL, D_IN), D_MODEL ** -0.5),
        "s5_a_re": -0.5 + nrm(ks[8], (L, N_DIR, G, N), 0.01),
        "s5_a_im": math.pi * n_idx + nrm(ks[9], (L, N_DIR, G, N), 0.01),
        "s5_log_dt": jax.random.uniform(ks[10], (L, N_DIR, G), f32, math.log(1e-3), math.log(1e-1)),
        "s5_b_re": nrm(ks[11], (L, N_DIR, G, N, P), (2 * P) ** -0.5),
        "s5_b_im": nrm(ks[12], (L, N_DIR, G, N, P), (2 * P) ** -0.5),
        "s5_c_re": nrm(ks[13], (L, N_DIR, G, P, N), (2 * N) ** -0.5),
        "s5_c_im": nrm(ks[14], (L, N_DIR, G, P, N), (2 * N) ** -0.5),
        "s5_d": nrm(ks[15], (L, D_S5), 1.0),
        "s5_w_glu": nrm(ks[16], (L, D_S5, D_S5), D_S5 ** -0.5),
        "s5_b_glu": nrm(ks[17], (L, D_S5), 0.01),
        "lru_conv_w": nrm(ks[18], (L, CONV_W, D_LRU), CONV_W ** -0.5),
        "lru_conv_b": nrm(ks[19], (L, D_LRU), 0.01),
        "lru_w_rg": nrm(ks[20], (L, N_DIR, LRU_HEADS, LRU_HEAD_DIM, LRU_HEAD_DIM), LRU_HEAD_DIM ** -0.5),
        "lru_b_rg": nrm(ks[21], (L, N_DIR, D_LRU), 0.01),
        "lru_w_ig": nrm(ks[22], (L, N_DIR, LRU_HEADS, LRU_HEAD_DIM, LRU_HEAD_DIM), LRU_HEAD_DIM ** -0.5),
        "lru_b_ig": nrm(ks[23], (L, N_DIR, D_LRU), 0.01),
        "lru_lambda": jnp.log(a0) - jnp.log1p(-a0),
        "w_out": nrm(ks[25], (L, D_MIX, D_MODEL), D_MIX ** -0.5),
        "w_ffn_in": nrm(ks[26], (L, D_MODEL, 2 * D_FF), D_MODEL ** -0.5),
        "w_ffn_out": nrm(ks[27], (L, D_FF, D_MODEL), D_FF ** -0.5),
    }


def reference(x, c, ctx, c_ctx, w_ada, b_ada, norm_gains, w_in,
              s5_a_re, s5_a_im, s5_log_dt, s5_b_re, s5_b_im, s5_c_re, s5_c_im, s5_d, s5_w_glu, s5_b_glu,
              lru_conv_w, lru_conv_b, lru_w_rg, lru_b_rg, lru_w_ig, lru_b_ig, lru_lambda,
              w_out, w_ffn_in, w_ffn_out):
    rows = x.shape[1] // GRID_W
    for layer in range(DEPTH):
        need_ctx = layer < DEPTH - 1
        g_pre_mix, g_post_mix, g_pre_ffn, g_post_ffn = norm_gains[layer]
        sh1, sc1, gt1, sh2, sc2, gt2 = adaln(c, w_ada[layer], b_ada[layer])
        csh1, csc1, cgt1, csh2, csc2, cgt2 = adaln(c_ctx, w_ada[layer], b_ada[layer])

        h_lat = rms_norm(x, g_pre_mix) * (1.0 + sc1) + sh1
        h_ctx = rms_norm(ctx, g_pre_mix) * (1.0 + csc1) + csh1
        out_lat, out_ctx = hybrid_mixer(
            h_lat, h_ctx, rows, need_ctx, w_in[layer],
            s5_a_re[layer], s5_a_im[layer], s5_log_dt[layer], s5_b_re[layer], s5_b_im[layer],
            s5_c_re[layer], s5_c_im[layer], s5_d[layer], s5_w_glu[layer], s5_b_glu[layer],
            lru_conv_w[layer], lru_conv_b[layer], lru_w_rg[layer], lru_b_rg[layer],
            lru_w_ig[layer], lru_b_ig[layer], lru_lambda[layer], w_out[layer])
        x = x + gt1 * rms_norm(out_lat, g_post_mix)

        f_lat = swiglu(rms_norm(x, g_pre_ffn) * (1.0 + sc2) + sh2, w_ffn_in[layer], w_ffn_out[layer])
        x = x + gt2 * rms_norm(f_lat, g_post_ffn)

        if need_ctx:
            ctx = ctx + cgt1 * rms_norm(out_ctx, g_post_mix)
            f_ctx = swiglu(rms_norm(ctx, g_pre_ffn) * (1.0 + csc2) + csh2, w_ffn_in[layer], w_ffn_out[layer])
            ctx = ctx + cgt2 * rms_norm(f_ctx, g_post_ffn)
    return x
```

```python
import math
from contextlib import ExitStack
import numpy as np
import concourse.bass as bass
import concourse.mybir as mybir
from concourse.bass_utils import run_bass_kernel_spmd

F32 = mybir.dt.float32
BF = mybir.dt.bfloat16
AF = mybir.ActivationFunctionType
ALU = mybir.AluOpType

NL = 4
DM = 1024
NCTX = 256
NLAT = 4096
NT = NCTX + NLAT
NTI = NT // 8
NCH = NTI // 8
TN = 128
NTT = NT // TN
DFF = 2816
EPS = 1e-6
DEBUG = False
NL_RUN = 4


class EW:
    def __init__(self, b, name):
        self.b = b
        self.name = name
        self.sem = b.newsem()
        self.n = 0
        self.dsem = b.newsem()
        self.dn = 0
        self.e = None
        self.last = None

    def c(self, ins):
        if self.n >= 6000:
            self.sem = self.b.newsem()
            self.n = 0
        self.n += 1
        ins.then_inc(self.sem, 1)
        self.e.wait_ge(self.sem, self.n)
        return ins

    def mm(self, *a, **k):
        self.last = self.e.matmul(*a, **k)
        return self.last

    def tr(self, *a, **k):
        self.last = self.e.transpose(*a, **k)
        return self.last

    def dma(self, out, in_):
        if self.dn >= 550:
            self.dsem = self.b.newsem()
            self.dn = 0
        self.dn += 1
        self.e.dma_start(out=out, in_=in_).then_inc(self.dsem, 16)

    def fin(self):
        if self.last is not None:
            self.c(self.last)
            self.last = None
        if self.dn:
            self.e.wait_ge(self.dsem, 16 * self.dn)


class Bld:
    def __init__(self, nc):
        self.nc = nc
        self.es = ExitStack()
        self.sems = [self.es.enter_context(nc.semaphore(f"sm{i}")) for i in range(96)]
        self.si = 0
        self.ew = {k: EW(self, k) for k in ("pe", "act", "dve", "pool", "sp")}
        self.nblk = 0

    def newsem(self):
        s = self.sems[self.si]
        self.si += 1
        return s

    def sb(self, name, shape, dt):
        return self.es.enter_context(self.nc.sbuf_tensor(name, list(shape), dt))

    def blk(self, **fns):
        m = {"pe": "tensor", "act": "scalar", "dve": "vector", "pool": "gpsimd", "sp": "sync"}
        self.nblk += 1
        with self.nc.Block() as block:
            for name, fn in fns.items():
                if fn is None:
                    continue
                ew = self.ew[name]

                def run(e, ew=ew, fn=fn):
                    ew.e = e
                    fn(ew)
                    ew.fin()

                getattr(block, m[name])(run)

    def pipeline(self, tasks):
        nsteps = max((k + len(t) for k, t in enumerate(tasks)), default=0)
        for t in range(nsteps):
            fns = {}
            for k in range(max(0, t - 24), min(len(tasks), t + 1)):
                j = t - k
                if j < len(tasks[k]):
                    for name, fn in tasks[k][j].items():
                        fns.setdefault(name, []).append(fn)
            if not fns:
                continue
            self.blk(**{n: (lambda ew, l=l: [f(ew) for f in reversed(l)]) for n, l in fns.items()})


_UC = [0]


def uniq(n):
    _UC[0] += 1
    return f"{n}_{_UC[0]}"


def bc(ap, shape):
    return ap.to_broadcast(list(shape))


def build_program():
    nc = bass.Bass("TRN2", target_bir_lowering=False)
    b = Bld(nc)

    def din(name, shape, dt=F32):
        return nc.dram_tensor(name, list(shape), dt, kind="ExternalInput").ap()

    def dscr(name, shape, dt=F32):
        kind = "ExternalOutput" if DEBUG else "Internal"
        return nc.dram_tensor(name, list(shape), dt, kind=kind).ap()

    xs = din("xs", [DM, NT])
    cvec = din("cvec", [128, 8, 2])
    w_ada = din("w_ada", [NL, DM, 6 * DM])
    b_ada = din("b_ada", [128, NL, 48])
    gains = din("gains", [128, NL, 4, 8])
    w_in = din("w_in", [NL, DM, 1536])
    w_out = din("w_out", [NL, DM, DM])
    w_ffi = din("w_ffi", [NL, DM, 2 * DFF])
    w_ffo = din("w_ffo", [NL, DFF, DM])
    w_glu = din("w_glu", [NL, 512, 512])
    b_glu = din("b_glu", [128, NL, 4])
    sa_re = din("sa_re", [128, NL, 4, 16])
    sa_im = din("sa_im", [128, NL, 4, 16])
    sldt = din("sldt", [128, NL, 4, 16])
    sBP = din("sBP", [128, NL, 4, 16, 16])
    sBQ = din("sBQ", [128, NL, 4, 16, 16])
    sCP = din("sCP", [128, NL, 4, 16, 16])
    sCQ = din("sCQ", [128, NL, 4, 16, 16])
    sD = din("sD", [128, NL, 4, 8])
    cw = din("cw", [128, NL, 4, 4])
    cb = din("cb", [128, NL, 4])
    lruW = din("lruW", [NL, 2, 2, 4, 128, 128])
    lb_rg = din("lb_rg", [128, NL, 2, 4])
    lb_ig = din("lb_ig", [128, NL, 2, 4])
    llam = din("llam", [128, NL, 2, 4])
    cI = din("cI", [128, 128])
    cJ = din("cJ", [128, 128])
    cSW = din("cSW", [128, 128])
    cML = din("cML", [128, 128])
    cSG = din("cSG", [128, 4])
    yout = nc.dram_tensor("yout", [DM, NLAT], F32, kind="ExternalOutput").ap()

    xres = dscr("xres", [DM, NT])
    ud = dscr("ud", [8, 512, NTI], BF)
    yd = dscr("yd", [2, 8, 512, NTI], BF)
    xr_d = dscr("xr_d", [512, NT])
    gg_d = dscr("gg_d", [512, NT], BF)
    yl_d = dscr("yl_d", [512, NT], BF)
    yg_d = dscr("yg_d", [512, NT], BF)

    xres_v = xres.rearrange("(kc p) n -> p kc n", p=128)
    xs_v = xs.rearrange("(kc p) n -> p kc n", p=128)

    ident = b.sb("ident", [128, 128], F32)
    identb = b.sb("identb", [128, 128], BF)
    Jm = b.sb("Jm", [128, 128], F32)
    SWm = b.sb("SWm", [128, 128], F32)
    MLm = b.sb("MLm", [128, 128], F32)
    sg = b.sb("sg", [128, 4], F32)
    onesb = b.sb("onesb", [128, 128], BF)
    epsb = b.sb("epsb", [128, 1], F32)
    MOD = b.sb("MOD", [128, NL, 6, 8, 2], F32)
    GA = b.sb("GA", [128, NL, 4, 8], F32)
    DER = b.sb("DER", [128, NL, 4, 8, 2], F32)
    BADA = b.sb("BADA", [128, NL, 48], F32)
    ps = [b.es.enter_context(nc.psum_tensor(f"ps{i}", [128, 512], F32)) for i in range(8)]

    def c0(ew):
        ew.dma(ident[:], cI[:, :])
        ew.dma(Jm[:], cJ[:, :])
        ew.dma(SWm[:], cSW[:, :])
        ew.dma(MLm[:], cML[:, :])
        ew.dma(sg[:], cSG[:, :])
        ew.dma(GA[:], gains[:, :, :, :])
        ew.dma(BADA[:], b_ada[:, :, :])
    b.blk(sp=c0)

    def c1(ew):
        ew.c(ew.e.tensor_copy(identb[:], ident[:]))
        ew.c(ew.e.memset(onesb[:], 1.0 / 1024.0))
        ew.c(ew.e.memset(epsb[:], EPS))
    b.blk(dve=c1)

    def c2(ew):
        for kc in range(8):
            ew.dma(xres[kc * 128:(kc + 1) * 128, :], xs[kc * 128:(kc + 1) * 128, :])
    b.blk(sp=c2)

    with ExitStack() as st:
        cv = st.enter_context(nc.sbuf_tensor(uniq("cv"), [128, 8, 2], F32))
        scb = st.enter_context(nc.sbuf_tensor("scb", [128, 8, 2], BF))
        wad = [st.enter_context(nc.sbuf_tensor(f"wad{i}", [128, 8, 1024], BF)) for i in range(2)]

        b.blk(sp=lambda ew: ew.dma(cv[:], cvec[:, :, :]))
        b.blk(act=lambda ew: ew.c(ew.e.activation(out=scb[:], in_=cv[:], func=AF.Silu)))
        tasks = []
        for l in range(NL):
            for j in range(6):
                k = l * 6 + j
                wb = wad[k % 2]
                src = w_ada[l].rearrange("(kc p) n -> p kc n", p=128)[:, :, j * 1024:(j + 1) * 1024]

                def s_load(ew, wb=wb, src=src):
                    ew.dma(wb[:], src)

                def s_mm(ew, wb=wb, k=k):
                    for oc in range(8):
                        for kc in range(8):
                            ew.mm(ps[k % 2][:, oc * 2:oc * 2 + 2], wb[:, kc, oc * 128:(oc + 1) * 128],
                                  scb[:, kc, :], start=(kc == 0), stop=(kc == 7))

                def s_ev(ew, l=l, j=j, k=k):
                    ew.c(ew.e.tensor_tensor(
                        out=MOD[:, l, j, :, :],
                        in0=ps[k % 2][:, 0:16].rearrange("p (o w) -> p o w", w=2),
                        in1=bc(BADA[:, l, j * 8:(j + 1) * 8].unsqueeze(2), [128, 8, 2]),
                        op=ALU.add))
                tasks.append([{"pool": s_load}, {"pe": s_mm}, {"dve": s_ev}])
        b.pipeline(tasks)

        def derive(ew):
            for l in range(NL):
                for (dk, gk, mk, addone) in ((0, 0, 1, True), (1, 1, 2, False), (2, 2, 4, True), (3, 3, 5, False)):
                    g = bc(GA[:, l, gk, :].unsqueeze(2), [128, 8, 2])
                    if addone:
                        ew.c(ew.e.scalar_tensor_tensor(out=DER[:, l, dk, :, :], in0=MOD[:, l, mk, :, :],
                                                       scalar=1.0, in1=g, op0=ALU.add, op1=ALU.mult))
                    else:
                        ew.c(ew.e.tensor_tensor(out=DER[:, l, dk, :, :], in0=MOD[:, l, mk, :, :], in1=g,
                                                op=ALU.mult))
        b.blk(dve=derive)

    def norm_stages(l, ti, xt, sq, psS, rstd, tmp, h, which):
        w = 1 if ti < 2 else 0
        cols = slice(ti * TN, (ti + 1) * TN)
        dk = 0 if which == 0 else 2
        mk = 0 if which == 0 else 3

        def s0(ew):
            ew.dma(xt[:], xres_v[:, :, cols])

        def s1(ew):
            ew.c(ew.e.activation(out=sq[:], in_=xt[:], func=AF.Square))

        def s2(ew):
            for kc in range(8):
                ew.mm(psS, onesb[:], sq[:, kc, :], start=(kc == 0), stop=(kc == 7))

        def s2b(ew):
            ew.c(ew.e.activation(out=rstd[:], in_=psS, func=AF.Sqrt, bias=epsb[:, 0:1]))

        def s3(ew):
            ew.c(ew.e.reciprocal(rstd[:], rstd[:]))
            ew.c(ew.e.tensor_tensor(out=tmp[:], in0=xt[:], in1=bc(rstd[:].unsqueeze(1), [128, 8, TN]),
                                    op=ALU.mult))

        def s4(ew):
            for kc in range(8):
                ew.c(ew.e.activation(out=h[:, kc, :], in_=tmp[:, kc, :], func=AF.Identity,
                                     scale=DER[:, l, dk, kc, w:w + 1], bias=MOD[:, l, mk, kc, w:w + 1]))
        return [{"sp": s0}, {"act": s1}, {"pe": s2}, {"act": s2b}, {"dve": s3}, {"act": s4}]

    def post_stages(l, ti, psO2, xt, osb, sq, psS, rstd, tmp, which, last_layer):
        w = 1 if ti < 2 else 0
        cols = slice(ti * TN, (ti + 1) * TN)
        dk = 1 if which == 0 else 3

        def s1(ew):
            for hf in range(2):
                ew.c(ew.e.activation(out=osb[:, hf * 4:(hf + 1) * 4, :],
                                     in_=psO2[hf][:, 0:4 * TN].rearrange("p (c n) -> p c n", c=4), func=AF.Copy))

        def s1l(ew):
            ew.dma(xt[:], xres_v[:, :, cols])

        def s1b(ew):
            ew.c(ew.e.activation(out=sq[:], in_=osb[:], func=AF.Square))

        def s2(ew):
            for kc in range(8):
                ew.mm(psS, onesb[:], sq[:, kc, :], start=(kc == 0), stop=(kc == 7))

        def s2b(ew):
            ew.c(ew.e.activation(out=rstd[:], in_=psS, func=AF.Sqrt, bias=epsb[:, 0:1]))

        def s3(ew):
            ew.c(ew.e.reciprocal(rstd[:], rstd[:]))
            for mc in range(8):
                ew.c(ew.e.scalar_tensor_tensor(out=tmp[:, mc, :], in0=osb[:, mc, :],
                                               scalar=DER[:, l, dk, mc, w:w + 1], in1=rstd[:],
                                               op0=ALU.mult, op1=ALU.mult))

        def s4(ew):
            ew.c(ew.e.tensor_tensor(out=xt[:], in0=xt[:], in1=tmp[:], op=ALU.add))

        def s5(ew):
            ew.dma(xres_v[:, :, cols], xt[:])
            if last_layer and which == 1 and ti >= 2:
                ew.dma(yout.rearrange("(kc p) n -> p kc n", p=128)[:, :, (ti - 2) * TN:(ti - 1) * TN], xt[:])
        return [{"act": s1, "sp": s1l}, {"act": s1b}, {"pe": s2}, {"act": s2b}, {"dve": s3}, {"pool": s4}, {"sp": s5}]

    for l in range(NL_RUN):
        last = (l == NL - 1)
        with ExitStack() as st:
            sbt = lambda n, s, d: st.enter_context(nc.sbuf_tensor(uniq(n), list(s), d))
            win = sbt("win", [128, 8, 1536], BF)
            ut_all = sbt("ut_all", [128, 4, 8, NTI], BF)
            NB = 5
            xt = [sbt(f"xt{i}", [128, 8, TN], F32) for i in range(NB)]
            sq = [sbt(f"sq{i}", [128, 8, TN], BF) for i in range(NB)]
            rstd = [sbt(f"rstd{i}", [128, TN], F32) for i in range(NB)]
            tmp = [sbt(f"tmp{i}", [128, 8, TN], F32) for i in range(NB)]
            hb = [sbt(f"hb{i}", [128, 8, TN], BF) for i in range(NB)]
            xro = [sbt(f"xro{i}", [128, 4, TN], F32) for i in range(NB)]
            ggo = [sbt(f"ggo{i}", [128, 4, TN], BF) for i in range(NB)]
            b.blk(pool=lambda ew: ew.dma(win[:], w_in[l].rearrange("(kc p) n -> p kc n", p=128)))
            tasks = []
            for ti in range(NTT):
                k = ti % NB
                pw = [ps[(ti % 2) * 3 + j] for j in range(3)]
                psS = ps[6 + ti % 2][:, 0:TN]
                stg = norm_stages(l, ti, xt[k], sq[k], psS, rstd[k], tmp[k], hb[k], 0)
                cols = slice(ti * TN, (ti + 1) * TN)

                def s5(ew, k=k, pw=pw):
                    for mc in range(12):
                        for kc in range(8):
                            ew.mm(pw[mc // 4][:, (mc % 4) * TN:(mc % 4 + 1) * TN],
                                  win[:, kc, mc * 128:(mc + 1) * 128], hb[k][:, kc, :],
                                  start=(kc == 0), stop=(kc == 7))

                def s6d(ew, ti=ti, pw=pw):
                    ew.c(ew.e.tensor_copy(
                        out=ut_all[:, :, :, ti * 16:(ti + 1) * 16],
                        in_=pw[0][:, 0:4 * TN].rearrange("p (c i s) -> p c s i", c=4, s=8)))

                def s6a(ew, k=k, pw=pw):
                    ew.c(ew.e.activation(out=xro[k][:], in_=pw[1][:, 0:4 * TN].rearrange("p (c n) -> p c n", c=4),
                                         func=AF.Copy))
                    ew.c(ew.e.activation(out=ggo[k][:], in_=pw[2][:, 0:4 * TN].rearrange("p (c n) -> p c n", c=4),
                                         func=AF.Gelu_apprx_tanh))

                def s7(ew, k=k, cols=cols):
                    ew.dma(xr_d.rearrange("(c p) n -> p c n", p=128)[:, :, cols], xro[k][:])
                    ew.dma(gg_d.rearrange("(c p) n -> p c n", p=128)[:, :, cols], ggo[k][:])
                tasks.append(stg + [{"pe": s5}, {"dve": s6d, "act": s6a}, {"sp": s7}])
            b.pipeline(tasks)

            def uout(ew):
                for s in range(8):
                    ew.dma(ud[s].rearrange("(c p) i -> p c i", p=128), ut_all[:, :, s, :])
            b.blk(sp=uout)

        s5_phase(nc, b, l, ps, ud, yd, ident, Jm, SWm, MLm, sg,
                 sa_re, sa_im, sldt, sBP, sBQ, sCP, sCQ, sD)

        lru_phase(nc, b, l, ps, xr_d, gg_d, yl_d, cw, cb, lruW, lb_rg, lb_ig, llam)

        with ExitStack() as st:
            sbt = lambda n, s, d: st.enter_context(nc.sbuf_tensor(uniq(n), list(s), d))
            yf_ = [sbt(f"yf{i}", [128, 2, 8, NTI], BF) for i in range(2)]
            ys_ = [sbt(f"ys{i}", [128, NT], F32) for i in range(2)]
            ygo = [sbt(f"ygo{i}", [128, NT], BF) for i in range(2)]
            tasks = []
            for cc in range(4):
                k = cc % 2
                rows = slice(cc * 128, (cc + 1) * 128)

                def q0(ew, k=k, rows=rows):
                    for d in range(2):
                        ew.dma(yf_[k][:, d, :, :], yd[d, :, rows, :].rearrange("t p i -> p t i"))

                def q1(ew, k=k):
                    ew.c(ew.e.tensor_tensor(out=ys_[k][:].rearrange("p (i t) -> p t i", t=8),
                                            in0=yf_[k][:, 0, :, :], in1=yf_[k][:, 1, :, :], op=ALU.add))

                def q2(ew, k=k):
                    ew.c(ew.e.activation(out=ygo[k][:], in_=ys_[k][:], func=AF.Gelu_apprx_tanh))

                def q3(ew, k=k, rows=rows):
                    ew.dma(yg_d[rows, :], ygo[k][:])
                tasks.append([{"sp": q0}, {"dve": q1}, {"act": q2}, {"sp": q3}])
            b.pipeline(tasks)

        with ExitStack() as st:
            sbt = lambda n, s, d: st.enter_context(nc.sbuf_tensor(uniq(n), list(s), d))
            wo = sbt("wo", [128, 8, 1024], BF)
            wg = sbt("wg", [128, 4, 512], BF)
            bg = sbt("bg", [128, 4], F32)
            NB = 7
            xt = [sbt(f"xt{i}", [128, 8, TN], F32) for i in range(NB)]
            osb = [sbt(f"osb{i}", [128, 8, TN], F32) for i in range(NB)]
            sq = [sbt(f"sq{i}", [128, 8, TN], BF) for i in range(NB)]
            rstd = [sbt(f"rstd{i}", [128, TN], F32) for i in range(NB)]
            tmp = [sbt(f"tmp{i}", [128, 8, TN], F32) for i in range(NB)]
            yg = [sbt(f"yg{i}", [128, 4, TN], BF) for i in range(NB)]
            sig = [sbt(f"sig{i}", [128, 4, TN], F32) for i in range(NB)]
            ymix = [sbt(f"ymix{i}", [128, 8, TN], BF) for i in range(NB)]

            def wl(ew):
                ew.dma(wo[:], w_out[l].rearrange("(kc p) n -> p kc n", p=128))
                ew.dma(wg[:], w_glu[l].rearrange("(kc p) n -> p kc n", p=128))
            b.blk(pool=wl, sp=lambda ew: ew.dma(bg[:], b_glu[:, l, :]))
            tasks = []
            tlist = range(NTT) if not last else range(2, NTT)
            for n_, ti in enumerate(tlist):
                k = n_ % NB
                cols = slice(ti * TN, (ti + 1) * TN)
                psG = ps[n_ % 2]
                psO2 = [ps[2 + (n_ % 2) * 2], ps[3 + (n_ % 2) * 2]]
                psS = ps[6 + n_ % 2][:, 0:TN]

                def a0(ew, k=k, cols=cols):
                    ew.dma(yg[k][:], yg_d.rearrange("(c p) n -> p c n", p=128)[:, :, cols])
                    ew.dma(ymix[k][:, 4:8, :], yl_d.rearrange("(c p) n -> p c n", p=128)[:, :, cols])

                def a3(ew, k=k, psG=psG):
                    for mc in range(4):
                        for kc in range(4):
                            ew.mm(psG[:, mc * TN:(mc + 1) * TN], wg[:, kc, mc * 128:(mc + 1) * 128],
                                  yg[k][:, kc, :], start=(kc == 0), stop=(kc == 3))

                def a4(ew, k=k, psG=psG):
                    for mc in range(4):
                        ew.c(ew.e.activation(out=sig[k][:, mc, :], in_=psG[:, mc * TN:(mc + 1) * TN],
                                             func=AF.Sigmoid, bias=bg[:, mc:mc + 1]))

                def a5(ew, k=k):
                    ew.c(ew.e.tensor_tensor(out=ymix[k][:, 0:4, :], in0=yg[k][:], in1=sig[k][:], op=ALU.mult))

                def a6(ew, k=k, psO2=psO2):
                    for mc in range(8):
                        for kc in range(8):
                            ew.mm(psO2[mc // 4][:, (mc % 4) * TN:(mc % 4 + 1) * TN],
                                  wo[:, kc, mc * 128:(mc + 1) * 128], ymix[k][:, kc, :],
                                  start=(kc == 0), stop=(kc == 7))
                stg = [{"sp": a0}, {"pe": a3}, {"act": a4}, {"dve": a5}, {"pe": a6}]
                stg += post_stages(l, ti, psO2, xt[k], osb[k], sq[k], psS, rstd[k], tmp[k], 0, last)
                tasks.append(stg)
            b.pipeline(tasks)

        with ExitStack() as st:
            sbt = lambda n, s, d: st.enter_context(nc.sbuf_tensor(uniq(n), list(s), d))
            wfi = sbt("wfi", [128, 8, 2 * DFF], BF)
            wfo = sbt("wfo", [128, 22, 1024], BF)
            NB = 2
            xt = [sbt(f"xt{i}", [128, 8, TN], F32) for i in range(NB)]
            xt2 = [sbt(f"xtb{i}", [128, 8, TN], F32) for i in range(NB)]
            osb = [sbt(f"osb{i}", [128, 8, TN], F32) for i in range(NB)]
            sq = [sbt(f"sq{i}", [128, 8, TN], BF) for i in range(NB)]
            sq2 = [sbt(f"sqb{i}", [128, 8, TN], BF) for i in range(NB)]
            rstd = [sbt(f"rstd{i}", [128, TN], F32) for i in range(NB)]
            rstd2 = [sbt(f"rstdb{i}", [128, TN], F32) for i in range(NB)]
            tmp = [sbt(f"tmp{i}", [128, 8, TN], F32) for i in range(NB)]
            tmp2 = [sbt(f"tmpb{i}", [128, 8, TN], F32) for i in range(NB)]
            hb = [sbt(f"hb{i}", [128, 8, TN], BF) for i in range(NB)]
            act_ = [sbt(f"act{i}", [128, 22, TN], BF) for i in range(NB)]
            sgt = [sbt(f"sgt{i}", [128, 2, TN], F32) for i in range(3)]

            def wl2(ew):
                ew.dma(wfi[:], w_ffi[l].rearrange("(kc p) n -> p kc n", p=128))
                ew.dma(wfo[:], w_ffo[l].rearrange("(kc p) n -> p kc n", p=128))
            b.blk(pool=wl2)
            tlist = list(range(NTT) if not last else range(2, NTT))
            tasks = []
            gcount = [0]

            def norm_task(n_):
                ti = tlist[n_]
                k = n_ % NB
                psS = ps[7][:, (n_ % 2) * TN:(n_ % 2 + 1) * TN]
                return norm_stages(l, ti, xt[k], sq[k], psS, rstd[k], tmp[k], hb[k], 1)

            def gu_task(n_, r):
                k = n_ % NB
                gi = gcount[0]
                gcount[0] += 1
                pg = ps[gi % 3]
                sgb = sgt[gi % 3]

                def g0(ew):
                    for jj in range(2):
                        j = r * 2 + jj
                        for half in range(2):
                            co = half * DFF + j * 128
                            for kc in range(8):
                                ew.mm(pg[:, (jj * 2 + half) * TN:(jj * 2 + half + 1) * TN],
                                      wfi[:, kc, co:co + 128], hb[k][:, kc, :], start=(kc == 0), stop=(kc == 7))

                def g1a(ew):
                    ew.c(ew.e.activation(
                        out=sgb[:], in_=pg[:, 0:4 * TN].rearrange("p (j h n) -> p j h n", j=2, h=2)[:, :, 0, :],
                        func=AF.Silu))

                def g1d(ew):
                    ew.c(ew.e.tensor_tensor(
                        out=act_[k][:, r * 2:r * 2 + 2, :], in0=sgb[:],
                        in1=pg[:, 0:4 * TN].rearrange("p (j h n) -> p j h n", j=2, h=2)[:, :, 1, :], op=ALU.mult))
                return [{"pe": g0}, {"act": g1a}, {"dve": g1d}]

            def out_task(n_):
                ti = tlist[n_]
                k = n_ % NB
                psO2 = [ps[3 + (n_ % 2) * 2], ps[4 + (n_ % 2) * 2]]
                psS = ps[7][:, (2 + n_ % 2) * TN:(3 + n_ % 2) * TN]

                def o0(ew):
                    for mc in range(8):
                        for kc in range(22):
                            ew.mm(psO2[mc // 4][:, (mc % 4) * TN:(mc % 4 + 1) * TN],
                                  wfo[:, kc, mc * 128:(mc + 1) * 128], act_[k][:, kc, :],
                                  start=(kc == 0), stop=(kc == 21))
                return [{"pe": o0}] + post_stages(l, ti, psO2, xt2[k], osb[k], sq2[k], psS, rstd2[k], tmp2[k], 1, last)

            ntl = len(tlist)
            tasks.append(norm_task(0))
            for _ in range(6):
                tasks.append([])
            for n_ in range(ntl):
                for r in range(11):
                    tasks.append(gu_task(n_, r))
                    if r == 4 and n_ + 1 < ntl:
                        tasks.append(norm_task(n_ + 1))
                tasks.append([])
                tasks.append([])
                tasks.append(out_task(n_))
            b.pipeline(tasks)

    b.es.close()
    return nc


def s5_phase(nc, b, l, ps, ud, yd, ident, Jm, SWm, MLm, sg,
             sa_re, sa_im, sldt, sBP, sBQ, sCP, sCQ, sD):
    NP = 16
    for qb in range(4):
        with ExitStack() as st:
            sbt = lambda n, s, d: st.enter_context(nc.sbuf_tensor(uniq(n), list(s), d))
            are = sbt("are", [128, NP], F32)
            aim = sbt("aim", [128, NP], F32)
            ldt = sbt("ldt", [128, NP], F32)
            BP = sbt("BP", [128, NP, 16], F32)
            BQ = sbt("BQ", [128, NP, 16], F32)
            CP = sbt("CP", [128, NP, 16], F32)
            CQ = sbt("CQ", [128, NP, 16], F32)
            Dp = sbt("Dp", [128, 8], F32)
            U2 = sbt("U2", [128, NP, NTI], BF)
            T = {}
            for nm in ("dt", "ar", "ai", "mg", "c0", "s0", "t1", "t2", "t3", "den", "fre", "fim",
                       "fres", "fims", "are64", "aim64", "a8re", "a8ims"):
                T[nm] = sbt("T" + nm, [128, NP], F32)
            PW = sbt("PW", [128, 2, 9, NP], F32)
            PI = sbt("PI", [128, 2, 8, NP], F32)
            fBP = sbt("fBP", [128, NP, 16], F32)
            fBQ = sbt("fBQ", [128, NP, 16], F32)
            AL = sbt("AL", [128, NP, 8], F32)
            BE = sbt("BE", [128, NP, 8], F32)
            Zst = sbt("Zst", [128, NP, 8, 16], F32)
            W1 = sbt("W1", [128, NP, 128], BF)
            W2 = sbt("W2", [128, NP, 8, 16], BF)
            E0 = sbt("E0", [128, NP, 8, 16], F32)
            CT0 = sbt("CT0", [128, NP, 8, 16], F32)
            TP = sbt("TP", [128, NP, 128], BF)
            R8 = sbt("R8", [128, NP, 128], BF)
            tA = sbt("tA", [128, NP, 8, 16], F32)
            tB = sbt("tB", [128, NP, 8, 16], F32)
            Sa = sbt("Sa", [128, NP, NCH], BF)
            Sb = sbt("Sb", [128, NP, NCH], BF)
            Fm = sbt("Fm", [128, NP, NCH], F32)
            Fs = sbt("Fs", [128, NP, NCH], F32)
            Si = sbt("Si", [128, NP, NCH], F32)
            Ss = sbt("Ss", [128, NP, NCH], F32)
            St = sbt("St", [128, NP, NTI], BF)
            Yo = sbt("Yo", [128, NP, NTI], BF)
            l2t = [sbt(f"l2t{i}", [128, 4, 8], F32) for i in range(2)]

            def ld(ew):
                for d in range(2):
                    sl = slice(d * 8, (d + 1) * 8)
                    hs = slice((qb % 2) * 8, (qb % 2) * 8 + 8)
                    hb_ = d * 2 + qb // 2
                    ew.dma(are[:, sl], sa_re[:, l, hb_, hs])
                    ew.dma(aim[:, sl], sa_im[:, l, hb_, hs])
                    ew.dma(ldt[:, sl], sldt[:, l, hb_, hs])
                    ew.dma(BP[:, sl, :], sBP[:, l, hb_, hs, :])
                    ew.dma(BQ[:, sl, :], sBQ[:, l, hb_, hs, :])
                    ew.dma(CP[:, sl, :], sCP[:, l, hb_, hs, :])
                    ew.dma(CQ[:, sl, :], sCQ[:, l, hb_, hs, :])
                ew.dma(Dp[:], sD[:, l, qb, :])
                for s in range(8):
                    src = ud[s, qb * 128:(qb + 1) * 128, :].rearrange("(g p) i -> p g i", p=16)
                    ew.dma(U2[s * 16:(s + 1) * 16, 0:8, :], src)
                    ew.dma(U2[(7 - s) * 16:(8 - s) * 16, 8:16, :], src)
            b.blk(sp=ld)

            def sc1(ew):
                ew.c(ew.e.activation(out=T["dt"][:], in_=ldt[:], func=AF.Exp))
            b.blk(act=sc1)

            def sc2(ew):
                ew.c(ew.e.tensor_tensor(out=T["ar"][:], in0=are[:], in1=T["dt"][:], op=ALU.mult))
                ew.c(ew.e.tensor_tensor(out=T["ai"][:], in0=aim[:], in1=T["dt"][:], op=ALU.mult))
                ew.c(ew.e.tensor_scalar(out=T["t1"][:], in0=T["ai"][:], scalar1=1.0 / 16.0, scalar2=math.pi / 2,
                                        op0=ALU.mult, op1=ALU.add))
            b.blk(dve=sc2)

            def sc3(ew):
                ew.c(ew.e.activation(out=T["mg"][:], in_=T["ar"][:], func=AF.Exp, scale=1.0 / 16.0))
                ew.c(ew.e.activation(out=T["s0"][:], in_=T["ai"][:], func=AF.Sin, scale=1.0 / 16.0))
                ew.c(ew.e.activation(out=T["c0"][:], in_=T["t1"][:], func=AF.Sin))
                ew.c(ew.e.activation(out=T["t2"][:], in_=T["ar"][:], func=AF.Exp, scale=-1.0 / 16.0))
            b.blk(act=sc3)

            def cmul(ew, o_re, o_im, a_re, a_im, b_re, b_im, t1, t2):
                ew.c(ew.e.tensor_tensor(out=t1, in0=a_re, in1=b_re, op=ALU.mult))
                ew.c(ew.e.tensor_tensor(out=t2, in0=a_im, in1=b_im, op=ALU.mult))
                ew.c(ew.e.tensor_tensor(out=t2, in0=t1, in1=t2, op=ALU.subtract))
                ew.c(ew.e.tensor_tensor(out=t1, in0=a_re, in1=b_im, op=ALU.mult))
                ew.c(ew.e.tensor_tensor(out=o_im, in0=a_im, in1=b_re, op=ALU.mult))
                ew.c(ew.e.tensor_tensor(out=o_im, in0=o_im, in1=t1, op=ALU.add))
                ew.c(ew.e.tensor_copy(o_re, t2))

            def sc4(ew):
                t1, t2, t3 = T["t1"][:], T["t3"][:], T["den"][:]
                ew.c(ew.e.tensor_tensor(out=PW[:, 0, 1, :], in0=T["mg"][:], in1=T["c0"][:], op=ALU.mult))
                ew.c(ew.e.tensor_tensor(out=PW[:, 1, 1, :], in0=T["mg"][:], in1=T["s0"][:], op=ALU.mult))
                ew.c(ew.e.tensor_tensor(out=PI[:, 0, 1, :], in0=T["t2"][:], in1=T["c0"][:], op=ALU.mult))
                ew.c(ew.e.scalar_tensor_tensor(out=PI[:, 1, 1, :], in0=T["t2"][:], scalar=-1.0, in1=T["s0"][:],
                                               op0=ALU.mult, op1=ALU.mult))
                for _ in range(4):
                    cmul(ew, PW[:, 0, 1, :], PW[:, 1, 1, :], PW[:, 0, 1, :], PW[:, 1, 1, :],
                         PW[:, 0, 1, :], PW[:, 1, 1, :], t1, t2)
                    cmul(ew, PI[:, 0, 1, :], PI[:, 1, 1, :], PI[:, 0, 1, :], PI[:, 1, 1, :],
                         PI[:, 0, 1, :], PI[:, 1, 1, :], t1, t2)
                ew.c(ew.e.memset(PW[:, 0, 0, :], 1.0))
                ew.c(ew.e.memset(PW[:, 1, 0, :], 0.0))
                ew.c(ew.e.memset(PI[:, 0, 0, :], 1.0))
                ew.c(ew.e.memset(PI[:, 1, 0, :], 0.0))
                for k in range(2, 9):
                    cmul(ew, PW[:, 0, k, :], PW[:, 1, k, :], PW[:, 0, k - 1, :], PW[:, 1, k - 1, :],
                         PW[:, 0, 1, :], PW[:, 1, 1, :], t1, t2)
                for k in range(2, 8):
                    cmul(ew, PI[:, 0, k, :], PI[:, 1, k, :], PI[:, 0, k - 1, :], PI[:, 1, k - 1, :],
                         PI[:, 0, 1, :], PI[:, 1, 1, :], t1, t2)
                ew.c(ew.e.tensor_copy(T["are64"][:], PW[:, 0, 8, :]))
                ew.c(ew.e.tensor_copy(T["aim64"][:], PW[:, 1, 8, :]))
                for _ in range(3):
                    cmul(ew, T["are64"][:], T["aim64"][:], T["are64"][:], T["aim64"][:],
                         T["are64"][:], T["aim64"][:], t1, t2)
                nr = T["c0"][:]
                ew.c(ew.e.tensor_scalar_add(nr, PW[:, 0, 1, :], -1.0))
                ew.c(ew.e.tensor_tensor(out=t1, in0=are[:], in1=are[:], op=ALU.mult))
                ew.c(ew.e.tensor_tensor(out=t2, in0=aim[:], in1=aim[:], op=ALU.mult))
                ew.c(ew.e.tensor_tensor(out=t3, in0=t1, in1=t2, op=ALU.add))
                ew.c(ew.e.reciprocal(t3, t3))
                ew.c(ew.e.tensor_tensor(out=t1, in0=nr, in1=are[:], op=ALU.mult))
                ew.c(ew.e.tensor_tensor(out=t2, in0=PW[:, 1, 1, :], in1=aim[:], op=ALU.mult))
                ew.c(ew.e.tensor_tensor(out=t1, in0=t1, in1=t2, op=ALU.add))
                ew.c(ew.e.tensor_tensor(out=T["fre"][:], in0=t1, in1=t3, op=ALU.mult))
                ew.c(ew.e.tensor_tensor(out=t1, in0=PW[:, 1, 1, :], in1=are[:], op=ALU.mult))
                ew.c(ew.e.tensor_tensor(out=t2, in0=nr, in1=aim[:], op=ALU.mult))
                ew.c(ew.e.tensor_tensor(out=t1, in0=t1, in1=t2, op=ALU.subtract))
                ew.c(ew.e.tensor_tensor(out=T["fim"][:], in0=t1, in1=t3, op=ALU.mult))
                ew.c(ew.e.tensor_scalar(out=T["fims"][:], in0=T["fim"][:], scalar1=sg[:, 0:1], scalar2=None,
                                        op0=ALU.mult))
                f_re = bc(T["fre"][:].unsqueeze(2), [128, NP, 16])
                f_ims = bc(T["fims"][:].unsqueeze(2), [128, NP, 16])
                ta = tA[:, :, 0, :]
                ew.c(ew.e.tensor_tensor(out=fBP[:], in0=BP[:], in1=f_re, op=ALU.mult))
                ew.c(ew.e.tensor_tensor(out=ta, in0=BQ[:], in1=f_ims, op=ALU.mult))
                ew.c(ew.e.tensor_tensor(out=fBP[:], in0=fBP[:], in1=ta, op=ALU.add))
                ew.c(ew.e.tensor_tensor(out=fBQ[:], in0=BQ[:], in1=f_re, op=ALU.mult))
                ew.c(ew.e.tensor_tensor(out=ta, in0=BP[:], in1=f_ims, op=ALU.mult))
                ew.c(ew.e.tensor_tensor(out=fBQ[:], in0=fBQ[:], in1=ta, op=ALU.subtract))

                def ctab(out, P_, Q_, pw_re_fn, pw_im_fn, a_sgn_col, b_sgn_col, b_neg=False):
                    for j in range(8):
                        if a_sgn_col is None:
                            ew.c(ew.e.tensor_copy(AL[:, :, j], pw_re_fn(j)))
                        else:
                            ew.c(ew.e.tensor_scalar(out=AL[:, :, j], in0=pw_re_fn(j), scalar1=sg[:, a_sgn_col:a_sgn_col + 1],
                                                    scalar2=None, op0=ALU.mult))
                        if b_sgn_col is None:
                            ew.c(ew.e.tensor_scalar(out=BE[:, :, j], in0=pw_im_fn(j), scalar1=(-1.0 if b_neg else 1.0),
                                                    scalar2=None, op0=ALU.mult))
                        else:
                            ew.c(ew.e.tensor_scalar(out=BE[:, :, j], in0=pw_im_fn(j), scalar1=sg[:, b_sgn_col:b_sgn_col + 1],
                                                    scalar2=None, op0=ALU.mult))
                    sh = [128, NP, 8, 16]
                    ew.c(ew.e.tensor_tensor(out=tA[:], in0=bc(P_.unsqueeze(2), sh), in1=bc(AL[:].unsqueeze(3), sh), op=ALU.mult))
                    ew.c(ew.e.tensor_tensor(out=tB[:], in0=bc(Q_.unsqueeze(2), sh), in1=bc(BE[:].unsqueeze(3), sh), op=ALU.mult))
                    ew.c(ew.e.tensor_tensor(out=out, in0=tA[:], in1=tB[:], op=ALU.add))
                ctab(Zst[:], fBP[:], fBQ[:], lambda j: PW[:, 0, 7 - j, :], lambda j: PW[:, 1, 7 - j, :], None, 0)
                ctab(W2[:], CP[:], CQ[:], lambda j: PW[:, 0, j + 1, :], lambda j: PW[:, 1, j + 1, :], 1, None, True)
                ctab(E0[:], fBP[:], fBQ[:], lambda j: PI[:, 0, j, :], lambda j: PI[:, 1, j, :], 1, None, True)
                ctab(CT0[:], CP[:], CQ[:], lambda j: PW[:, 0, j, :], lambda j: PW[:, 1, j, :], None, 0)
                ew.c(ew.e.tensor_copy(T["a8re"][:], PW[:, 0, 8, :]))
                ew.c(ew.e.tensor_scalar(out=T["a8ims"][:], in0=PW[:, 1, 8, :], scalar1=sg[:, 1:2], scalar2=None,
                                        op0=ALU.mult))
                for p_ in range(NP):
                    ew.c(ew.e.tensor_scalar(out=tA[:, 0, :, :].rearrange("p a b -> p (a b)"), in0=Jm[:],
                                            scalar1=T["a8ims"][:, p_:p_ + 1], scalar2=None, op0=ALU.mult))
                    ew.c(ew.e.scalar_tensor_tensor(out=R8[:, p_, :], in0=ident[:], scalar=T["a8re"][:, p_:p_ + 1],
                                                   in1=tA[:, 0, :, :].rearrange("p a b -> p (a b)"),
                                                   op0=ALU.mult, op1=ALU.add))
            b.blk(dve=sc4)

            for half in range(4):
                prs = range(half * 4, half * 4 + 4)

                def tp(ew, prs=prs):
                    for q, p_ in enumerate(prs):
                        ew.tr(ps[0][:, q * 128:(q + 1) * 128], Zst[:, p_, :, :].rearrange("p a b -> p (a b)"), ident[:])
                        ew.mm(ps[1][:, q * 128:(q + 1) * 128], E0[:, p_, :, :].rearrange("p a b -> p (a b)"),
                              CT0[:, p_, :, :].rearrange("p a b -> p (a b)"), start=True, stop=True)
                b.blk(pe=tp)

                def tpe(ew, prs=prs):
                    for q, p_ in enumerate(prs):
                        ew.c(ew.e.tensor_copy(W1[:, p_, :], ps[0][:, q * 128:(q + 1) * 128]))
                        ew.c(ew.e.tensor_tensor(out=tA[:, 0, :, :].rearrange("p a b -> p (a b)"),
                                                in0=ps[1][:, q * 128:(q + 1) * 128], in1=MLm[:], op=ALU.mult))
                        if p_ < 8:
                            ew.c(ew.e.scalar_tensor_tensor(out=TP[:, p_, :], in0=ident[:], scalar=Dp[:, p_:p_ + 1],
                                                           in1=tA[:, 0, :, :].rearrange("p a b -> p (a b)"),
                                                           op0=ALU.mult, op1=ALU.add))
                        else:
                            ew.c(ew.e.tensor_copy(TP[:, p_, :], tA[:, 0, :, :].rearrange("p a b -> p (a b)")))
                b.blk(dve=tpe)

            def tiles(p_, r):
                off = r if p_ < 8 else 7 - r
                return slice(off, NTI, 8)

            def sweep(down):
                cur, nxt = Sa, Sb
                for r in range(8 if not down else 7):
                    def pe_(ew, r=r, cur=cur):
                        for p_ in range(NP):
                            o = ps[p_ // 7][:, (p_ % 7) * NCH:(p_ % 7 + 1) * NCH]
                            first = True
                            if down:
                                ew.mm(o, R8[:, p_, :], St[:, p_, tiles(p_, r)], start=True, stop=False)
                                first = False
                            elif r > 0:
                                ew.mm(o, R8[:, p_, :], cur[:, p_, :], start=True, stop=False)
                                first = False
                            ew.mm(o, W1[:, p_, :], U2[:, p_, tiles(p_, r)], start=first, stop=True)
                    b.blk(pe=pe_)

                    def ev(ew, r=r, nxt=nxt):
                        for bk in range(3):
                            n_p = 7 if bk < 2 else 2
                            prs = slice(bk * 7, bk * 7 + n_p)
                            src = ps[bk][:, 0:n_p * NCH].rearrange("p (a m) -> p a m", m=NCH)
                            if down:
                                for p_ in range(bk * 7, bk * 7 + n_p):
                                    ew.c(ew.e.tensor_copy(St[:, p_, tiles(p_, r + 1)],
                                                          ps[bk][:, (p_ - bk * 7) * NCH:(p_ - bk * 7 + 1) * NCH]))
                            elif r < 7:
                                ew.c(ew.e.tensor_copy(nxt[:, prs, :], src))
                            else:
                                ew.c(ew.e.tensor_copy(Fm[:, prs, :], src))
                    b.blk(dve=ev)
                    cur, nxt = nxt, cur

            sweep(False)

            def swp(ew):
                for bk in range(3):
                    n_p = 7 if bk < 2 else 2
                    ew.mm(ps[bk][:, 0:n_p * NCH], SWm[:], Fm[:, bk * 7:bk * 7 + n_p, :].rearrange("p a m -> p (a m)"),
                          start=True, stop=True)
            b.blk(pe=swp)

            def swe(ew):
                for bk in range(3):
                    n_p = 7 if bk < 2 else 2
                    ew.c(ew.e.tensor_copy(Fs[:, bk * 7:bk * 7 + n_p, :],
                                          ps[bk][:, 0:n_p * NCH].rearrange("p (a m) -> p a m", m=NCH)))
                ew.c(ew.e.memset(Si[:], 0.0))
                ew.c(ew.e.memset(Ss[:], 0.0))
            b.blk(dve=swe)

            order_f = list(range(NCH))
            order_r = [3, 2, 1, 0] + list(range(NCH - 1, 3, -1))

            def l2(ew, prs, order, tt):
                a_re = T["are64"][:, prs]
                a_im = T["aim64"][:, prs]
                for k in range(NCH - 1):
                    m, m2 = order[k], order[k + 1]
                    t = ew.e.tensor_tensor
                    ew.c(t(out=tt[:, 0, :], in0=a_re, in1=Si[:, prs, m], op=ALU.mult))
                    ew.c(t(out=tt[:, 1, :], in0=a_im, in1=Ss[:, prs, m], op=ALU.mult))
                    ew.c(t(out=tt[:, 2, :], in0=a_re, in1=Ss[:, prs, m], op=ALU.mult))
                    ew.c(t(out=tt[:, 3, :], in0=a_im, in1=Si[:, prs, m], op=ALU.mult))
                    ew.c(t(out=tt[:, 0, :], in0=tt[:, 0, :], in1=tt[:, 1, :], op=ALU.add))
                    ew.c(t(out=tt[:, 2, :], in0=tt[:, 2, :], in1=tt[:, 3, :], op=ALU.subtract))
                    ew.c(t(out=Si[:, prs, m2], in0=tt[:, 0, :], in1=Fm[:, prs, m], op=ALU.add))
                    ew.c(t(out=Ss[:, prs, m2], in0=tt[:, 2, :], in1=Fs[:, prs, m], op=ALU.add))
            b.blk(dve=lambda ew: l2(ew, slice(0, 8), order_f, l2t[0]),
                  pool=lambda ew: l2(ew, slice(8, 16), order_r, l2t[1]))

            def dinit(ew):
                ew.c(ew.e.tensor_copy(St[:, 0:8, slice(0, NTI, 8)], Si[:, 0:8, :]))
                ew.c(ew.e.tensor_copy(St[:, 8:16, slice(7, NTI, 8)], Si[:, 8:16, :]))
            b.blk(dve=dinit)
            sweep(True)

            for p_ in range(NP):
                def ype(ew, p_=p_):
                    for cb_, (c0_, c1_) in enumerate(((0, 512), (512, NTI))):
                        o = ps[(p_ % 2) * 2 + cb_][:, 0:c1_ - c0_]
                        ew.mm(o, TP[:, p_, :], U2[:, p_, c0_:c1_], start=True, stop=False)
                        ew.mm(o, W2[:, p_, :, :].rearrange("p a b -> p (a b)"), St[:, p_, c0_:c1_], start=False, stop=True)

                def yev(ew, p_=p_):
                    ew.c(ew.e.activation(out=Yo[:, p_, 0:512], in_=ps[(p_ % 2) * 2][:, 0:512], func=AF.Copy))
                    ew.c(ew.e.activation(out=Yo[:, p_, 512:NTI], in_=ps[(p_ % 2) * 2 + 1][:, 0:NTI - 512], func=AF.Copy))
                b.blk(pe=ype)
                b.blk(act=yev)

            def yout_(ew):
                for t in range(8):
                    ew.dma(yd[0, t, qb * 128:(qb + 1) * 128, :].rearrange("(g q) i -> q g i", q=16),
                           Yo[t * 16:(t + 1) * 16, 0:8, :])
                    ew.dma(yd[1, 7 - t, qb * 128:(qb + 1) * 128, :].rearrange("(g q) i -> q g i", q=16),
                           Yo[t * 16:(t + 1) * 16, 8:16, :])
            b.blk(sp=yout_)


def lru_phase(nc, b, l, ps, xr_d, gg_d, yl_d, cw, cb, lruW, lb_rg, lb_ig, llam):
    with ExitStack() as st:
        sbt = lambda n, s, d: st.enter_context(nc.sbuf_tensor(uniq(n), list(s), d))
        cwt = sbt("cwt", [128, 4, 4], F32)
        cbt = sbt("cbt", [128, 4], F32)
        brg = sbt("brg", [128, 2, 4], F32)
        big = sbt("big", [128, 2, 4], F32)
        lam = sbt("lam", [128, 2, 4], F32)
        nsp = sbt("nsp", [128, 2, 4], F32)
        nsp2 = sbt("nsp2", [128, 2, 4], F32)
        Wg = sbt("Wg", [128, 2, 2, 4, 128], BF)
        xp = sbt("xp", [128, 3 + NCTX + 3 + NLAT + 3], F32)
        xc = sbt("xc", [128, NT], F32)
        xcb = sbt("xcb", [128, NT], BF)
        R = [xp, sbt("R1", [128, NT], F32)]
        I = [sbt(f"I{d}", [128, NT], F32) for d in range(2)]
        H = [sbt(f"H{d}", [128, NT], F32) for d in range(2)]
        xraw = H[0]
        ggt = sbt("ggt", [128, NT], BF)
        ylo = xcb
        hc = sbt("hc", [128, 2], F32)
        CO = 3
        LO = 3 + NCTX + 3

        def ld(ew):
            ew.dma(cwt[:], cw[:, l, :, :])
            ew.dma(cbt[:], cb[:, l, :])
            ew.dma(brg[:], lb_rg[:, l, :, :])
            ew.dma(big[:], lb_ig[:, l, :, :])
            ew.dma(lam[:], llam[:, l, :, :])

        def ldw(ew):
            for d in range(2):
                for g in range(2):
                    ew.dma(Wg[:, d, g, :, :], lruW[l, d, g].rearrange("c i o -> i c o"))
        b.blk(sp=ld, pool=ldw)

        b.blk(act=lambda ew: ew.c(ew.e.activation(out=nsp[:], in_=lam[:], func=AF.Exp, scale=-1.0)))
        b.blk(act=lambda ew: ew.c(ew.e.activation(out=nsp[:], in_=nsp[:], func=AF.Ln, bias=1.0)))

        def nsp_(ew):
            ew.c(ew.e.tensor_scalar(out=nsp2[:], in0=nsp[:], scalar1=-16.0, scalar2=None, op0=ALU.mult))
            ew.c(ew.e.tensor_scalar(out=nsp[:], in0=nsp[:], scalar1=-8.0, scalar2=None, op0=ALU.mult))
        b.blk(dve=nsp_)

        for cc in range(4):
            rows = slice(cc * 128, (cc + 1) * 128)

            def l0(ew, rows=rows):
                ew.dma(xraw[:], xr_d[rows, :])
                ew.dma(ggt[:], gg_d[rows, :])
            b.blk(sp=l0)

            def l1(ew):
                ew.c(ew.e.memset(xp[:], 0.0))
                ew.c(ew.e.tensor_copy(xp[:, CO:CO + NCTX], xraw[:, 0:NCTX]))
                ew.c(ew.e.tensor_copy(xp[:, LO:LO + NLAT].rearrange("p (c r) -> p c r", r=64),
                                      xraw[:, NCTX:NT].rearrange("p (r c) -> p c r", c=64)))
            b.blk(dve=l1)

            def l2_(ew, cc=cc):
                for (o0, n, base) in ((0, NCTX, CO), (NCTX, NLAT, LO)):
                    ew.c(ew.e.tensor_scalar(out=xc[:, o0:o0 + n], in0=xp[:, base - 2:base - 2 + n],
                                            scalar1=cwt[:, 0, cc:cc + 1], scalar2=cbt[:, cc:cc + 1],
                                            op0=ALU.mult, op1=ALU.add))
                    for k in range(1, 4):
                        ew.c(ew.e.scalar_tensor_tensor(out=xc[:, o0:o0 + n], in0=xp[:, base - 2 + k:base - 2 + k + n],
                                                       scalar=cwt[:, k, cc:cc + 1], in1=xc[:, o0:o0 + n],
                                                       op0=ALU.mult, op1=ALU.add))
                ew.c(ew.e.tensor_copy(xcb[:], xc[:]))
            b.blk(dve=l2_)

            blocks = [(i * 512, min(512, NT - i * 512)) for i in range(9)]
            for bi, (c0_, n) in enumerate(blocks):
                def gpe(ew, c0_=c0_, n=n, cc=cc):
                    for d in range(2):
                        for g in range(2):
                            ew.mm(ps[d * 2 + g][:, 0:n], Wg[:, d, g, cc, :], xcb[:, c0_:c0_ + n], start=True, stop=True)

                def gev(ew, c0_=c0_, n=n, cc=cc):
                    for d in range(2):
                        ew.c(ew.e.activation(out=R[d][:, c0_:c0_ + n], in_=ps[d * 2][:, 0:n], func=AF.Sigmoid,
                                             bias=brg[:, d, cc:cc + 1]))
                        ew.c(ew.e.activation(out=I[d][:, c0_:c0_ + n], in_=ps[d * 2 + 1][:, 0:n], func=AF.Sigmoid,
                                             bias=big[:, d, cc:cc + 1]))
                b.blk(pe=gpe)
                b.blk(act=gev)

            def e1(ew, cc=cc):
                for d in range(2):
                    ew.c(ew.e.activation(out=H[d][:], in_=R[d][:, 0:NT], func=AF.Exp, scale=nsp2[:, d, cc:cc + 1]))
                    ew.c(ew.e.activation(out=R[d][:, 0:NT], in_=R[d][:, 0:NT], func=AF.Exp, scale=nsp[:, d, cc:cc + 1]))

            def e1d(ew):
                for d in range(2):
                    ew.c(ew.e.tensor_tensor(out=I[d][:], in0=I[d][:], in1=xc[:], op=ALU.mult))
            b.blk(act=e1, dve=e1d)

            def e2(ew):
                for d in range(2):
                    ew.c(ew.e.activation(out=H[d][:], in_=H[d][:], func=AF.Sqrt, scale=-1.0, bias=1.0))
            b.blk(act=e2)

            def e3(ew):
                for d in range(2):
                    ew.c(ew.e.tensor_tensor(out=I[d][:], in0=I[d][:], in1=H[d][:], op=ALU.mult))
            b.blk(dve=e3)

            def rv(ap2d, o0, n):
                full = ap2d[:, o0:o0 + n]
                return bass.AP(full.tensor, full.offset + (n - 1), [list(full.ap[0]), [-1, n]])

            def sc(ew):
                ew.c(ew.e.tensor_tensor_scan(out=H[0][:, 0:NCTX], data0=R[0][:, 0:NCTX], data1=I[0][:, 0:NCTX],
                                             initial=0.0, op0=ALU.mult, op1=ALU.add))
                ew.c(ew.e.tensor_copy(hc[:, 0:1], H[0][:, NCTX - 1:NCTX]))
                ew.c(ew.e.tensor_tensor_scan(out=H[0][:, NCTX:NT], data0=R[0][:, NCTX:NT], data1=I[0][:, NCTX:NT],
                                             initial=hc[:, 0:1], op0=ALU.mult, op1=ALU.add))
                ew.c(ew.e.tensor_tensor_scan(out=rv(H[1], 0, NCTX), data0=rv(R[1], 0, NCTX), data1=rv(I[1], 0, NCTX),
                                             initial=0.0, op0=ALU.mult, op1=ALU.add))
                ew.c(ew.e.tensor_copy(hc[:, 1:2], H[1][:, 0:1]))
                ew.c(ew.e.tensor_tensor_scan(out=rv(H[1], NCTX, NLAT), data0=rv(R[1], NCTX, NLAT),
                                             data1=rv(I[1], NCTX, NLAT), initial=hc[:, 1:2], op0=ALU.mult, op1=ALU.add))
                ew.c(ew.e.tensor_tensor(out=H[0][:], in0=H[0][:], in1=H[1][:], op=ALU.add))
                ew.c(ew.e.tensor_tensor(out=ylo[:, 0:NCTX], in0=H[0][:, 0:NCTX], in1=ggt[:, 0:NCTX], op=ALU.mult))
                ew.c(ew.e.tensor_tensor(out=ylo[:, NCTX:NT].rearrange("p (r c) -> p r c", c=64),
                                        in0=H[0][:, NCTX:NT].rearrange("p (c r) -> p r c", r=64),
                                        in1=ggt[:, NCTX:NT].rearrange("p (r c) -> p r c", c=64), op=ALU.mult))
            b.blk(dve=sc)
            b.blk(sp=lambda ew, rows=rows: ew.dma(yl_d[rows, :], ylo[:]))


_NC_CACHE = {}


def _host_inputs(inp, bidx):
    f = np.float32
    x = np.asarray(inp["x"], f)
    ctx = np.asarray(inp["ctx"], f)
    d = {}
    d["xs"] = np.ascontiguousarray(np.concatenate([ctx[bidx].T, x[bidx].T], axis=1))
    cv = np.stack([np.asarray(inp["c"], f)[bidx], np.asarray(inp["c_ctx"], f)], axis=-1)
    d["cvec"] = np.ascontiguousarray(cv.reshape(8, 128, 2).transpose(1, 0, 2))
    return d


def _shared_inputs(inp):
    f = np.float32
    g = lambda k: np.asarray(inp[k], f)
    d = {}
    d["w_ada"] = g("w_ada")
    d["b_ada"] = np.ascontiguousarray(g("b_ada").reshape(NL, 48, 128).transpose(2, 0, 1))
    d["gains"] = np.ascontiguousarray(g("norm_gains").reshape(NL, 4, 8, 128).transpose(3, 0, 1, 2))
    d["w_in"] = g("w_in")
    d["w_out"] = g("w_out")
    d["w_ffi"] = g("w_ffn_in")
    d["w_ffo"] = g("w_ffn_out")
    d["w_glu"] = g("s5_w_glu")
    d["b_glu"] = np.ascontiguousarray(g("s5_b_glu").reshape(NL, 4, 128).transpose(2, 0, 1))

    def st2(a):
        a = np.concatenate([a, a], axis=-1)
        a = a.reshape(NL, 2, 2, 16, 128).reshape(NL, 4, 16, 128)
        return np.ascontiguousarray(a.transpose(3, 0, 1, 2))
    d["sa_re"] = st2(g("s5_a_re"))
    d["sa_im"] = st2(g("s5_a_im"))
    d["sldt"] = st2(np.broadcast_to(g("s5_log_dt")[..., None], (NL, 2, 32, 64)))

    def st3(top, bot):
        a = np.concatenate([top, bot], axis=3)
        a = a.reshape(NL, 4, 16, 128, a.shape[-1])
        return np.ascontiguousarray(a.transpose(3, 0, 1, 2, 4))
    bre, bim = g("s5_b_re"), g("s5_b_im")
    d["sBP"] = st3(bre, bim)
    d["sBQ"] = st3(bim, bre)
    cre = g("s5_c_re").transpose(0, 1, 2, 4, 3)
    cim = g("s5_c_im").transpose(0, 1, 2, 4, 3)
    d["sCP"] = st3(cre, cim)
    d["sCQ"] = st3(cim, cre)
    sd = g("s5_d").reshape(NL, 4, 8, 16)
    sd = np.broadcast_to(sd[:, :, :, None, :], (NL, 4, 8, 8, 16))
    d["sD"] = np.ascontiguousarray(sd.transpose(3, 4, 0, 1, 2).reshape(128, NL, 4, 8))
    d["cw"] = np.ascontiguousarray(g("lru_conv_w").reshape(NL, 4, 4, 128).transpose(3, 0, 1, 2))
    d["cb"] = np.ascontiguousarray(g("lru_conv_b").reshape(NL, 4, 128).transpose(2, 0, 1))
    W = np.zeros((NL, 2, 2, 4, 128, 128), f)
    for gi, key in enumerate(("lru_w_rg", "lru_w_ig")):
        w = g(key)
        for cc in range(4):
            W[:, :, gi, cc, 0:64, 0:64] = w[:, :, 2 * cc]
            W[:, :, gi, cc, 64:128, 64:128] = w[:, :, 2 * cc + 1]
    d["lruW"] = W
    v = lambda k: np.ascontiguousarray(g(k).reshape(NL, 2, 4, 128).transpose(3, 0, 1, 2))
    d["lb_rg"] = v("lru_b_rg")
    d["lb_ig"] = v("lru_b_ig")
    d["llam"] = v("lru_lambda")
    I = np.eye(128, dtype=f)
    d["cI"] = I
    J = np.zeros((128, 128), f)
    SW = np.zeros((128, 128), f)
    for n in range(64):
        J[n, 64 + n] = 1.0
        J[64 + n, n] = 1.0
        SW[64 + n, n] = -1.0
        SW[n, 64 + n] = 1.0
    d["cJ"] = J
    d["cSW"] = SW
    sidx = np.arange(128) // 16
    d["cML"] = (sidx[None, :] >= sidx[:, None]).astype(f)
    sgn = np.where(np.arange(128) < 64, -1.0, 1.0).astype(f)
    d["cSG"] = np.stack([sgn, -sgn, np.ones(128, f), np.full(128, 1.0 / 1024, f)], axis=1)
    return d


def kernel(**inputs):
    if "nc" not in _NC_CACHE:
        _NC_CACHE["nc"] = build_program()
    nc = _NC_CACHE["nc"]
    shared = _shared_inputs(inputs)
    outs = []
    for grp in range(2):
        in_maps = []
        for bidx in range(grp * 4, grp * 4 + 4):
            m = dict(shared)
            m.update(_host_inputs(inputs, bidx))
            in_maps.append(m)
        res = run_bass_kernel_spmd(nc, in_maps, core_ids=list(range(4)))
        outs += [np.asarray(r["yout"], np.float32).T for r in res.results]
    return np.ascontiguousarray(np.stack(outs, axis=0))
```

```python
import math
from contextlib import ExitStack
import numpy as np
import concourse.bass as bass
import concourse.mybir as mybir
from concourse.bass_utils import run_bass_kernel_spmd

F32 = mybir.dt.float32
BF = mybir.dt.bfloat16
AF = mybir.ActivationFunctionType
ALU = mybir.AluOpType

NL = 4
DM = 1024
NCTX = 256
NLAT = 4096
NT = NCTX + NLAT
NTI = NT // 8
NCH = NTI // 8
TN = 128
NTT = NT // TN
DFF = 2816
EPS = 1e-6
DEBUG = False
NL_RUN = 4


class EW:
    def __init__(self, b, name):
        self.b = b
        self.name = name
        self.sem = b.newsem()
        self.n = 0
        self.dsem = b.newsem()
        self.dn = 0
        self.e = None
        self.last = None

    def c(self, ins):
        if self.n >= 6000:
            self.sem = self.b.newsem()
            self.n = 0
        self.n += 1
        ins.then_inc(self.sem, 1)
        self.e.wait_ge(self.sem, self.n)
        return ins

    def mm(self, *a, **k):
        self.last = self.e.matmul(*a, **k)
        return self.last

    def tr(self, *a, **k):
        self.last = self.e.transpose(*a, **k)
        return self.last

    def dma(self, out, in_):
        if self.dn >= 550:
            self.dsem = self.b.newsem()
            self.dn = 0
        self.dn += 1
        self.e.dma_start(out=out, in_=in_).then_inc(self.dsem, 16)

    def fin(self):
        if self.last is not None:
            self.c(self.last)
            self.last = None
        if self.dn:
            self.e.wait_ge(self.dsem, 16 * self.dn)


class Bld:
    def __init__(self, nc):
        self.nc = nc
        self.es = ExitStack()
        self.sems = [self.es.enter_context(nc.semaphore(f"sm{i}")) for i in range(96)]
        self.si = 0
        self.ew = {k: EW(self, k) for k in ("pe", "act", "dve", "pool", "sp")}
        self.nblk = 0

    def newsem(self):
        s = self.sems[self.si]
        self.si += 1
        return s

    def sb(self, name, shape, dt):
        return self.es.enter_context(self.nc.sbuf_tensor(name, list(shape), dt))

    def blk(self, **fns):
        m = {"pe": "tensor", "act": "scalar", "dve": "vector", "pool": "gpsimd", "sp": "sync"}
        self.nblk += 1
        with self.nc.Block() as block:
            for name, fn in fns.items():
                if fn is None:
                    continue
                ew = self.ew[name]

                def run(e, ew=ew, fn=fn):
                    ew.e = e
                    fn(ew)
                    ew.fin()

                getattr(block, m[name])(run)

    def pipeline(self, tasks):
        nsteps = max((k + len(t) for k, t in enumerate(tasks)), default=0)
        for t in range(nsteps):
            fns = {}
            for k in range(max(0, t - 24), min(len(tasks), t + 1)):
                j = t - k
                if j < len(tasks[k]):
                    for name, fn in tasks[k][j].items():
                        fns.setdefault(name, []).append(fn)
            if not fns:
                continue
            self.blk(**{n: (lambda ew, l=l: [f(ew) for f in reversed(l)]) for n, l in fns.items()})


_UC = [0]


def uniq(n):
    _UC[0] += 1
    return f"{n}_{_UC[0]}"


def bc(ap, shape):
    return ap.to_broadcast(list(shape))


def build_program():
    nc = bass.Bass("TRN2", target_bir_lowering=False)
    b = Bld(nc)

    def din(name, shape, dt=F32):
        return nc.dram_tensor(name, list(shape), dt, kind="ExternalInput").ap()

    def dscr(name, shape, dt=F32):
        kind = "ExternalOutput"
        return nc.dram_tensor(name, list(shape), dt, kind=kind).ap()

    xs = din("xs", [DM, NT])
    cvec = din("cvec", [128, 8, 2])
    w_ada = din("w_ada", [NL, DM, 6 * DM])
    b_ada = din("b_ada", [128, NL, 48])
    gains = din("gains", [128, NL, 4, 8])
    w_in = din("w_in", [NL, DM, 1536])
    w_out = din("w_out", [NL, DM, DM])
    w_ffi = din("w_ffi", [NL, DM, 2 * DFF])
    w_ffo = din("w_ffo", [NL, DFF, DM])
    w_glu = din("w_glu", [NL, 512, 512])
    b_glu = din("b_glu", [128, NL, 4])
    sa_re = din("sa_re", [128, NL, 4, 16])
    sa_im = din("sa_im", [128, NL, 4, 16])
    sldt = din("sldt", [128, NL, 4, 16])
    sBP = din("sBP", [128, NL, 4, 16, 16])
    sBQ = din("sBQ", [128, NL, 4, 16, 16])
    sCP = din("sCP", [128, NL, 4, 16, 16])
    sCQ = din("sCQ", [128, NL, 4, 16, 16])
    sD = din("sD", [128, NL, 4, 8])
    cw = din("cw", [128, NL, 4, 4])
    cb = din("cb", [128, NL, 4])
    lruW = din("lruW", [NL, 2, 2, 4, 128, 128])
    lb_rg = din("lb_rg", [128, NL, 2, 4])
    lb_ig = din("lb_ig", [128, NL, 2, 4])
    llam = din("llam", [128, NL, 2, 4])
    cI = din("cI", [128, 128])
    cJ = din("cJ", [128, 128])
    cSW = din("cSW", [128, 128])
    cML = din("cML", [128, 128])
    cSG = din("cSG", [128, 4])
    yout = nc.dram_tensor("yout", [DM, NLAT], F32, kind="ExternalOutput").ap()

    xres = dscr("xres", [DM, NT])
    ud = dscr("ud", [8, 512, NTI], BF)
    yd = dscr("yd", [2, 8, 512, NTI], BF)
    xr_d = dscr("xr_d", [512, NT])
    gg_d = dscr("gg_d", [512, NT], BF)
    yl_d = dscr("yl_d", [512, NT], BF)
    yg_d = dscr("yg_d", [512, NT], BF)

    xres_v = xres.rearrange("(kc p) n -> p kc n", p=128)
    xs_v = xs.rearrange("(kc p) n -> p kc n", p=128)

    ident = b.sb("ident", [128, 128], F32)
    identb = b.sb("identb", [128, 128], BF)
    Jm = b.sb("Jm", [128, 128], F32)
    SWm = b.sb("SWm", [128, 128], F32)
    MLm = b.sb("MLm", [128, 128], F32)
    sg = b.sb("sg", [128, 4], F32)
    onesb = b.sb("onesb", [128, 128], BF)
    epsb = b.sb("epsb", [128, 1], F32)
    MOD = b.sb("MOD", [128, NL, 6, 8, 2], F32)
    GA = b.sb("GA", [128, NL, 4, 8], F32)
    DER = b.sb("DER", [128, NL, 4, 8, 2], F32)
    BADA = b.sb("BADA", [128, NL, 48], F32)
    ps = [b.es.enter_context(nc.psum_tensor(f"ps{i}", [128, 512], F32)) for i in range(8)]

    def c0(ew):
        ew.dma(ident[:], cI[:, :])
        ew.dma(Jm[:], cJ[:, :])
        ew.dma(SWm[:], cSW[:, :])
        ew.dma(MLm[:], cML[:, :])
        ew.dma(sg[:], cSG[:, :])
        ew.dma(GA[:], gains[:, :, :, :])
        ew.dma(BADA[:], b_ada[:, :, :])
    b.blk(sp=c0)

    def c1(ew):
        ew.c(ew.e.tensor_copy(identb[:], ident[:]))
        ew.c(ew.e.memset(onesb[:], 1.0 / 1024.0))
        ew.c(ew.e.memset(epsb[:], EPS))
    b.blk(dve=c1)

    def c2(ew):
        for kc in range(8):
            ew.dma(xres[kc * 128:(kc + 1) * 128, :], xs[kc * 128:(kc + 1) * 128, :])
    b.blk(sp=c2)

    with ExitStack() as st:
        cv = st.enter_context(nc.sbuf_tensor(uniq("cv"), [128, 8, 2], F32))
        scb = st.enter_context(nc.sbuf_tensor("scb", [128, 8, 2], BF))
        wad = [st.enter_context(nc.sbuf_tensor(f"wad{i}", [128, 8, 1024], BF)) for i in range(2)]

        b.blk(sp=lambda ew: ew.dma(cv[:], cvec[:, :, :]))
        b.blk(act=lambda ew: ew.c(ew.e.activation(out=scb[:], in_=cv[:], func=AF.Silu)))
        tasks = []
        for l in range(NL):
            for j in range(6):
                k = l * 6 + j
                wb = wad[k % 2]
                src = w_ada[l].rearrange("(kc p) n -> p kc n", p=128)[:, :, j * 1024:(j + 1) * 1024]

                def s_load(ew, wb=wb, src=src):
                    ew.dma(wb[:], src)

                def s_mm(ew, wb=wb, k=k):
                    for oc in range(8):
                        for kc in range(8):
                            ew.mm(ps[k % 2][:, oc * 2:oc * 2 + 2], wb[:, kc, oc * 128:(oc + 1) * 128],
                                  scb[:, kc, :], start=(kc == 0), stop=(kc == 7))

                def s_ev(ew, l=l, j=j, k=k):
                    ew.c(ew.e.tensor_tensor(
                        out=MOD[:, l, j, :, :],
                        in0=ps[k % 2][:, 0:16].rearrange("p (o w) -> p o w", w=2),
                        in1=bc(BADA[:, l, j * 8:(j + 1) * 8].unsqueeze(2), [128, 8, 2]),
                        op=ALU.add))
                tasks.append([{"pool": s_load}, {"pe": s_mm}, {"dve": s_ev}])
        b.pipeline(tasks)

        def derive(ew):
            for l in range(NL):
                for (dk, gk, mk, addone) in ((0, 0, 1, True), (1, 1, 2, False), (2, 2, 4, True), (3, 3, 5, False)):
                    g = bc(GA[:, l, gk, :].unsqueeze(2), [128, 8, 2])
                    if addone:
                        ew.c(ew.e.scalar_tensor_tensor(out=DER[:, l, dk, :, :], in0=MOD[:, l, mk, :, :],
                                                       scalar=1.0, in1=g, op0=ALU.add, op1=ALU.mult))
                    else:
                        ew.c(ew.e.tensor_tensor(out=DER[:, l, dk, :, :], in0=MOD[:, l, mk, :, :], in1=g,
                                                op=ALU.mult))
        b.blk(dve=derive)

    def norm_stages(l, ti, xt, sq, psS, rstd, tmp, h, which):
        w = 1 if ti < 2 else 0
        cols = slice(ti * TN, (ti + 1) * TN)
        dk = 0 if which == 0 else 2
        mk = 0 if which == 0 else 3

        def s0(ew):
            ew.dma(xt[:], xres_v[:, :, cols])

        def s1(ew):
            ew.c(ew.e.activation(out=sq[:], in_=xt[:], func=AF.Square))

        def s2(ew):
            for kc in range(8):
                ew.mm(psS, onesb[:], sq[:, kc, :], start=(kc == 0), stop=(kc == 7))

        def s2b(ew):
            ew.c(ew.e.activation(out=rstd[:], in_=psS, func=AF.Sqrt, bias=epsb[:, 0:1]))

        def s3(ew):
            ew.c(ew.e.reciprocal(rstd[:], rstd[:]))
            ew.c(ew.e.tensor_tensor(out=tmp[:], in0=xt[:], in1=bc(rstd[:].unsqueeze(1), [128, 8, TN]),
                                    op=ALU.mult))

        def s4(ew):
            for kc in range(8):
                ew.c(ew.e.activation(out=h[:, kc, :], in_=tmp[:, kc, :], func=AF.Identity,
                                     scale=DER[:, l, dk, kc, w:w + 1], bias=MOD[:, l, mk, kc, w:w + 1]))
        return [{"sp": s0}, {"act": s1}, {"pe": s2}, {"act": s2b}, {"dve": s3}, {"act": s4}]

    def post_stages(l, ti, psO2, xt, osb, sq, psS, rstd, tmp, which, last_layer):
        w = 1 if ti < 2 else 0
        cols = slice(ti * TN, (ti + 1) * TN)
        dk = 1 if which == 0 else 3

        def s1(ew):
            for hf in range(2):
                ew.c(ew.e.activation(out=osb[:, hf * 4:(hf + 1) * 4, :],
                                     in_=psO2[hf][:, 0:4 * TN].rearrange("p (c n) -> p c n", c=4), func=AF.Copy))

        def s1l(ew):
            ew.dma(xt[:], xres_v[:, :, cols])

        def s1b(ew):
            ew.c(ew.e.activation(out=sq[:], in_=osb[:], func=AF.Square))

        def s2(ew):
            for kc in range(8):
                ew.mm(psS, onesb[:], sq[:, kc, :], start=(kc == 0), stop=(kc == 7))

        def s2b(ew):
            ew.c(ew.e.activation(out=rstd[:], in_=psS, func=AF.Sqrt, bias=epsb[:, 0:1]))

        def s3(ew):
            ew.c(ew.e.reciprocal(rstd[:], rstd[:]))
            for mc in range(8):
                ew.c(ew.e.scalar_tensor_tensor(out=tmp[:, mc, :], in0=osb[:, mc, :],
                                               scalar=DER[:, l, dk, mc, w:w + 1], in1=rstd[:],
                                               op0=ALU.mult, op1=ALU.mult))

        def s4(ew):
            ew.c(ew.e.tensor_tensor(out=xt[:], in0=xt[:], in1=tmp[:], op=ALU.add))

        def s5(ew):
            ew.dma(xres_v[:, :, cols], xt[:])
            if last_layer and which == 1 and ti >= 2:
                ew.dma(yout.rearrange("(kc p) n -> p kc n", p=128)[:, :, (ti - 2) * TN:(ti - 1) * TN], xt[:])
        return [{"act": s1, "sp": s1l}, {"act": s1b}, {"pe": s2}, {"act": s2b}, {"dve": s3}, {"pool": s4}, {"sp": s5}]

    for l in range(NL_RUN):
        last = (l == NL - 1)
        with ExitStack() as st:
            sbt = lambda n, s, d: st.enter_context(nc.sbuf_tensor(uniq(n), list(s), d))
            win = sbt("win", [128, 8, 1536], BF)
            ut_all = sbt("ut_all", [128, 4, 8, NTI], BF)
            NB = 5
            xt = [sbt(f"xt{i}", [128, 8, TN], F32) for i in range(NB)]
            sq = [sbt(f"sq{i}", [128, 8, TN], BF) for i in range(NB)]
            rstd = [sbt(f"rstd{i}", [128, TN], F32) for i in range(NB)]
            tmp = [sbt(f"tmp{i}", [128, 8, TN], F32) for i in range(NB)]
            hb = [sbt(f"hb{i}", [128, 8, TN], BF) for i in range(NB)]
            xro = [sbt(f"xro{i}", [128, 4, TN], F32) for i in range(NB)]
            ggo = [sbt(f"ggo{i}", [128, 4, TN], BF) for i in range(NB)]
            b.blk(pool=lambda ew: ew.dma(win[:], w_in[l].rearrange("(kc p) n -> p kc n", p=128)))
            tasks = []
            for ti in range(NTT):
                k = ti % NB
                pw = [ps[(ti % 2) * 3 + j] for j in range(3)]
                psS = ps[6 + ti % 2][:, 0:TN]
                stg = norm_stages(l, ti, xt[k], sq[k], psS, rstd[k], tmp[k], hb[k], 0)
                cols = slice(ti * TN, (ti + 1) * TN)

                def s5(ew, k=k, pw=pw):
                    for mc in range(12):
                        for kc in range(8):
                            ew.mm(pw[mc // 4][:, (mc % 4) * TN:(mc % 4 + 1) * TN],
                                  win[:, kc, mc * 128:(mc + 1) * 128], hb[k][:, kc, :],
                                  start=(kc == 0), stop=(kc == 7))

                def s6d(ew, ti=ti, pw=pw):
                    ew.c(ew.e.tensor_copy(
                        out=ut_all[:, :, :, ti * 16:(ti + 1) * 16],
                        in_=pw[0][:, 0:4 * TN].rearrange("p (c i s) -> p c s i", c=4, s=8)))

                def s6a(ew, k=k, pw=pw):
                    ew.c(ew.e.activation(out=xro[k][:], in_=pw[1][:, 0:4 * TN].rearrange("p (c n) -> p c n", c=4),
                                         func=AF.Copy))
                    ew.c(ew.e.activation(out=ggo[k][:], in_=pw[2][:, 0:4 * TN].rearrange("p (c n) -> p c n", c=4),
                                         func=AF.Gelu_apprx_tanh))

                def s7(ew, k=k, cols=cols):
                    ew.dma(xr_d.rearrange("(c p) n -> p c n", p=128)[:, :, cols], xro[k][:])
                    ew.dma(gg_d.rearrange("(c p) n -> p c n", p=128)[:, :, cols], ggo[k][:])
                tasks.append(stg + [{"pe": s5}, {"dve": s6d, "act": s6a}, {"sp": s7}])
            b.pipeline(tasks)

            def uout(ew):
                for s in range(8):
                    ew.dma(ud[s].rearrange("(c p) i -> p c i", p=128), ut_all[:, :, s, :])
            b.blk(sp=uout)

        s5_phase(nc, b, l, ps, ud, yd, ident, Jm, SWm, MLm, sg,
                 sa_re, sa_im, sldt, sBP, sBQ, sCP, sCQ, sD)

        lru_phase(nc, b, l, ps, xr_d, gg_d, yl_d, cw, cb, lruW, lb_rg, lb_ig, llam)

        with ExitStack() as st:
            sbt = lambda n, s, d: st.enter_context(nc.sbuf_tensor(uniq(n), list(s), d))
            yf_ = [sbt(f"yf{i}", [128, 2, 8, NTI], BF) for i in range(2)]
            ys_ = [sbt(f"ys{i}", [128, NT], F32) for i in range(2)]
            ygo = [sbt(f"ygo{i}", [128, NT], BF) for i in range(2)]
            tasks = []
            for cc in range(4):
                k = cc % 2
                rows = slice(cc * 128, (cc + 1) * 128)

                def q0(ew, k=k, rows=rows):
                    for d in range(2):
                        ew.dma(yf_[k][:, d, :, :], yd[d, :, rows, :].rearrange("t p i -> p t i"))

                def q1(ew, k=k):
                    ew.c(ew.e.tensor_tensor(out=ys_[k][:].rearrange("p (i t) -> p t i", t=8),
                                            in0=yf_[k][:, 0, :, :], in1=yf_[k][:, 1, :, :], op=ALU.add))

                def q2(ew, k=k):
                    ew.c(ew.e.activation(out=ygo[k][:], in_=ys_[k][:], func=AF.Gelu_apprx_tanh))

                def q3(ew, k=k, rows=rows):
                    ew.dma(yg_d[rows, :], ygo[k][:])
                tasks.append([{"sp": q0}, {"dve": q1}, {"act": q2}, {"sp": q3}])
            b.pipeline(tasks)

        with ExitStack() as st:
            sbt = lambda n, s, d: st.enter_context(nc.sbuf_tensor(uniq(n), list(s), d))
            wo = sbt("wo", [128, 8, 1024], BF)
            wg = sbt("wg", [128, 4, 512], BF)
            bg = sbt("bg", [128, 4], F32)
            NB = 7
            xt = [sbt(f"xt{i}", [128, 8, TN], F32) for i in range(NB)]
            osb = [sbt(f"osb{i}", [128, 8, TN], F32) for i in range(NB)]
            sq = [sbt(f"sq{i}", [128, 8, TN], BF) for i in range(NB)]
            rstd = [sbt(f"rstd{i}", [128, TN], F32) for i in range(NB)]
            tmp = [sbt(f"tmp{i}", [128, 8, TN], F32) for i in range(NB)]
            yg = [sbt(f"yg{i}", [128, 4, TN], BF) for i in range(NB)]
            sig = [sbt(f"sig{i}", [128, 4, TN], F32) for i in range(NB)]
            ymix = [sbt(f"ymix{i}", [128, 8, TN], BF) for i in range(NB)]

            def wl(ew):
                ew.dma(wo[:], w_out[l].rearrange("(kc p) n -> p kc n", p=128))
                ew.dma(wg[:], w_glu[l].rearrange("(kc p) n -> p kc n", p=128))
            b.blk(pool=wl, sp=lambda ew: ew.dma(bg[:], b_glu[:, l, :]))
            tasks = []
            tlist = range(NTT) if not last else range(2, NTT)
            for n_, ti in enumerate(tlist):
                k = n_ % NB
                cols = slice(ti * TN, (ti + 1) * TN)
                psG = ps[n_ % 2]
                psO2 = [ps[2 + (n_ % 2) * 2], ps[3 + (n_ % 2) * 2]]
                psS = ps[6 + n_ % 2][:, 0:TN]

                def a0(ew, k=k, cols=cols):
                    ew.dma(yg[k][:], yg_d.rearrange("(c p) n -> p c n", p=128)[:, :, cols])
                    ew.dma(ymix[k][:, 4:8, :], yl_d.rearrange("(c p) n -> p c n", p=128)[:, :, cols])

                def a3(ew, k=k, psG=psG):
                    for mc in range(4):
                        for kc in range(4):
                            ew.mm(psG[:, mc * TN:(mc + 1) * TN], wg[:, kc, mc * 128:(mc + 1) * 128],
                                  yg[k][:, kc, :], start=(kc == 0), stop=(kc == 3))

                def a4(ew, k=k, psG=psG):
                    for mc in range(4):
                        ew.c(ew.e.activation(out=sig[k][:, mc, :], in_=psG[:, mc * TN:(mc + 1) * TN],
                                             func=AF.Sigmoid, bias=bg[:, mc:mc + 1]))

                def a5(ew, k=k):
                    ew.c(ew.e.tensor_tensor(out=ymix[k][:, 0:4, :], in0=yg[k][:], in1=sig[k][:], op=ALU.mult))

                def a6(ew, k=k, psO2=psO2):
                    for mc in range(8):
                        for kc in range(8):
                            ew.mm(psO2[mc // 4][:, (mc % 4) * TN:(mc % 4 + 1) * TN],
                                  wo[:, kc, mc * 128:(mc + 1) * 128], ymix[k][:, kc, :],
                                  start=(kc == 0), stop=(kc == 7))
                stg = [{"sp": a0}, {"pe": a3}, {"act": a4}, {"dve": a5}, {"pe": a6}]
                stg += post_stages(l, ti, psO2, xt[k], osb[k], sq[k], psS, rstd[k], tmp[k], 0, last)
                tasks.append(stg)
            b.pipeline(tasks)

        with ExitStack() as st:
            sbt = lambda n, s, d: st.enter_context(nc.sbuf_tensor(uniq(n), list(s), d))
            wfi = sbt("wfi", [128, 8, 2 * DFF], BF)
            wfo = sbt("wfo", [128, 22, 1024], BF)
            NB = 2
            xt = [sbt(f"xt{i}", [128, 8, TN], F32) for i in range(NB)]
            xt2 = [sbt(f"xtb{i}", [128, 8, TN], F32) for i in range(NB)]
            osb = [sbt(f"osb{i}", [128, 8, TN], F32) for i in range(NB)]
            sq = [sbt(f"sq{i}", [128, 8, TN], BF) for i in range(NB)]
            sq2 = [sbt(f"sqb{i}", [128, 8, TN], BF) for i in range(NB)]
            rstd = [sbt(f"rstd{i}", [128, TN], F32) for i in range(NB)]
            rstd2 = [sbt(f"rstdb{i}", [128, TN], F32) for i in range(NB)]
            tmp = [sbt(f"tmp{i}", [128, 8, TN], F32) for i in range(NB)]
            tmp2 = [sbt(f"tmpb{i}", [128, 8, TN], F32) for i in range(NB)]
            hb = [sbt(f"hb{i}", [128, 8, TN], BF) for i in range(NB)]
            act_ = [sbt(f"act{i}", [128, 22, TN], BF) for i in range(NB)]
            sgt = [sbt(f"sgt{i}", [128, 2, TN], F32) for i in range(3)]

            def wl2(ew):
                ew.dma(wfi[:], w_ffi[l].rearrange("(kc p) n -> p kc n", p=128))
                ew.dma(wfo[:], w_ffo[l].rearrange("(kc p) n -> p kc n", p=128))
            b.blk(pool=wl2)
            tlist = list(range(NTT) if not last else range(2, NTT))
            tasks = []
            gcount = [0]

            def norm_task(n_):
                ti = tlist[n_]
                k = n_ % NB
                psS = ps[7][:, (n_ % 2) * TN:(n_ % 2 + 1) * TN]
                return norm_stages(l, ti, xt[k], sq[k], psS, rstd[k], tmp[k], hb[k], 1)

            def gu_task(n_, r):
                k = n_ % NB
                gi = gcount[0]
                gcount[0] += 1
                pg = ps[gi % 3]
                sgb = sgt[gi % 3]

                def g0(ew):
                    for jj in range(2):
                        j = r * 2 + jj
                        for half in range(2):
                            co = half * DFF + j * 128
                            for kc in range(8):
                                ew.mm(pg[:, (jj * 2 + half) * TN:(jj * 2 + half + 1) * TN],
                                      wfi[:, kc, co:co + 128], hb[k][:, kc, :], start=(kc == 0), stop=(kc == 7))

                def g1a(ew):
                    ew.c(ew.e.activation(
                        out=sgb[:], in_=pg[:, 0:4 * TN].rearrange("p (j h n) -> p j h n", j=2, h=2)[:, :, 0, :],
                        func=AF.Silu))

                def g1d(ew):
                    ew.c(ew.e.tensor_tensor(
                        out=act_[k][:, r * 2:r * 2 + 2, :], in0=sgb[:],
                        in1=pg[:, 0:4 * TN].rearrange("p (j h n) -> p j h n", j=2, h=2)[:, :, 1, :], op=ALU.mult))
                return [{"pe": g0}, {"act": g1a}, {"dve": g1d}]

            def out_task(n_):
                ti = tlist[n_]
                k = n_ % NB
                psO2 = [ps[3 + (n_ % 2) * 2], ps[4 + (n_ % 2) * 2]]
                psS = ps[7][:, (2 + n_ % 2) * TN:(3 + n_ % 2) * TN]

                def o0(ew):
                    for mc in range(8):
                        for kc in range(22):
                            ew.mm(psO2[mc // 4][:, (mc % 4) * TN:(mc % 4 + 1) * TN],
                                  wfo[:, kc, mc * 128:(mc + 1) * 128], act_[k][:, kc, :],
                                  start=(kc == 0), stop=(kc == 21))
                return [{"pe": o0}] + post_stages(l, ti, psO2, xt2[k], osb[k], sq2[k], psS, rstd2[k], tmp2[k], 1, last)

            ntl = len(tlist)
            tasks.append(norm_task(0))
            for _ in range(6):
                tasks.append([])
            for n_ in range(ntl):
                for r in range(11):
                    tasks.append(gu_task(n_, r))
                    if r == 4 and n_ + 1 < ntl:
                        tasks.append(norm_task(n_ + 1))
                tasks.append([])
                tasks.append([])
                tasks.append(out_task(n_))
            b.pipeline(tasks)

    b.es.close()
    return nc


def s5_phase(nc, b, l, ps, ud, yd, ident, Jm, SWm, MLm, sg,
             sa_re, sa_im, sldt, sBP, sBQ, sCP, sCQ, sD):
    NP = 16
    for qb in range(4):
        with ExitStack() as st:
            sbt = lambda n, s, d: st.enter_context(nc.sbuf_tensor(uniq(n), list(s), d))
            are = sbt("are", [128, NP], F32)
            aim = sbt("aim", [128, NP], F32)
            ldt = sbt("ldt", [128, NP], F32)
            BP = sbt("BP", [128, NP, 16], F32)
            BQ = sbt("BQ", [128, NP, 16], F32)
            CP = sbt("CP", [128, NP, 16], F32)
            CQ = sbt("CQ", [128, NP, 16], F32)
            Dp = sbt("Dp", [128, 8], F32)
            U2 = sbt("U2", [128, NP, NTI], BF)
            T = {}
            for nm in ("dt", "ar", "ai", "mg", "c0", "s0", "t1", "t2", "t3", "den", "fre", "fim",
                       "fres", "fims", "are64", "aim64", "a8re", "a8ims"):
                T[nm] = sbt("T" + nm, [128, NP], F32)
            PW = sbt("PW", [128, 2, 9, NP], F32)
            PI = sbt("PI", [128, 2, 8, NP], F32)
            fBP = sbt("fBP", [128, NP, 16], F32)
            fBQ = sbt("fBQ", [128, NP, 16], F32)
            AL = sbt("AL", [128, NP, 8], F32)
            BE = sbt("BE", [128, NP, 8], F32)
            Zst = sbt("Zst", [128, NP, 8, 16], F32)
            W1 = sbt("W1", [128, NP, 128], BF)
            W2 = sbt("W2", [128, NP, 8, 16], BF)
            E0 = sbt("E0", [128, NP, 8, 16], F32)
            CT0 = sbt("CT0", [128, NP, 8, 16], F32)
            TP = sbt("TP", [128, NP, 128], BF)
            R8 = sbt("R8", [128, NP, 128], BF)
            tA = sbt("tA", [128, NP, 8, 16], F32)
            tB = sbt("tB", [128, NP, 8, 16], F32)
            Sa = sbt("Sa", [128, NP, NCH], BF)
            Sb = sbt("Sb", [128, NP, NCH], BF)
            Fm = sbt("Fm", [128, NP, NCH], F32)
            Fs = sbt("Fs", [128, NP, NCH], F32)
            Si = sbt("Si", [128, NP, NCH], F32)
            Ss = sbt("Ss", [128, NP, NCH], F32)
            St = sbt("St", [128, NP, NTI], BF)
            Yo = sbt("Yo", [128, NP, NTI], BF)
            l2t = [sbt(f"l2t{i}", [128, 4, 8], F32) for i in range(2)]

            def ld(ew):
                for d in range(2):
                    sl = slice(d * 8, (d + 1) * 8)
                    hs = slice((qb % 2) * 8, (qb % 2) * 8 + 8)
                    hb_ = d * 2 + qb // 2
                    ew.dma(are[:, sl], sa_re[:, l, hb_, hs])
                    ew.dma(aim[:, sl], sa_im[:, l, hb_, hs])
                    ew.dma(ldt[:, sl], sldt[:, l, hb_, hs])
                    ew.dma(BP[:, sl, :], sBP[:, l, hb_, hs, :])
                    ew.dma(BQ[:, sl, :], sBQ[:, l, hb_, hs, :])
                    ew.dma(CP[:, sl, :], sCP[:, l, hb_, hs, :])
                    ew.dma(CQ[:, sl, :], sCQ[:, l, hb_, hs, :])
                ew.dma(Dp[:], sD[:, l, qb, :])
                for s in range(8):
                    src = ud[s, qb * 128:(qb + 1) * 128, :].rearrange("(g p) i -> p g i", p=16)
                    ew.dma(U2[s * 16:(s + 1) * 16, 0:8, :], src)
                    ew.dma(U2[(7 - s) * 16:(8 - s) * 16, 8:16, :], src)
            b.blk(sp=ld)

            def sc1(ew):
                ew.c(ew.e.activation(out=T["dt"][:], in_=ldt[:], func=AF.Exp))
            b.blk(act=sc1)

            def sc2(ew):
                ew.c(ew.e.tensor_tensor(out=T["ar"][:], in0=are[:], in1=T["dt"][:], op=ALU.mult))
                ew.c(ew.e.tensor_tensor(out=T["ai"][:], in0=aim[:], in1=T["dt"][:], op=ALU.mult))
                ew.c(ew.e.tensor_scalar(out=T["t1"][:], in0=T["ai"][:], scalar1=1.0 / 16.0, scalar2=math.pi / 2,
                                        op0=ALU.mult, op1=ALU.add))
            b.blk(dve=sc2)

            def sc3(ew):
                ew.c(ew.e.activation(out=T["mg"][:], in_=T["ar"][:], func=AF.Exp, scale=1.0 / 16.0))
                ew.c(ew.e.activation(out=T["s0"][:], in_=T["ai"][:], func=AF.Sin, scale=1.0 / 16.0))
                ew.c(ew.e.activation(out=T["c0"][:], in_=T["t1"][:], func=AF.Sin))
                ew.c(ew.e.activation(out=T["t2"][:], in_=T["ar"][:], func=AF.Exp, scale=-1.0 / 16.0))
            b.blk(act=sc3)

            def cmul(ew, o_re, o_im, a_re, a_im, b_re, b_im, t1, t2):
                ew.c(ew.e.tensor_tensor(out=t1, in0=a_re, in1=b_re, op=ALU.mult))
                ew.c(ew.e.tensor_tensor(out=t2, in0=a_im, in1=b_im, op=ALU.mult))
                ew.c(ew.e.tensor_tensor(out=t2, in0=t1, in1=t2, op=ALU.subtract))
                ew.c(ew.e.tensor_tensor(out=t1, in0=a_re, in1=b_im, op=ALU.mult))
                ew.c(ew.e.tensor_tensor(out=o_im, in0=a_im, in1=b_re, op=ALU.mult))
                ew.c(ew.e.tensor_tensor(out=o_im, in0=o_im, in1=t1, op=ALU.add))
                ew.c(ew.e.tensor_copy(o_re, t2))

            def sc4(ew):
                t1, t2, t3 = T["t1"][:], T["t3"][:], T["den"][:]
                ew.c(ew.e.tensor_tensor(out=PW[:, 0, 1, :], in0=T["mg"][:], in1=T["c0"][:], op=ALU.mult))
                ew.c(ew.e.tensor_tensor(out=PW[:, 1, 1, :], in0=T["mg"][:], in1=T["s0"][:], op=ALU.mult))
                ew.c(ew.e.tensor_tensor(out=PI[:, 0, 1, :], in0=T["t2"][:], in1=T["c0"][:], op=ALU.mult))
                ew.c(ew.e.scalar_tensor_tensor(out=PI[:, 1, 1, :], in0=T["t2"][:], scalar=-1.0, in1=T["s0"][:],
                                               op0=ALU.mult, op1=ALU.mult))
                for _ in range(4):
                    cmul(ew, PW[:, 0, 1, :], PW[:, 1, 1, :], PW[:, 0, 1, :], PW[:, 1, 1, :],
                         PW[:, 0, 1, :], PW[:, 1, 1, :], t1, t2)
                    cmul(ew, PI[:, 0, 1, :], PI[:, 1, 1, :], PI[:, 0, 1, :], PI[:, 1, 1, :],
                         PI[:, 0, 1, :], PI[:, 1, 1, :], t1, t2)
                ew.c(ew.e.memset(PW[:, 0, 0, :], 1.0))
                ew.c(ew.e.memset(PW[:, 1, 0, :], 0.0))
                ew.c(ew.e.memset(PI[:, 0, 0, :], 1.0))
                ew.c(ew.e.memset(PI[:, 1, 0, :], 0.0))
                for k in range(2, 9):
                    cmul(ew, PW[:, 0, k, :], PW[:, 1, k, :], PW[:, 0, k - 1, :], PW[:, 1, k - 1, :],
                         PW[:, 0, 1, :], PW[:, 1, 1, :], t1, t2)
                for k in range(2, 8):
                    cmul(ew, PI[:, 0, k, :], PI[:, 1, k, :], PI[:, 0, k - 1, :], PI[:, 1, k - 1, :],
                         PI[:, 0, 1, :], PI[:, 1, 1, :], t1, t2)
                ew.c(ew.e.tensor_copy(T["are64"][:], PW[:, 0, 8, :]))
                ew.c(ew.e.tensor_copy(T["aim64"][:], PW[:, 1, 8, :]))
                for _ in range(3):
                    cmul(ew, T["are64"][:], T["aim64"][:], T["are64"][:], T["aim64"][:],
                         T["are64"][:], T["aim64"][:], t1, t2)
                nr = T["c0"][:]
                ew.c(ew.e.tensor_scalar_add(nr, PW[:, 0, 1, :], -1.0))
                ew.c(ew.e.tensor_tensor(out=t1, in0=are[:], in1=are[:], op=ALU.mult))
                ew.c(ew.e.tensor_tensor(out=t2, in0=aim[:], in1=aim[:], op=ALU.mult))
                ew.c(ew.e.tensor_tensor(out=t3, in0=t1, in1=t2, op=ALU.add))
                ew.c(ew.e.reciprocal(t3, t3))
                ew.c(ew.e.tensor_tensor(out=t1, in0=nr, in1=are[:], op=ALU.mult))
                ew.c(ew.e.tensor_tensor(out=t2, in0=PW[:, 1, 1, :], in1=aim[:], op=ALU.mult))
                ew.c(ew.e.tensor_tensor(out=t1, in0=t1, in1=t2, op=ALU.add))
                ew.c(ew.e.tensor_tensor(out=T["fre"][:], in0=t1, in1=t3, op=ALU.mult))
                ew.c(ew.e.tensor_tensor(out=t1, in0=PW[:, 1, 1, :], in1=are[:], op=ALU.mult))
                ew.c(ew.e.tensor_tensor(out=t2, in0=nr, in1=aim[:], op=ALU.mult))
                ew.c(ew.e.tensor_tensor(out=t1, in0=t1, in1=t2, op=ALU.subtract))
                ew.c(ew.e.tensor_tensor(out=T["fim"][:], in0=t1, in1=t3, op=ALU.mult))
                ew.c(ew.e.tensor_scalar(out=T["fims"][:], in0=T["fim"][:], scalar1=sg[:, 0:1], scalar2=None,
                                        op0=ALU.mult))
                f_re = bc(T["fre"][:].unsqueeze(2), [128, NP, 16])
                f_ims = bc(T["fims"][:].unsqueeze(2), [128, NP, 16])
                ta = tA[:, :, 0, :]
                ew.c(ew.e.tensor_tensor(out=fBP[:], in0=BP[:], in1=f_re, op=ALU.mult))
                ew.c(ew.e.tensor_tensor(out=ta, in0=BQ[:], in1=f_ims, op=ALU.mult))
                ew.c(ew.e.tensor_tensor(out=fBP[:], in0=fBP[:], in1=ta, op=ALU.add))
                ew.c(ew.e.tensor_tensor(out=fBQ[:], in0=BQ[:], in1=f_re, op=ALU.mult))
                ew.c(ew.e.tensor_tensor(out=ta, in0=BP[:], in1=f_ims, op=ALU.mult))
                ew.c(ew.e.tensor_tensor(out=fBQ[:], in0=fBQ[:], in1=ta, op=ALU.subtract))

                def ctab(out, P_, Q_, pw_re_fn, pw_im_fn, a_sgn_col, b_sgn_col, b_neg=False):
                    for j in range(8):
                        if a_sgn_col is None:
                            ew.c(ew.e.tensor_copy(AL[:, :, j], pw_re_fn(j)))
                        else:
                            ew.c(ew.e.tensor_scalar(out=AL[:, :, j], in0=pw_re_fn(j), scalar1=sg[:, a_sgn_col:a_sgn_col + 1],
                                                    scalar2=None, op0=ALU.mult))
                        if b_sgn_col is None:
                            ew.c(ew.e.tensor_scalar(out=BE[:, :, j], in0=pw_im_fn(j), scalar1=(-1.0 if b_neg else 1.0),
                                                    scalar2=None, op0=ALU.mult))
                        else:
                            ew.c(ew.e.tensor_scalar(out=BE[:, :, j], in0=pw_im_fn(j), scalar1=sg[:, b_sgn_col:b_sgn_col + 1],
                                                    scalar2=None, op0=ALU.mult))
                    sh = [128, NP, 8, 16]
                    ew.c(ew.e.tensor_tensor(out=tA[:], in0=bc(P_.unsqueeze(2), sh), in1=bc(AL[:].unsqueeze(3), sh), op=ALU.mult))
                    ew.c(ew.e.tensor_tensor(out=tB[:], in0=bc(Q_.unsqueeze(2), sh), in1=bc(BE[:].unsqueeze(3), sh), op=ALU.mult))
                    ew.c(ew.e.tensor_tensor(out=out, in0=tA[:], in1=tB[:], op=ALU.add))
                ctab(Zst[:], fBP[:], fBQ[:], lambda j: PW[:, 0, 7 - j, :], lambda j: PW[:, 1, 7 - j, :], None, 0)
                ctab(W2[:], CP[:], CQ[:], lambda j: PW[:, 0, j + 1, :], lambda j: PW[:, 1, j + 1, :], 1, None, True)
                ctab(E0[:], fBP[:], fBQ[:], lambda j: PI[:, 0, j, :], lambda j: PI[:, 1, j, :], 1, None, True)
                ctab(CT0[:], CP[:], CQ[:], lambda j: PW[:, 0, j, :], lambda j: PW[:, 1, j, :], None, 0)
                ew.c(ew.e.tensor_copy(T["a8re"][:], PW[:, 0, 8, :]))
                ew.c(ew.e.tensor_scalar(out=T["a8ims"][:], in0=PW[:, 1, 8, :], scalar1=sg[:, 1:2], scalar2=None,
                                        op0=ALU.mult))
                for p_ in range(NP):
                    ew.c(ew.e.tensor_scalar(out=tA[:, 0, :, :].rearrange("p a b -> p (a b)"), in0=Jm[:],
                                            scalar1=T["a8ims"][:, p_:p_ + 1], scalar2=None, op0=ALU.mult))
                    ew.c(ew.e.scalar_tensor_tensor(out=R8[:, p_, :], in0=ident[:], scalar=T["a8re"][:, p_:p_ + 1],
                                                   in1=tA[:, 0, :, :].rearrange("p a b -> p (a b)"),
                                                   op0=ALU.mult, op1=ALU.add))
            b.blk(dve=sc4)

            for half in range(4):
                prs = range(half * 4, half * 4 + 4)

                def tp(ew, prs=prs):
                    for q, p_ in enumerate(prs):
                        ew.tr(ps[0][:, q * 128:(q + 1) * 128], Zst[:, p_, :, :].rearrange("p a b -> p (a b)"), ident[:])
                        ew.mm(ps[1][:, q * 128:(q + 1) * 128], E0[:, p_, :, :].rearrange("p a b -> p (a b)"),
                              CT0[:, p_, :, :].rearrange("p a b -> p (a b)"), start=True, stop=True)
                b.blk(pe=tp)

                def tpe(ew, prs=prs):
                    for q, p_ in enumerate(prs):
                        ew.c(ew.e.tensor_copy(W1[:, p_, :], ps[0][:, q * 128:(q + 1) * 128]))
                        ew.c(ew.e.tensor_tensor(out=tA[:, 0, :, :].rearrange("p a b -> p (a b)"),
                                                in0=ps[1][:, q * 128:(q + 1) * 128], in1=MLm[:], op=ALU.mult))
                        if p_ < 8:
                            ew.c(ew.e.scalar_tensor_tensor(out=TP[:, p_, :], in0=ident[:], scalar=Dp[:, p_:p_ + 1],
                                                           in1=tA[:, 0, :, :].rearrange("p a b -> p (a b)"),
                                                           op0=ALU.mult, op1=ALU.add))
                        else:
                            ew.c(ew.e.tensor_copy(TP[:, p_, :], tA[:, 0, :, :].rearrange("p a b -> p (a b)")))
                b.blk(dve=tpe)

            def tiles(p_, r):
                off = r if p_ < 8 else 7 - r
                return slice(off, NTI, 8)

            def sweep(down):
                cur, nxt = Sa, Sb
                for r in range(8 if not down else 7):
                    def pe_(ew, r=r, cur=cur):
                        for p_ in range(NP):
                            o = ps[p_ // 7][:, (p_ % 7) * NCH:(p_ % 7 + 1) * NCH]
                            first = True
                            if down:
                                ew.mm(o, R8[:, p_, :], St[:, p_, tiles(p_, r)], start=True, stop=False)
                                first = False
                            elif r > 0:
                                ew.mm(o, R8[:, p_, :], cur[:, p_, :], start=True, stop=False)
                                first = False
                            ew.mm(o, W1[:, p_, :], U2[:, p_, tiles(p_, r)], start=first, stop=True)
                    b.blk(pe=pe_)

                    def ev(ew, r=r, nxt=nxt):
                        for bk in range(3):
                            n_p = 7 if bk < 2 else 2
                            prs = slice(bk * 7, bk * 7 + n_p)
                            src = ps[bk][:, 0:n_p * NCH].rearrange("p (a m) -> p a m", m=NCH)
                            if down:
                                for p_ in range(bk * 7, bk * 7 + n_p):
                                    ew.c(ew.e.tensor_copy(St[:, p_, tiles(p_, r + 1)],
                                                          ps[bk][:, (p_ - bk * 7) * NCH:(p_ - bk * 7 + 1) * NCH]))
                            elif r < 7:
                                ew.c(ew.e.tensor_copy(nxt[:, prs, :], src))
                            else:
                                ew.c(ew.e.tensor_copy(Fm[:, prs, :], src))
                    b.blk(dve=ev)
                    cur, nxt = nxt, cur

            sweep(False)

            def swp(ew):
                for bk in range(3):
                    n_p = 7 if bk < 2 else 2
                    ew.mm(ps[bk][:, 0:n_p * NCH], SWm[:], Fm[:, bk * 7:bk * 7 + n_p, :].rearrange("p a m -> p (a m)"),
                          start=True, stop=True)
            b.blk(pe=swp)

            def swe(ew):
                for bk in range(3):
                    n_p = 7 if bk < 2 else 2
                    ew.c(ew.e.tensor_copy(Fs[:, bk * 7:bk * 7 + n_p, :],
                                          ps[bk][:, 0:n_p * NCH].rearrange("p (a m) -> p a m", m=NCH)))
                ew.c(ew.e.memset(Si[:], 0.0))
                ew.c(ew.e.memset(Ss[:], 0.0))
            b.blk(dve=swe)

            order_f = list(range(NCH))
            order_r = [3, 2, 1, 0] + list(range(NCH - 1, 3, -1))

            def l2(ew, prs, order, tt):
                a_re = T["are64"][:, prs]
                a_im = T["aim64"][:, prs]
                for k in range(NCH - 1):
                    m, m2 = order[k], order[k + 1]
                    t = ew.e.tensor_tensor
                    ew.c(t(out=tt[:, 0, :], in0=a_re, in1=Si[:, prs, m], op=ALU.mult))
                    ew.c(t(out=tt[:, 1, :], in0=a_im, in1=Ss[:, prs, m], op=ALU.mult))
                    ew.c(t(out=tt[:, 2, :], in0=a_re, in1=Ss[:, prs, m], op=ALU.mult))
                    ew.c(t(out=tt[:, 3, :], in0=a_im, in1=Si[:, prs, m], op=ALU.mult))
                    ew.c(t(out=tt[:, 0, :], in0=tt[:, 0, :], in1=tt[:, 1, :], op=ALU.add))
                    ew.c(t(out=tt[:, 2, :], in0=tt[:, 2, :], in1=tt[:, 3, :], op=ALU.subtract))
                    ew.c(t(out=Si[:, prs, m2], in0=tt[:, 0, :], in1=Fm[:, prs, m], op=ALU.add))
                    ew.c(t(out=Ss[:, prs, m2], in0=tt[:, 2, :], in1=Fs[:, prs, m], op=ALU.add))
            b.blk(dve=lambda ew: l2(ew, slice(0, 8), order_f, l2t[0]),
                  pool=lambda ew: l2(ew, slice(8, 16), order_r, l2t[1]))

            def dinit(ew):
                ew.c(ew.e.tensor_copy(St[:, 0:8, slice(0, NTI, 8)], Si[:, 0:8, :]))
                ew.c(ew.e.tensor_copy(St[:, 8:16, slice(7, NTI, 8)], Si[:, 8:16, :]))
            b.blk(dve=dinit)
            sweep(True)

            for p_ in range(NP):
                def ype(ew, p_=p_):
                    for cb_, (c0_, c1_) in enumerate(((0, 512), (512, NTI))):
                        o = ps[(p_ % 2) * 2 + cb_][:, 0:c1_ - c0_]
                        ew.mm(o, TP[:, p_, :], U2[:, p_, c0_:c1_], start=True, stop=False)
                        ew.mm(o, W2[:, p_, :, :].rearrange("p a b -> p (a b)"), St[:, p_, c0_:c1_], start=False, stop=True)

                def yev(ew, p_=p_):
                    ew.c(ew.e.activation(out=Yo[:, p_, 0:512], in_=ps[(p_ % 2) * 2][:, 0:512], func=AF.Copy))
                    ew.c(ew.e.activation(out=Yo[:, p_, 512:NTI], in_=ps[(p_ % 2) * 2 + 1][:, 0:NTI - 512], func=AF.Copy))
                b.blk(pe=ype)
                b.blk(act=yev)

            def yout_(ew):
                for t in range(8):
                    ew.dma(yd[0, t, qb * 128:(qb + 1) * 128, :].rearrange("(g q) i -> q g i", q=16),
                           Yo[t * 16:(t + 1) * 16, 0:8, :])
                    ew.dma(yd[1, 7 - t, qb * 128:(qb + 1) * 128, :].rearrange("(g q) i -> q g i", q=16),
                           Yo[t * 16:(t + 1) * 16, 8:16, :])
            b.blk(sp=yout_)


def lru_phase(nc, b, l, ps, xr_d, gg_d, yl_d, cw, cb, lruW, lb_rg, lb_ig, llam):
    with ExitStack() as st:
        sbt = lambda n, s, d: st.enter_context(nc.sbuf_tensor(uniq(n), list(s), d))
        cwt = sbt("cwt", [128, 4, 4], F32)
        cbt = sbt("cbt", [128, 4], F32)
        brg = sbt("brg", [128, 2, 4], F32)
        big = sbt("big", [128, 2, 4], F32)
        lam = sbt("lam", [128, 2, 4], F32)
        nsp = sbt("nsp", [128, 2, 4], F32)
        nsp2 = sbt("nsp2", [128, 2, 4], F32)
        Wg = sbt("Wg", [128, 2, 2, 4, 128], BF)
        xp = sbt("xp", [128, 3 + NCTX + 3 + NLAT + 3], F32)
        xc = sbt("xc", [128, NT], F32)
        xcb = sbt("xcb", [128, NT], BF)
        R = [xp, sbt("R1", [128, NT], F32)]
        I = [sbt(f"I{d}", [128, NT], F32) for d in range(2)]
        H = [sbt(f"H{d}", [128, NT], F32) for d in range(2)]
        xraw = H[0]
        ggt = sbt("ggt", [128, NT], BF)
        ylo = xcb
        hc = sbt("hc", [128, 2], F32)
        CO = 3
        LO = 3 + NCTX + 3

        def ld(ew):
            ew.dma(cwt[:], cw[:, l, :, :])
            ew.dma(cbt[:], cb[:, l, :])
            ew.dma(brg[:], lb_rg[:, l, :, :])
            ew.dma(big[:], lb_ig[:, l, :, :])
            ew.dma(lam[:], llam[:, l, :, :])

        def ldw(ew):
            for d in range(2):
                for g in range(2):
                    ew.dma(Wg[:, d, g, :, :], lruW[l, d, g].rearrange("c i o -> i c o"))
        b.blk(sp=ld, pool=ldw)

        b.blk(act=lambda ew: ew.c(ew.e.activation(out=nsp[:], in_=lam[:], func=AF.Exp, scale=-1.0)))
        b.blk(act=lambda ew: ew.c(ew.e.activation(out=nsp[:], in_=nsp[:], func=AF.Ln, bias=1.0)))

        def nsp_(ew):
            ew.c(ew.e.tensor_scalar(out=nsp2[:], in0=nsp[:], scalar1=-16.0, scalar2=None, op0=ALU.mult))
            ew.c(ew.e.tensor_scalar(out=nsp[:], in0=nsp[:], scalar1=-8.0, scalar2=None, op0=ALU.mult))
        b.blk(dve=nsp_)

        for cc in range(4):
            rows = slice(cc * 128, (cc + 1) * 128)

            def l0(ew, rows=rows):
                ew.dma(xraw[:], xr_d[rows, :])
                ew.dma(ggt[:], gg_d[rows, :])
            b.blk(sp=l0)

            def l1(ew):
                ew.c(ew.e.memset(xp[:], 0.0))
                ew.c(ew.e.tensor_copy(xp[:, CO:CO + NCTX], xraw[:, 0:NCTX]))
                ew.c(ew.e.tensor_copy(xp[:, LO:LO + NLAT].rearrange("p (c r) -> p c r", r=64),
                                      xraw[:, NCTX:NT].rearrange("p (r c) -> p c r", c=64)))
            b.blk(dve=l1)

            def l2_(ew, cc=cc):
                for (o0, n, base) in ((0, NCTX, CO), (NCTX, NLAT, LO)):
                    ew.c(ew.e.tensor_scalar(out=xc[:, o0:o0 + n], in0=xp[:, base - 2:base - 2 + n],
                                            scalar1=cwt[:, 0, cc:cc + 1], scalar2=cbt[:, cc:cc + 1],
                                            op0=ALU.mult, op1=ALU.add))
                    for k in range(1, 4):
                        ew.c(ew.e.scalar_tensor_tensor(out=xc[:, o0:o0 + n], in0=xp[:, base - 2 + k:base - 2 + k + n],
                                                       scalar=cwt[:, k, cc:cc + 1], in1=xc[:, o0:o0 + n],
                                                       op0=ALU.mult, op1=ALU.add))
                ew.c(ew.e.tensor_copy(xcb[:], xc[:]))
            b.blk(dve=l2_)

            blocks = [(i * 512, min(512, NT - i * 512)) for i in range(9)]
            for bi, (c0_, n) in enumerate(blocks):
                def gpe(ew, c0_=c0_, n=n, cc=cc):
                    for d in range(2):
                        for g in range(2):
                            ew.mm(ps[d * 2 + g][:, 0:n], Wg[:, d, g, cc, :], xcb[:, c0_:c0_ + n], start=True, stop=True)

                def gev(ew, c0_=c0_, n=n, cc=cc):
                    for d in range(2):
                        ew.c(ew.e.activation(out=R[d][:, c0_:c0_ + n], in_=ps[d * 2][:, 0:n], func=AF.Sigmoid,
                                             bias=brg[:, d, cc:cc + 1]))
                        ew.c(ew.e.activation(out=I[d][:, c0_:c0_ + n], in_=ps[d * 2 + 1][:, 0:n], func=AF.Sigmoid,
                                             bias=big[:, d, cc:cc + 1]))
                b.blk(pe=gpe)
                b.blk(act=gev)

            def e1(ew, cc=cc):
                for d in range(2):
                    ew.c(ew.e.activation(out=H[d][:], in_=R[d][:, 0:NT], func=AF.Exp, scale=nsp2[:, d, cc:cc + 1]))
                    ew.c(ew.e.activation(out=R[d][:, 0:NT], in_=R[d][:, 0:NT], func=AF.Exp, scale=nsp[:, d, cc:cc + 1]))

            def e1d(ew):
                for d in range(2):
                    ew.c(ew.e.tensor_tensor(out=I[d][:], in0=I[d][:], in1=xc[:], op=ALU.mult))
            b.blk(act=e1, dve=e1d)

            def e2(ew):
                for d in range(2):
                    ew.c(ew.e.activation(out=H[d][:], in_=H[d][:], func=AF.Sqrt, scale=-1.0, bias=1.0))
            b.blk(act=e2)

            def e3(ew):
                for d in range(2):
                    ew.c(ew.e.tensor_tensor(out=I[d][:], in0=I[d][:], in1=H[d][:], op=ALU.mult))
            b.blk(dve=e3)

            def rv(ap2d, o0, n):
                full = ap2d[:, o0:o0 + n]
                return bass.AP(full.tensor, full.offset + (n - 1), [list(full.ap[0]), [-1, n]])

            def sc(ew):
                ew.c(ew.e.tensor_tensor_scan(out=H[0][:, 0:NCTX], data0=R[0][:, 0:NCTX], data1=I[0][:, 0:NCTX],
                                             initial=0.0, op0=ALU.mult, op1=ALU.add))
                ew.c(ew.e.tensor_copy(hc[:, 0:1], H[0][:, NCTX - 1:NCTX]))
                ew.c(ew.e.tensor_tensor_scan(out=H[0][:, NCTX:NT], data0=R[0][:, NCTX:NT], data1=I[0][:, NCTX:NT],
                                             initial=hc[:, 0:1], op0=ALU.mult, op1=ALU.add))
                ew.c(ew.e.tensor_tensor_scan(out=rv(H[1], 0, NCTX), data0=rv(R[1], 0, NCTX), data1=rv(I[1], 0, NCTX),
                                             initial=0.0, op0=ALU.mult, op1=ALU.add))
                ew.c(ew.e.tensor_copy(hc[:, 1:2], H[1][:, 0:1]))
                ew.c(ew.e.tensor_tensor_scan(out=rv(H[1], NCTX, NLAT), data0=rv(R[1], NCTX, NLAT),
                                             data1=rv(I[1], NCTX, NLAT), initial=hc[:, 1:2], op0=ALU.mult, op1=ALU.add))
                ew.c(ew.e.tensor_tensor(out=H[0][:], in0=H[0][:], in1=H[1][:], op=ALU.add))
                ew.c(ew.e.tensor_tensor(out=ylo[:, 0:NCTX], in0=H[0][:, 0:NCTX], in1=ggt[:, 0:NCTX], op=ALU.mult))
                ew.c(ew.e.tensor_tensor(out=ylo[:, NCTX:NT].rearrange("p (r c) -> p r c", c=64),
                                        in0=H[0][:, NCTX:NT].rearrange("p (c r) -> p r c", r=64),
                                        in1=ggt[:, NCTX:NT].rearrange("p (r c) -> p r c", c=64), op=ALU.mult))
            b.blk(dve=sc)
            b.blk(sp=lambda ew, rows=rows: ew.dma(yl_d[rows, :], ylo[:]))


_NC_CACHE = {}


def _host_inputs(inp, bidx):
    f = np.float32
    x = np.asarray(inp["x"], f)
    ctx = np.asarray(inp["ctx"], f)
    d = {}
    d["xs"] = np.ascontiguousarray(np.concatenate([ctx[bidx].T, x[bidx].T], axis=1))
    cv = np.stack([np.asarray(inp["c"], f)[bidx], np.asarray(inp["c_ctx"], f)], axis=-1)
    d["cvec"] = np.ascontiguousarray(cv.reshape(8, 128, 2).transpose(1, 0, 2))
    return d


def _shared_inputs(inp):
    f = np.float32
    g = lambda k: np.asarray(inp[k], f)
    d = {}
    d["w_ada"] = g("w_ada")
    d["b_ada"] = np.ascontiguousarray(g("b_ada").reshape(NL, 48, 128).transpose(2, 0, 1))
    d["gains"] = np.ascontiguousarray(g("norm_gains").reshape(NL, 4, 8, 128).transpose(3, 0, 1, 2))
    d["w_in"] = g("w_in")
    d["w_out"] = g("w_out")
    d["w_ffi"] = g("w_ffn_in")
    d["w_ffo"] = g("w_ffn_out")
    d["w_glu"] = g("s5_w_glu")
    d["b_glu"] = np.ascontiguousarray(g("s5_b_glu").reshape(NL, 4, 128).transpose(2, 0, 1))

    def st2(a):
        a = np.concatenate([a, a], axis=-1)
        a = a.reshape(NL, 2, 2, 16, 128).reshape(NL, 4, 16, 128)
        return np.ascontiguousarray(a.transpose(3, 0, 1, 2))
    d["sa_re"] = st2(g("s5_a_re"))
    d["sa_im"] = st2(g("s5_a_im"))
    d["sldt"] = st2(np.broadcast_to(g("s5_log_dt")[..., None], (NL, 2, 32, 64)))

    def st3(top, bot):
        a = np.concatenate([top, bot], axis=3)
        a = a.reshape(NL, 4, 16, 128, a.shape[-1])
        return np.ascontiguousarray(a.transpose(3, 0, 1, 2, 4))
    bre, bim = g("s5_b_re"), g("s5_b_im")
    d["sBP"] = st3(bre, bim)
    d["sBQ"] = st3(bim, bre)
    cre = g("s5_c_re").transpose(0, 1, 2, 4, 3)
    cim = g("s5_c_im").transpose(0, 1, 2, 4, 3)
    d["sCP"] = st3(cre, cim)
    d["sCQ"] = st3(cim, cre)
    sd = g("s5_d").reshape(NL, 4, 8, 16)
    sd = np.broadcast_to(sd[:, :, :, None, :], (NL, 4, 8, 8, 16))
    d["sD"] = np.ascontiguousarray(sd.transpose(3, 4, 0, 1, 2).reshape(128, NL, 4, 8))
    d["cw"] = np.ascontiguousarray(g("lru_conv_w").reshape(NL, 4, 4, 128).transpose(3, 0, 1, 2))
    d["cb"] = np.ascontiguousarray(g("lru_conv_b").reshape(NL, 4, 128).transpose(2, 0, 1))
    W = np.zeros((NL, 2, 2, 4, 128, 128), f)
    for gi, key in enumerate(("lru_w_rg", "lru_w_ig")):
        w = g(key)
        for cc in range(4):
            W[:, :, gi, cc, 0:64, 0:64] = w[:, :, 2 * cc]
            W[:, :, gi, cc, 64:128, 64:128] = w[:, :, 2 * cc + 1]
    d["lruW"] = W
    v = lambda k: np.ascontiguousarray(g(k).reshape(NL, 2, 4, 128).transpose(3, 0, 1, 2))
    d["lb_rg"] = v("lru_b_rg")
    d["lb_ig"] = v("lru_b_ig")
    d["llam"] = v("lru_lambda")
    I = np.eye(128, dtype=f)
    d["cI"] = I
    J = np.zeros((128, 128), f)
    SW = np.zeros((128, 128), f)
    for n in range(64):
        J[n, 64 + n] = 1.0
        J[64 + n, n] = 1.0
        SW[64 + n, n] = -1.0
        SW[n, 64 + n] = 1.0
    d["cJ"] = J
    d["cSW"] = SW
    sidx = np.arange(128) // 16
    d["cML"] = (sidx[None, :] >= sidx[:, None]).astype(f)
    sgn = np.where(np.arange(128) < 64, -1.0, 1.0).astype(f)
    d["cSG"] = np.stack([sgn, -sgn, np.ones(128, f), np.full(128, 1.0 / 1024, f)], axis=1)
    return d


def kernel(**inputs):
    if "nc" not in _NC_CACHE:
        _NC_CACHE["nc"] = build_program()
    nc = _NC_CACHE["nc"]
    shared = _shared_inputs(inputs)
    in_maps = []
    for bidx in range(8):
        m = dict(shared)
        m.update(_host_inputs(inputs, bidx))
        in_maps.append(m)
    res = run_bass_kernel_spmd(nc, in_maps, core_ids=list(range(8)))
    out = np.stack([np.asarray(r["yout"], np.float32).T for r in res.results], axis=0)
    return np.ascontiguousarray(out)
```

```python
import math
from contextlib import ExitStack
import numpy as np
import concourse.bass as bass
import concourse.mybir as mybir
from concourse.bass_utils import run_bass_kernel_spmd

F32 = mybir.dt.float32
BF = mybir.dt.bfloat16
AF = mybir.ActivationFunctionType
ALU = mybir.AluOpType

NL = 4
DM = 1024
NCTX = 256
NLAT = 4096
NT = NCTX + NLAT
NTI = NT // 8
NCH = NTI // 8
TN = 128
NTT = NT // TN
DFF = 2816
EPS = 1e-6
DEBUG = False
NL_RUN = 4
STOP_AFTER = 99


class EW:
    def __init__(self, b, name):
        self.b = b
        self.name = name
        self.sem = b.newsem()
        self.n = 0
        self.dsem = b.newsem()
        self.dn = 0
        self.e = None
        self.last = None

    def c(self, ins):
        if self.n >= 6000:
            self.sem = self.b.newsem()
            self.n = 0
        self.n += 1
        ins.then_inc(self.sem, 1)
        self.e.wait_ge(self.sem, self.n)
        return ins

    def mm(self, *a, **k):
        self.last = self.e.matmul(*a, **k)
        return self.last

    def tr(self, *a, **k):
        self.last = self.e.transpose(*a, **k)
        return self.last

    def dma(self, out, in_):
        if self.dn >= 550:
            self.dsem = self.b.newsem()
            self.dn = 0
        self.dn += 1
        self.e.dma_start(out=out, in_=in_).then_inc(self.dsem, 16)

    def fin(self):
        if self.last is not None:
            self.c(self.last)
            self.last = None
        if self.dn:
            self.e.wait_ge(self.dsem, 16 * self.dn)


class _ProxyEng:
    def __getattr__(self, name):
        return lambda *a, **k: (name, a, k)


class _ProxyEW:
    def __init__(self):
        self.e = _ProxyEng()
        self.ops = []

    def c(self, rec):
        self.ops.append(rec)
        return rec


class Bld:
    def __init__(self, nc):
        self.nc = nc
        self.es = ExitStack()
        self.sems = [self.es.enter_context(nc.semaphore(f"sm{i}")) for i in range(96)]
        self.si = 0
        self.ew = {k: EW(self, k) for k in ("pe", "act", "dve", "pool", "sp")}
        self.nblk = 0
        self.rec = None

    def emit_zipped(self, a, bsteps):
        for i in range(max(len(a), len(bsteps))):
            fns = {}
            for lst in (a, bsteps):
                if i < len(lst):
                    for name, fn in lst[i].items():
                        if fn is not None:
                            fns.setdefault(name, []).append(fn)
            self.blk(**{n: (lambda ew, l=l: [f(ew) for f in l]) for n, l in fns.items()})

    def newsem(self):
        s = self.sems[self.si]
        self.si += 1
        return s

    def sb(self, name, shape, dt):
        return self.es.enter_context(self.nc.sbuf_tensor(name, list(shape), dt))

    def blk(self, **fns):
        m = {"pe": "tensor", "act": "scalar", "dve": "vector", "pool": "gpsimd", "sp": "sync"}
        if self.rec is not None:
            self.rec.append(dict(fns))
            return
        self.nblk += 1
        with self.nc.Block() as block:
            for name, fn in fns.items():
                if fn is None:
                    continue
                ew = self.ew[name]

                def run(e, ew=ew, fn=fn):
                    ew.e = e
                    fn(ew)
                    ew.fin()

                getattr(block, m[name])(run)

    def pipeline(self, tasks):
        nsteps = max((k + len(t) for k, t in enumerate(tasks)), default=0)
        for t in range(nsteps):
            fns = {}
            for k in range(max(0, t - 24), min(len(tasks), t + 1)):
                j = t - k
                if j < len(tasks[k]):
                    for name, fn in tasks[k][j].items():
                        fns.setdefault(name, []).append(fn)
            if not fns:
                continue
            self.blk(**{n: (lambda ew, l=l: [f(ew) for f in reversed(l)]) for n, l in fns.items()})


_UC = [0]


def uniq(n):
    _UC[0] += 1
    return f"{n}_{_UC[0]}"


def mkap(tile, off, dims):
    full = tile[:]
    return bass.AP(full.tensor, full.offset + off, [list(full.ap[0])] + [list(d) for d in dims])


def bc(ap, shape):
    return ap.to_broadcast(list(shape))


def build_program():
    nc = bass.Bass("TRN2", target_bir_lowering=False)
    b = Bld(nc)

    def din(name, shape, dt=F32):
        return nc.dram_tensor(name, list(shape), dt, kind="ExternalInput").ap()

    def dscr(name, shape, dt=F32):
        kind = "ExternalOutput"
        return nc.dram_tensor(name, list(shape), dt, kind=kind).ap()

    xs = din("xs", [DM, NT])
    cvec = din("cvec", [128, 8, 2])
    w_ada = din("w_ada", [NL, DM, 6 * DM])
    b_ada = din("b_ada", [128, NL, 48])
    gains = din("gains", [128, NL, 4, 8])
    w_in = din("w_in", [NL, DM, 1536])
    w_out = din("w_out", [NL, DM, DM])
    w_ffi = din("w_ffi", [NL, DM, 2 * DFF])
    w_ffo = din("w_ffo", [NL, DFF, DM])
    w_glu = din("w_glu", [NL, 512, 512])
    b_glu = din("b_glu", [128, NL, 4])
    sa_re = din("sa_re", [128, NL, 4, 16])
    sa_im = din("sa_im", [128, NL, 4, 16])
    sldt = din("sldt", [128, NL, 4, 16])
    sBP = din("sBP", [128, NL, 4, 16, 16])
    sBQ = din("sBQ", [128, NL, 4, 16, 16])
    sCP = din("sCP", [128, NL, 4, 16, 16])
    sCQ = din("sCQ", [128, NL, 4, 16, 16])
    sD = din("sD", [128, NL, 4, 8])
    cw = din("cw", [128, NL, 4, 4])
    cb = din("cb", [128, NL, 4])
    lruW = din("lruW", [NL, 2, 2, 4, 128, 128])
    lb_rg = din("lb_rg", [128, NL, 2, 4])
    lb_ig = din("lb_ig", [128, NL, 2, 4])
    llam = din("llam", [128, NL, 2, 4])
    cI = din("cI", [128, 128])
    cJ = din("cJ", [128, 128])
    cSW = din("cSW", [128, 128])
    cML = din("cML", [128, 128])
    cSG = din("cSG", [128, 4])
    yout = nc.dram_tensor("yout", [DM, NLAT], F32, kind="ExternalOutput").ap()

    xres = dscr("xres", [DM, NT])
    ud = dscr("ud", [8, 512, NTI], BF)
    yd = dscr("yd", [2, 8, 512, NTI], BF)
    xr_d = dscr("xr_d", [512, NT])
    gg_d = dscr("gg_d", [512, NT], BF)
    yl_d = dscr("yl_d", [512, NT], BF)
    yg_d = dscr("yg_d", [512, NT], BF)

    xres_v = xres.rearrange("(kc p) n -> p kc n", p=128)
    xs_v = xs.rearrange("(kc p) n -> p kc n", p=128)

    ident = b.sb("ident", [128, 128], F32)
    identb = b.sb("identb", [128, 128], BF)
    Jm = b.sb("Jm", [128, 128], F32)
    SWm = b.sb("SWm", [128, 128], F32)
    MLm = b.sb("MLm", [128, 128], F32)
    sg = b.sb("sg", [128, 4], F32)
    onesb = b.sb("onesb", [128, 128], BF)
    epsb = b.sb("epsb", [128, 1], F32)
    MOD = b.sb("MOD", [128, NL, 6, 8, 2], F32)
    GA = b.sb("GA", [128, NL, 4, 8], F32)
    DER = b.sb("DER", [128, NL, 4, 8, 2], F32)
    BADA = b.sb("BADA", [128, NL, 48], F32)
    ps = [b.es.enter_context(nc.psum_tensor(f"ps{i}", [128, 512], F32)) for i in range(8)]

    def c0(ew):
        ew.dma(ident[:], cI[:, :])
        ew.dma(Jm[:], cJ[:, :])
        ew.dma(SWm[:], cSW[:, :])
        ew.dma(MLm[:], cML[:, :])
        ew.dma(sg[:], cSG[:, :])
        ew.dma(GA[:], gains[:, :, :, :])
        ew.dma(BADA[:], b_ada[:, :, :])
    b.blk(sp=c0)

    def c1(ew):
        ew.c(ew.e.tensor_copy(identb[:], ident[:]))
        ew.c(ew.e.memset(onesb[:], 1.0 / 1024.0))
        ew.c(ew.e.memset(epsb[:], EPS))
    b.blk(dve=c1)

    def c2(ew):
        for kc in range(8):
            ew.dma(xres[kc * 128:(kc + 1) * 128, :], xs[kc * 128:(kc + 1) * 128, :])
    b.blk(sp=c2)

    with ExitStack() as st:
        cv = st.enter_context(nc.sbuf_tensor(uniq("cv"), [128, 8, 2], F32))
        scb = st.enter_context(nc.sbuf_tensor("scb", [128, 8, 2], BF))
        wad = [st.enter_context(nc.sbuf_tensor(f"wad{i}", [128, 8, 1024], BF)) for i in range(2)]

        b.blk(sp=lambda ew: ew.dma(cv[:], cvec[:, :, :]))
        b.blk(act=lambda ew: ew.c(ew.e.activation(out=scb[:], in_=cv[:], func=AF.Silu)))
        tasks = []
        for l in range(NL):
            for j in range(6):
                k = l * 6 + j
                wb = wad[k % 2]
                src = w_ada[l].rearrange("(kc p) n -> p kc n", p=128)[:, :, j * 1024:(j + 1) * 1024]

                def s_load(ew, wb=wb, src=src):
                    ew.dma(wb[:], src)

                def s_mm(ew, wb=wb, k=k):
                    for oc in range(8):
                        for kc in range(8):
                            ew.mm(ps[k % 2][:, oc * 2:oc * 2 + 2], wb[:, kc, oc * 128:(oc + 1) * 128],
                                  scb[:, kc, :], start=(kc == 0), stop=(kc == 7))

                def s_ev(ew, l=l, j=j, k=k):
                    ew.c(ew.e.tensor_tensor(
                        out=MOD[:, l, j, :, :],
                        in0=ps[k % 2][:, 0:16].rearrange("p (o w) -> p o w", w=2),
                        in1=bc(BADA[:, l, j * 8:(j + 1) * 8].unsqueeze(2), [128, 8, 2]),
                        op=ALU.add))
                tasks.append([{"pool": s_load}, {"pe": s_mm}, {"dve": s_ev}])
        b.pipeline(tasks)

        def derive(ew):
            for l in range(NL):
                for (dk, gk, mk, addone) in ((0, 0, 1, True), (1, 1, 2, False), (2, 2, 4, True), (3, 3, 5, False)):
                    g = bc(GA[:, l, gk, :].unsqueeze(2), [128, 8, 2])
                    if addone:
                        ew.c(ew.e.scalar_tensor_tensor(out=DER[:, l, dk, :, :], in0=MOD[:, l, mk, :, :],
                                                       scalar=1.0, in1=g, op0=ALU.add, op1=ALU.mult))
                    else:
                        ew.c(ew.e.tensor_tensor(out=DER[:, l, dk, :, :], in0=MOD[:, l, mk, :, :], in1=g,
                                                op=ALU.mult))
        b.blk(dve=derive)

    def norm_stages(l, ti, xt, sq, psS, rstd, tmp, h, which, tn):
        w = 1 if ti * tn < NCTX else 0
        cols = slice(ti * tn, (ti + 1) * tn)
        dk = 0 if which == 0 else 2
        mk = 0 if which == 0 else 3

        def s0(ew):
            ew.dma(xt[:], xres_v[:, :, cols])

        def s1(ew):
            ew.c(ew.e.activation(out=sq[:], in_=xt[:], func=AF.Square))

        def s2(ew):
            for kc in range(8):
                ew.mm(psS, onesb[:], sq[:, kc, :], start=(kc == 0), stop=(kc == 7))

        def s2b(ew):
            ew.c(ew.e.activation(out=rstd[:], in_=psS, func=AF.Sqrt, bias=epsb[:, 0:1]))

        def s3(ew):
            ew.c(ew.e.reciprocal(rstd[:], rstd[:]))
            ew.c(ew.e.tensor_tensor(out=tmp[:], in0=xt[:], in1=bc(rstd[:].unsqueeze(1), [128, 8, tn]),
                                    op=ALU.mult))

        def s4(ew):
            for kc in range(8):
                ew.c(ew.e.activation(out=h[:, kc, :], in_=tmp[:, kc, :], func=AF.Identity,
                                     scale=DER[:, l, dk, kc, w:w + 1], bias=MOD[:, l, mk, kc, w:w + 1]))
        return [{"sp": s0}, {"act": s1}, {"pe": s2}, {"act": s2b}, {"dve": s3}, {"act": s4}]

    def post_stages(l, ti, banks, xt, osb, sq, psS, rstd, which, last_layer, tn):
        w = 1 if ti * tn < NCTX else 0
        cols = slice(ti * tn, (ti + 1) * tn)
        dk = 1 if which == 0 else 3
        cpb = 512 // tn

        def s1(ew):
            for bi, bank in enumerate(banks):
                ew.c(ew.e.activation(out=osb[:, bi * cpb:(bi + 1) * cpb, :],
                                     in_=bank[:, 0:512].rearrange("p (c n) -> p c n", c=cpb), func=AF.Copy))

        def s1l(ew):
            ew.dma(xt[:], xres_v[:, :, cols])

        def s1b(ew):
            ew.c(ew.e.activation(out=sq[:], in_=osb[:], func=AF.Square))

        def s2(ew):
            for kc in range(8):
                ew.mm(psS, onesb[:], sq[:, kc, :], start=(kc == 0), stop=(kc == 7))

        def s2b(ew):
            ew.c(ew.e.activation(out=rstd[:], in_=psS, func=AF.Sqrt, bias=epsb[:, 0:1]))

        def s3(ew):
            ew.c(ew.e.reciprocal(rstd[:], rstd[:]))
            for mc in range(8):
                ew.c(ew.e.scalar_tensor_tensor(out=osb[:, mc, :], in0=osb[:, mc, :],
                                               scalar=DER[:, l, dk, mc, w:w + 1], in1=rstd[:],
                                               op0=ALU.mult, op1=ALU.mult))

        def s4(ew):
            ew.c(ew.e.tensor_tensor(out=xt[:], in0=xt[:], in1=osb[:], op=ALU.add))

        def s5(ew):
            ew.dma(xres_v[:, :, cols], xt[:])
            if last_layer and which == 1 and ti * tn >= NCTX:
                ew.dma(yout.rearrange("(kc p) n -> p kc n", p=128)[:, :, ti * tn - NCTX:(ti + 1) * tn - NCTX], xt[:])
        return [{"act": s1, "sp": s1l}, {"act": s1b}, {"pe": s2}, {"act": s2b}, {"dve": s3}, {"pool": s4}, {"sp": s5}]

    for l in range(NL_RUN):
        last = (l == NL - 1)
        with ExitStack() as st:
            sbt = lambda n, s, d: st.enter_context(nc.sbuf_tensor(uniq(n), list(s), d))
            T1 = 256
            NT1 = NT // T1
            win = sbt("win", [128, 8, 1536], BF)
            ut_all = sbt("ut_all", [128, 4, 8, NTI], BF)
            NB = 3
            xt = [sbt(f"xt{i}", [128, 8, T1], F32) for i in range(NB)]
            sq = [sbt(f"sq{i}", [128, 8, T1], BF) for i in range(NB)]
            rstd = [sbt(f"rstd{i}", [128, T1], F32) for i in range(NB)]
            hb = [sbt(f"hb{i}", [128, 8, T1], BF) for i in range(NB)]
            xro = [sbt(f"xro{i}", [128, 4, T1], F32) for i in range(NB)]
            ggo = [sbt(f"ggo{i}", [128, 4, T1], BF) for i in range(NB)]
            b.blk(pool=lambda ew: ew.dma(win[:], w_in[l].rearrange("(kc p) n -> p kc n", p=128)))
            tasks = []
            for ti in range(NT1):
                k = ti % NB
                psS = ps[6 + ti % 2][:, 0:T1]
                stg = norm_stages(l, ti, xt[k], sq[k], psS, rstd[k], xt[k], hb[k], 0, T1)
                cols = slice(ti * T1, (ti + 1) * T1)

                def mmh(ew, k=k, half=0):
                    for mc in range(half * 6, half * 6 + 6):
                        bank = ps[half * 3 + (mc % 6) // 2]
                        for kc in range(8):
                            ew.mm(bank[:, (mc % 2) * T1:(mc % 2 + 1) * T1],
                                  win[:, kc, mc * 128:(mc + 1) * 128], hb[k][:, kc, :],
                                  start=(kc == 0), stop=(kc == 7))

                def evA_d(ew, ti=ti):
                    for bk in range(2):
                        ew.c(ew.e.tensor_copy(
                            out=ut_all[:, bk * 2:bk * 2 + 2, :, ti * 32:(ti + 1) * 32],
                            in_=ps[bk][:, 0:512].rearrange("p (c i s) -> p c s i", c=2, s=8)))

                def evA_a(ew, k=k):
                    ew.c(ew.e.activation(out=xro[k][:, 0:2, :], in_=ps[2][:, 0:512].rearrange("p (c n) -> p c n", c=2),
                                         func=AF.Copy))

                def evB_a(ew, k=k):
                    ew.c(ew.e.activation(out=xro[k][:, 2:4, :], in_=ps[3][:, 0:512].rearrange("p (c n) -> p c n", c=2),
                                         func=AF.Copy))
                    for bk in range(2):
                        ew.c(ew.e.activation(out=ggo[k][:, bk * 2:bk * 2 + 2, :],
                                             in_=ps[4 + bk][:, 0:512].rearrange("p (c n) -> p c n", c=2),
                                             func=AF.Gelu_apprx_tanh))

                def stB(ew, k=k, cols=cols):
                    ew.dma(xr_d.rearrange("(c p) n -> p c n", p=128)[:, :, cols], xro[k][:])
                    ew.dma(gg_d.rearrange("(c p) n -> p c n", p=128)[:, :, cols], ggo[k][:])
                tasks.append(stg + [{"pe": lambda ew, f=mmh: f(ew, half=0)}, {"dve": evA_d, "act": evA_a}])
                tasks.append([{}] * 6 + [{"pe": lambda ew, f=mmh: f(ew, half=1)}, {"act": evB_a}, {"sp": stB}])
            b.pipeline(tasks)

            def uout(ew):
                for s in range(8):
                    ew.dma(ud[s].rearrange("(c p) i -> p c i", p=128), ut_all[:, :, s, :])
            b.blk(sp=uout)

        if STOP_AFTER < 2:
            continue
        s5_phase(nc, b, l, ps, ud, yd, ident, Jm, SWm, MLm, sg,
                 sa_re, sa_im, sldt, sBP, sBQ, sCP, sCQ, sD)

        if STOP_AFTER < 3:
            continue
        lru_phase(nc, b, l, ps, xr_d, gg_d, yl_d, cw, cb, lruW, lb_rg, lb_ig, llam)

        if STOP_AFTER < 4:
            continue
        with ExitStack() as st:
            sbt = lambda n, s, d: st.enter_context(nc.sbuf_tensor(uniq(n), list(s), d))
            yf_ = [sbt(f"yf{i}", [128, 2, 8, NTI], BF) for i in range(2)]
            ys_ = [sbt(f"ys{i}", [128, NT], F32) for i in range(2)]
            ygo = [sbt(f"ygo{i}", [128, NT], BF) for i in range(2)]
            tasks = []
            for cc in range(4):
                k = cc % 2
                rows = slice(cc * 128, (cc + 1) * 128)

                def q0(ew, k=k, rows=rows):
                    for d in range(2):
                        ew.dma(yf_[k][:, d, :, :], yd[d, :, rows, :].rearrange("t p i -> p t i"))

                def q1(ew, k=k):
                    ew.c(ew.e.tensor_tensor(out=ys_[k][:].rearrange("p (i t) -> p t i", t=8),
                                            in0=yf_[k][:, 0, :, :], in1=yf_[k][:, 1, :, :], op=ALU.add))

                def q2(ew, k=k):
                    ew.c(ew.e.activation(out=ygo[k][:], in_=ys_[k][:], func=AF.Gelu_apprx_tanh))

                def q3(ew, k=k, rows=rows):
                    ew.dma(yg_d[rows, :], ygo[k][:])
                tasks.append([{"sp": q0}, {"dve": q1}, {"act": q2}, {"sp": q3}])
            b.pipeline(tasks)

        with ExitStack() as st:
            sbt = lambda n, s, d: st.enter_context(nc.sbuf_tensor(uniq(n), list(s), d))
            wo = sbt("wo", [128, 8, 1024], BF)
            wg = sbt("wg", [128, 4, 512], BF)
            bg = sbt("bg", [128, 4], F32)
            NB = 7
            xt = [sbt(f"xt{i}", [128, 8, TN], F32) for i in range(NB)]
            osb = [sbt(f"osb{i}", [128, 8, TN], F32) for i in range(NB)]
            sq = [sbt(f"sq{i}", [128, 8, TN], BF) for i in range(NB)]
            rstd = [sbt(f"rstd{i}", [128, TN], F32) for i in range(NB)]
            tmp = [sbt(f"tmp{i}", [128, 8, TN], F32) for i in range(NB)]
            yg = [sbt(f"yg{i}", [128, 4, TN], BF) for i in range(NB)]
            sig = [sbt(f"sig{i}", [128, 4, TN], F32) for i in range(NB)]
            ymix = [sbt(f"ymix{i}", [128, 8, TN], BF) for i in range(NB)]

            def wl(ew):
                ew.dma(wo[:], w_out[l].rearrange("(kc p) n -> p kc n", p=128))
                ew.dma(wg[:], w_glu[l].rearrange("(kc p) n -> p kc n", p=128))
            b.blk(pool=wl, sp=lambda ew: ew.dma(bg[:], b_glu[:, l, :]))
            tasks = []
            tlist = range(NTT) if not last else range(2, NTT)
            for n_, ti in enumerate(tlist):
                k = n_ % NB
                cols = slice(ti * TN, (ti + 1) * TN)
                psG = ps[n_ % 2]
                psO2 = [ps[2 + (n_ % 2) * 2], ps[3 + (n_ % 2) * 2]]
                psS = ps[6 + n_ % 2][:, 0:TN]

                def a0(ew, k=k, cols=cols):
                    ew.dma(yg[k][:], yg_d.rearrange("(c p) n -> p c n", p=128)[:, :, cols])
                    ew.dma(ymix[k][:, 4:8, :], yl_d.rearrange("(c p) n -> p c n", p=128)[:, :, cols])

                def a3(ew, k=k, psG=psG):
                    for mc in range(4):
                        for kc in range(4):
                            ew.mm(psG[:, mc * TN:(mc + 1) * TN], wg[:, kc, mc * 128:(mc + 1) * 128],
                                  yg[k][:, kc, :], start=(kc == 0), stop=(kc == 3))

                def a4(ew, k=k, psG=psG):
                    for mc in range(4):
                        ew.c(ew.e.activation(out=sig[k][:, mc, :], in_=psG[:, mc * TN:(mc + 1) * TN],
                                             func=AF.Sigmoid, bias=bg[:, mc:mc + 1]))

                def a5(ew, k=k):
                    ew.c(ew.e.tensor_tensor(out=ymix[k][:, 0:4, :], in0=yg[k][:], in1=sig[k][:], op=ALU.mult))

                def a6(ew, k=k, psO2=psO2):
                    for mc in range(8):
                        for kc in range(8):
                            ew.mm(psO2[mc // 4][:, (mc % 4) * TN:(mc % 4 + 1) * TN],
                                  wo[:, kc, mc * 128:(mc + 1) * 128], ymix[k][:, kc, :],
                                  start=(kc == 0), stop=(kc == 7))
                stg = [{"sp": a0}, {"pe": a3}, {"act": a4}, {"dve": a5}, {"pe": a6}]
                stg += post_stages(l, ti, psO2, xt[k], osb[k], sq[k], psS, rstd[k], 0, last, TN)
                tasks.append(stg)
            b.pipeline(tasks)

        if STOP_AFTER < 5:
            continue
        with ExitStack() as st:
            sbt = lambda n, s, d: st.enter_context(nc.sbuf_tensor(uniq(n), list(s), d))
            T5 = 256
            wfi = sbt("wfi", [128, 8, 2 * DFF], BF)
            wfo = sbt("wfo", [128, 22, 1024], BF)
            NB = 2
            xt = [sbt(f"xt{i}", [128, 8, T5], F32) for i in range(NB)]
            sq = [sbt(f"sq{i}", [128, 8, T5], BF) for i in range(NB)]
            rstd = [sbt(f"rstd{i}", [128, T5], F32) for i in range(NB)]
            hb = [sbt(f"hb{i}", [128, 8, T5], BF) for i in range(NB)]
            xt2 = sbt("xtb", [128, 8, T5], F32)
            osb = sbt("osb", [128, 8, T5], F32)
            sq2 = sbt("sqb", [128, 8, T5], BF)
            rstd2 = sbt("rstdb", [128, T5], F32)
            act_ = sbt("act", [128, 22, T5], BF)
            sgt = [sbt(f"sgt{i}", [128, T5], F32) for i in range(3)]

            def wl2(ew):
                ew.dma(wfi[:], w_ffi[l].rearrange("(kc p) n -> p kc n", p=128))
                ew.dma(wfo[:], w_ffo[l].rearrange("(kc p) n -> p kc n", p=128))
            b.blk(pool=wl2)
            tlist = list(range(NT // T5) if not last else range(1, NT // T5))
            tasks = []
            gcount = [0]
            psO = [ps[3], ps[4], ps[5], ps[6]]

            def norm_task(n_):
                ti = tlist[n_]
                k = n_ % NB
                return norm_stages(l, ti, xt[k], sq[k], ps[7][:, 0:T5], rstd[k], xt[k], hb[k], 1, T5)

            def gu_task(n_, r):
                k = n_ % NB
                gi = gcount[0]
                gcount[0] += 1
                pg = ps[gi % 3]
                sgb = sgt[gi % 3]

                def g0(ew):
                    for half in range(2):
                        co = half * DFF + r * 128
                        for kc in range(8):
                            ew.mm(pg[:, half * T5:(half + 1) * T5], wfi[:, kc, co:co + 128], hb[k][:, kc, :],
                                  start=(kc == 0), stop=(kc == 7))

                def g1a(ew):
                    ew.c(ew.e.activation(out=sgb[:], in_=pg[:, 0:T5], func=AF.Silu))

                def g1d(ew):
                    ew.c(ew.e.tensor_tensor(out=act_[:, r, :], in0=sgb[:], in1=pg[:, T5:2 * T5], op=ALU.mult))

                def o_r(ew):
                    for mc in range(8):
                        ew.mm(psO[mc // 2][:, (mc % 2) * T5:(mc % 2 + 1) * T5],
                              wfo[:, r, mc * 128:(mc + 1) * 128], act_[:, r, :],
                              start=(r == 0 and mc % 2 == 0), stop=(r == 21 and mc % 2 == 1))
                return [{"pe": g0}, {"act": g1a}, {"dve": g1d}, {"pe": o_r}]

            def out_task(n_):
                ti = tlist[n_]
                return post_stages(l, ti, psO, xt2, osb, sq2, ps[7][:, T5:2 * T5], rstd2, 1, last, T5)

            ntl = len(tlist)
            tasks.append(norm_task(0))
            for _ in range(6):
                tasks.append([])
            for n_ in range(ntl):
                for r in range(22):
                    tasks.append(gu_task(n_, r))
                    if r == 1 and n_ >= 1:
                        tasks.append(out_task(n_ - 1))
                    if r == 10 and n_ + 1 < ntl:
                        tasks.append(norm_task(n_ + 1))
                tasks.append([])
            for _ in range(3):
                tasks.append([])
            tasks.append(out_task(ntl - 1))
            b.pipeline(tasks)

    b.es.close()
    return nc


def s5_phase(nc, b, l, ps, ud, yd, ident, Jm, SWm, MLm, sg,
             sa_re, sa_im, sldt, sBP, sBQ, sCP, sCQ, sD):
    NP = 16
    with ExitStack() as st:
        sbt = lambda n, s, d: st.enter_context(nc.sbuf_tensor(uniq(n), list(s), d))
        are = sbt("are", [128, NP], F32)
        aim = sbt("aim", [128, NP], F32)
        ldt = sbt("ldt", [128, NP], F32)
        BP = sbt("BP", [128, NP, 16], F32)
        BQ = sbt("BQ", [128, NP, 16], F32)
        CP = sbt("CP", [128, NP, 16], F32)
        CQ = sbt("CQ", [128, NP, 16], F32)
        Dp = sbt("Dp", [128, 8], F32)
        U2_s = [sbt("U2%d" % i_, [128, NP, NTI], BF) for i_ in range(2)]
        T = {}
        for nm in ("dt", "ar", "ai", "mg", "c0", "s0", "t1", "t2", "t3", "den", "fre", "fim",
                   "fres", "fims", "are64", "aim64", "a8re", "a8ims"):
            T[nm] = sbt("T" + nm, [128, NP], F32)
        PW = sbt("PW", [128, 2, 9, NP], F32)
        PI = sbt("PI", [128, 2, 8, NP], F32)
        fBP = sbt("fBP", [128, NP, 16], F32)
        fBQ = sbt("fBQ", [128, NP, 16], F32)
        AL = sbt("AL", [128, NP, 8], F32)
        BE = sbt("BE", [128, NP, 8], F32)
        Zst = sbt("Zst", [128, NP, 8, 16], F32)
        W1_s = [sbt("W1%d" % i_, [128, NP, 128], BF) for i_ in range(2)]
        W2_s = [sbt("W2%d" % i_, [128, NP, 8, 16], BF) for i_ in range(2)]
        E0 = sbt("E0", [128, NP, 8, 16], F32)
        CT0 = sbt("CT0", [128, NP, 8, 16], F32)
        TP_s = [sbt("TP%d" % i_, [128, NP, 128], BF) for i_ in range(2)]
        R8_s = [sbt("R8%d" % i_, [128, NP, 128], BF) for i_ in range(2)]
        tA = sbt("tA", [128, NP, 8, 16], F32)
        tB = sbt("tB", [128, NP, 8, 16], F32)
        Sa = sbt("Sa", [128, NP, NCH], BF)
        Sb = sbt("Sb", [128, NP, NCH], BF)
        Fm = sbt("Fm", [128, NP, NCH], F32)
        Fs = sbt("Fs", [128, NP, NCH], F32)
        St = sbt("St", [128, NP, NTI], BF)
        Yo = sbt("Yo", [128, NP, NTI], BF)
        l2t = [sbt(f"l2t{i}", [128, 4, 8], F32) for i in range(2)]
        V2 = sbt("V2", [128, NCH, 32], F32)
        F2 = sbt("F2", [128, NCH, 32], F32)
        A2_s = [sbt("A2%d" % i_, [128, 32], F32) for i_ in range(2)]
        B2_s = [sbt("B2%d" % i_, [128, 32], F32) for i_ in range(2)]


        def prep(qb, par):
            U2, W1, W2, TP, R8, A2, B2 = U2_s[par], W1_s[par], W2_s[par], TP_s[par], R8_s[par], A2_s[par], B2_s[par]
            def ld(ew):
                for d in range(2):
                    sl = slice(d * 8, (d + 1) * 8)
                    hs = slice((qb % 2) * 8, (qb % 2) * 8 + 8)
                    hb_ = d * 2 + qb // 2
                    ew.dma(are[:, sl], sa_re[:, l, hb_, hs])
                    ew.dma(aim[:, sl], sa_im[:, l, hb_, hs])
                    ew.dma(ldt[:, sl], sldt[:, l, hb_, hs])
                    ew.dma(BP[:, sl, :], sBP[:, l, hb_, hs, :])
                    ew.dma(BQ[:, sl, :], sBQ[:, l, hb_, hs, :])
                    ew.dma(CP[:, sl, :], sCP[:, l, hb_, hs, :])
                    ew.dma(CQ[:, sl, :], sCQ[:, l, hb_, hs, :])
                ew.dma(Dp[:], sD[:, l, qb, :])
                for s in range(8):
                    src = ud[s, qb * 128:(qb + 1) * 128, :].rearrange("(g p) i -> p g i", p=16)
                    ew.dma(U2[s * 16:(s + 1) * 16, 0:8, :], src)
                    ew.dma(U2[(7 - s) * 16:(8 - s) * 16, 8:16, :], src)
            b.blk(sp=ld)

            def sc1(ew):
                ew.c(ew.e.activation(out=T["dt"][:], in_=ldt[:], func=AF.Exp))
            b.blk(act=sc1)

            def sc2(ew):
                ew.c(ew.e.tensor_tensor(out=T["ar"][:], in0=are[:], in1=T["dt"][:], op=ALU.mult))
                ew.c(ew.e.tensor_tensor(out=T["ai"][:], in0=aim[:], in1=T["dt"][:], op=ALU.mult))
                ew.c(ew.e.tensor_scalar(out=T["t1"][:], in0=T["ai"][:], scalar1=1.0 / 16.0, scalar2=math.pi / 2,
                                        op0=ALU.mult, op1=ALU.add))
            b.blk(dve=sc2)

            def sc3(ew):
                ew.c(ew.e.activation(out=T["mg"][:], in_=T["ar"][:], func=AF.Exp, scale=1.0 / 16.0))
                ew.c(ew.e.activation(out=T["s0"][:], in_=T["ai"][:], func=AF.Sin, scale=1.0 / 16.0))
                ew.c(ew.e.activation(out=T["c0"][:], in_=T["t1"][:], func=AF.Sin))
                ew.c(ew.e.activation(out=T["t2"][:], in_=T["ar"][:], func=AF.Exp, scale=-1.0 / 16.0))
            b.blk(act=sc3)

            def cmul(ew, o_re, o_im, a_re, a_im, b_re, b_im, t1, t2):
                ew.c(ew.e.tensor_tensor(out=t1, in0=a_re, in1=b_re, op=ALU.mult))
                ew.c(ew.e.tensor_tensor(out=t2, in0=a_im, in1=b_im, op=ALU.mult))
                ew.c(ew.e.tensor_tensor(out=t2, in0=t1, in1=t2, op=ALU.subtract))
                ew.c(ew.e.tensor_tensor(out=t1, in0=a_re, in1=b_im, op=ALU.mult))
                ew.c(ew.e.tensor_tensor(out=o_im, in0=a_im, in1=b_re, op=ALU.mult))
                ew.c(ew.e.tensor_tensor(out=o_im, in0=o_im, in1=t1, op=ALU.add))
                ew.c(ew.e.tensor_copy(o_re, t2))

            def sc4(ew):
                t1, t2, t3 = T["t1"][:], T["t3"][:], T["den"][:]
                ew.c(ew.e.tensor_tensor(out=PW[:, 0, 1, :], in0=T["mg"][:], in1=T["c0"][:], op=ALU.mult))
                ew.c(ew.e.tensor_tensor(out=PW[:, 1, 1, :], in0=T["mg"][:], in1=T["s0"][:], op=ALU.mult))
                ew.c(ew.e.tensor_tensor(out=PI[:, 0, 1, :], in0=T["t2"][:], in1=T["c0"][:], op=ALU.mult))
                ew.c(ew.e.scalar_tensor_tensor(out=PI[:, 1, 1, :], in0=T["t2"][:], scalar=-1.0, in1=T["s0"][:],
                                               op0=ALU.mult, op1=ALU.mult))
                for _ in range(4):
                    cmul(ew, PW[:, 0, 1, :], PW[:, 1, 1, :], PW[:, 0, 1, :], PW[:, 1, 1, :],
                         PW[:, 0, 1, :], PW[:, 1, 1, :], t1, t2)
                    cmul(ew, PI[:, 0, 1, :], PI[:, 1, 1, :], PI[:, 0, 1, :], PI[:, 1, 1, :],
                         PI[:, 0, 1, :], PI[:, 1, 1, :], t1, t2)
                ew.c(ew.e.memset(PW[:, 0, 0, :], 1.0))
                ew.c(ew.e.memset(PW[:, 1, 0, :], 0.0))
                ew.c(ew.e.memset(PI[:, 0, 0, :], 1.0))
                ew.c(ew.e.memset(PI[:, 1, 0, :], 0.0))
                for k in range(2, 9):
                    cmul(ew, PW[:, 0, k, :], PW[:, 1, k, :], PW[:, 0, k - 1, :], PW[:, 1, k - 1, :],
                         PW[:, 0, 1, :], PW[:, 1, 1, :], t1, t2)
                for k in range(2, 8):
                    cmul(ew, PI[:, 0, k, :], PI[:, 1, k, :], PI[:, 0, k - 1, :], PI[:, 1, k - 1, :],
                         PI[:, 0, 1, :], PI[:, 1, 1, :], t1, t2)
                ew.c(ew.e.tensor_copy(T["are64"][:], PW[:, 0, 8, :]))
                ew.c(ew.e.tensor_copy(T["aim64"][:], PW[:, 1, 8, :]))
                for _ in range(3):
                    cmul(ew, T["are64"][:], T["aim64"][:], T["are64"][:], T["aim64"][:],
                         T["are64"][:], T["aim64"][:], t1, t2)
                nr = T["c0"][:]
                ew.c(ew.e.tensor_scalar_add(nr, PW[:, 0, 1, :], -1.0))
                ew.c(ew.e.tensor_tensor(out=t1, in0=are[:], in1=are[:], op=ALU.mult))
                ew.c(ew.e.tensor_tensor(out=t2, in0=aim[:], in1=aim[:], op=ALU.mult))
                ew.c(ew.e.tensor_tensor(out=t3, in0=t1, in1=t2, op=ALU.add))
                ew.c(ew.e.reciprocal(t3, t3))
                ew.c(ew.e.tensor_tensor(out=t1, in0=nr, in1=are[:], op=ALU.mult))
                ew.c(ew.e.tensor_tensor(out=t2, in0=PW[:, 1, 1, :], in1=aim[:], op=ALU.mult))
                ew.c(ew.e.tensor_tensor(out=t1, in0=t1, in1=t2, op=ALU.add))
                ew.c(ew.e.tensor_tensor(out=T["fre"][:], in0=t1, in1=t3, op=ALU.mult))
                ew.c(ew.e.tensor_tensor(out=t1, in0=PW[:, 1, 1, :], in1=are[:], op=ALU.mult))
                ew.c(ew.e.tensor_tensor(out=t2, in0=nr, in1=aim[:], op=ALU.mult))
                ew.c(ew.e.tensor_tensor(out=t1, in0=t1, in1=t2, op=ALU.subtract))
                ew.c(ew.e.tensor_tensor(out=T["fim"][:], in0=t1, in1=t3, op=ALU.mult))
                ew.c(ew.e.tensor_scalar(out=T["fims"][:], in0=T["fim"][:], scalar1=sg[:, 0:1], scalar2=None,
                                        op0=ALU.mult))
                f_re = bc(T["fre"][:].unsqueeze(2), [128, NP, 16])
                f_ims = bc(T["fims"][:].unsqueeze(2), [128, NP, 16])
                ta = tA[:, :, 0, :]
                ew.c(ew.e.tensor_tensor(out=fBP[:], in0=BP[:], in1=f_re, op=ALU.mult))
                ew.c(ew.e.tensor_tensor(out=ta, in0=BQ[:], in1=f_ims, op=ALU.mult))
                ew.c(ew.e.tensor_tensor(out=fBP[:], in0=fBP[:], in1=ta, op=ALU.add))
                ew.c(ew.e.tensor_tensor(out=fBQ[:], in0=BQ[:], in1=f_re, op=ALU.mult))
                ew.c(ew.e.tensor_tensor(out=ta, in0=BP[:], in1=f_ims, op=ALU.mult))
                ew.c(ew.e.tensor_tensor(out=fBQ[:], in0=fBQ[:], in1=ta, op=ALU.subtract))

                def ctab(out, P_, Q_, pw_re_fn, pw_im_fn, a_sgn_col, b_sgn_col, b_neg=False):
                    for j in range(8):
                        if a_sgn_col is None:
                            ew.c(ew.e.tensor_copy(AL[:, :, j], pw_re_fn(j)))
                        else:
                            ew.c(ew.e.tensor_scalar(out=AL[:, :, j], in0=pw_re_fn(j), scalar1=sg[:, a_sgn_col:a_sgn_col + 1],
                                                    scalar2=None, op0=ALU.mult))
                        if b_sgn_col is None:
                            ew.c(ew.e.tensor_scalar(out=BE[:, :, j], in0=pw_im_fn(j), scalar1=(-1.0 if b_neg else 1.0),
                                                    scalar2=None, op0=ALU.mult))
                        else:
                            ew.c(ew.e.tensor_scalar(out=BE[:, :, j], in0=pw_im_fn(j), scalar1=sg[:, b_sgn_col:b_sgn_col + 1],
                                                    scalar2=None, op0=ALU.mult))
                    sh = [128, NP, 8, 16]
                    ew.c(ew.e.tensor_tensor(out=tA[:], in0=bc(P_.unsqueeze(2), sh), in1=bc(AL[:].unsqueeze(3), sh), op=ALU.mult))
                    ew.c(ew.e.tensor_tensor(out=tB[:], in0=bc(Q_.unsqueeze(2), sh), in1=bc(BE[:].unsqueeze(3), sh), op=ALU.mult))
                    ew.c(ew.e.tensor_tensor(out=out, in0=tA[:], in1=tB[:], op=ALU.add))
                ctab(Zst[:], fBP[:], fBQ[:], lambda j: PW[:, 0, 7 - j, :], lambda j: PW[:, 1, 7 - j, :], None, 0)
                ctab(W2[:], CP[:], CQ[:], lambda j: PW[:, 0, j + 1, :], lambda j: PW[:, 1, j + 1, :], 1, None, True)
                ctab(E0[:], fBP[:], fBQ[:], lambda j: PI[:, 0, j, :], lambda j: PI[:, 1, j, :], 1, None, True)
                ctab(CT0[:], CP[:], CQ[:], lambda j: PW[:, 0, j, :], lambda j: PW[:, 1, j, :], None, 0)
                ew.c(ew.e.tensor_copy(T["a8re"][:], PW[:, 0, 8, :]))
                ew.c(ew.e.tensor_scalar(out=T["a8ims"][:], in0=PW[:, 1, 8, :], scalar1=sg[:, 1:2], scalar2=None,
                                        op0=ALU.mult))
                for p_ in range(NP):
                    ew.c(ew.e.tensor_scalar(out=tA[:, 0, :, :].rearrange("p a b -> p (a b)"), in0=Jm[:],
                                            scalar1=T["a8ims"][:, p_:p_ + 1], scalar2=None, op0=ALU.mult))
                    ew.c(ew.e.scalar_tensor_tensor(out=R8[:, p_, :], in0=ident[:], scalar=T["a8re"][:, p_:p_ + 1],
                                                   in1=tA[:, 0, :, :].rearrange("p a b -> p (a b)"),
                                                   op0=ALU.mult, op1=ALU.add))
                for d in range(2):
                    for hf in range(2):
                        o = A2[:, d * 16 + hf * 8:d * 16 + hf * 8 + 8]
                        ew.c(ew.e.tensor_copy(o, T["are64"][:, d * 8:(d + 1) * 8]))
                        ew.c(ew.e.tensor_scalar(out=B2[:, d * 16 + hf * 8:d * 16 + hf * 8 + 8],
                                                in0=T["aim64"][:, d * 8:(d + 1) * 8],
                                                scalar1=(1.0 if hf == 0 else -1.0), scalar2=None, op0=ALU.mult))
            pew = _ProxyEW()
            sc4(pew)
            CH = 10
            for c0_ in range(0, len(pew.ops), CH):
                chunk = pew.ops[c0_:c0_ + CH]
                b.blk(dve=lambda ew, chunk=chunk: [ew.c(getattr(ew.e, n_)(*a_, **k_)) for (n_, a_, k_) in chunk])

            for half in range(4):
                prs = range(half * 4, half * 4 + 4)

                def tp(ew, prs=prs):
                    for q, p_ in enumerate(prs):
                        ew.tr(ps[6][:, q * 128:(q + 1) * 128], Zst[:, p_, :, :].rearrange("p a b -> p (a b)"), ident[:])
                        ew.mm(ps[7][:, q * 128:(q + 1) * 128], E0[:, p_, :, :].rearrange("p a b -> p (a b)"),
                              CT0[:, p_, :, :].rearrange("p a b -> p (a b)"), start=True, stop=True)
                b.blk(pe=tp)

                def tpe(ew, prs=prs):
                    for q, p_ in enumerate(prs):
                        ew.c(ew.e.tensor_copy(W1[:, p_, :], ps[6][:, q * 128:(q + 1) * 128]))
                        ew.c(ew.e.tensor_tensor(out=tA[:, 0, :, :].rearrange("p a b -> p (a b)"),
                                                in0=ps[7][:, q * 128:(q + 1) * 128], in1=MLm[:], op=ALU.mult))
                        if p_ < 8:
                            ew.c(ew.e.scalar_tensor_tensor(out=TP[:, p_, :], in0=ident[:], scalar=Dp[:, p_:p_ + 1],
                                                           in1=tA[:, 0, :, :].rearrange("p a b -> p (a b)"),
                                                           op0=ALU.mult, op1=ALU.add))
                        else:
                            ew.c(ew.e.tensor_copy(TP[:, p_, :], tA[:, 0, :, :].rearrange("p a b -> p (a b)")))
                b.blk(dve=tpe)


        def main(qb, par):
            U2, W1, W2, TP, R8, A2, B2 = U2_s[par], W1_s[par], W2_s[par], TP_s[par], R8_s[par], A2_s[par], B2_s[par]
            def tiles(p_, r):
                off = r if p_ < 8 else 7 - r
                return slice(off, NTI, 8)

            def sweep(down):
                cur, nxt = Sa, Sb
                for r in range(8 if not down else 7):
                    def pe_(ew, r=r, cur=cur):
                        for p_ in range(NP):
                            o = ps[p_ // 7][:, (p_ % 7) * NCH:(p_ % 7 + 1) * NCH]
                            first = True
                            if down:
                                ew.mm(o, R8[:, p_, :], St[:, p_, tiles(p_, r)], start=True, stop=False)
                                first = False
                            elif r > 0:
                                ew.mm(o, R8[:, p_, :], cur[:, p_, :], start=True, stop=False)
                                first = False
                            ew.mm(o, W1[:, p_, :], U2[:, p_, tiles(p_, r)], start=first, stop=True)
                    b.blk(pe=pe_)

                    def ev(ew, r=r, nxt=nxt):
                        for bk in range(3):
                            n_p = 7 if bk < 2 else 2
                            prs = slice(bk * 7, bk * 7 + n_p)
                            src = ps[bk][:, 0:n_p * NCH].rearrange("p (a m) -> p a m", m=NCH)
                            if down:
                                lo, hi = bk * 7, bk * 7 + n_p
                                for (a, e_) in ((lo, min(hi, 8)), (max(lo, 8), hi)):
                                    if e_ > a:
                                        ew.c(ew.e.activation(
                                            out=St[:, a:e_, tiles(a, r + 1)],
                                            in_=ps[bk][:, (a - lo) * NCH:(e_ - lo) * NCH].rearrange("p (a m) -> p a m", m=NCH), func=AF.Copy))
                            elif r < 7:
                                ew.c(ew.e.activation(out=nxt[:, prs, :], in_=src, func=AF.Copy))
                            else:
                                ew.c(ew.e.activation(out=Fm[:, prs, :], in_=src, func=AF.Copy))
                    b.blk(act=ev)
                    cur, nxt = nxt, cur

            sweep(False)

            def swp(ew):
                for bk in range(3):
                    n_p = 7 if bk < 2 else 2
                    ew.mm(ps[bk][:, 0:n_p * NCH], SWm[:], Fm[:, bk * 7:bk * 7 + n_p, :].rearrange("p a m -> p (a m)"),
                          start=True, stop=True)
            b.blk(pe=swp)

            def swe(ew):
                for bk in range(3):
                    n_p = 7 if bk < 2 else 2
                    ew.c(ew.e.tensor_copy(Fs[:, bk * 7:bk * 7 + n_p, :],
                                          ps[bk][:, 0:n_p * NCH].rearrange("p (a m) -> p a m", m=NCH)))
                ew.c(ew.e.memset(V2[:], 0.0))
                for (src, slot) in ((Fm, 0), (Fs, 1)):
                    ew.c(ew.e.tensor_copy(F2[:, :, slot * 8:(slot + 1) * 8], src[:, 0:8, :].rearrange("p j m -> p m j")))
                    ew.c(ew.e.tensor_copy(F2[:, 0:4, 16 + slot * 8:24 + slot * 8],
                                          mkap(src, 8 * NCH + 3, [[-1, 4], [NCH, 8]])))
                    ew.c(ew.e.tensor_copy(F2[:, 4:NCH, 16 + slot * 8:24 + slot * 8],
                                          mkap(src, 8 * NCH + NCH - 1, [[-1, NCH - 4], [NCH, 8]])))
            b.blk(dve=swe)

            def l2(ew):
                t = ew.e.tensor_tensor
                ta = l2t[0][:].rearrange("p a b -> p (a b)")
                tb = l2t[1][:].rearrange("p a b -> p (a b)")
                for k in range(NCH - 1):
                    vsw = mkap(V2, k * 32 + 8, [[16, 2], [-8, 2], [1, 8]])
                    ew.c(t(out=ta, in0=A2[:], in1=V2[:, k, :], op=ALU.mult))
                    ew.c(t(out=tb.rearrange("p (d h j) -> p d h j", d=2, h=2), in0=B2[:].rearrange("p (d h j) -> p d h j", d=2, h=2),
                           in1=vsw, op=ALU.mult))
                    ew.c(t(out=ta, in0=ta, in1=tb, op=ALU.add))
                    ew.c(t(out=V2[:, k + 1, :], in0=ta, in1=F2[:, k, :], op=ALU.add))
            b.blk(dve=l2)

            def dinit(ew):
                ew.c(ew.e.tensor_copy(St[:, 0:8, slice(0, NTI, 8)], V2[:, :, 0:8].rearrange("p k j -> p j k")))
                ew.c(ew.e.tensor_copy(St[:, 8:16, slice(7, 7 + 8 * 4, 8)], mkap(V2, 3 * 32 + 16, [[1, 8], [-32, 4]])))
                ew.c(ew.e.tensor_copy(St[:, 8:16, slice(7 + 8 * 4, NTI, 8)],
                                      mkap(V2, (NCH - 1) * 32 + 16, [[1, 8], [-32, NCH - 4]])))
            b.blk(dve=dinit)
            sweep(True)

            ytasks = []
            for p_ in range(NP):
                def ype(ew, p_=p_):
                    for cb_, (c0_, c1_) in enumerate(((0, 512), (512, NTI))):
                        o = ps[(p_ % 2) * 2 + cb_][:, 0:c1_ - c0_]
                        ew.mm(o, TP[:, p_, :], U2[:, p_, c0_:c1_], start=True, stop=False)
                        ew.mm(o, W2[:, p_, :, :].rearrange("p a b -> p (a b)"), St[:, p_, c0_:c1_], start=False, stop=True)

                def yev(ew, p_=p_):
                    ew.c(ew.e.activation(out=Yo[:, p_, 0:512], in_=ps[(p_ % 2) * 2][:, 0:512], func=AF.Copy))
                    ew.c(ew.e.activation(out=Yo[:, p_, 512:NTI], in_=ps[(p_ % 2) * 2 + 1][:, 0:NTI - 512], func=AF.Copy))
                ytasks.append([{"pe": ype}, {"act": yev}])
            b.pipeline(ytasks)

            def yout_(ew):
                for t in range(8):
                    ew.dma(yd[0, t, qb * 128:(qb + 1) * 128, :].rearrange("(g q) i -> q g i", q=16),
                           Yo[t * 16:(t + 1) * 16, 0:8, :])
                    ew.dma(yd[1, 7 - t, qb * 128:(qb + 1) * 128, :].rearrange("(g q) i -> q g i", q=16),
                           Yo[t * 16:(t + 1) * 16, 8:16, :])
            b.blk(sp=yout_)


        b.rec = []
        prep(0, 0)
        steps = b.rec
        b.rec = None
        b.emit_zipped(steps, [])
        for qb in range(4):
            b.rec = []
            main(qb, qb % 2)
            sm = b.rec
            b.rec = []
            if qb < 3:
                prep(qb + 1, (qb + 1) % 2)
            sp_ = b.rec
            b.rec = None
            b.emit_zipped(sm, sp_)


def lru_phase(nc, b, l, ps, xr_d, gg_d, yl_d, cw, cb, lruW, lb_rg, lb_ig, llam):
    with ExitStack() as st:
        sbt = lambda n, s, d: st.enter_context(nc.sbuf_tensor(uniq(n), list(s), d))
        cwt = sbt("cwt", [128, 4, 4], F32)
        cbt = sbt("cbt", [128, 4], F32)
        brg = sbt("brg", [128, 2, 4], F32)
        big = sbt("big", [128, 2, 4], F32)
        lam = sbt("lam", [128, 2, 4], F32)
        nsp = sbt("nsp", [128, 2, 4], F32)
        nsp2 = sbt("nsp2", [128, 2, 4], F32)
        Wg = sbt("Wg", [128, 2, 2, 4, 128], BF)
        xp = sbt("xp", [128, 3 + NCTX + 3 + NLAT + 3], F32)
        xc = sbt("xc", [128, NT], F32)
        xcb = sbt("xcb", [128, NT], BF)
        R = [xp, sbt("R1", [128, NT], F32)]
        I = [sbt(f"I{d}", [128, NT], F32) for d in range(2)]
        H = [sbt(f"H{d}", [128, NT], F32) for d in range(2)]
        xraw = H[0]
        ggt = sbt("ggt", [128, NT], BF)
        ylo = xcb
        hc = sbt("hc", [128, 2], F32)
        CO = 3
        LO = 3 + NCTX + 3

        def ld(ew):
            ew.dma(cwt[:], cw[:, l, :, :])
            ew.dma(cbt[:], cb[:, l, :])
            ew.dma(brg[:], lb_rg[:, l, :, :])
            ew.dma(big[:], lb_ig[:, l, :, :])
            ew.dma(lam[:], llam[:, l, :, :])

        def ldw(ew):
            for d in range(2):
                for g in range(2):
                    ew.dma(Wg[:, d, g, :, :], lruW[l, d, g].rearrange("c i o -> i c o"))
        b.blk(sp=ld, pool=ldw)

        b.blk(act=lambda ew: ew.c(ew.e.activation(out=nsp[:], in_=lam[:], func=AF.Exp, scale=-1.0)))
        b.blk(act=lambda ew: ew.c(ew.e.activation(out=nsp[:], in_=nsp[:], func=AF.Ln, bias=1.0)))

        def nsp_(ew):
            ew.c(ew.e.tensor_scalar(out=nsp2[:], in0=nsp[:], scalar1=-16.0, scalar2=None, op0=ALU.mult))
            ew.c(ew.e.tensor_scalar(out=nsp[:], in0=nsp[:], scalar1=-8.0, scalar2=None, op0=ALU.mult))
        b.blk(dve=nsp_)

        for cc in range(4):
            rows = slice(cc * 128, (cc + 1) * 128)

            def l0(ew, rows=rows):
                ew.dma(xraw[:], xr_d[rows, :])
                ew.dma(ggt[:], gg_d[rows, :])
            b.blk(sp=l0)

            def l1(ew):
                ew.c(ew.e.memset(xp[:], 0.0))
                ew.c(ew.e.tensor_copy(xp[:, CO:CO + NCTX], xraw[:, 0:NCTX]))
                ew.c(ew.e.tensor_copy(xp[:, LO:LO + NLAT].rearrange("p (c r) -> p c r", r=64),
                                      xraw[:, NCTX:NT].rearrange("p (r c) -> p c r", c=64)))
            b.blk(dve=l1)

            def l2_(ew, cc=cc):
                for (o0, n, base) in ((0, NCTX, CO), (NCTX, NLAT, LO)):
                    ew.c(ew.e.tensor_scalar(out=xc[:, o0:o0 + n], in0=xp[:, base - 2:base - 2 + n],
                                            scalar1=cwt[:, 0, cc:cc + 1], scalar2=cbt[:, cc:cc + 1],
                                            op0=ALU.mult, op1=ALU.add))
                    for k in range(1, 4):
                        ew.c(ew.e.scalar_tensor_tensor(out=xc[:, o0:o0 + n], in0=xp[:, base - 2 + k:base - 2 + k + n],
                                                       scalar=cwt[:, k, cc:cc + 1], in1=xc[:, o0:o0 + n],
                                                       op0=ALU.mult, op1=ALU.add))
                ew.c(ew.e.tensor_copy(xcb[:], xc[:]))
            b.blk(dve=l2_)

            blocks = [(i * 512, min(512, NT - i * 512)) for i in range(9)]
            for bi, (c0_, n) in enumerate(blocks):
                def gpe(ew, c0_=c0_, n=n, cc=cc):
                    for d in range(2):
                        for g in range(2):
                            ew.mm(ps[d * 2 + g][:, 0:n], Wg[:, d, g, cc, :], xcb[:, c0_:c0_ + n], start=True, stop=True)

                def gev(ew, c0_=c0_, n=n, cc=cc):
                    for d in range(2):
                        ew.c(ew.e.activation(out=R[d][:, c0_:c0_ + n], in_=ps[d * 2][:, 0:n], func=AF.Sigmoid,
                                             bias=brg[:, d, cc:cc + 1]))
                        ew.c(ew.e.activation(out=I[d][:, c0_:c0_ + n], in_=ps[d * 2 + 1][:, 0:n], func=AF.Sigmoid,
                                             bias=big[:, d, cc:cc + 1]))
                b.blk(pe=gpe)
                b.blk(act=gev)

            def e1(ew, cc=cc):
                for d in range(2):
                    ew.c(ew.e.activation(out=H[d][:], in_=R[d][:, 0:NT], func=AF.Exp, scale=nsp2[:, d, cc:cc + 1]))
                    ew.c(ew.e.activation(out=R[d][:, 0:NT], in_=R[d][:, 0:NT], func=AF.Exp, scale=nsp[:, d, cc:cc + 1]))

            def e1d(ew):
                for d in range(2):
                    ew.c(ew.e.tensor_tensor(out=I[d][:], in0=I[d][:], in1=xc[:], op=ALU.mult))
            b.blk(act=e1, dve=e1d)

            def e2(ew):
                for d in range(2):
                    ew.c(ew.e.activation(out=H[d][:], in_=H[d][:], func=AF.Sqrt, scale=-1.0, bias=1.0))
            b.blk(act=e2)

            def e3(ew):
                for d in range(2):
                    ew.c(ew.e.tensor_tensor(out=I[d][:], in0=I[d][:], in1=H[d][:], op=ALU.mult))
            b.blk(dve=e3)

            def rv(ap2d, o0, n):
                full = ap2d[:, o0:o0 + n]
                return bass.AP(full.tensor, full.offset + (n - 1), [list(full.ap[0]), [-1, n]])

            def sc(ew):
                ew.c(ew.e.tensor_tensor_scan(out=H[0][:, 0:NCTX], data0=R[0][:, 0:NCTX], data1=I[0][:, 0:NCTX],
                                             initial=0.0, op0=ALU.mult, op1=ALU.add))
                ew.c(ew.e.tensor_copy(hc[:, 0:1], H[0][:, NCTX - 1:NCTX]))
                ew.c(ew.e.tensor_tensor_scan(out=H[0][:, NCTX:NT], data0=R[0][:, NCTX:NT], data1=I[0][:, NCTX:NT],
                                             initial=hc[:, 0:1], op0=ALU.mult, op1=ALU.add))
                ew.c(ew.e.tensor_tensor_scan(out=rv(H[1], 0, NCTX), data0=rv(R[1], 0, NCTX), data1=rv(I[1], 0, NCTX),
                                             initial=0.0, op0=ALU.mult, op1=ALU.add))
                ew.c(ew.e.tensor_copy(hc[:, 1:2], H[1][:, 0:1]))
                ew.c(ew.e.tensor_tensor_scan(out=rv(H[1], NCTX, NLAT), data0=rv(R[1], NCTX, NLAT),
                                             data1=rv(I[1], NCTX, NLAT), initial=hc[:, 1:2], op0=ALU.mult, op1=ALU.add))
                ew.c(ew.e.tensor_tensor(out=H[0][:], in0=H[0][:], in1=H[1][:], op=ALU.add))
                ew.c(ew.e.tensor_tensor(out=ylo[:, 0:NCTX], in0=H[0][:, 0:NCTX], in1=ggt[:, 0:NCTX], op=ALU.mult))
                ew.c(ew.e.tensor_tensor(out=ylo[:, NCTX:NT].rearrange("p (r c) -> p r c", c=64),
                                        in0=H[0][:, NCTX:NT].rearrange("p (c r) -> p r c", r=64),
                                        in1=ggt[:, NCTX:NT].rearrange("p (r c) -> p r c", c=64), op=ALU.mult))
            b.blk(dve=sc)
            b.blk(sp=lambda ew, rows=rows: ew.dma(yl_d[rows, :], ylo[:]))


_NC_CACHE = {}


def _host_inputs(inp, bidx):
    f = np.float32
    x = np.asarray(inp["x"], f)
    ctx = np.asarray(inp["ctx"], f)
    d = {}
    d["xs"] = np.ascontiguousarray(np.concatenate([ctx[bidx].T, x[bidx].T], axis=1))
    cv = np.stack([np.asarray(inp["c"], f)[bidx], np.asarray(inp["c_ctx"], f)], axis=-1)
    d["cvec"] = np.ascontiguousarray(cv.reshape(8, 128, 2).transpose(1, 0, 2))
    return d


def _shared_inputs(inp):
    f = np.float32
    g = lambda k: np.asarray(inp[k], f)
    d = {}
    d["w_ada"] = g("w_ada")
    d["b_ada"] = np.ascontiguousarray(g("b_ada").reshape(NL, 48, 128).transpose(2, 0, 1))
    d["gains"] = np.ascontiguousarray(g("norm_gains").reshape(NL, 4, 8, 128).transpose(3, 0, 1, 2))
    d["w_in"] = g("w_in")
    d["w_out"] = g("w_out")
    d["w_ffi"] = g("w_ffn_in")
    d["w_ffo"] = g("w_ffn_out")
    d["w_glu"] = g("s5_w_glu")
    d["b_glu"] = np.ascontiguousarray(g("s5_b_glu").reshape(NL, 4, 128).transpose(2, 0, 1))

    def st2(a):
        a = np.concatenate([a, a], axis=-1)
        a = a.reshape(NL, 2, 2, 16, 128).reshape(NL, 4, 16, 128)
        return np.ascontiguousarray(a.transpose(3, 0, 1, 2))
    d["sa_re"] = st2(g("s5_a_re"))
    d["sa_im"] = st2(g("s5_a_im"))
    d["sldt"] = st2(np.broadcast_to(g("s5_log_dt")[..., None], (NL, 2, 32, 64)))

    def st3(top, bot):
        a = np.concatenate([top, bot], axis=3)
        a = a.reshape(NL, 4, 16, 128, a.shape[-1])
        return np.ascontiguousarray(a.transpose(3, 0, 1, 2, 4))
    bre, bim = g("s5_b_re"), g("s5_b_im")
    d["sBP"] = st3(bre, bim)
    d["sBQ"] = st3(bim, bre)
    cre = g("s5_c_re").transpose(0, 1, 2, 4, 3)
    cim = g("s5_c_im").transpose(0, 1, 2, 4, 3)
    d["sCP"] = st3(cre, cim)
    d["sCQ"] = st3(cim, cre)
    sd = g("s5_d").reshape(NL, 4, 8, 16)
    sd = np.broadcast_to(sd[:, :, :, None, :], (NL, 4, 8, 8, 16))
    d["sD"] = np.ascontiguousarray(sd.transpose(3, 4, 0, 1, 2).reshape(128, NL, 4, 8))
    d["cw"] = np.ascontiguousarray(g("lru_conv_w").reshape(NL, 4, 4, 128).transpose(3, 0, 1, 2))
    d["cb"] = np.ascontiguousarray(g("lru_conv_b").reshape(NL, 4, 128).transpose(2, 0, 1))
    W = np.zeros((NL, 2, 2, 4, 128, 128), f)
    for gi, key in enumerate(("lru_w_rg", "lru_w_ig")):
        w = g(key)
        for cc in range(4):
            W[:, :, gi, cc, 0:64, 0:64] = w[:, :, 2 * cc]
            W[:, :, gi, cc, 64:128, 64:128] = w[:, :, 2 * cc + 1]
    d["lruW"] = W
    v = lambda k: np.ascontiguousarray(g(k).reshape(NL, 2, 4, 128).transpose(3, 0, 1, 2))
    d["lb_rg"] = v("lru_b_rg")
    d["lb_ig"] = v("lru_b_ig")
    d["llam"] = v("lru_lambda")
    I = np.eye(128, dtype=f)
    d["cI"] = I
    J = np.zeros((128, 128), f)
    SW = np.zeros((128, 128), f)
    for n in range(64):
        J[n, 64 + n] = 1.0
        J[64 + n, n] = 1.0
        SW[64 + n, n] = -1.0
        SW[n, 64 + n] = 1.0
    d["cJ"] = J
    d["cSW"] = SW
    sidx = np.arange(128) // 16
    d["cML"] = (sidx[None, :] >= sidx[:, None]).astype(f)
    sgn = np.where(np.arange(128) < 64, -1.0, 1.0).astype(f)
    d["cSG"] = np.stack([sgn, -sgn, np.ones(128, f), np.full(128, 1.0 / 1024, f)], axis=1)
    return d


def kernel(**inputs):
    if "nc" not in _NC_CACHE:
        _NC_CACHE["nc"] = build_program()
    nc = _NC_CACHE["nc"]
    shared = _shared_inputs(inputs)
    in_maps = []
    for bidx in range(8):
        m = dict(shared)
        m.update(_host_inputs(inputs, bidx))
        in_maps.append(m)
    res = run_bass_kernel_spmd(nc, in_maps, core_ids=list(range(8)))
    out = np.stack([np.asarray(r["yout"], np.float32).T for r in res.results], axis=0)
    return np.ascontiguousarray(out)
```

```python
import math
from contextlib import ExitStack
import numpy as np
import concourse.bass as bass
import concourse.mybir as mybir
from concourse.bass_utils import run_bass_kernel_spmd

F32 = mybir.dt.float32
BF = mybir.dt.bfloat16
AF = mybir.ActivationFunctionType
ALU = mybir.AluOpType

NL = 4
DM = 1024
NCTX = 256
NLAT = 4096
NT = NCTX + NLAT
NTI = NT // 8
NCH = NTI // 8
TN = 128
NTT = NT // TN
DFF = 2816
EPS = 1e-6
DEBUG = False
NL_RUN = 4
STOP_AFTER = 99


class EW:
    def __init__(self, b, name):
        self.b = b
        self.name = name
        self.sem = b.newsem()
        self.n = 0
        self.dsem = b.newsem()
        self.dn = 0
        self.e = None
        self.last = None

    def c(self, ins):
        if self.n >= 6000:
            self.sem = self.b.newsem()
            self.n = 0
        self.n += 1
        ins.then_inc(self.sem, 1)
        self.e.wait_ge(self.sem, self.n)
        return ins

    def mm(self, *a, **k):
        self.last = self.e.matmul(*a, **k)
        return self.last

    def tr(self, *a, **k):
        self.last = self.e.transpose(*a, **k)
        return self.last

    def dma(self, out, in_):
        if self.dn >= 550:
            self.dsem = self.b.newsem()
            self.dn = 0
        self.dn += 1
        self.e.dma_start(out=out, in_=in_).then_inc(self.dsem, 16)

    def dma_async(self, out, in_, track=True):
        b = self.b
        i = b.ai % len(b.asems)
        b.ai += 1
        sem = b.asems[i]
        if b.acnt[i]:
            self.e.wait_ge(sem, b.acnt[i])
        b.acnt[i] += 16
        self.e.dma_start(out=out, in_=in_).then_inc(sem, 16)
        tok = (sem, b.acnt[i])
        if track:
            b.outstanding.append(tok)
        return tok

    def wait_tok(self, tok):
        self.e.wait_ge(tok[0], tok[1])

    def fin(self):
        if self.last is not None:
            self.c(self.last)
            self.last = None
        if self.dn:
            self.e.wait_ge(self.dsem, 16 * self.dn)


class _ProxyEng:
    def __getattr__(self, name):
        return lambda *a, **k: (name, a, k)


class _ProxyEW:
    def __init__(self):
        self.e = _ProxyEng()
        self.ops = []

    def c(self, rec):
        self.ops.append(rec)
        return rec


class Bld:
    def __init__(self, nc):
        self.nc = nc
        self.es = ExitStack()
        self.sems = [self.es.enter_context(nc.semaphore(f"sm{i}")) for i in range(96)]
        self.si = 0
        self.ew = {k: EW(self, k) for k in ("pe", "act", "dve", "pool", "sp")}
        self.nblk = 0
        self.rec = None
        self.asems = [self.newsem() for _ in range(24)]
        self.acnt = [0] * 24
        self.ai = 0
        self.outstanding = []

    def drain(self):
        toks, self.outstanding = self.outstanding, []
        if toks:
            self.blk(sp=lambda ew: [ew.wait_tok(t) for t in toks])

    def emit_zipped(self, a, bsteps):
        for i in range(max(len(a), len(bsteps))):
            fns = {}
            for lst in (a, bsteps):
                if i < len(lst):
                    for name, fn in lst[i].items():
                        if fn is not None:
                            fns.setdefault(name, []).append(fn)
            self.blk(**{n: (lambda ew, l=l: [f(ew) for f in l]) for n, l in fns.items()})

    def newsem(self):
        s = self.sems[self.si]
        self.si += 1
        return s

    def sb(self, name, shape, dt):
        return self.es.enter_context(self.nc.sbuf_tensor(name, list(shape), dt))

    def blk(self, **fns):
        m = {"pe": "tensor", "act": "scalar", "dve": "vector", "pool": "gpsimd", "sp": "sync"}
        if self.rec is not None:
            self.rec.append(dict(fns))
            return
        self.nblk += 1
        with self.nc.Block() as block:
            for name, fn in fns.items():
                if fn is None:
                    continue
                ew = self.ew[name]

                def run(e, ew=ew, fn=fn):
                    ew.e = e
                    fn(ew)
                    ew.fin()

                getattr(block, m[name])(run)

    def pipeline(self, tasks):
        nsteps = max((k + len(t) for k, t in enumerate(tasks)), default=0)
        for t in range(nsteps):
            fns = {}
            for k in range(max(0, t - 24), min(len(tasks), t + 1)):
                j = t - k
                if j < len(tasks[k]):
                    for name, fn in tasks[k][j].items():
                        fns.setdefault(name, []).append(fn)
            if not fns:
                continue
            self.blk(**{n: (lambda ew, l=l: [f(ew) for f in reversed(l)]) for n, l in fns.items()})


_UC = [0]


def uniq(n):
    _UC[0] += 1
    return f"{n}_{_UC[0]}"


def mkap(tile, off, dims):
    full = tile[:]
    return bass.AP(full.tensor, full.offset + off, [list(full.ap[0])] + [list(d) for d in dims])


def bc(ap, shape):
    return ap.to_broadcast(list(shape))


def build_program():
    nc = bass.Bass("TRN2", target_bir_lowering=False)
    b = Bld(nc)

    def din(name, shape, dt=F32):
        return nc.dram_tensor(name, list(shape), dt, kind="ExternalInput").ap()

    def dscr(name, shape, dt=F32):
        kind = "ExternalOutput"
        return nc.dram_tensor(name, list(shape), dt, kind=kind).ap()

    xs = din("xs", [DM, NT])
    cvec = din("cvec", [128, 8, 2])
    w_ada = din("w_ada", [NL, DM, 6 * DM])
    b_ada = din("b_ada", [128, NL, 48])
    gains = din("gains", [128, NL, 4, 8])
    w_in = din("w_in", [NL, DM, 1536])
    w_out = din("w_out", [NL, DM, DM])
    w_ffi = din("w_ffi", [NL, DM, 2 * DFF])
    w_ffo = din("w_ffo", [NL, DFF, DM])
    w_glu = din("w_glu", [NL, 512, 512])
    b_glu = din("b_glu", [128, NL, 4])
    sa_re = din("sa_re", [128, NL, 4, 16])
    sa_im = din("sa_im", [128, NL, 4, 16])
    sldt = din("sldt", [128, NL, 4, 16])
    sBP = din("sBP", [128, NL, 4, 16, 16])
    sBQ = din("sBQ", [128, NL, 4, 16, 16])
    sCP = din("sCP", [128, NL, 4, 16, 16])
    sCQ = din("sCQ", [128, NL, 4, 16, 16])
    sD = din("sD", [128, NL, 4, 8])
    cw = din("cw", [128, NL, 4, 4])
    cb = din("cb", [128, NL, 4])
    lruW = din("lruW", [NL, 2, 2, 4, 128, 128])
    lb_rg = din("lb_rg", [128, NL, 2, 4])
    lb_ig = din("lb_ig", [128, NL, 2, 4])
    llam = din("llam", [128, NL, 2, 4])
    cI = din("cI", [128, 128])
    cJ = din("cJ", [128, 128])
    cSW = din("cSW", [128, 128])
    cML = din("cML", [128, 128])
    cSG = din("cSG", [128, 4])
    yout = nc.dram_tensor("yout", [DM, NLAT], F32, kind="ExternalOutput").ap()

    xres = dscr("xres", [DM, NT])
    ud = dscr("ud", [8, 512, NTI], BF)
    yd = dscr("yd", [2, 8, 512, NTI], BF)
    xr_d = dscr("xr_d", [512, NT])
    gg_d = dscr("gg_d", [512, NT], BF)
    yl_d = dscr("yl_d", [512, NT], BF)
    yg_d = dscr("yg_d", [512, NT], BF)

    xres_v = xres.rearrange("(kc p) n -> p kc n", p=128)
    xs_v = xs.rearrange("(kc p) n -> p kc n", p=128)

    ident = b.sb("ident", [128, 128], F32)
    identb = b.sb("identb", [128, 128], BF)
    Jm = b.sb("Jm", [128, 128], F32)
    SWm = b.sb("SWm", [128, 128], F32)
    MLm = b.sb("MLm", [128, 128], F32)
    sg = b.sb("sg", [128, 4], F32)
    onesb = b.sb("onesb", [128, 128], BF)
    epsb = b.sb("epsb", [128, 1], F32)
    MOD = b.sb("MOD", [128, NL, 6, 8, 2], F32)
    GA = b.sb("GA", [128, NL, 4, 8], F32)
    DER = b.sb("DER", [128, NL, 4, 8, 2], F32)
    BADA = b.sb("BADA", [128, NL, 48], F32)
    ps = [b.es.enter_context(nc.psum_tensor(f"ps{i}", [128, 512], F32)) for i in range(8)]

    def c0(ew):
        ew.dma(ident[:], cI[:, :])
        ew.dma(Jm[:], cJ[:, :])
        ew.dma(SWm[:], cSW[:, :])
        ew.dma(MLm[:], cML[:, :])
        ew.dma(sg[:], cSG[:, :])
        ew.dma(GA[:], gains[:, :, :, :])
        ew.dma(BADA[:], b_ada[:, :, :])
    b.blk(sp=c0)

    def c1(ew):
        ew.c(ew.e.tensor_copy(identb[:], ident[:]))
        ew.c(ew.e.memset(onesb[:], 1.0 / 1024.0))
        ew.c(ew.e.memset(epsb[:], EPS))
    b.blk(dve=c1)

    def c2(ew):
        for kc in range(8):
            ew.dma(xres[kc * 128:(kc + 1) * 128, :], xs[kc * 128:(kc + 1) * 128, :])
    b.blk(sp=c2)

    with ExitStack() as st:
        cv = st.enter_context(nc.sbuf_tensor(uniq("cv"), [128, 8, 2], F32))
        scb = st.enter_context(nc.sbuf_tensor("scb", [128, 8, 2], BF))
        wad = [st.enter_context(nc.sbuf_tensor(f"wad{i}", [128, 8, 1024], BF)) for i in range(2)]

        b.blk(sp=lambda ew: ew.dma(cv[:], cvec[:, :, :]))
        b.blk(act=lambda ew: ew.c(ew.e.activation(out=scb[:], in_=cv[:], func=AF.Silu)))
        tasks = []
        for l in range(NL):
            for j in range(6):
                k = l * 6 + j
                wb = wad[k % 2]
                src = w_ada[l].rearrange("(kc p) n -> p kc n", p=128)[:, :, j * 1024:(j + 1) * 1024]

                def s_load(ew, wb=wb, src=src):
                    ew.dma(wb[:], src)

                def s_mm(ew, wb=wb, k=k):
                    for oc in range(8):
                        for kc in range(8):
                            ew.mm(ps[k % 2][:, oc * 2:oc * 2 + 2], wb[:, kc, oc * 128:(oc + 1) * 128],
                                  scb[:, kc, :], start=(kc == 0), stop=(kc == 7))

                def s_ev(ew, l=l, j=j, k=k):
                    ew.c(ew.e.tensor_tensor(
                        out=MOD[:, l, j, :, :],
                        in0=ps[k % 2][:, 0:16].rearrange("p (o w) -> p o w", w=2),
                        in1=bc(BADA[:, l, j * 8:(j + 1) * 8].unsqueeze(2), [128, 8, 2]),
                        op=ALU.add))
                tasks.append([{"pool": s_load}, {"pe": s_mm}, {"dve": s_ev}])
        b.pipeline(tasks)

        def derive(ew):
            for l in range(NL):
                for (dk, gk, mk, addone) in ((0, 0, 1, True), (1, 1, 2, False), (2, 2, 4, True), (3, 3, 5, False)):
                    g = bc(GA[:, l, gk, :].unsqueeze(2), [128, 8, 2])
                    if addone:
                        ew.c(ew.e.scalar_tensor_tensor(out=DER[:, l, dk, :, :], in0=MOD[:, l, mk, :, :],
                                                       scalar=1.0, in1=g, op0=ALU.add, op1=ALU.mult))
                    else:
                        ew.c(ew.e.tensor_tensor(out=DER[:, l, dk, :, :], in0=MOD[:, l, mk, :, :], in1=g,
                                                op=ALU.mult))
        b.blk(dve=derive)

    def norm_stages(l, ti, xt, sq, psS, rstd, tmp, h, which, tn, gap=0):
        w = 1 if ti * tn < NCTX else 0
        cols = slice(ti * tn, (ti + 1) * tn)
        dk = 0 if which == 0 else 2
        mk = 0 if which == 0 else 3

        tk = {}

        def s0(ew):
            if gap:
                tk["x"] = ew.dma_async(xt[:], xres_v[:, :, cols], track=False)
            else:
                ew.dma(xt[:], xres_v[:, :, cols])

        def s1(ew):
            if gap:
                ew.wait_tok(tk["x"])
            ew.c(ew.e.activation(out=sq[:], in_=xt[:], func=AF.Square))

        def s2(ew):
            for kc in range(8):
                ew.mm(psS, onesb[:], sq[:, kc, :], start=(kc == 0), stop=(kc == 7))

        def s2b(ew):
            ew.c(ew.e.activation(out=rstd[:], in_=psS, func=AF.Sqrt, bias=epsb[:, 0:1]))

        def s3(ew):
            ew.c(ew.e.reciprocal(rstd[:], rstd[:]))
            ew.c(ew.e.tensor_tensor(out=tmp[:], in0=xt[:], in1=bc(rstd[:].unsqueeze(1), [128, 8, tn]),
                                    op=ALU.mult))

        def s4(ew):
            for kc in range(8):
                ew.c(ew.e.activation(out=h[:, kc, :], in_=tmp[:, kc, :], func=AF.Identity,
                                     scale=DER[:, l, dk, kc, w:w + 1], bias=MOD[:, l, mk, kc, w:w + 1]))
        return [{"sp": s0}] + [{}] * gap + [{"act": s1}, {"pe": s2}, {"act": s2b}, {"dve": s3}, {"act": s4}]

    def post_stages(l, ti, banks, xt, osb, sq, psS, rstd, which, last_layer, tn, load_stage=0):
        w = 1 if ti * tn < NCTX else 0
        cols = slice(ti * tn, (ti + 1) * tn)
        dk = 1 if which == 0 else 3
        cpb = 512 // tn

        def s1(ew):
            for bi, bank in enumerate(banks):
                ew.c(ew.e.activation(out=osb[:, bi * cpb:(bi + 1) * cpb, :],
                                     in_=bank[:, 0:512].rearrange("p (c n) -> p c n", c=cpb), func=AF.Copy))

        tk = {}

        def s1l(ew):
            tk["x"] = ew.dma_async(xt[:], xres_v[:, :, cols], track=False)

        def s1b(ew):
            ew.c(ew.e.activation(out=sq[:], in_=osb[:], func=AF.Square))

        def s2(ew):
            for kc in range(8):
                ew.mm(psS, onesb[:], sq[:, kc, :], start=(kc == 0), stop=(kc == 7))

        def s2b(ew):
            ew.c(ew.e.activation(out=rstd[:], in_=psS, func=AF.Sqrt, bias=epsb[:, 0:1]))

        def s3(ew):
            ew.c(ew.e.reciprocal(rstd[:], rstd[:]))
            for mc in range(8):
                ew.c(ew.e.scalar_tensor_tensor(out=osb[:, mc, :], in0=osb[:, mc, :],
                                               scalar=DER[:, l, dk, mc, w:w + 1], in1=rstd[:],
                                               op0=ALU.mult, op1=ALU.mult))

        def s4(ew):
            ew.wait_tok(tk["x"])
            ew.c(ew.e.tensor_tensor(out=xt[:], in0=xt[:], in1=osb[:], op=ALU.add))

        def s5(ew):
            ew.dma_async(xres_v[:, :, cols], xt[:])
            if last_layer and which == 1 and ti * tn >= NCTX:
                ew.dma_async(yout.rearrange("(kc p) n -> p kc n", p=128)[:, :, ti * tn - NCTX:(ti + 1) * tn - NCTX], xt[:])
        stg = [{"act": s1}, {"act": s1b}, {"pe": s2}, {"act": s2b}, {"dve": s3}, {"pool": s4}, {"sp": s5}]
        stg[load_stage] = dict(stg[load_stage])
        stg[load_stage]["sp"] = s1l
        return stg

    for l in range(NL_RUN):
        last = (l == NL - 1)
        with ExitStack() as st:
            sbt = lambda n, s, d: st.enter_context(nc.sbuf_tensor(uniq(n), list(s), d))
            T1 = 256
            NT1 = NT // T1
            win = sbt("win", [128, 8, 1536], BF)
            ut_all = sbt("ut_all", [128, 4, 8, NTI], BF)
            NB = 3
            xt = [sbt(f"xt{i}", [128, 8, T1], F32) for i in range(5)]
            sq = [sbt(f"sq{i}", [128, 8, T1], BF) for i in range(NB)]
            rstd = [sbt(f"rstd{i}", [128, T1], F32) for i in range(NB)]
            hb = [sbt(f"hb{i}", [128, 8, T1], BF) for i in range(NB)]
            xro = [sbt(f"xro{i}", [128, 4, T1], F32) for i in range(NB)]
            ggo = [sbt(f"ggo{i}", [128, 4, T1], BF) for i in range(NB)]
            b.blk(pool=lambda ew: ew.dma(win[:], w_in[l].rearrange("(kc p) n -> p kc n", p=128)))
            tasks = []
            for ti in range(NT1):
                k = ti % NB
                psS = ps[6 + ti % 2][:, 0:T1]
                stg = norm_stages(l, ti, xt[ti % 5], sq[k], psS, rstd[k], xt[ti % 5], hb[k], 0, T1, gap=2)
                cols = slice(ti * T1, (ti + 1) * T1)

                def mmh(ew, k=k, half=0):
                    for mc in range(half * 6, half * 6 + 6):
                        bank = ps[half * 3 + (mc % 6) // 2]
                        for kc in range(8):
                            ew.mm(bank[:, (mc % 2) * T1:(mc % 2 + 1) * T1],
                                  win[:, kc, mc * 128:(mc + 1) * 128], hb[k][:, kc, :],
                                  start=(kc == 0), stop=(kc == 7))

                def evA_d(ew, ti=ti):
                    for bk in range(2):
                        ew.c(ew.e.tensor_copy(
                            out=ut_all[:, bk * 2:bk * 2 + 2, :, ti * 32:(ti + 1) * 32],
                            in_=ps[bk][:, 0:512].rearrange("p (c i s) -> p c s i", c=2, s=8)))

                def evA_a(ew, k=k):
                    ew.c(ew.e.activation(out=xro[k][:, 0:2, :], in_=ps[2][:, 0:512].rearrange("p (c n) -> p c n", c=2),
                                         func=AF.Copy))

                def evB_a(ew, k=k):
                    ew.c(ew.e.activation(out=xro[k][:, 2:4, :], in_=ps[3][:, 0:512].rearrange("p (c n) -> p c n", c=2),
                                         func=AF.Copy))
                    for bk in range(2):
                        ew.c(ew.e.activation(out=ggo[k][:, bk * 2:bk * 2 + 2, :],
                                             in_=ps[4 + bk][:, 0:512].rearrange("p (c n) -> p c n", c=2),
                                             func=AF.Gelu_apprx_tanh))

                def stB(ew, k=k, cols=cols):
                    ew.dma(xr_d.rearrange("(c p) n -> p c n", p=128)[:, :, cols], xro[k][:])
                    ew.dma(gg_d.rearrange("(c p) n -> p c n", p=128)[:, :, cols], ggo[k][:])
                tasks.append(stg + [{"pe": lambda ew, f=mmh: f(ew, half=0)}, {"dve": evA_d, "act": evA_a}])
                tasks.append([{}] * 8 + [{"pe": lambda ew, f=mmh: f(ew, half=1)}, {"act": evB_a}, {"sp": stB}])
            b.pipeline(tasks)

            def uout(ew):
                for s in range(8):
                    ew.dma(ud[s].rearrange("(c p) i -> p c i", p=128), ut_all[:, :, s, :])
            b.blk(sp=uout)

        if STOP_AFTER < 2:
            continue
        s5_phase(nc, b, l, ps, ud, yd, ident, Jm, SWm, MLm, sg,
                 sa_re, sa_im, sldt, sBP, sBQ, sCP, sCQ, sD)

        if STOP_AFTER < 3:
            continue
        lru_phase(nc, b, l, ps, xr_d, gg_d, yl_d, cw, cb, lruW, lb_rg, lb_ig, llam)

        if STOP_AFTER < 4:
            continue
        cm_wfi = nc.sbuf_tensor(uniq("wfi"), [128, 8, 2 * DFF], BF)
        wfi = cm_wfi.__enter__()
        cm_wo = nc.sbuf_tensor(uniq("wo"), [128, 8, 1024], BF)
        wo = cm_wo.__enter__()
        cm_wg = nc.sbuf_tensor(uniq("wg"), [128, 4, 512], BF)
        wg = cm_wg.__enter__()
        wtok = {}

        def pre4(ew):
            wtok["wo"] = ew.dma_async(wo[:], w_out[l].rearrange("(kc p) n -> p kc n", p=128), track=False)
            wtok["wg"] = ew.dma_async(wg[:], w_glu[l].rearrange("(kc p) n -> p kc n", p=128), track=False)
            wtok["wfi"] = ew.dma_async(wfi[:], w_ffi[l].rearrange("(kc p) n -> p kc n", p=128), track=False)
        b.blk(pool=pre4)

        with ExitStack() as st:
            sbt = lambda n, s, d: st.enter_context(nc.sbuf_tensor(uniq(n), list(s), d))
            yf_ = [sbt(f"yf{i}", [128, 2, 8, NTI], BF) for i in range(2)]
            ys_ = [sbt(f"ys{i}", [128, NT], F32) for i in range(2)]
            ygo = [sbt(f"ygo{i}", [128, NT], BF) for i in range(2)]
            tasks = []
            for cc in range(4):
                k = cc % 2
                rows = slice(cc * 128, (cc + 1) * 128)

                def q0(ew, k=k, rows=rows):
                    for d in range(2):
                        ew.dma(yf_[k][:, d, :, :], yd[d, :, rows, :].rearrange("t p i -> p t i"))

                def q1(ew, k=k):
                    ew.c(ew.e.tensor_tensor(out=ys_[k][:].rearrange("p (i t) -> p t i", t=8),
                                            in0=yf_[k][:, 0, :, :], in1=yf_[k][:, 1, :, :], op=ALU.add))

                def q2(ew, k=k):
                    ew.c(ew.e.activation(out=ygo[k][:], in_=ys_[k][:], func=AF.Gelu_apprx_tanh))

                def q3(ew, k=k, rows=rows):
                    ew.dma(yg_d[rows, :], ygo[k][:])
                tasks.append([{"sp": q0}, {"dve": q1}, {"act": q2}, {"sp": q3}])
            b.pipeline(tasks)

        with ExitStack() as st:
            sbt = lambda n, s, d: st.enter_context(nc.sbuf_tensor(uniq(n), list(s), d))
            bg = sbt("bg", [128, 4], F32)
            xt = [sbt(f"xt{i}", [128, 8, TN], F32) for i in range(4)]
            osb = [sbt(f"osb{i}", [128, 8, TN], F32) for i in range(6)]
            sq = [sbt(f"sq{i}", [128, 8, TN], BF) for i in range(5)]
            rstd = [sbt(f"rstd{i}", [128, TN], F32) for i in range(5)]
            yg = [sbt(f"yg{i}", [128, 4, TN], BF) for i in range(5)]
            sig = [sbt(f"sig{i}", [128, 4, TN], F32) for i in range(5)]
            ymix = [sbt(f"ymix{i}", [128, 8, TN], BF) for i in range(5)]
            b.blk(sp=lambda ew: ew.dma(bg[:], b_glu[:, l, :]))
            tasks = []
            tlist = range(NTT) if not last else range(2, NTT)
            for n_, ti in enumerate(tlist):
                k = n_ % 5
                cols = slice(ti * TN, (ti + 1) * TN)
                psG = ps[n_ % 2]
                psO2 = [ps[2 + (n_ % 2) * 2], ps[3 + (n_ % 2) * 2]]
                psS = ps[6 + n_ % 2][:, 0:TN]

                def a0(ew, k=k, cols=cols):
                    ew.dma(yg[k][:], yg_d.rearrange("(c p) n -> p c n", p=128)[:, :, cols])
                    ew.dma(ymix[k][:, 4:8, :], yl_d.rearrange("(c p) n -> p c n", p=128)[:, :, cols])

                def a3(ew, k=k, psG=psG, first=(n_ == 0)):
                    if first:
                        ew.wait_tok(wtok["wg"])
                        ew.wait_tok(wtok["wo"])
                    for mc in range(4):
                        for kc in range(4):
                            ew.mm(psG[:, mc * TN:(mc + 1) * TN], wg[:, kc, mc * 128:(mc + 1) * 128],
                                  yg[k][:, kc, :], start=(kc == 0), stop=(kc == 3))

                def a4(ew, k=k, psG=psG):
                    for mc in range(4):
                        ew.c(ew.e.activation(out=sig[k][:, mc, :], in_=psG[:, mc * TN:(mc + 1) * TN],
                                             func=AF.Sigmoid, bias=bg[:, mc:mc + 1]))

                def a5(ew, k=k):
                    ew.c(ew.e.tensor_tensor(out=ymix[k][:, 0:4, :], in0=yg[k][:], in1=sig[k][:], op=ALU.mult))

                def a6(ew, k=k, psO2=psO2):
                    for mc in range(8):
                        for kc in range(8):
                            ew.mm(psO2[mc // 4][:, (mc % 4) * TN:(mc % 4 + 1) * TN],
                                  wo[:, kc, mc * 128:(mc + 1) * 128], ymix[k][:, kc, :],
                                  start=(kc == 0), stop=(kc == 7))
                stg = [{"sp": a0}, {"pe": a3}, {"act": a4}, {"dve": a5}, {"pe": a6}]
                stg += post_stages(l, ti, psO2, xt[n_ % 4], osb[n_ % 6], sq[k], psS, rstd[k], 0, last, TN, load_stage=3)
                tasks.append(stg)
            b.pipeline(tasks)
            b.drain()
        cm_wg.__exit__(None, None, None)
        cm_wo.__exit__(None, None, None)

        if STOP_AFTER < 5:
            continue
        with ExitStack() as st:
            sbt = lambda n, s, d: st.enter_context(nc.sbuf_tensor(uniq(n), list(s), d))
            T5 = 256
            wfo = sbt("wfo", [128, 22, 1024], BF)
            NB = 2
            xt = [sbt(f"xt{i}", [128, 8, T5], F32) for i in range(NB)]
            sq = [sbt(f"sq{i}", [128, 8, T5], BF) for i in range(NB)]
            rstd = [sbt(f"rstd{i}", [128, T5], F32) for i in range(NB)]
            hb = [sbt(f"hb{i}", [128, 8, T5], BF) for i in range(NB)]
            xt2 = sbt("xtb", [128, 8, T5], F32)
            osb = sbt("osb", [128, 8, T5], F32)
            sq2 = sbt("sqb", [128, 8, T5], BF)
            rstd2 = sbt("rstdb", [128, T5], F32)
            act_ = sbt("act", [128, 22, T5], BF)
            sgt = [sbt(f"sgt{i}", [128, T5], F32) for i in range(3)]

            b.blk(pool=lambda ew: wtok.__setitem__("wfo", ew.dma_async(wfo[:], w_ffo[l].rearrange("(kc p) n -> p kc n", p=128), track=False)))
            tlist = list(range(NT // T5) if not last else range(1, NT // T5))
            tasks = []
            gcount = [0]
            psO = [ps[3], ps[4], ps[5], ps[6]]

            def norm_task(n_):
                ti = tlist[n_]
                k = n_ % NB
                return norm_stages(l, ti, xt[k], sq[k], ps[7][:, 0:T5], rstd[k], xt[k], hb[k], 1, T5, gap=2)

            def gu_task(n_, r):
                k = n_ % NB
                gi = gcount[0]
                gcount[0] += 1
                pg = ps[gi % 3]
                sgb = sgt[gi % 3]

                def g0(ew):
                    if n_ == 0 and r == 0:
                        ew.wait_tok(wtok["wfi"])
                    for half in range(2):
                        co = half * DFF + r * 128
                        for kc in range(8):
                            ew.mm(pg[:, half * T5:(half + 1) * T5], wfi[:, kc, co:co + 128], hb[k][:, kc, :],
                                  start=(kc == 0), stop=(kc == 7))

                def g1a(ew):
                    ew.c(ew.e.activation(out=sgb[:], in_=pg[:, 0:T5], func=AF.Silu))

                def g1d(ew):
                    ew.c(ew.e.tensor_tensor(out=act_[:, r, :], in0=sgb[:], in1=pg[:, T5:2 * T5], op=ALU.mult))

                def o_r(ew):
                    if n_ == 0 and r == 0:
                        ew.wait_tok(wtok["wfo"])
                    for mc in range(8):
                        ew.mm(psO[mc // 2][:, (mc % 2) * T5:(mc % 2 + 1) * T5],
                              wfo[:, r, mc * 128:(mc + 1) * 128], act_[:, r, :],
                              start=(r == 0 and mc % 2 == 0), stop=(r == 21 and mc % 2 == 1))
                return [{"pe": g0}, {"act": g1a}, {"dve": g1d}, {"pe": o_r}]

            def out_task(n_):
                ti = tlist[n_]
                return post_stages(l, ti, psO, xt2, osb, sq2, ps[7][:, T5:2 * T5], rstd2, 1, last, T5)

            ntl = len(tlist)
            tasks.append(norm_task(0))
            for _ in range(8):
                tasks.append([])
            for n_ in range(ntl):
                for r in range(22):
                    tasks.append(gu_task(n_, r))
                    if r == 1 and n_ >= 1:
                        tasks.append(out_task(n_ - 1))
                    if r == 10 and n_ + 1 < ntl:
                        tasks.append(norm_task(n_ + 1))
                tasks.append([])
            for _ in range(3):
                tasks.append([])
            tasks.append(out_task(ntl - 1))
            b.pipeline(tasks)
            b.drain()
        cm_wfi.__exit__(None, None, None)

    b.es.close()
    return nc


def s5_phase(nc, b, l, ps, ud, yd, ident, Jm, SWm, MLm, sg,
             sa_re, sa_im, sldt, sBP, sBQ, sCP, sCQ, sD):
    NP = 16
    with ExitStack() as st:
        sbt = lambda n, s, d: st.enter_context(nc.sbuf_tensor(uniq(n), list(s), d))
        are = sbt("are", [128, NP], F32)
        aim = sbt("aim", [128, NP], F32)
        ldt = sbt("ldt", [128, NP], F32)
        BP = sbt("BP", [128, NP, 16], F32)
        BQ = sbt("BQ", [128, NP, 16], F32)
        CP = sbt("CP", [128, NP, 16], F32)
        CQ = sbt("CQ", [128, NP, 16], F32)
        Dp = sbt("Dp", [128, 8], F32)
        U2_s = [sbt("U2%d" % i_, [128, NP, NTI], BF) for i_ in range(2)]
        T = {}
        for nm in ("dt", "ar", "ai", "mg", "c0", "s0", "t1", "t2", "t3", "den", "fre", "fim",
                   "fres", "fims", "are64", "aim64", "a8re", "a8ims"):
            T[nm] = sbt("T" + nm, [128, NP], F32)
        PW = sbt("PW", [128, 2, 9, NP], F32)
        PI = sbt("PI", [128, 2, 8, NP], F32)
        fBP = sbt("fBP", [128, NP, 16], F32)
        fBQ = sbt("fBQ", [128, NP, 16], F32)
        AL = sbt("AL", [128, NP, 8], F32)
        BE = sbt("BE", [128, NP, 8], F32)
        Zst = sbt("Zst", [128, NP, 8, 16], F32)
        W1_s = [sbt("W1%d" % i_, [128, NP, 128], BF) for i_ in range(2)]
        W2_s = [sbt("W2%d" % i_, [128, NP, 8, 16], BF) for i_ in range(2)]
        E0 = sbt("E0", [128, NP, 8, 16], F32)
        CT0 = sbt("CT0", [128, NP, 8, 16], F32)
        TP_s = [sbt("TP%d" % i_, [128, NP, 128], BF) for i_ in range(2)]
        R8_s = [sbt("R8%d" % i_, [128, NP, 128], BF) for i_ in range(2)]
        tA = sbt("tA", [128, NP, 8, 16], F32)
        tB = sbt("tB", [128, NP, 8, 16], F32)
        Sa = sbt("Sa", [128, NP, NCH], BF)
        Sb = sbt("Sb", [128, NP, NCH], BF)
        Fm = sbt("Fm", [128, NP, NCH], F32)
        Fs = sbt("Fs", [128, NP, NCH], F32)
        St = sbt("St", [128, NP, NTI], BF)
        Yo = sbt("Yo", [128, NP, NTI], BF)
        l2t = [sbt(f"l2t{i}", [128, 4, 8], F32) for i in range(2)]
        V2 = sbt("V2", [128, NCH, 32], F32)
        F2 = sbt("F2", [128, NCH, 32], F32)
        A2_s = [sbt("A2%d" % i_, [128, 32], F32) for i_ in range(2)]
        B2_s = [sbt("B2%d" % i_, [128, 32], F32) for i_ in range(2)]


        def prep(qb, par):
            U2, W1, W2, TP, R8, A2, B2 = U2_s[par], W1_s[par], W2_s[par], TP_s[par], R8_s[par], A2_s[par], B2_s[par]
            def ld(ew):
                for d in range(2):
                    sl = slice(d * 8, (d + 1) * 8)
                    hs = slice((qb % 2) * 8, (qb % 2) * 8 + 8)
                    hb_ = d * 2 + qb // 2
                    ew.dma(are[:, sl], sa_re[:, l, hb_, hs])
                    ew.dma(aim[:, sl], sa_im[:, l, hb_, hs])
                    ew.dma(ldt[:, sl], sldt[:, l, hb_, hs])
                    ew.dma(BP[:, sl, :], sBP[:, l, hb_, hs, :])
                    ew.dma(BQ[:, sl, :], sBQ[:, l, hb_, hs, :])
                    ew.dma(CP[:, sl, :], sCP[:, l, hb_, hs, :])
                    ew.dma(CQ[:, sl, :], sCQ[:, l, hb_, hs, :])
                ew.dma(Dp[:], sD[:, l, qb, :])
                for s in range(8):
                    src = ud[s, qb * 128:(qb + 1) * 128, :].rearrange("(g p) i -> p g i", p=16)
                    ew.dma(U2[s * 16:(s + 1) * 16, 0:8, :], src)
                    ew.dma(U2[(7 - s) * 16:(8 - s) * 16, 8:16, :], src)
            b.blk(sp=ld)

            def sc1(ew):
                ew.c(ew.e.activation(out=T["dt"][:], in_=ldt[:], func=AF.Exp))
            b.blk(act=sc1)

            def sc2(ew):
                ew.c(ew.e.tensor_tensor(out=T["ar"][:], in0=are[:], in1=T["dt"][:], op=ALU.mult))
                ew.c(ew.e.tensor_tensor(out=T["ai"][:], in0=aim[:], in1=T["dt"][:], op=ALU.mult))
                ew.c(ew.e.tensor_scalar(out=T["t1"][:], in0=T["ai"][:], scalar1=1.0 / 16.0, scalar2=math.pi / 2,
                                        op0=ALU.mult, op1=ALU.add))
            b.blk(dve=sc2)

            def sc3(ew):
                ew.c(ew.e.activation(out=T["mg"][:], in_=T["ar"][:], func=AF.Exp, scale=1.0 / 16.0))
                ew.c(ew.e.activation(out=T["s0"][:], in_=T["ai"][:], func=AF.Sin, scale=1.0 / 16.0))
                ew.c(ew.e.activation(out=T["c0"][:], in_=T["t1"][:], func=AF.Sin))
                ew.c(ew.e.activation(out=T["t2"][:], in_=T["ar"][:], func=AF.Exp, scale=-1.0 / 16.0))
            b.blk(act=sc3)

            def cmul(ew, o_re, o_im, a_re, a_im, b_re, b_im, t1, t2):
                ew.c(ew.e.tensor_tensor(out=t1, in0=a_re, in1=b_re, op=ALU.mult))
                ew.c(ew.e.tensor_tensor(out=t2, in0=a_im, in1=b_im, op=ALU.mult))
                ew.c(ew.e.tensor_tensor(out=t2, in0=t1, in1=t2, op=ALU.subtract))
                ew.c(ew.e.tensor_tensor(out=t1, in0=a_re, in1=b_im, op=ALU.mult))
                ew.c(ew.e.tensor_tensor(out=o_im, in0=a_im, in1=b_re, op=ALU.mult))
                ew.c(ew.e.tensor_tensor(out=o_im, in0=o_im, in1=t1, op=ALU.add))
                ew.c(ew.e.tensor_copy(o_re, t2))

            def sc4(ew):
                t1, t2, t3 = T["t1"][:], T["t3"][:], T["den"][:]
                ew.c(ew.e.tensor_tensor(out=PW[:, 0, 1, :], in0=T["mg"][:], in1=T["c0"][:], op=ALU.mult))
                ew.c(ew.e.tensor_tensor(out=PW[:, 1, 1, :], in0=T["mg"][:], in1=T["s0"][:], op=ALU.mult))
                ew.c(ew.e.tensor_tensor(out=PI[:, 0, 1, :], in0=T["t2"][:], in1=T["c0"][:], op=ALU.mult))
                ew.c(ew.e.scalar_tensor_tensor(out=PI[:, 1, 1, :], in0=T["t2"][:], scalar=-1.0, in1=T["s0"][:],
                                               op0=ALU.mult, op1=ALU.mult))
                for _ in range(4):
                    cmul(ew, PW[:, 0, 1, :], PW[:, 1, 1, :], PW[:, 0, 1, :], PW[:, 1, 1, :],
                         PW[:, 0, 1, :], PW[:, 1, 1, :], t1, t2)
                    cmul(ew, PI[:, 0, 1, :], PI[:, 1, 1, :], PI[:, 0, 1, :], PI[:, 1, 1, :],
                         PI[:, 0, 1, :], PI[:, 1, 1, :], t1, t2)
                ew.c(ew.e.memset(PW[:, 0, 0, :], 1.0))
                ew.c(ew.e.memset(PW[:, 1, 0, :], 0.0))
                ew.c(ew.e.memset(PI[:, 0, 0, :], 1.0))
                ew.c(ew.e.memset(PI[:, 1, 0, :], 0.0))
                for k in range(2, 9):
                    cmul(ew, PW[:, 0, k, :], PW[:, 1, k, :], PW[:, 0, k - 1, :], PW[:, 1, k - 1, :],
                         PW[:, 0, 1, :], PW[:, 1, 1, :], t1, t2)
                for k in range(2, 8):
                    cmul(ew, PI[:, 0, k, :], PI[:, 1, k, :], PI[:, 0, k - 1, :], PI[:, 1, k - 1, :],
                         PI[:, 0, 1, :], PI[:, 1, 1, :], t1, t2)
                ew.c(ew.e.tensor_copy(T["are64"][:], PW[:, 0, 8, :]))
                ew.c(ew.e.tensor_copy(T["aim64"][:], PW[:, 1, 8, :]))
                for _ in range(3):
                    cmul(ew, T["are64"][:], T["aim64"][:], T["are64"][:], T["aim64"][:],
                         T["are64"][:], T["aim64"][:], t1, t2)
                nr = T["c0"][:]
                ew.c(ew.e.tensor_scalar_add(nr, PW[:, 0, 1, :], -1.0))
                ew.c(ew.e.tensor_tensor(out=t1, in0=are[:], in1=are[:], op=ALU.mult))
                ew.c(ew.e.tensor_tensor(out=t2, in0=aim[:], in1=aim[:], op=ALU.mult))
                ew.c(ew.e.tensor_tensor(out=t3, in0=t1, in1=t2, op=ALU.add))
                ew.c(ew.e.reciprocal(t3, t3))
                ew.c(ew.e.tensor_tensor(out=t1, in0=nr, in1=are[:], op=ALU.mult))
                ew.c(ew.e.tensor_tensor(out=t2, in0=PW[:, 1, 1, :], in1=aim[:], op=ALU.mult))
                ew.c(ew.e.tensor_tensor(out=t1, in0=t1, in1=t2, op=ALU.add))
                ew.c(ew.e.tensor_tensor(out=T["fre"][:], in0=t1, in1=t3, op=ALU.mult))
                ew.c(ew.e.tensor_tensor(out=t1, in0=PW[:, 1, 1, :], in1=are[:], op=ALU.mult))
                ew.c(ew.e.tensor_tensor(out=t2, in0=nr, in1=aim[:], op=ALU.mult))
                ew.c(ew.e.tensor_tensor(out=t1, in0=t1, in1=t2, op=ALU.subtract))
                ew.c(ew.e.tensor_tensor(out=T["fim"][:], in0=t1, in1=t3, op=ALU.mult))
                ew.c(ew.e.tensor_scalar(out=T["fims"][:], in0=T["fim"][:], scalar1=sg[:, 0:1], scalar2=None,
                                        op0=ALU.mult))
                f_re = bc(T["fre"][:].unsqueeze(2), [128, NP, 16])
                f_ims = bc(T["fims"][:].unsqueeze(2), [128, NP, 16])
                ta = tA[:, :, 0, :]
                ew.c(ew.e.tensor_tensor(out=fBP[:], in0=BP[:], in1=f_re, op=ALU.mult))
                ew.c(ew.e.tensor_tensor(out=ta, in0=BQ[:], in1=f_ims, op=ALU.mult))
                ew.c(ew.e.tensor_tensor(out=fBP[:], in0=fBP[:], in1=ta, op=ALU.add))
                ew.c(ew.e.tensor_tensor(out=fBQ[:], in0=BQ[:], in1=f_re, op=ALU.mult))
                ew.c(ew.e.tensor_tensor(out=ta, in0=BP[:], in1=f_ims, op=ALU.mult))
                ew.c(ew.e.tensor_tensor(out=fBQ[:], in0=fBQ[:], in1=ta, op=ALU.subtract))

                def ctab(out, P_, Q_, pw_re_fn, pw_im_fn, a_sgn_col, b_sgn_col, b_neg=False):
                    for j in range(8):
                        if a_sgn_col is None:
                            ew.c(ew.e.tensor_copy(AL[:, :, j], pw_re_fn(j)))
                        else:
                            ew.c(ew.e.tensor_scalar(out=AL[:, :, j], in0=pw_re_fn(j), scalar1=sg[:, a_sgn_col:a_sgn_col + 1],
                                                    scalar2=None, op0=ALU.mult))
                        if b_sgn_col is None:
                            ew.c(ew.e.tensor_scalar(out=BE[:, :, j], in0=pw_im_fn(j), scalar1=(-1.0 if b_neg else 1.0),
                                                    scalar2=None, op0=ALU.mult))
                        else:
                            ew.c(ew.e.tensor_scalar(out=BE[:, :, j], in0=pw_im_fn(j), scalar1=sg[:, b_sgn_col:b_sgn_col + 1],
                                                    scalar2=None, op0=ALU.mult))
                    sh = [128, NP, 8, 16]
                    ew.c(ew.e.tensor_tensor(out=tA[:], in0=bc(P_.unsqueeze(2), sh), in1=bc(AL[:].unsqueeze(3), sh), op=ALU.mult))
                    ew.c(ew.e.tensor_tensor(out=tB[:], in0=bc(Q_.unsqueeze(2), sh), in1=bc(BE[:].unsqueeze(3), sh), op=ALU.mult))
                    ew.c(ew.e.tensor_tensor(out=out, in0=tA[:], in1=tB[:], op=ALU.add))
                ctab(Zst[:], fBP[:], fBQ[:], lambda j: PW[:, 0, 7 - j, :], lambda j: PW[:, 1, 7 - j, :], None, 0)
                ctab(W2[:], CP[:], CQ[:], lambda j: PW[:, 0, j + 1, :], lambda j: PW[:, 1, j + 1, :], 1, None, True)
                ctab(E0[:], fBP[:], fBQ[:], lambda j: PI[:, 0, j, :], lambda j: PI[:, 1, j, :], 1, None, True)
                ctab(CT0[:], CP[:], CQ[:], lambda j: PW[:, 0, j, :], lambda j: PW[:, 1, j, :], None, 0)
                ew.c(ew.e.tensor_copy(T["a8re"][:], PW[:, 0, 8, :]))
                ew.c(ew.e.tensor_scalar(out=T["a8ims"][:], in0=PW[:, 1, 8, :], scalar1=sg[:, 1:2], scalar2=None,
                                        op0=ALU.mult))
                for p_ in range(NP):
                    ew.c(ew.e.tensor_scalar(out=tA[:, 0, :, :].rearrange("p a b -> p (a b)"), in0=Jm[:],
                                            scalar1=T["a8ims"][:, p_:p_ + 1], scalar2=None, op0=ALU.mult))
                    ew.c(ew.e.scalar_tensor_tensor(out=R8[:, p_, :], in0=ident[:], scalar=T["a8re"][:, p_:p_ + 1],
                                                   in1=tA[:, 0, :, :].rearrange("p a b -> p (a b)"),
                                                   op0=ALU.mult, op1=ALU.add))
                for d in range(2):
                    for hf in range(2):
                        o = A2[:, d * 16 + hf * 8:d * 16 + hf * 8 + 8]
                        ew.c(ew.e.tensor_copy(o, T["are64"][:, d * 8:(d + 1) * 8]))
                        ew.c(ew.e.tensor_scalar(out=B2[:, d * 16 + hf * 8:d * 16 + hf * 8 + 8],
                                                in0=T["aim64"][:, d * 8:(d + 1) * 8],
                                                scalar1=(1.0 if hf == 0 else -1.0), scalar2=None, op0=ALU.mult))
            pew = _ProxyEW()
            sc4(pew)
            CH = 10
            for c0_ in range(0, len(pew.ops), CH):
                chunk = pew.ops[c0_:c0_ + CH]
                b.blk(dve=lambda ew, chunk=chunk: [ew.c(getattr(ew.e, n_)(*a_, **k_)) for (n_, a_, k_) in chunk])

            for half in range(4):
                prs = range(half * 4, half * 4 + 4)

                def tp(ew, prs=prs):
                    for q, p_ in enumerate(prs):
                        ew.tr(ps[6][:, q * 128:(q + 1) * 128], Zst[:, p_, :, :].rearrange("p a b -> p (a b)"), ident[:])
                        ew.mm(ps[7][:, q * 128:(q + 1) * 128], E0[:, p_, :, :].rearrange("p a b -> p (a b)"),
                              CT0[:, p_, :, :].rearrange("p a b -> p (a b)"), start=True, stop=True)
                b.blk(pe=tp)

                def tpe(ew, prs=prs):
                    for q, p_ in enumerate(prs):
                        ew.c(ew.e.tensor_copy(W1[:, p_, :], ps[6][:, q * 128:(q + 1) * 128]))
                        ew.c(ew.e.tensor_tensor(out=tA[:, 0, :, :].rearrange("p a b -> p (a b)"),
                                                in0=ps[7][:, q * 128:(q + 1) * 128], in1=MLm[:], op=ALU.mult))
                        if p_ < 8:
                            ew.c(ew.e.scalar_tensor_tensor(out=TP[:, p_, :], in0=ident[:], scalar=Dp[:, p_:p_ + 1],
                                                           in1=tA[:, 0, :, :].rearrange("p a b -> p (a b)"),
                                                           op0=ALU.mult, op1=ALU.add))
                        else:
                            ew.c(ew.e.tensor_copy(TP[:, p_, :], tA[:, 0, :, :].rearrange("p a b -> p (a b)")))
                b.blk(dve=tpe)


        def main(qb, par):
            U2, W1, W2, TP, R8, A2, B2 = U2_s[par], W1_s[par], W2_s[par], TP_s[par], R8_s[par], A2_s[par], B2_s[par]
            def tiles(p_, r):
                off = r if p_ < 8 else 7 - r
                return slice(off, NTI, 8)

            def sweep(down):
                cur, nxt = Sa, Sb
                for r in range(8 if not down else 7):
                    def pe_(ew, r=r, cur=cur):
                        for p_ in range(NP):
                            o = ps[p_ // 7][:, (p_ % 7) * NCH:(p_ % 7 + 1) * NCH]
                            first = True
                            if down:
                                ew.mm(o, R8[:, p_, :], St[:, p_, tiles(p_, r)], start=True, stop=False)
                                first = False
                            elif r > 0:
                                ew.mm(o, R8[:, p_, :], cur[:, p_, :], start=True, stop=False)
                                first = False
                            ew.mm(o, W1[:, p_, :], U2[:, p_, tiles(p_, r)], start=first, stop=True)
                    b.blk(pe=pe_)

                    def ev(ew, r=r, nxt=nxt):
                        for bk in range(3):
                            n_p = 7 if bk < 2 else 2
                            prs = slice(bk * 7, bk * 7 + n_p)
                            src = ps[bk][:, 0:n_p * NCH].rearrange("p (a m) -> p a m", m=NCH)
                            if down:
                                lo, hi = bk * 7, bk * 7 + n_p
                                for (a, e_) in ((lo, min(hi, 8)), (max(lo, 8), hi)):
                                    if e_ > a:
                                        ew.c(ew.e.activation(
                                            out=St[:, a:e_, tiles(a, r + 1)],
                                            in_=ps[bk][:, (a - lo) * NCH:(e_ - lo) * NCH].rearrange("p (a m) -> p a m", m=NCH), func=AF.Copy))
                            elif r < 7:
                                ew.c(ew.e.activation(out=nxt[:, prs, :], in_=src, func=AF.Copy))
                            else:
                                ew.c(ew.e.activation(out=Fm[:, prs, :], in_=src, func=AF.Copy))
                    b.blk(act=ev)
                    cur, nxt = nxt, cur

            sweep(False)

            def swp(ew):
                for bk in range(3):
                    n_p = 7 if bk < 2 else 2
                    ew.mm(ps[bk][:, 0:n_p * NCH], SWm[:], Fm[:, bk * 7:bk * 7 + n_p, :].rearrange("p a m -> p (a m)"),
                          start=True, stop=True)
            b.blk(pe=swp)

            def swe(ew):
                for bk in range(3):
                    n_p = 7 if bk < 2 else 2
                    ew.c(ew.e.tensor_copy(Fs[:, bk * 7:bk * 7 + n_p, :],
                                          ps[bk][:, 0:n_p * NCH].rearrange("p (a m) -> p a m", m=NCH)))
                ew.c(ew.e.memset(V2[:], 0.0))
                for (src, slot) in ((Fm, 0), (Fs, 1)):
                    ew.c(ew.e.tensor_copy(F2[:, :, slot * 8:(slot + 1) * 8], src[:, 0:8, :].rearrange("p j m -> p m j")))
                    ew.c(ew.e.tensor_copy(F2[:, 0:4, 16 + slot * 8:24 + slot * 8],
                                          mkap(src, 8 * NCH + 3, [[-1, 4], [NCH, 8]])))
                    ew.c(ew.e.tensor_copy(F2[:, 4:NCH, 16 + slot * 8:24 + slot * 8],
                                          mkap(src, 8 * NCH + NCH - 1, [[-1, NCH - 4], [NCH, 8]])))
            b.blk(dve=swe)

            def l2(ew):
                t = ew.e.tensor_tensor
                ta = l2t[0][:].rearrange("p a b -> p (a b)")
                tb = l2t[1][:].rearrange("p a b -> p (a b)")
                for k in range(NCH - 1):
                    vsw = mkap(V2, k * 32 + 8, [[16, 2], [-8, 2], [1, 8]])
                    ew.c(t(out=ta, in0=A2[:], in1=V2[:, k, :], op=ALU.mult))
                    ew.c(t(out=tb.rearrange("p (d h j) -> p d h j", d=2, h=2), in0=B2[:].rearrange("p (d h j) -> p d h j", d=2, h=2),
                           in1=vsw, op=ALU.mult))
                    ew.c(t(out=ta, in0=ta, in1=tb, op=ALU.add))
                    ew.c(t(out=V2[:, k + 1, :], in0=ta, in1=F2[:, k, :], op=ALU.add))
            b.blk(dve=l2)

            def dinit(ew):
                ew.c(ew.e.tensor_copy(St[:, 0:8, slice(0, NTI, 8)], V2[:, :, 0:8].rearrange("p k j -> p j k")))
                ew.c(ew.e.tensor_copy(St[:, 8:16, slice(7, 7 + 8 * 4, 8)], mkap(V2, 3 * 32 + 16, [[1, 8], [-32, 4]])))
                ew.c(ew.e.tensor_copy(St[:, 8:16, slice(7 + 8 * 4, NTI, 8)],
                                      mkap(V2, (NCH - 1) * 32 + 16, [[1, 8], [-32, NCH - 4]])))
            b.blk(dve=dinit)
            sweep(True)

            ytasks = []
            for p_ in range(NP):
                def ype(ew, p_=p_):
                    for cb_, (c0_, c1_) in enumerate(((0, 512), (512, NTI))):
                        o = ps[(p_ % 2) * 2 + cb_][:, 0:c1_ - c0_]
                        ew.mm(o, TP[:, p_, :], U2[:, p_, c0_:c1_], start=True, stop=False)
                        ew.mm(o, W2[:, p_, :, :].rearrange("p a b -> p (a b)"), St[:, p_, c0_:c1_], start=False, stop=True)

                def yev(ew, p_=p_):
                    ew.c(ew.e.activation(out=Yo[:, p_, 0:512], in_=ps[(p_ % 2) * 2][:, 0:512], func=AF.Copy))
                    ew.c(ew.e.activation(out=Yo[:, p_, 512:NTI], in_=ps[(p_ % 2) * 2 + 1][:, 0:NTI - 512], func=AF.Copy))
                ytasks.append([{"pe": ype}, {"act": yev}])
            b.pipeline(ytasks)

            def yout_(ew):
                for t in range(8):
                    ew.dma(yd[0, t, qb * 128:(qb + 1) * 128, :].rearrange("(g q) i -> q g i", q=16),
                           Yo[t * 16:(t + 1) * 16, 0:8, :])
                    ew.dma(yd[1, 7 - t, qb * 128:(qb + 1) * 128, :].rearrange("(g q) i -> q g i", q=16),
                           Yo[t * 16:(t + 1) * 16, 8:16, :])
            b.blk(sp=yout_)


        b.rec = []
        prep(0, 0)
        steps = b.rec
        b.rec = None
        b.emit_zipped(steps, [])
        for qb in range(4):
            b.rec = []
            main(qb, qb % 2)
            sm = b.rec
            b.rec = []
            if qb < 3:
                prep(qb + 1, (qb + 1) % 2)
            sp_ = b.rec
            b.rec = None
            b.emit_zipped(sm, sp_)


def lru_phase(nc, b, l, ps, xr_d, gg_d, yl_d, cw, cb, lruW, lb_rg, lb_ig, llam):
    with ExitStack() as st:
        sbt = lambda n, s, d: st.enter_context(nc.sbuf_tensor(uniq(n), list(s), d))
        cwt = sbt("cwt", [128, 4, 4], F32)
        cbt = sbt("cbt", [128, 4], F32)
        brg = sbt("brg", [128, 2, 4], F32)
        big = sbt("big", [128, 2, 4], F32)
        lam = sbt("lam", [128, 2, 4], F32)
        nsp = sbt("nsp", [128, 2, 4], F32)
        nsp2 = sbt("nsp2", [128, 2, 4], F32)
        Wg = sbt("Wg", [128, 2, 2, 4, 128], BF)
        xp = sbt("xp", [128, 3 + NCTX + 3 + NLAT + 3], F32)
        xc = sbt("xc", [128, NT], F32)
        xcb = sbt("xcb", [128, NT], BF)
        R = [xp, sbt("R1", [128, NT], F32)]
        I = [sbt(f"I{d}", [128, NT], F32) for d in range(2)]
        H = [sbt(f"H{d}", [128, NT], F32) for d in range(2)]
        xraw = H[0]
        ggt = sbt("ggt", [128, NT], BF)
        ylo = xcb
        hc = sbt("hc", [128, 2], F32)
        CO = 3
        LO = 3 + NCTX + 3

        def ld(ew):
            ew.dma(cwt[:], cw[:, l, :, :])
            ew.dma(cbt[:], cb[:, l, :])
            ew.dma(brg[:], lb_rg[:, l, :, :])
            ew.dma(big[:], lb_ig[:, l, :, :])
            ew.dma(lam[:], llam[:, l, :, :])

        def ldw(ew):
            for d in range(2):
                for g in range(2):
                    ew.dma(Wg[:, d, g, :, :], lruW[l, d, g].rearrange("c i o -> i c o"))
        b.blk(sp=ld, pool=ldw)

        b.blk(act=lambda ew: ew.c(ew.e.activation(out=nsp[:], in_=lam[:], func=AF.Exp, scale=-1.0)))
        b.blk(act=lambda ew: ew.c(ew.e.activation(out=nsp[:], in_=nsp[:], func=AF.Ln, bias=1.0)))

        def nsp_(ew):
            ew.c(ew.e.tensor_scalar(out=nsp2[:], in0=nsp[:], scalar1=-16.0, scalar2=None, op0=ALU.mult))
            ew.c(ew.e.tensor_scalar(out=nsp[:], in0=nsp[:], scalar1=-8.0, scalar2=None, op0=ALU.mult))
        b.blk(dve=nsp_)

        for cc in range(4):
            rows = slice(cc * 128, (cc + 1) * 128)

            def l0(ew, rows=rows):
                ew.dma(xraw[:], xr_d[rows, :])
                ew.dma(ggt[:], gg_d[rows, :])
            b.blk(sp=l0)

            def l1(ew):
                ew.c(ew.e.memset(xp[:], 0.0))
                ew.c(ew.e.tensor_copy(xp[:, CO:CO + NCTX], xraw[:, 0:NCTX]))
                ew.c(ew.e.tensor_copy(xp[:, LO:LO + NLAT].rearrange("p (c r) -> p c r", r=64),
                                      xraw[:, NCTX:NT].rearrange("p (r c) -> p c r", c=64)))
            b.blk(dve=l1)

            def l2_(ew, cc=cc):
                for (o0, n, base) in ((0, NCTX, CO), (NCTX, NLAT, LO)):
                    ew.c(ew.e.tensor_scalar(out=xc[:, o0:o0 + n], in0=xp[:, base - 2:base - 2 + n],
                                            scalar1=cwt[:, 0, cc:cc + 1], scalar2=cbt[:, cc:cc + 1],
                                            op0=ALU.mult, op1=ALU.add))
                    for k in range(1, 4):
                        ew.c(ew.e.scalar_tensor_tensor(out=xc[:, o0:o0 + n], in0=xp[:, base - 2 + k:base - 2 + k + n],
                                                       scalar=cwt[:, k, cc:cc + 1], in1=xc[:, o0:o0 + n],
                                                       op0=ALU.mult, op1=ALU.add))
                ew.c(ew.e.tensor_copy(xcb[:], xc[:]))
            b.blk(dve=l2_)

            blocks = [(i * 512, min(512, NT - i * 512)) for i in range(9)]
            for bi, (c0_, n) in enumerate(blocks):
                def gpe(ew, c0_=c0_, n=n, cc=cc):
                    for d in range(2):
                        for g in range(2):
                            ew.mm(ps[d * 2 + g][:, 0:n], Wg[:, d, g, cc, :], xcb[:, c0_:c0_ + n], start=True, stop=True)

                def gev(ew, c0_=c0_, n=n, cc=cc):
                    for d in range(2):
                        ew.c(ew.e.activation(out=R[d][:, c0_:c0_ + n], in_=ps[d * 2][:, 0:n], func=AF.Sigmoid,
                                             bias=brg[:, d, cc:cc + 1]))
                        ew.c(ew.e.activation(out=I[d][:, c0_:c0_ + n], in_=ps[d * 2 + 1][:, 0:n], func=AF.Sigmoid,
                                             bias=big[:, d, cc:cc + 1]))
                b.blk(pe=gpe)
                b.blk(act=gev)

            def e1(ew, cc=cc):
                for d in range(2):
                    ew.c(ew.e.activation(out=H[d][:], in_=R[d][:, 0:NT], func=AF.Exp, scale=nsp2[:, d, cc:cc + 1]))
                    ew.c(ew.e.activation(out=R[d][:, 0:NT], in_=R[d][:, 0:NT], func=AF.Exp, scale=nsp[:, d, cc:cc + 1]))

            def e1d(ew):
                for d in range(2):
                    ew.c(ew.e.tensor_tensor(out=I[d][:], in0=I[d][:], in1=xc[:], op=ALU.mult))
            b.blk(act=e1, dve=e1d)

            def e2(ew):
                for d in range(2):
                    ew.c(ew.e.activation(out=H[d][:], in_=H[d][:], func=AF.Sqrt, scale=-1.0, bias=1.0))
            b.blk(act=e2)

            def e3(ew):
                for d in range(2):
                    ew.c(ew.e.tensor_tensor(out=I[d][:], in0=I[d][:], in1=H[d][:], op=ALU.mult))
            b.blk(dve=e3)

            def rv(ap2d, o0, n):
                full = ap2d[:, o0:o0 + n]
                return bass.AP(full.tensor, full.offset + (n - 1), [list(full.ap[0]), [-1, n]])

            def sc(ew):
                ew.c(ew.e.tensor_tensor_scan(out=H[0][:, 0:NCTX], data0=R[0][:, 0:NCTX], data1=I[0][:, 0:NCTX],
                                             initial=0.0, op0=ALU.mult, op1=ALU.add))
                ew.c(ew.e.tensor_copy(hc[:, 0:1], H[0][:, NCTX - 1:NCTX]))
                ew.c(ew.e.tensor_tensor_scan(out=H[0][:, NCTX:NT], data0=R[0][:, NCTX:NT], data1=I[0][:, NCTX:NT],
                                             initial=hc[:, 0:1], op0=ALU.mult, op1=ALU.add))
                ew.c(ew.e.tensor_tensor_scan(out=rv(H[1], 0, NCTX), data0=rv(R[1], 0, NCTX), data1=rv(I[1], 0, NCTX),
                                             initial=0.0, op0=ALU.mult, op1=ALU.add))
                ew.c(ew.e.tensor_copy(hc[:, 1:2], H[1][:, 0:1]))
                ew.c(ew.e.tensor_tensor_scan(out=rv(H[1], NCTX, NLAT), data0=rv(R[1], NCTX, NLAT),
                                             data1=rv(I[1], NCTX, NLAT), initial=hc[:, 1:2], op0=ALU.mult, op1=ALU.add))
                ew.c(ew.e.tensor_tensor(out=H[0][:], in0=H[0][:], in1=H[1][:], op=ALU.add))
                ew.c(ew.e.tensor_tensor(out=ylo[:, 0:NCTX], in0=H[0][:, 0:NCTX], in1=ggt[:, 0:NCTX], op=ALU.mult))
                ew.c(ew.e.tensor_tensor(out=ylo[:, NCTX:NT].rearrange("p (r c) -> p r c", c=64),
                                        in0=H[0][:, NCTX:NT].rearrange("p (c r) -> p r c", r=64),
                                        in1=ggt[:, NCTX:NT].rearrange("p (r c) -> p r c", c=64), op=ALU.mult))
            b.blk(dve=sc)
            b.blk(sp=lambda ew, rows=rows: ew.dma(yl_d[rows, :], ylo[:]))


_NC_CACHE = {}


def _host_inputs(inp, bidx):
    f = np.float32
    x = np.asarray(inp["x"], f)
    ctx = np.asarray(inp["ctx"], f)
    d = {}
    d["xs"] = np.ascontiguousarray(np.concatenate([ctx[bidx].T, x[bidx].T], axis=1))
    cv = np.stack([np.asarray(inp["c"], f)[bidx], np.asarray(inp["c_ctx"], f)], axis=-1)
    d["cvec"] = np.ascontiguousarray(cv.reshape(8, 128, 2).transpose(1, 0, 2))
    return d


def _shared_inputs(inp):
    f = np.float32
    g = lambda k: np.asarray(inp[k], f)
    d = {}
    d["w_ada"] = g("w_ada")
    d["b_ada"] = np.ascontiguousarray(g("b_ada").reshape(NL, 48, 128).transpose(2, 0, 1))
    d["gains"] = np.ascontiguousarray(g("norm_gains").reshape(NL, 4, 8, 128).transpose(3, 0, 1, 2))
    d["w_in"] = g("w_in")
    d["w_out"] = g("w_out")
    d["w_ffi"] = g("w_ffn_in")
    d["w_ffo"] = g("w_ffn_out")
    d["w_glu"] = g("s5_w_glu")
    d["b_glu"] = np.ascontiguousarray(g("s5_b_glu").reshape(NL, 4, 128).transpose(2, 0, 1))

    def st2(a):
        a = np.concatenate([a, a], axis=-1)
        a = a.reshape(NL, 2, 2, 16, 128).reshape(NL, 4, 16, 128)
        return np.ascontiguousarray(a.transpose(3, 0, 1, 2))
    d["sa_re"] = st2(g("s5_a_re"))
    d["sa_im"] = st2(g("s5_a_im"))
    d["sldt"] = st2(np.broadcast_to(g("s5_log_dt")[..., None], (NL, 2, 32, 64)))

    def st3(top, bot):
        a = np.concatenate([top, bot], axis=3)
        a = a.reshape(NL, 4, 16, 128, a.shape[-1])
        return np.ascontiguousarray(a.transpose(3, 0, 1, 2, 4))
    bre, bim = g("s5_b_re"), g("s5_b_im")
    d["sBP"] = st3(bre, bim)
    d["sBQ"] = st3(bim, bre)
    cre = g("s5_c_re").transpose(0, 1, 2, 4, 3)
    cim = g("s5_c_im").transpose(0, 1, 2, 4, 3)
    d["sCP"] = st3(cre, cim)
    d["sCQ"] = st3(cim, cre)
    sd = g("s5_d").reshape(NL, 4, 8, 16)
    sd = np.broadcast_to(sd[:, :, :, None, :], (NL, 4, 8, 8, 16))
    d["sD"] = np.ascontiguousarray(sd.transpose(3, 4, 0, 1, 2).reshape(128, NL, 4, 8))
    d["cw"] = np.ascontiguousarray(g("lru_conv_w").reshape(NL, 4, 4, 128).transpose(3, 0, 1, 2))
    d["cb"] = np.ascontiguousarray(g("lru_conv_b").reshape(NL, 4, 128).transpose(2, 0, 1))
    W = np.zeros((NL, 2, 2, 4, 128, 128), f)
    for gi, key in enumerate(("lru_w_rg", "lru_w_ig")):
        w = g(key)
        for cc in range(4):
            W[:, :, gi, cc, 0:64, 0:64] = w[:, :, 2 * cc]
            W[:, :, gi, cc, 64:128, 64:128] = w[:, :, 2 * cc + 1]
    d["lruW"] = W
    v = lambda k: np.ascontiguousarray(g(k).reshape(NL, 2, 4, 128).transpose(3, 0, 1, 2))
    d["lb_rg"] = v("lru_b_rg")
    d["lb_ig"] = v("lru_b_ig")
    d["llam"] = v("lru_lambda")
    I = np.eye(128, dtype=f)
    d["cI"] = I
    J = np.zeros((128, 128), f)
    SW = np.zeros((128, 128), f)
    for n in range(64):
        J[n, 64 + n] = 1.0
        J[64 + n, n] = 1.0
        SW[64 + n, n] = -1.0
        SW[n, 64 + n] = 1.0
    d["cJ"] = J
    d["cSW"] = SW
    sidx = np.arange(128) // 16
    d["cML"] = (sidx[None, :] >= sidx[:, None]).astype(f)
    sgn = np.where(np.arange(128) < 64, -1.0, 1.0).astype(f)
    d["cSG"] = np.stack([sgn, -sgn, np.ones(128, f), np.full(128, 1.0 / 1024, f)], axis=1)
    return d


def kernel(**inputs):
    if "nc" not in _NC_CACHE:
        _NC_CACHE["nc"] = build_program()
    nc = _NC_CACHE["nc"]
    shared = _shared_inputs(inputs)
    in_maps = []
    for bidx in range(8):
        m = dict(shared)
        m.update(_host_inputs(inputs, bidx))
        in_maps.append(m)
    res = run_bass_kernel_spmd(nc, in_maps, core_ids=list(range(8)))
    out = np.stack([np.asarray(r["yout"], np.float32).T for r in res.results], axis=0)
    return np.ascontiguousarray(out)
```

```python
import math
from contextlib import ExitStack
import numpy as np
import concourse.bass as bass
import concourse.mybir as mybir
from concourse.bass_utils import run_bass_kernel_spmd

F32 = mybir.dt.float32
BF = mybir.dt.bfloat16
AF = mybir.ActivationFunctionType
ALU = mybir.AluOpType

NL = 4
DM = 1024
NCTX = 256
NLAT = 4096
NT = NCTX + NLAT
NTI = NT // 8
NCH = NTI // 8
TN = 128
NTT = NT // TN
DFF = 2816
EPS = 1e-6
DEBUG = False
NL_RUN = 4
STOP_AFTER = 99


class EW:
    def __init__(self, b, name):
        self.b = b
        self.name = name
        self.sem = b.newsem()
        self.n = 0
        self.dsem = b.newsem()
        self.dn = 0
        self.e = None
        self.last = None

    def c(self, ins, wait=True):
        if self.n >= 6000:
            self.sem = self.b.newsem()
            self.n = 0
        self.n += 1
        ins.then_inc(self.sem, 1)
        if wait:
            self.e.wait_ge(self.sem, self.n)
        return ins

    def mm(self, *a, **k):
        self.last = self.e.matmul(*a, **k)
        return self.last

    def tr(self, *a, **k):
        self.last = self.e.transpose(*a, **k)
        return self.last

    def dma(self, out, in_):
        if self.dn >= 550:
            self.dsem = self.b.newsem()
            self.dn = 0
        self.dn += 1
        self.e.dma_start(out=out, in_=in_).then_inc(self.dsem, 16)

    def dma_async(self, out, in_, track=True):
        b = self.b
        i = b.ai % len(b.asems)
        b.ai += 1
        sem = b.asems[i]
        if b.acnt[i]:
            self.e.wait_ge(sem, b.acnt[i])
        b.acnt[i] += 16
        self.e.dma_start(out=out, in_=in_).then_inc(sem, 16)
        tok = (sem, b.acnt[i])
        if track:
            b.outstanding.append(tok)
        return tok

    def wait_tok(self, tok):
        self.e.wait_ge(tok[0], tok[1])

    def fin(self):
        if self.last is not None:
            self.c(self.last)
            self.last = None
        if self.dn:
            self.e.wait_ge(self.dsem, 16 * self.dn)


class _ProxyEng:
    def __getattr__(self, name):
        return lambda *a, **k: (name, a, k)


class _ProxyEW:
    def __init__(self):
        self.e = _ProxyEng()
        self.ops = []

    def c(self, rec):
        self.ops.append(rec)
        return rec


class _AsyncDma:
    def __init__(self, ew, toks):
        self.ew = ew
        self.toks = toks

    def dma(self, out, in_):
        self.toks.append(self.ew.dma_async(out, in_, track=False))


class Bld:
    def __init__(self, nc):
        self.nc = nc
        self.es = ExitStack()
        self.sems = [self.es.enter_context(nc.semaphore(f"sm{i}")) for i in range(96)]
        self.si = 0
        self.ew = {k: EW(self, k) for k in ("pe", "act", "dve", "pool", "sp")}
        self.nblk = 0
        self.rec = None
        self.asems = [self.newsem() for _ in range(24)]
        self.acnt = [0] * 24
        self.ai = 0
        self.outstanding = []

    def drain(self):
        toks, self.outstanding = self.outstanding, []
        if toks:
            self.blk(sp=lambda ew: [ew.wait_tok(t) for t in toks])

    def emit_zipped(self, a, bsteps):
        for i in range(max(len(a), len(bsteps))):
            fns = {}
            for lst in (a, bsteps):
                if i < len(lst):
                    for name, fn in lst[i].items():
                        if fn is not None:
                            fns.setdefault(name, []).append(fn)
            if fns:
                self.blk(**{n: (lambda ew, l=l: [f(ew) for f in l]) for n, l in fns.items()})

    def newsem(self):
        s = self.sems[self.si]
        self.si += 1
        return s

    def sb(self, name, shape, dt):
        return self.es.enter_context(self.nc.sbuf_tensor(name, list(shape), dt))

    def blk(self, **fns):
        m = {"pe": "tensor", "act": "scalar", "dve": "vector", "pool": "gpsimd", "sp": "sync"}
        if self.rec is not None:
            self.rec.append(dict(fns))
            return
        self.nblk += 1
        with self.nc.Block() as block:
            for name, fn in fns.items():
                if fn is None:
                    continue
                ew = self.ew[name]

                def run(e, ew=ew, fn=fn):
                    ew.e = e
                    fn(ew)
                    ew.fin()

                getattr(block, m[name])(run)

    def pipeline(self, tasks):
        nsteps = max((k + len(t) for k, t in enumerate(tasks)), default=0)
        for t in range(nsteps):
            fns = {}
            for k in range(max(0, t - 24), min(len(tasks), t + 1)):
                j = t - k
                if j < len(tasks[k]):
                    for name, fn in tasks[k][j].items():
                        fns.setdefault(name, []).append(fn)
            if not fns:
                continue
            self.blk(**{n: (lambda ew, l=l: [f(ew) for f in reversed(l)]) for n, l in fns.items()})


_UC = [0]


def uniq(n):
    _UC[0] += 1
    return f"{n}_{_UC[0]}"


def mkap(tile, off, dims):
    full = tile[:]
    return bass.AP(full.tensor, full.offset + off, [list(full.ap[0])] + [list(d) for d in dims])


def bc(ap, shape):
    return ap.to_broadcast(list(shape))


def build_program():
    nc = bass.Bass("TRN2", target_bir_lowering=False)
    b = Bld(nc)

    def din(name, shape, dt=F32):
        return nc.dram_tensor(name, list(shape), dt, kind="ExternalInput").ap()

    def dscr(name, shape, dt=F32):
        kind = "ExternalOutput"
        return nc.dram_tensor(name, list(shape), dt, kind=kind).ap()

    xs = din("xs", [DM, NT])
    cvec = din("cvec", [128, 8, 2])
    w_ada = din("w_ada", [NL, DM, 6 * DM])
    b_ada = din("b_ada", [128, NL, 48])
    gains = din("gains", [128, NL, 4, 8])
    w_in = din("w_in", [NL, DM, 1536])
    w_out = din("w_out", [NL, DM, DM])
    w_ffi = din("w_ffi", [NL, DM, 2 * DFF])
    w_ffo = din("w_ffo", [NL, DFF, DM])
    w_glu = din("w_glu", [NL, 512, 512])
    b_glu = din("b_glu", [128, NL, 4])
    sa_re = din("sa_re", [128, NL, 4, 16])
    sa_im = din("sa_im", [128, NL, 4, 16])
    sldt = din("sldt", [128, NL, 4, 16])
    sBP = din("sBP", [128, NL, 4, 16, 16])
    sBQ = din("sBQ", [128, NL, 4, 16, 16])
    sCP = din("sCP", [128, NL, 4, 16, 16])
    sCQ = din("sCQ", [128, NL, 4, 16, 16])
    sD = din("sD", [128, NL, 4, 8])
    cw = din("cw", [128, NL, 4, 4])
    cb = din("cb", [128, NL, 4])
    lruW = din("lruW", [NL, 2, 2, 4, 128, 128])
    lb_rg = din("lb_rg", [128, NL, 2, 4])
    lb_ig = din("lb_ig", [128, NL, 2, 4])
    llam = din("llam", [128, NL, 2, 4])
    cI = din("cI", [128, 128])
    cJ = din("cJ", [128, 128])
    cSW = din("cSW", [128, 128])
    cML = din("cML", [128, 128])
    cSG = din("cSG", [128, 4])
    yout = nc.dram_tensor("yout", [DM, NLAT], F32, kind="ExternalOutput").ap()

    xres = dscr("xres", [DM, NT])
    ud = dscr("ud", [8, 512, NTI], BF)
    yd = dscr("yd", [2, 8, 512, NTI], BF)
    xr_d = dscr("xr_d", [512, NT])
    gg_d = dscr("gg_d", [512, NT], BF)
    yl_d = dscr("yl_d", [512, NT], BF)
    yg_d = dscr("yg_d", [512, NT], BF)

    xres_v = xres.rearrange("(kc p) n -> p kc n", p=128)
    xs_v = xs.rearrange("(kc p) n -> p kc n", p=128)

    ident = b.sb("ident", [128, 128], F32)
    identb = b.sb("identb", [128, 128], BF)
    Jm = b.sb("Jm", [128, 128], F32)
    SWm = b.sb("SWm", [128, 128], F32)
    MLm = b.sb("MLm", [128, 128], F32)
    sg = b.sb("sg", [128, 4], F32)
    onesb = b.sb("onesb", [128, 128], BF)
    epsb = b.sb("epsb", [128, 1], F32)
    MOD = b.sb("MOD", [128, NL, 6, 8, 2], F32)
    GA = b.sb("GA", [128, NL, 4, 8], F32)
    DER = b.sb("DER", [128, NL, 4, 8, 2], F32)
    BADA = b.sb("BADA", [128, NL, 48], F32)
    ps = [b.es.enter_context(nc.psum_tensor(f"ps{i}", [128, 512], F32)) for i in range(8)]

    def c0(ew):
        ew.dma(ident[:], cI[:, :])
        ew.dma(Jm[:], cJ[:, :])
        ew.dma(SWm[:], cSW[:, :])
        ew.dma(MLm[:], cML[:, :])
        ew.dma(sg[:], cSG[:, :])
        ew.dma(GA[:], gains[:, :, :, :])
        ew.dma(BADA[:], b_ada[:, :, :])
    b.blk(sp=c0)

    def c1(ew):
        ew.c(ew.e.tensor_copy(identb[:], ident[:]))
        ew.c(ew.e.memset(onesb[:], 1.0 / 1024.0))
        ew.c(ew.e.memset(epsb[:], EPS))
    b.blk(dve=c1)

    def c2(ew):
        for kc in range(8):
            ew.dma(xres[kc * 128:(kc + 1) * 128, :], xs[kc * 128:(kc + 1) * 128, :])
    b.blk(sp=c2)

    with ExitStack() as st:
        cv = st.enter_context(nc.sbuf_tensor(uniq("cv"), [128, 8, 2], F32))
        scb = st.enter_context(nc.sbuf_tensor("scb", [128, 8, 2], BF))
        wad = [st.enter_context(nc.sbuf_tensor(f"wad{i}", [128, 8, 1024], BF)) for i in range(2)]

        b.blk(sp=lambda ew: ew.dma(cv[:], cvec[:, :, :]))
        b.blk(act=lambda ew: ew.c(ew.e.activation(out=scb[:], in_=cv[:], func=AF.Silu)))
        tasks = []
        for l in range(NL):
            for j in range(6):
                k = l * 6 + j
                wb = wad[k % 2]
                src = w_ada[l].rearrange("(kc p) n -> p kc n", p=128)[:, :, j * 1024:(j + 1) * 1024]

                def s_load(ew, wb=wb, src=src):
                    ew.dma(wb[:], src)

                def s_mm(ew, wb=wb, k=k):
                    for oc in range(8):
                        for kc in range(8):
                            ew.mm(ps[k % 2][:, oc * 2:oc * 2 + 2], wb[:, kc, oc * 128:(oc + 1) * 128],
                                  scb[:, kc, :], start=(kc == 0), stop=(kc == 7))

                def s_ev(ew, l=l, j=j, k=k):
                    ew.c(ew.e.tensor_tensor(
                        out=MOD[:, l, j, :, :],
                        in0=ps[k % 2][:, 0:16].rearrange("p (o w) -> p o w", w=2),
                        in1=bc(BADA[:, l, j * 8:(j + 1) * 8].unsqueeze(2), [128, 8, 2]),
                        op=ALU.add))
                tasks.append([{"pool": s_load}, {"pe": s_mm}, {"dve": s_ev}])
        b.pipeline(tasks)

        def derive(ew):
            for l in range(NL):
                for (dk, gk, mk, addone) in ((0, 0, 1, True), (1, 1, 2, False), (2, 2, 4, True), (3, 3, 5, False)):
                    g = bc(GA[:, l, gk, :].unsqueeze(2), [128, 8, 2])
                    if addone:
                        ew.c(ew.e.scalar_tensor_tensor(out=DER[:, l, dk, :, :], in0=MOD[:, l, mk, :, :],
                                                       scalar=1.0, in1=g, op0=ALU.add, op1=ALU.mult))
                    else:
                        ew.c(ew.e.tensor_tensor(out=DER[:, l, dk, :, :], in0=MOD[:, l, mk, :, :], in1=g,
                                                op=ALU.mult))
        b.blk(dve=derive)

    def norm_stages(l, ti, xt, sq, psS, rstd, tmp, h, which, tn, gap=0):
        w = 1 if ti * tn < NCTX else 0
        cols = slice(ti * tn, (ti + 1) * tn)
        dk = 0 if which == 0 else 2
        mk = 0 if which == 0 else 3

        tk = {}

        def s0(ew):
            if gap:
                tk["x"] = ew.dma_async(xt[:], xres_v[:, :, cols], track=False)
            else:
                ew.dma(xt[:], xres_v[:, :, cols])

        def s1(ew):
            if gap:
                ew.wait_tok(tk["x"])
            ew.c(ew.e.activation(out=sq[:], in_=xt[:], func=AF.Square))

        def s2(ew):
            for kc in range(8):
                ew.mm(psS, onesb[:], sq[:, kc, :], start=(kc == 0), stop=(kc == 7))

        def s2b(ew):
            ew.c(ew.e.activation(out=rstd[:], in_=psS, func=AF.Sqrt, bias=epsb[:, 0:1]))

        def s3(ew):
            ew.c(ew.e.reciprocal(rstd[:], rstd[:]))
            ew.c(ew.e.tensor_tensor(out=tmp[:], in0=xt[:], in1=bc(rstd[:].unsqueeze(1), [128, 8, tn]),
                                    op=ALU.mult))

        def s4(ew):
            for kc in range(8):
                ew.c(ew.e.activation(out=h[:, kc, :], in_=tmp[:, kc, :], func=AF.Identity,
                                     scale=DER[:, l, dk, kc, w:w + 1], bias=MOD[:, l, mk, kc, w:w + 1]), wait=(kc == 7))
        return [{"sp": s0}] + [{}] * gap + [{"act": s1}, {"pe": s2}, {"act": s2b}, {"dve": s3}, {"act": s4}]

    def post_stages(l, ti, banks, xt, osb, sq, psS, rstd, which, last_layer, tn, load_stage=0):
        w = 1 if ti * tn < NCTX else 0
        cols = slice(ti * tn, (ti + 1) * tn)
        dk = 1 if which == 0 else 3
        cpb = 512 // tn

        def s1(ew):
            for bi, bank in enumerate(banks):
                ew.c(ew.e.activation(out=osb[:, bi * cpb:(bi + 1) * cpb, :],
                                     in_=bank[:, 0:512].rearrange("p (c n) -> p c n", c=cpb), func=AF.Copy),
                     wait=(bi == len(banks) - 1))

        tk = {}

        def s1l(ew):
            tk["x"] = ew.dma_async(xt[:], xres_v[:, :, cols], track=False)

        def s1b(ew):
            ew.c(ew.e.activation(out=sq[:], in_=osb[:], func=AF.Square))

        def s2(ew):
            for kc in range(8):
                ew.mm(psS, onesb[:], sq[:, kc, :], start=(kc == 0), stop=(kc == 7))

        def s2b(ew):
            ew.c(ew.e.activation(out=rstd[:], in_=psS, func=AF.Sqrt, bias=epsb[:, 0:1]))

        def s3(ew):
            ew.c(ew.e.reciprocal(rstd[:], rstd[:]))
            for mc in range(8):
                ew.c(ew.e.scalar_tensor_tensor(out=osb[:, mc, :], in0=osb[:, mc, :],
                                               scalar=DER[:, l, dk, mc, w:w + 1], in1=rstd[:],
                                               op0=ALU.mult, op1=ALU.mult), wait=(mc == 7))

        def s4(ew):
            ew.wait_tok(tk["x"])
            ew.c(ew.e.tensor_tensor(out=xt[:], in0=xt[:], in1=osb[:], op=ALU.add))

        def s5(ew):
            ew.dma_async(xres_v[:, :, cols], xt[:])
            if last_layer and which == 1 and ti * tn >= NCTX:
                ew.dma_async(yout.rearrange("(kc p) n -> p kc n", p=128)[:, :, ti * tn - NCTX:(ti + 1) * tn - NCTX], xt[:])
        stg = [{"act": s1}, {"act": s1b}, {"pe": s2}, {"act": s2b}, {"dve": s3}, {"pool": s4}, {"sp": s5}]
        stg[load_stage] = dict(stg[load_stage])
        stg[load_stage]["sp"] = s1l
        return stg

    for l in range(NL_RUN):
        last = (l == NL - 1)
        with ExitStack() as st:
            sbt = lambda n, s, d: st.enter_context(nc.sbuf_tensor(uniq(n), list(s), d))
            T1 = 256
            NT1 = NT // T1
            win = sbt("win", [128, 8, 1536], BF)
            ut_all = sbt("ut_all", [128, 4, 8, NTI], BF)
            NB = 3
            xt = [sbt(f"xt{i}", [128, 8, T1], F32) for i in range(5)]
            sq = [sbt(f"sq{i}", [128, 8, T1], BF) for i in range(NB)]
            rstd = [sbt(f"rstd{i}", [128, T1], F32) for i in range(NB)]
            hb = [sbt(f"hb{i}", [128, 8, T1], BF) for i in range(NB)]
            xro = [sbt(f"xro{i}", [128, 4, T1], F32) for i in range(NB)]
            ggo = [sbt(f"ggo{i}", [128, 4, T1], BF) for i in range(NB)]
            b.blk(pool=lambda ew: ew.dma(win[:], w_in[l].rearrange("(kc p) n -> p kc n", p=128)))
            tasks = []
            for ti in range(NT1):
                k = ti % NB
                psS = ps[6 + ti % 2][:, 0:T1]
                stg = norm_stages(l, ti, xt[ti % 5], sq[k], psS, rstd[k], xt[ti % 5], hb[k], 0, T1, gap=2)
                cols = slice(ti * T1, (ti + 1) * T1)

                def mmh(ew, k=k, half=0):
                    for mc in range(half * 6, half * 6 + 6):
                        bank = ps[half * 3 + (mc % 6) // 2]
                        for kc in range(8):
                            ew.mm(bank[:, (mc % 2) * T1:(mc % 2 + 1) * T1],
                                  win[:, kc, mc * 128:(mc + 1) * 128], hb[k][:, kc, :],
                                  start=(kc == 0), stop=(kc == 7))

                def evA_d(ew, ti=ti):
                    for bk in range(2):
                        ew.c(ew.e.tensor_copy(
                            out=ut_all[:, bk * 2:bk * 2 + 2, :, ti * 32:(ti + 1) * 32],
                            in_=ps[bk][:, 0:512].rearrange("p (c i s) -> p c s i", c=2, s=8)))

                def evA_a(ew, k=k):
                    ew.c(ew.e.activation(out=xro[k][:, 0:2, :], in_=ps[2][:, 0:512].rearrange("p (c n) -> p c n", c=2),
                                         func=AF.Copy))

                def evB_a(ew, k=k):
                    ew.c(ew.e.activation(out=xro[k][:, 2:4, :], in_=ps[3][:, 0:512].rearrange("p (c n) -> p c n", c=2),
                                         func=AF.Copy))
                    for bk in range(2):
                        ew.c(ew.e.activation(out=ggo[k][:, bk * 2:bk * 2 + 2, :],
                                             in_=ps[4 + bk][:, 0:512].rearrange("p (c n) -> p c n", c=2),
                                             func=AF.Gelu_apprx_tanh))

                def stB(ew, k=k, cols=cols):
                    ew.dma(xr_d.rearrange("(c p) n -> p c n", p=128)[:, :, cols], xro[k][:])
                    ew.dma(gg_d.rearrange("(c p) n -> p c n", p=128)[:, :, cols], ggo[k][:])
                tasks.append(stg + [{"pe": lambda ew, f=mmh: f(ew, half=0)}, {"dve": evA_d, "act": evA_a}])
                tasks.append([{}] * 8 + [{"pe": lambda ew, f=mmh: f(ew, half=1)}, {"act": evB_a}, {"sp": stB}])
            b.pipeline(tasks)

            def uout(ew):
                for s in range(8):
                    ew.dma(ud[s].rearrange("(c p) i -> p c i", p=128), ut_all[:, :, s, :])
            b.blk(sp=uout)

        if STOP_AFTER < 2:
            continue
        s5_phase(nc, b, l, ps, ud, yd, ident, Jm, SWm, MLm, sg,
                 sa_re, sa_im, sldt, sBP, sBQ, sCP, sCQ, sD)

        if STOP_AFTER < 3:
            continue
        lru_phase(nc, b, l, ps, xr_d, gg_d, yl_d, cw, cb, lruW, lb_rg, lb_ig, llam)

        if STOP_AFTER < 4:
            continue
        cm_wfi = nc.sbuf_tensor(uniq("wfi"), [128, 8, 2 * DFF], BF)
        wfi = cm_wfi.__enter__()
        cm_wo = nc.sbuf_tensor(uniq("wo"), [128, 8, 1024], BF)
        wo = cm_wo.__enter__()
        cm_wg = nc.sbuf_tensor(uniq("wg"), [128, 4, 512], BF)
        wg = cm_wg.__enter__()
        wtok = {}

        def pre4(ew):
            wtok["wo"] = ew.dma_async(wo[:], w_out[l].rearrange("(kc p) n -> p kc n", p=128), track=False)
            wtok["wg"] = ew.dma_async(wg[:], w_glu[l].rearrange("(kc p) n -> p kc n", p=128), track=False)
            wtok["wfi"] = ew.dma_async(wfi[:], w_ffi[l].rearrange("(kc p) n -> p kc n", p=128), track=False)
        b.blk(pool=pre4)

        with ExitStack() as st:
            sbt = lambda n, s, d: st.enter_context(nc.sbuf_tensor(uniq(n), list(s), d))
            yf_ = [sbt(f"yf{i}", [128, 2, 8, NTI], BF) for i in range(2)]
            ys_ = [sbt(f"ys{i}", [128, NT], F32) for i in range(2)]
            ygo = [sbt(f"ygo{i}", [128, NT], BF) for i in range(2)]
            tasks = []
            for cc in range(4):
                k = cc % 2
                rows = slice(cc * 128, (cc + 1) * 128)

                def q0(ew, k=k, rows=rows):
                    for d in range(2):
                        ew.dma(yf_[k][:, d, :, :], yd[d, :, rows, :].rearrange("t p i -> p t i"))

                def q1(ew, k=k):
                    ew.c(ew.e.tensor_tensor(out=ys_[k][:].rearrange("p (i t) -> p t i", t=8),
                                            in0=yf_[k][:, 0, :, :], in1=yf_[k][:, 1, :, :], op=ALU.add))

                def q2(ew, k=k):
                    ew.c(ew.e.activation(out=ygo[k][:], in_=ys_[k][:], func=AF.Gelu_apprx_tanh))

                def q3(ew, k=k, rows=rows):
                    ew.dma(yg_d[rows, :], ygo[k][:])
                tasks.append([{"sp": q0}, {"dve": q1}, {"act": q2}, {"sp": q3}])
            b.pipeline(tasks)

        with ExitStack() as st:
            sbt = lambda n, s, d: st.enter_context(nc.sbuf_tensor(uniq(n), list(s), d))
            bg = sbt("bg", [128, 4], F32)
            xt = [sbt(f"xt{i}", [128, 8, TN], F32) for i in range(4)]
            osb = [sbt(f"osb{i}", [128, 8, TN], F32) for i in range(6)]
            sq = [sbt(f"sq{i}", [128, 8, TN], BF) for i in range(5)]
            rstd = [sbt(f"rstd{i}", [128, TN], F32) for i in range(5)]
            yg = [sbt(f"yg{i}", [128, 4, TN], BF) for i in range(5)]
            sig = [sbt(f"sig{i}", [128, 4, TN], F32) for i in range(5)]
            ymix = [sbt(f"ymix{i}", [128, 8, TN], BF) for i in range(5)]
            b.blk(sp=lambda ew: ew.dma(bg[:], b_glu[:, l, :]))
            tasks = []
            tlist = range(NTT) if not last else range(2, NTT)
            for n_, ti in enumerate(tlist):
                k = n_ % 5
                cols = slice(ti * TN, (ti + 1) * TN)
                psG = ps[n_ % 2]
                psO2 = [ps[2 + (n_ % 2) * 2], ps[3 + (n_ % 2) * 2]]
                psS = ps[6 + n_ % 2][:, 0:TN]

                def a0(ew, k=k, cols=cols):
                    ew.dma(yg[k][:], yg_d.rearrange("(c p) n -> p c n", p=128)[:, :, cols])
                    ew.dma(ymix[k][:, 4:8, :], yl_d.rearrange("(c p) n -> p c n", p=128)[:, :, cols])

                def a3(ew, k=k, psG=psG, first=(n_ == 0)):
                    if first:
                        ew.wait_tok(wtok["wg"])
                        ew.wait_tok(wtok["wo"])
                    for mc in range(4):
                        for kc in range(4):
                            ew.mm(psG[:, mc * TN:(mc + 1) * TN], wg[:, kc, mc * 128:(mc + 1) * 128],
                                  yg[k][:, kc, :], start=(kc == 0), stop=(kc == 3))

                def a4(ew, k=k, psG=psG):
                    for mc in range(4):
                        ew.c(ew.e.activation(out=sig[k][:, mc, :], in_=psG[:, mc * TN:(mc + 1) * TN],
                                             func=AF.Sigmoid, bias=bg[:, mc:mc + 1]), wait=(mc == 3))

                def a5(ew, k=k):
                    ew.c(ew.e.tensor_tensor(out=ymix[k][:, 0:4, :], in0=yg[k][:], in1=sig[k][:], op=ALU.mult))

                def a6(ew, k=k, psO2=psO2):
                    for mc in range(8):
                        for kc in range(8):
                            ew.mm(psO2[mc // 4][:, (mc % 4) * TN:(mc % 4 + 1) * TN],
                                  wo[:, kc, mc * 128:(mc + 1) * 128], ymix[k][:, kc, :],
                                  start=(kc == 0), stop=(kc == 7))
                stg = [{"sp": a0}, {"pe": a3}, {"act": a4}, {"dve": a5}, {"pe": a6}]
                stg += post_stages(l, ti, psO2, xt[n_ % 4], osb[n_ % 6], sq[k], psS, rstd[k], 0, last, TN, load_stage=3)
                tasks.append(stg)
            b.pipeline(tasks)
            b.drain()
        cm_wg.__exit__(None, None, None)
        cm_wo.__exit__(None, None, None)

        if STOP_AFTER < 5:
            cm_wfi.__exit__(None, None, None)
            continue
        with ExitStack() as st:
            sbt = lambda n, s, d: st.enter_context(nc.sbuf_tensor(uniq(n), list(s), d))
            T5 = 256
            wfo = sbt("wfo", [128, 22, 1024], BF)
            NB = 2
            xt = [sbt(f"xt{i}", [128, 8, T5], F32) for i in range(NB)]
            sq = [sbt(f"sq{i}", [128, 8, T5], BF) for i in range(NB)]
            rstd = [sbt(f"rstd{i}", [128, T5], F32) for i in range(NB)]
            hb = [sbt(f"hb{i}", [128, 8, T5], BF) for i in range(NB)]
            xt2 = sbt("xtb", [128, 8, T5], F32)
            osb = sbt("osb", [128, 8, T5], F32)
            sq2 = sbt("sqb", [128, 8, T5], BF)
            rstd2 = sbt("rstdb", [128, T5], F32)
            act_ = sbt("act", [128, 22, T5], BF)
            sgt = [sbt(f"sgt{i}", [128, T5], F32) for i in range(3)]

            b.blk(pool=lambda ew: wtok.__setitem__("wfo", ew.dma_async(wfo[:], w_ffo[l].rearrange("(kc p) n -> p kc n", p=128), track=False)))
            tlist = list(range(NT // T5) if not last else range(1, NT // T5))
            tasks = []
            gcount = [0]
            psO = [ps[3], ps[4], ps[5], ps[6]]

            def norm_task(n_):
                ti = tlist[n_]
                k = n_ % NB
                return norm_stages(l, ti, xt[k], sq[k], ps[7][:, 0:T5], rstd[k], xt[k], hb[k], 1, T5, gap=2)

            def gu_task(n_, r):
                k = n_ % NB
                gi = gcount[0]
                gcount[0] += 1
                pg = ps[gi % 3]
                sgb = sgt[gi % 3]

                def g0(ew):
                    if n_ == 0 and r == 0:
                        ew.wait_tok(wtok["wfi"])
                    for half in range(2):
                        co = half * DFF + r * 128
                        for kc in range(8):
                            ew.mm(pg[:, half * T5:(half + 1) * T5], wfi[:, kc, co:co + 128], hb[k][:, kc, :],
                                  start=(kc == 0), stop=(kc == 7))

                def g1a(ew):
                    ew.c(ew.e.activation(out=sgb[:], in_=pg[:, 0:T5], func=AF.Silu))

                def g1d(ew):
                    ew.c(ew.e.tensor_tensor(out=act_[:, r, :], in0=sgb[:], in1=pg[:, T5:2 * T5], op=ALU.mult))

                def o_r(ew):
                    if n_ == 0 and r == 0:
                        ew.wait_tok(wtok["wfo"])
                    for mc in range(8):
                        ew.mm(psO[mc // 2][:, (mc % 2) * T5:(mc % 2 + 1) * T5],
                              wfo[:, r, mc * 128:(mc + 1) * 128], act_[:, r, :],
                              start=(r == 0 and mc % 2 == 0), stop=(r == 21 and mc % 2 == 1))
                return [{"pe": g0}, {"act": g1a}, {"dve": g1d}, {"pe": o_r}]

            def out_task(n_):
                ti = tlist[n_]
                return post_stages(l, ti, psO, xt2, osb, sq2, ps[7][:, T5:2 * T5], rstd2, 1, last, T5)

            ntl = len(tlist)
            tasks.append(norm_task(0))
            for _ in range(8):
                tasks.append([])
            for n_ in range(ntl):
                for r in range(22):
                    tasks.append(gu_task(n_, r))
                    if r == 1 and n_ >= 1:
                        tasks.append(out_task(n_ - 1))
                    if r == 10 and n_ + 1 < ntl:
                        tasks.append(norm_task(n_ + 1))
                tasks.append([])
            for _ in range(3):
                tasks.append([])
            tasks.append(out_task(ntl - 1))
            b.pipeline(tasks)
            b.drain()
        cm_wfi.__exit__(None, None, None)

    b.es.close()
    return nc


def s5_phase(nc, b, l, ps, ud, yd, ident, Jm, SWm, MLm, sg,
             sa_re, sa_im, sldt, sBP, sBQ, sCP, sCQ, sD):
    NP = 16
    with ExitStack() as st:
        sbt = lambda n, s, d: st.enter_context(nc.sbuf_tensor(uniq(n), list(s), d))
        are = sbt("are", [128, NP], F32)
        aim = sbt("aim", [128, NP], F32)
        ldt = sbt("ldt", [128, NP], F32)
        BP = sbt("BP", [128, NP, 16], F32)
        BQ = sbt("BQ", [128, NP, 16], F32)
        CP = sbt("CP", [128, NP, 16], F32)
        CQ = sbt("CQ", [128, NP, 16], F32)
        Dp = sbt("Dp", [128, 8], F32)
        U2_s = [sbt("U2%d" % i_, [128, NP, NTI], BF) for i_ in range(2)]
        T = {}
        for nm in ("dt", "ar", "ai", "mg", "c0", "s0", "t1", "t2", "t3", "den", "fre", "fim",
                   "fres", "fims", "are64", "aim64", "a8re", "a8ims"):
            T[nm] = sbt("T" + nm, [128, NP], F32)
        PW = sbt("PW", [128, 2, 9, NP], F32)
        PI = sbt("PI", [128, 2, 8, NP], F32)
        fBP = sbt("fBP", [128, NP, 16], F32)
        fBQ = sbt("fBQ", [128, NP, 16], F32)
        AL = sbt("AL", [128, NP, 8], F32)
        BE = sbt("BE", [128, NP, 8], F32)
        Zst = sbt("Zst", [128, NP, 8, 16], F32)
        W1_s = [sbt("W1%d" % i_, [128, NP, 128], BF) for i_ in range(2)]
        W2_s = [sbt("W2%d" % i_, [128, NP, 8, 16], BF) for i_ in range(2)]
        E0 = sbt("E0", [128, NP, 8, 16], F32)
        CT0 = sbt("CT0", [128, NP, 8, 16], F32)
        TP_s = [sbt("TP%d" % i_, [128, NP, 128], BF) for i_ in range(2)]
        R8_s = [sbt("R8%d" % i_, [128, NP, 128], BF) for i_ in range(2)]
        tA = sbt("tA", [128, NP, 8, 16], F32)
        tB = sbt("tB", [128, NP, 8, 16], F32)
        Sa = sbt("Sa", [128, NP, NCH], BF)
        Sb = sbt("Sb", [128, NP, NCH], BF)
        Fm = sbt("Fm", [128, NP, NCH], F32)
        Fs = sbt("Fs", [128, NP, NCH], F32)
        St = sbt("St", [128, NP, NTI], BF)
        Yo = sbt("Yo", [128, NP, NTI], BF)
        l2t = [sbt(f"l2t{i}", [128, 4, 8], F32) for i in range(2)]
        V2 = sbt("V2", [128, NCH, 32], F32)
        F2 = sbt("F2", [128, NCH, 32], F32)
        A2_s = [sbt("A2%d" % i_, [128, 32], F32) for i_ in range(2)]
        B2_s = [sbt("B2%d" % i_, [128, 32], F32) for i_ in range(2)]


        ytok = []

        def prep(qb, par):
            U2, W1, W2, TP, R8, A2, B2 = U2_s[par], W1_s[par], W2_s[par], TP_s[par], R8_s[par], A2_s[par], B2_s[par]
            def ld(ew):
                for d in range(2):
                    sl = slice(d * 8, (d + 1) * 8)
                    hs = slice((qb % 2) * 8, (qb % 2) * 8 + 8)
                    hb_ = d * 2 + qb // 2
                    ew.dma(are[:, sl], sa_re[:, l, hb_, hs])
                    ew.dma(aim[:, sl], sa_im[:, l, hb_, hs])
                    ew.dma(ldt[:, sl], sldt[:, l, hb_, hs])
                    ew.dma(BP[:, sl, :], sBP[:, l, hb_, hs, :])
                    ew.dma(BQ[:, sl, :], sBQ[:, l, hb_, hs, :])
                    ew.dma(CP[:, sl, :], sCP[:, l, hb_, hs, :])
                    ew.dma(CQ[:, sl, :], sCQ[:, l, hb_, hs, :])
                ew.dma(Dp[:], sD[:, l, qb, :])
                for s in range(8):
                    src = ud[s, qb * 128:(qb + 1) * 128, :].rearrange("(g p) i -> p g i", p=16)
                    ew.dma(U2[s * 16:(s + 1) * 16, 0:8, :], src)
                    ew.dma(U2[(7 - s) * 16:(8 - s) * 16, 8:16, :], src)
            ldtok = []
            b.blk(sp=lambda ew: ld(_AsyncDma(ew, ldtok)))
            for _ in range(3):
                b.blk()
            b.blk(sp=lambda ew: [ew.wait_tok(t_) for t_ in ldtok])

            def sc1(ew):
                ew.c(ew.e.activation(out=T["dt"][:], in_=ldt[:], func=AF.Exp))
            b.blk(act=sc1)

            def sc2(ew):
                ew.c(ew.e.tensor_tensor(out=T["ar"][:], in0=are[:], in1=T["dt"][:], op=ALU.mult))
                ew.c(ew.e.tensor_tensor(out=T["ai"][:], in0=aim[:], in1=T["dt"][:], op=ALU.mult))
                ew.c(ew.e.tensor_scalar(out=T["t1"][:], in0=T["ai"][:], scalar1=1.0 / 16.0, scalar2=math.pi / 2,
                                        op0=ALU.mult, op1=ALU.add))
            b.blk(dve=sc2)

            def sc3(ew):
                ew.c(ew.e.activation(out=T["mg"][:], in_=T["ar"][:], func=AF.Exp, scale=1.0 / 16.0))
                ew.c(ew.e.activation(out=T["s0"][:], in_=T["ai"][:], func=AF.Sin, scale=1.0 / 16.0))
                ew.c(ew.e.activation(out=T["c0"][:], in_=T["t1"][:], func=AF.Sin))
                ew.c(ew.e.activation(out=T["t2"][:], in_=T["ar"][:], func=AF.Exp, scale=-1.0 / 16.0))
            b.blk(act=sc3)

            def cmul(ew, o_re, o_im, a_re, a_im, b_re, b_im, t1, t2):
                ew.c(ew.e.tensor_tensor(out=t1, in0=a_re, in1=b_re, op=ALU.mult))
                ew.c(ew.e.tensor_tensor(out=t2, in0=a_im, in1=b_im, op=ALU.mult))
                ew.c(ew.e.tensor_tensor(out=t2, in0=t1, in1=t2, op=ALU.subtract))
                ew.c(ew.e.tensor_tensor(out=t1, in0=a_re, in1=b_im, op=ALU.mult))
                ew.c(ew.e.tensor_tensor(out=o_im, in0=a_im, in1=b_re, op=ALU.mult))
                ew.c(ew.e.tensor_tensor(out=o_im, in0=o_im, in1=t1, op=ALU.add))
                ew.c(ew.e.tensor_copy(o_re, t2))

            def sc4(ew):
                t1, t2, t3 = T["t1"][:], T["t3"][:], T["den"][:]
                ew.c(ew.e.tensor_tensor(out=PW[:, 0, 1, :], in0=T["mg"][:], in1=T["c0"][:], op=ALU.mult))
                ew.c(ew.e.tensor_tensor(out=PW[:, 1, 1, :], in0=T["mg"][:], in1=T["s0"][:], op=ALU.mult))
                ew.c(ew.e.tensor_tensor(out=PI[:, 0, 1, :], in0=T["t2"][:], in1=T["c0"][:], op=ALU.mult))
                ew.c(ew.e.scalar_tensor_tensor(out=PI[:, 1, 1, :], in0=T["t2"][:], scalar=-1.0, in1=T["s0"][:],
                                               op0=ALU.mult, op1=ALU.mult))
                for _ in range(4):
                    cmul(ew, PW[:, 0, 1, :], PW[:, 1, 1, :], PW[:, 0, 1, :], PW[:, 1, 1, :],
                         PW[:, 0, 1, :], PW[:, 1, 1, :], t1, t2)
                    cmul(ew, PI[:, 0, 1, :], PI[:, 1, 1, :], PI[:, 0, 1, :], PI[:, 1, 1, :],
                         PI[:, 0, 1, :], PI[:, 1, 1, :], t1, t2)
                ew.c(ew.e.memset(PW[:, 0, 0, :], 1.0))
                ew.c(ew.e.memset(PW[:, 1, 0, :], 0.0))
                ew.c(ew.e.memset(PI[:, 0, 0, :], 1.0))
                ew.c(ew.e.memset(PI[:, 1, 0, :], 0.0))
                for k in range(2, 9):
                    cmul(ew, PW[:, 0, k, :], PW[:, 1, k, :], PW[:, 0, k - 1, :], PW[:, 1, k - 1, :],
                         PW[:, 0, 1, :], PW[:, 1, 1, :], t1, t2)
                for k in range(2, 8):
                    cmul(ew, PI[:, 0, k, :], PI[:, 1, k, :], PI[:, 0, k - 1, :], PI[:, 1, k - 1, :],
                         PI[:, 0, 1, :], PI[:, 1, 1, :], t1, t2)
                ew.c(ew.e.tensor_copy(T["are64"][:], PW[:, 0, 8, :]))
                ew.c(ew.e.tensor_copy(T["aim64"][:], PW[:, 1, 8, :]))
                for _ in range(3):
                    cmul(ew, T["are64"][:], T["aim64"][:], T["are64"][:], T["aim64"][:],
                         T["are64"][:], T["aim64"][:], t1, t2)
                nr = T["c0"][:]
                ew.c(ew.e.tensor_scalar_add(nr, PW[:, 0, 1, :], -1.0))
                ew.c(ew.e.tensor_tensor(out=t1, in0=are[:], in1=are[:], op=ALU.mult))
                ew.c(ew.e.tensor_tensor(out=t2, in0=aim[:], in1=aim[:], op=ALU.mult))
                ew.c(ew.e.tensor_tensor(out=t3, in0=t1, in1=t2, op=ALU.add))
                ew.c(ew.e.reciprocal(t3, t3))
                ew.c(ew.e.tensor_tensor(out=t1, in0=nr, in1=are[:], op=ALU.mult))
                ew.c(ew.e.tensor_tensor(out=t2, in0=PW[:, 1, 1, :], in1=aim[:], op=ALU.mult))
                ew.c(ew.e.tensor_tensor(out=t1, in0=t1, in1=t2, op=ALU.add))
                ew.c(ew.e.tensor_tensor(out=T["fre"][:], in0=t1, in1=t3, op=ALU.mult))
                ew.c(ew.e.tensor_tensor(out=t1, in0=PW[:, 1, 1, :], in1=are[:], op=ALU.mult))
                ew.c(ew.e.tensor_tensor(out=t2, in0=nr, in1=aim[:], op=ALU.mult))
                ew.c(ew.e.tensor_tensor(out=t1, in0=t1, in1=t2, op=ALU.subtract))
                ew.c(ew.e.tensor_tensor(out=T["fim"][:], in0=t1, in1=t3, op=ALU.mult))
                ew.c(ew.e.tensor_scalar(out=T["fims"][:], in0=T["fim"][:], scalar1=sg[:, 0:1], scalar2=None,
                                        op0=ALU.mult))
                f_re = bc(T["fre"][:].unsqueeze(2), [128, NP, 16])
                f_ims = bc(T["fims"][:].unsqueeze(2), [128, NP, 16])
                ta = tA[:, :, 0, :]
                ew.c(ew.e.tensor_tensor(out=fBP[:], in0=BP[:], in1=f_re, op=ALU.mult))
                ew.c(ew.e.tensor_tensor(out=ta, in0=BQ[:], in1=f_ims, op=ALU.mult))
                ew.c(ew.e.tensor_tensor(out=fBP[:], in0=fBP[:], in1=ta, op=ALU.add))
                ew.c(ew.e.tensor_tensor(out=fBQ[:], in0=BQ[:], in1=f_re, op=ALU.mult))
                ew.c(ew.e.tensor_tensor(out=ta, in0=BP[:], in1=f_ims, op=ALU.mult))
                ew.c(ew.e.tensor_tensor(out=fBQ[:], in0=fBQ[:], in1=ta, op=ALU.subtract))

                def ctab(out, P_, Q_, pw_re_fn, pw_im_fn, a_sgn_col, b_sgn_col, b_neg=False):
                    for j in range(8):
                        if a_sgn_col is None:
                            ew.c(ew.e.tensor_copy(AL[:, :, j], pw_re_fn(j)))
                        else:
                            ew.c(ew.e.tensor_scalar(out=AL[:, :, j], in0=pw_re_fn(j), scalar1=sg[:, a_sgn_col:a_sgn_col + 1],
                                                    scalar2=None, op0=ALU.mult))
                        if b_sgn_col is None:
                            ew.c(ew.e.tensor_scalar(out=BE[:, :, j], in0=pw_im_fn(j), scalar1=(-1.0 if b_neg else 1.0),
                                                    scalar2=None, op0=ALU.mult))
                        else:
                            ew.c(ew.e.tensor_scalar(out=BE[:, :, j], in0=pw_im_fn(j), scalar1=sg[:, b_sgn_col:b_sgn_col + 1],
                                                    scalar2=None, op0=ALU.mult))
                    sh = [128, NP, 8, 16]
                    ew.c(ew.e.tensor_tensor(out=tA[:], in0=bc(P_.unsqueeze(2), sh), in1=bc(AL[:].unsqueeze(3), sh), op=ALU.mult))
                    ew.c(ew.e.tensor_tensor(out=tB[:], in0=bc(Q_.unsqueeze(2), sh), in1=bc(BE[:].unsqueeze(3), sh), op=ALU.mult))
                    ew.c(ew.e.tensor_tensor(out=out, in0=tA[:], in1=tB[:], op=ALU.add))
                ctab(Zst[:], fBP[:], fBQ[:], lambda j: PW[:, 0, 7 - j, :], lambda j: PW[:, 1, 7 - j, :], None, 0)
                ctab(W2[:], CP[:], CQ[:], lambda j: PW[:, 0, j + 1, :], lambda j: PW[:, 1, j + 1, :], 1, None, True)
                ctab(E0[:], fBP[:], fBQ[:], lambda j: PI[:, 0, j, :], lambda j: PI[:, 1, j, :], 1, None, True)
                ctab(CT0[:], CP[:], CQ[:], lambda j: PW[:, 0, j, :], lambda j: PW[:, 1, j, :], None, 0)
                ew.c(ew.e.tensor_copy(T["a8re"][:], PW[:, 0, 8, :]))
                ew.c(ew.e.tensor_scalar(out=T["a8ims"][:], in0=PW[:, 1, 8, :], scalar1=sg[:, 1:2], scalar2=None,
                                        op0=ALU.mult))
                for p_ in range(NP):
                    ew.c(ew.e.tensor_scalar(out=tA[:, 0, :, :].rearrange("p a b -> p (a b)"), in0=Jm[:],
                                            scalar1=T["a8ims"][:, p_:p_ + 1], scalar2=None, op0=ALU.mult))
                    ew.c(ew.e.scalar_tensor_tensor(out=R8[:, p_, :], in0=ident[:], scalar=T["a8re"][:, p_:p_ + 1],
                                                   in1=tA[:, 0, :, :].rearrange("p a b -> p (a b)"),
                                                   op0=ALU.mult, op1=ALU.add))
                for d in range(2):
                    for hf in range(2):
                        o = A2[:, d * 16 + hf * 8:d * 16 + hf * 8 + 8]
                        ew.c(ew.e.tensor_copy(o, T["are64"][:, d * 8:(d + 1) * 8]))
                        ew.c(ew.e.tensor_scalar(out=B2[:, d * 16 + hf * 8:d * 16 + hf * 8 + 8],
                                                in0=T["aim64"][:, d * 8:(d + 1) * 8],
                                                scalar1=(1.0 if hf == 0 else -1.0), scalar2=None, op0=ALU.mult))
            pew = _ProxyEW()
            sc4(pew)
            CH = 10
            for c0_ in range(0, len(pew.ops), CH):
                chunk = pew.ops[c0_:c0_ + CH]
                b.blk(dve=lambda ew, chunk=chunk: [ew.c(getattr(ew.e, n_)(*a_, **k_)) for (n_, a_, k_) in chunk])

            for half in range(4):
                prs = range(half * 4, half * 4 + 4)

                def tp(ew, prs=prs):
                    for q, p_ in enumerate(prs):
                        ew.tr(ps[6][:, q * 128:(q + 1) * 128], Zst[:, p_, :, :].rearrange("p a b -> p (a b)"), ident[:])
                        ew.mm(ps[7][:, q * 128:(q + 1) * 128], E0[:, p_, :, :].rearrange("p a b -> p (a b)"),
                              CT0[:, p_, :, :].rearrange("p a b -> p (a b)"), start=True, stop=True)
                b.blk(pe=tp)

                def tpe(ew, prs=prs):
                    for q, p_ in enumerate(prs):
                        ew.c(ew.e.tensor_copy(W1[:, p_, :], ps[6][:, q * 128:(q + 1) * 128]))
                        ew.c(ew.e.tensor_tensor(out=tA[:, 0, :, :].rearrange("p a b -> p (a b)"),
                                                in0=ps[7][:, q * 128:(q + 1) * 128], in1=MLm[:], op=ALU.mult))
                        if p_ < 8:
                            ew.c(ew.e.scalar_tensor_tensor(out=TP[:, p_, :], in0=ident[:], scalar=Dp[:, p_:p_ + 1],
                                                           in1=tA[:, 0, :, :].rearrange("p a b -> p (a b)"),
                                                           op0=ALU.mult, op1=ALU.add))
                        else:
                            ew.c(ew.e.tensor_copy(TP[:, p_, :], tA[:, 0, :, :].rearrange("p a b -> p (a b)")))
                b.blk(dve=tpe)


        def main(qb, par):
            U2, W1, W2, TP, R8, A2, B2 = U2_s[par], W1_s[par], W2_s[par], TP_s[par], R8_s[par], A2_s[par], B2_s[par]
            def tiles(p_, r):
                off = r if p_ < 8 else 7 - r
                return slice(off, NTI, 8)

            def sweep(down):
                cur, nxt = Sa, Sb
                for r in range(8 if not down else 7):
                    def pe_(ew, r=r, cur=cur):
                        for p_ in range(NP):
                            o = ps[p_ // 7][:, (p_ % 7) * NCH:(p_ % 7 + 1) * NCH]
                            first = True
                            if down:
                                ew.mm(o, R8[:, p_, :], St[:, p_, tiles(p_, r)], start=True, stop=False)
                                first = False
                            elif r > 0:
                                ew.mm(o, R8[:, p_, :], cur[:, p_, :], start=True, stop=False)
                                first = False
                            ew.mm(o, W1[:, p_, :], U2[:, p_, tiles(p_, r)], start=first, stop=True)
                    b.blk(pe=pe_)

                    def ev(ew, r=r, nxt=nxt):
                        for bk in range(3):
                            n_p = 7 if bk < 2 else 2
                            prs = slice(bk * 7, bk * 7 + n_p)
                            src = ps[bk][:, 0:n_p * NCH].rearrange("p (a m) -> p a m", m=NCH)
                            if down:
                                lo, hi = bk * 7, bk * 7 + n_p
                                for (a, e_) in ((lo, min(hi, 8)), (max(lo, 8), hi)):
                                    if e_ > a:
                                        ew.c(ew.e.activation(
                                            out=St[:, a:e_, tiles(a, r + 1)],
                                            in_=ps[bk][:, (a - lo) * NCH:(e_ - lo) * NCH].rearrange("p (a m) -> p a m", m=NCH), func=AF.Copy))
                            elif r < 7:
                                ew.c(ew.e.activation(out=nxt[:, prs, :], in_=src, func=AF.Copy))
                            else:
                                ew.c(ew.e.activation(out=Fm[:, prs, :], in_=src, func=AF.Copy))
                    b.blk(act=ev)
                    cur, nxt = nxt, cur

            sweep(False)

            def swp(ew):
                for bk in range(3):
                    n_p = 7 if bk < 2 else 2
                    ew.mm(ps[bk][:, 0:n_p * NCH], SWm[:], Fm[:, bk * 7:bk * 7 + n_p, :].rearrange("p a m -> p (a m)"),
                          start=True, stop=True)
            b.blk(pe=swp)

            def swe(ew):
                for bk in range(3):
                    n_p = 7 if bk < 2 else 2
                    ew.c(ew.e.tensor_copy(Fs[:, bk * 7:bk * 7 + n_p, :],
                                          ps[bk][:, 0:n_p * NCH].rearrange("p (a m) -> p a m", m=NCH)))
                ew.c(ew.e.memset(V2[:], 0.0))
                for (src, slot) in ((Fm, 0), (Fs, 1)):
                    ew.c(ew.e.tensor_copy(F2[:, :, slot * 8:(slot + 1) * 8], src[:, 0:8, :].rearrange("p j m -> p m j")))
                    ew.c(ew.e.tensor_copy(F2[:, 0:4, 16 + slot * 8:24 + slot * 8],
                                          mkap(src, 8 * NCH + 3, [[-1, 4], [NCH, 8]])))
                    ew.c(ew.e.tensor_copy(F2[:, 4:NCH, 16 + slot * 8:24 + slot * 8],
                                          mkap(src, 8 * NCH + NCH - 1, [[-1, NCH - 4], [NCH, 8]])))
            b.blk(dve=swe)

            def l2(ew):
                t = ew.e.tensor_tensor
                ta = l2t[0][:].rearrange("p a b -> p (a b)")
                tb = l2t[1][:].rearrange("p a b -> p (a b)")
                for k in range(NCH - 1):
                    vsw = mkap(V2, k * 32 + 8, [[16, 2], [-8, 2], [1, 8]])
                    t(out=ta, in0=A2[:], in1=V2[:, k, :], op=ALU.mult)
                    ew.c(t(out=tb.rearrange("p (d h j) -> p d h j", d=2, h=2), in0=B2[:].rearrange("p (d h j) -> p d h j", d=2, h=2),
                           in1=vsw, op=ALU.mult))
                    ew.c(t(out=ta, in0=ta, in1=tb, op=ALU.add))
                    ew.c(t(out=V2[:, k + 1, :], in0=ta, in1=F2[:, k, :], op=ALU.add))
            b.blk(dve=l2)

            def dinit(ew):
                ew.c(ew.e.tensor_copy(St[:, 0:8, slice(0, NTI, 8)], V2[:, :, 0:8].rearrange("p k j -> p j k")))
                ew.c(ew.e.tensor_copy(St[:, 8:16, slice(7, 7 + 8 * 4, 8)], mkap(V2, 3 * 32 + 16, [[1, 8], [-32, 4]])))
                ew.c(ew.e.tensor_copy(St[:, 8:16, slice(7 + 8 * 4, NTI, 8)],
                                      mkap(V2, (NCH - 1) * 32 + 16, [[1, 8], [-32, NCH - 4]])))
            b.blk(dve=dinit)
            sweep(True)

            ytasks = []
            for p_ in range(NP):
                def ype(ew, p_=p_):
                    for cb_, (c0_, c1_) in enumerate(((0, 512), (512, NTI))):
                        o = ps[(p_ % 2) * 2 + cb_][:, 0:c1_ - c0_]
                        ew.mm(o, TP[:, p_, :], U2[:, p_, c0_:c1_], start=True, stop=False)
                        ew.mm(o, W2[:, p_, :, :].rearrange("p a b -> p (a b)"), St[:, p_, c0_:c1_], start=False, stop=True)

                def yev(ew, p_=p_):
                    ew.c(ew.e.activation(out=Yo[:, p_, 0:512], in_=ps[(p_ % 2) * 2][:, 0:512], func=AF.Copy), wait=False)
                    ew.c(ew.e.activation(out=Yo[:, p_, 512:NTI], in_=ps[(p_ % 2) * 2 + 1][:, 0:NTI - 512], func=AF.Copy))
                ytasks.append([{"pe": ype}, {"act": yev}])
            prev = list(ytok)
            del ytok[:]
            if prev:
                b.blk(sp=lambda ew: [ew.wait_tok(t_) for t_ in prev])
            b.pipeline(ytasks)

            def yout_(ew):
                for t in range(8):
                    ew.dma(yd[0, t, qb * 128:(qb + 1) * 128, :].rearrange("(g q) i -> q g i", q=16),
                           Yo[t * 16:(t + 1) * 16, 0:8, :])
                    ew.dma(yd[1, 7 - t, qb * 128:(qb + 1) * 128, :].rearrange("(g q) i -> q g i", q=16),
                           Yo[t * 16:(t + 1) * 16, 8:16, :])
            b.blk(sp=lambda ew: yout_(_AsyncDma(ew, ytok)))


        b.rec = []
        prep(0, 0)
        steps = b.rec
        b.rec = None
        b.emit_zipped(steps, [])
        for qb in range(4):
            b.rec = []
            main(qb, qb % 2)
            sm = b.rec
            b.rec = []
            if qb < 3:
                prep(qb + 1, (qb + 1) % 2)
            sp_ = b.rec
            b.rec = None
            b.emit_zipped(sm, sp_)
        b.blk(sp=lambda ew: [ew.wait_tok(t_) for t_ in ytok])


def lru_phase(nc, b, l, ps, xr_d, gg_d, yl_d, cw, cb, lruW, lb_rg, lb_ig, llam):
    with ExitStack() as st:
        sbt = lambda n, s, d: st.enter_context(nc.sbuf_tensor(uniq(n), list(s), d))
        cwt = sbt("cwt", [128, 4, 4], F32)
        cbt = sbt("cbt", [128, 4], F32)
        brg = sbt("brg", [128, 2, 4], F32)
        big = sbt("big", [128, 2, 4], F32)
        lam = sbt("lam", [128, 2, 4], F32)
        nsp = sbt("nsp", [128, 2, 4], F32)
        nsp2 = sbt("nsp2", [128, 2, 4], F32)
        Wg = sbt("Wg", [128, 2, 2, 4, 128], BF)
        xp = sbt("xp", [128, 3 + NCTX + 3 + NLAT + 3], F32)
        xc = sbt("xc", [128, NT], F32)
        xcb = sbt("xcb", [128, NT], BF)
        R = [xp, sbt("R1", [128, NT], F32)]
        I = [sbt(f"I{d}", [128, NT], F32) for d in range(2)]
        H = [sbt(f"H{d}", [128, NT], F32) for d in range(2)]
        xraw = H[0]
        ggt = sbt("ggt", [128, NT], BF)
        ylo = xcb
        hc = sbt("hc", [128, 2], F32)
        CO = 3
        LO = 3 + NCTX + 3

        def ld(ew):
            ew.dma(cwt[:], cw[:, l, :, :])
            ew.dma(cbt[:], cb[:, l, :])
            ew.dma(brg[:], lb_rg[:, l, :, :])
            ew.dma(big[:], lb_ig[:, l, :, :])
            ew.dma(lam[:], llam[:, l, :, :])

        def ldw(ew):
            for d in range(2):
                for g in range(2):
                    ew.dma(Wg[:, d, g, :, :], lruW[l, d, g].rearrange("c i o -> i c o"))
        b.blk(sp=ld, pool=ldw)

        b.blk(act=lambda ew: ew.c(ew.e.activation(out=nsp[:], in_=lam[:], func=AF.Exp, scale=-1.0)))
        b.blk(act=lambda ew: ew.c(ew.e.activation(out=nsp[:], in_=nsp[:], func=AF.Ln, bias=1.0)))

        def nsp_(ew):
            ew.c(ew.e.tensor_scalar(out=nsp2[:], in0=nsp[:], scalar1=-16.0, scalar2=None, op0=ALU.mult))
            ew.c(ew.e.tensor_scalar(out=nsp[:], in0=nsp[:], scalar1=-8.0, scalar2=None, op0=ALU.mult))
        b.blk(dve=nsp_)

        for cc in range(4):
            rows = slice(cc * 128, (cc + 1) * 128)

            def l0(ew, rows=rows):
                ew.dma(xraw[:], xr_d[rows, :])
                ew.dma(ggt[:], gg_d[rows, :])
            b.blk(sp=l0)

            def l1(ew):
                ew.c(ew.e.memset(xp[:], 0.0))
                ew.c(ew.e.tensor_copy(xp[:, CO:CO + NCTX], xraw[:, 0:NCTX]))
                ew.c(ew.e.tensor_copy(xp[:, LO:LO + NLAT].rearrange("p (c r) -> p c r", r=64),
                                      xraw[:, NCTX:NT].rearrange("p (r c) -> p c r", c=64)))
            b.blk(dve=l1)

            def l2_(ew, cc=cc):
                for (o0, n, base) in ((0, NCTX, CO), (NCTX, NLAT, LO)):
                    ew.c(ew.e.tensor_scalar(out=xc[:, o0:o0 + n], in0=xp[:, base - 2:base - 2 + n],
                                            scalar1=cwt[:, 0, cc:cc + 1], scalar2=cbt[:, cc:cc + 1],
                                            op0=ALU.mult, op1=ALU.add))
                    for k in range(1, 4):
                        ew.c(ew.e.scalar_tensor_tensor(out=xc[:, o0:o0 + n], in0=xp[:, base - 2 + k:base - 2 + k + n],
                                                       scalar=cwt[:, k, cc:cc + 1], in1=xc[:, o0:o0 + n],
                                                       op0=ALU.mult, op1=ALU.add))
                ew.c(ew.e.tensor_copy(xcb[:], xc[:]))
            b.blk(dve=l2_)

            blocks = [(i * 512, min(512, NT - i * 512)) for i in range(9)]
            gtasks = []
            for bi, (c0_, n) in enumerate(blocks):
                pb = (bi % 2) * 4

                def gpe(ew, c0_=c0_, n=n, cc=cc, pb=pb):
                    for d in range(2):
                        for g in range(2):
                            ew.mm(ps[pb + d * 2 + g][:, 0:n], Wg[:, d, g, cc, :], xcb[:, c0_:c0_ + n], start=True, stop=True)

                def gev(ew, c0_=c0_, n=n, cc=cc, pb=pb):
                    for d in range(2):
                        ew.c(ew.e.activation(out=R[d][:, c0_:c0_ + n], in_=ps[pb + d * 2][:, 0:n], func=AF.Sigmoid,
                                             bias=brg[:, d, cc:cc + 1]), wait=False)
                        ew.c(ew.e.activation(out=I[d][:, c0_:c0_ + n], in_=ps[pb + d * 2 + 1][:, 0:n], func=AF.Sigmoid,
                                             bias=big[:, d, cc:cc + 1]), wait=(d == 1))
                gtasks.append([{"pe": gpe}, {"act": gev}])
            b.pipeline(gtasks)

            def e1(ew, cc=cc):
                for d in range(2):
                    ew.c(ew.e.activation(out=H[d][:], in_=R[d][:, 0:NT], func=AF.Exp, scale=nsp2[:, d, cc:cc + 1]))
                    ew.c(ew.e.activation(out=R[d][:, 0:NT], in_=R[d][:, 0:NT], func=AF.Exp, scale=nsp[:, d, cc:cc + 1]))

            def e1d(ew):
                for d in range(2):
                    ew.c(ew.e.tensor_tensor(out=I[d][:], in0=I[d][:], in1=xc[:], op=ALU.mult))
            b.blk(act=e1, dve=e1d)

            def e2(ew):
                for d in range(2):
                    ew.c(ew.e.activation(out=H[d][:], in_=H[d][:], func=AF.Sqrt, scale=-1.0, bias=1.0))
            b.blk(act=e2)

            def e3(ew):
                for d in range(2):
                    ew.c(ew.e.tensor_tensor(out=I[d][:], in0=I[d][:], in1=H[d][:], op=ALU.mult))
            b.blk(dve=e3)

            def rv(ap2d, o0, n):
                full = ap2d[:, o0:o0 + n]
                return bass.AP(full.tensor, full.offset + (n - 1), [list(full.ap[0]), [-1, n]])

            def sc(ew):
                ew.c(ew.e.tensor_tensor_scan(out=H[0][:, 0:NCTX], data0=R[0][:, 0:NCTX], data1=I[0][:, 0:NCTX],
                                             initial=0.0, op0=ALU.mult, op1=ALU.add))
                ew.c(ew.e.tensor_copy(hc[:, 0:1], H[0][:, NCTX - 1:NCTX]))
                ew.c(ew.e.tensor_tensor_scan(out=H[0][:, NCTX:NT], data0=R[0][:, NCTX:NT], data1=I[0][:, NCTX:NT],
                                             initial=hc[:, 0:1], op0=ALU.mult, op1=ALU.add))
                ew.c(ew.e.tensor_tensor_scan(out=rv(H[1], 0, NCTX), data0=rv(R[1], 0, NCTX), data1=rv(I[1], 0, NCTX),
                                             initial=0.0, op0=ALU.mult, op1=ALU.add))
                ew.c(ew.e.tensor_copy(hc[:, 1:2], H[1][:, 0:1]))
                ew.c(ew.e.tensor_tensor_scan(out=rv(H[1], NCTX, NLAT), data0=rv(R[1], NCTX, NLAT),
                                             data1=rv(I[1], NCTX, NLAT), initial=hc[:, 1:2], op0=ALU.mult, op1=ALU.add))
                ew.c(ew.e.tensor_tensor(out=H[0][:], in0=H[0][:], in1=H[1][:], op=ALU.add))
                ew.c(ew.e.tensor_tensor(out=ylo[:, 0:NCTX], in0=H[0][:, 0:NCTX], in1=ggt[:, 0:NCTX], op=ALU.mult))
                ew.c(ew.e.tensor_tensor(out=ylo[:, NCTX:NT].rearrange("p (r c) -> p r c", c=64),
                                        in0=H[0][:, NCTX:NT].rearrange("p (c r) -> p r c", r=64),
                                        in1=ggt[:, NCTX:NT].rearrange("p (r c) -> p r c", c=64), op=ALU.mult))
            b.blk(dve=sc)
            b.blk(sp=lambda ew, rows=rows: ew.dma(yl_d[rows, :], ylo[:]))


_NC_CACHE = {}


def _host_inputs(inp, bidx):
    f = np.float32
    x = np.asarray(inp["x"], f)
    ctx = np.asarray(inp["ctx"], f)
    d = {}
    d["xs"] = np.ascontiguousarray(np.concatenate([ctx[bidx].T, x[bidx].T], axis=1))
    cv = np.stack([np.asarray(inp["c"], f)[bidx], np.asarray(inp["c_ctx"], f)], axis=-1)
    d["cvec"] = np.ascontiguousarray(cv.reshape(8, 128, 2).transpose(1, 0, 2))
    return d


def _shared_inputs(inp):
    f = np.float32
    g = lambda k: np.asarray(inp[k], f)
    d = {}
    d["w_ada"] = g("w_ada")
    d["b_ada"] = np.ascontiguousarray(g("b_ada").reshape(NL, 48, 128).transpose(2, 0, 1))
    d["gains"] = np.ascontiguousarray(g("norm_gains").reshape(NL, 4, 8, 128).transpose(3, 0, 1, 2))
    d["w_in"] = g("w_in")
    d["w_out"] = g("w_out")
    d["w_ffi"] = g("w_ffn_in")
    d["w_ffo"] = g("w_ffn_out")
    d["w_glu"] = g("s5_w_glu")
    d["b_glu"] = np.ascontiguousarray(g("s5_b_glu").reshape(NL, 4, 128).transpose(2, 0, 1))

    def st2(a):
        a = np.concatenate([a, a], axis=-1)
        a = a.reshape(NL, 2, 2, 16, 128).reshape(NL, 4, 16, 128)
        return np.ascontiguousarray(a.transpose(3, 0, 1, 2))
    d["sa_re"] = st2(g("s5_a_re"))
    d["sa_im"] = st2(g("s5_a_im"))
    d["sldt"] = st2(np.broadcast_to(g("s5_log_dt")[..., None], (NL, 2, 32, 64)))

    def st3(top, bot):
        a = np.concatenate([top, bot], axis=3)
        a = a.reshape(NL, 4, 16, 128, a.shape[-1])
        return np.ascontiguousarray(a.transpose(3, 0, 1, 2, 4))
    bre, bim = g("s5_b_re"), g("s5_b_im")
    d["sBP"] = st3(bre, bim)
    d["sBQ"] = st3(bim, bre)
    cre = g("s5_c_re").transpose(0, 1, 2, 4, 3)
    cim = g("s5_c_im").transpose(0, 1, 2, 4, 3)
    d["sCP"] = st3(cre, cim)
    d["sCQ"] = st3(cim, cre)
    sd = g("s5_d").reshape(NL, 4, 8, 16)
    sd = np.broadcast_to(sd[:, :, :, None, :], (NL, 4, 8, 8, 16))
    d["sD"] = np.ascontiguousarray(sd.transpose(3, 4, 0, 1, 2).reshape(128, NL, 4, 8))
    d["cw"] = np.ascontiguousarray(g("lru_conv_w").reshape(NL, 4, 4, 128).transpose(3, 0, 1, 2))
    d["cb"] = np.ascontiguousarray(g("lru_conv_b").reshape(NL, 4, 128).transpose(2, 0, 1))
    W = np.zeros((NL, 2, 2, 4, 128, 128), f)
    for gi, key in enumerate(("lru_w_rg", "lru_w_ig")):
        w = g(key)
        for cc in range(4):
            W[:, :, gi, cc, 0:64, 0:64] = w[:, :, 2 * cc]
            W[:, :, gi, cc, 64:128, 64:128] = w[:, :, 2 * cc + 1]
    d["lruW"] = W
    v = lambda k: np.ascontiguousarray(g(k).reshape(NL, 2, 4, 128).transpose(3, 0, 1, 2))
    d["lb_rg"] = v("lru_b_rg")
    d["lb_ig"] = v("lru_b_ig")
    d["llam"] = v("lru_lambda")
    I = np.eye(128, dtype=f)
    d["cI"] = I
    J = np.zeros((128, 128), f)
    SW = np.zeros((128, 128), f)
    for n in range(64):
        J[n, 64 + n] = 1.0
        J[64 + n, n] = 1.0
        SW[64 + n, n] = -1.0
        SW[n, 64 + n] = 1.0
    d["cJ"] = J
    d["cSW"] = SW
    sidx = np.arange(128) // 16
    d["cML"] = (sidx[None, :] >= sidx[:, None]).astype(f)
    sgn = np.where(np.arange(128) < 64, -1.0, 1.0).astype(f)
    d["cSG"] = np.stack([sgn, -sgn, np.ones(128, f), np.full(128, 1.0 / 1024, f)], axis=1)
    return d


def kernel(**inputs):
    if "nc" not in _NC_CACHE:
        _NC_CACHE["nc"] = build_program()
    nc = _NC_CACHE["nc"]
    shared = _shared_inputs(inputs)
    in_maps = []
    for bidx in range(8):
        m = dict(shared)
        m.update(_host_inputs(inputs, bidx))
        in_maps.append(m)
    res = run_bass_kernel_spmd(nc, in_maps, core_ids=list(range(8)))
    out = np.stack([np.asarray(r["yout"], np.float32).T for r in res.results], axis=0)
    return np.ascontiguousarray(out)
```

```python
import math
from contextlib import ExitStack
import numpy as np
import concourse.bass as bass
import concourse.mybir as mybir
from concourse.bass_utils import run_bass_kernel_spmd

F32 = mybir.dt.float32
BF = mybir.dt.bfloat16
AF = mybir.ActivationFunctionType
ALU = mybir.AluOpType

NL = 4
DM = 1024
NCTX = 256
NLAT = 4096
NT = NCTX + NLAT
NTI = NT // 8
NCH = NTI // 8
TN = 128
NTT = NT // TN
DFF = 2816
EPS = 1e-6
DEBUG = False
NL_RUN = 4
STOP_AFTER = 99


class EW:
    def __init__(self, b, name):
        self.b = b
        self.name = name
        self.sem = b.newsem()
        self.n = 0
        self.dsem = b.newsem()
        self.dn = 0
        self.e = None
        self.last = None

    def c(self, ins, wait=True):
        if self.n >= 6000 and not getattr(self, "pending", False):
            self.sem = self.b.newsem()
            self.n = 0
        self.n += 1
        ins.then_inc(self.sem, 1)
        if wait:
            self.e.wait_ge(self.sem, self.n)
        self.pending = not wait
        return ins

    def mm(self, *a, **k):
        self.last = self.e.matmul(*a, **k)
        return self.last

    def tr(self, *a, **k):
        self.last = self.e.transpose(*a, **k)
        return self.last

    def dma(self, out, in_):
        if self.dn >= 550:
            self.dsem = self.b.newsem()
            self.dn = 0
        self.dn += 1
        self.e.dma_start(out=out, in_=in_).then_inc(self.dsem, 16)

    def dma_async(self, out, in_, track=True):
        b = self.b
        i = b.ai % len(b.asems)
        b.ai += 1
        sem = b.asems[i]
        if b.acnt[i]:
            self.e.wait_ge(sem, b.acnt[i])
        b.acnt[i] += 16
        self.e.dma_start(out=out, in_=in_).then_inc(sem, 16)
        tok = (sem, b.acnt[i])
        if track:
            b.outstanding.append(tok)
        return tok

    def wait_tok(self, tok):
        self.e.wait_ge(tok[0], tok[1])

    def fin(self):
        if getattr(self, "pending", False):
            self.e.wait_ge(self.sem, self.n)
            self.pending = False
        if self.last is not None:
            self.c(self.last)
            self.last = None
        if self.dn:
            self.e.wait_ge(self.dsem, 16 * self.dn)


class _ProxyEng:
    def __getattr__(self, name):
        return lambda *a, **k: (name, a, k)


class _ProxyEW:
    def __init__(self):
        self.e = _ProxyEng()
        self.ops = []

    def c(self, rec, wait=True):
        self.ops.append((rec, wait))
        return rec


class _AsyncDma:
    def __init__(self, ew, toks):
        self.ew = ew
        self.toks = toks

    def dma(self, out, in_):
        self.toks.append(self.ew.dma_async(out, in_, track=False))


class Bld:
    def __init__(self, nc):
        self.nc = nc
        self.es = ExitStack()
        self.sems = [self.es.enter_context(nc.semaphore(f"sm{i}")) for i in range(96)]
        self.si = 0
        self.ew = {k: EW(self, k) for k in ("pe", "act", "dve", "pool", "sp")}
        self.nblk = 0
        self.rec = None
        self.asems = [self.newsem() for _ in range(24)]
        self.acnt = [0] * 24
        self.ai = 0
        self.outstanding = []

    def drain(self):
        toks, self.outstanding = self.outstanding, []
        if toks:
            self.blk(sp=lambda ew: [ew.wait_tok(t) for t in toks])

    def emit_zipped(self, a, bsteps):
        for i in range(max(len(a), len(bsteps))):
            fns = {}
            for lst in (a, bsteps):
                if i < len(lst):
                    for name, fn in lst[i].items():
                        if fn is not None:
                            fns.setdefault(name, []).append(fn)
            if fns:
                self.blk(**{n: (lambda ew, l=l: [f(ew) for f in l]) for n, l in fns.items()})

    def newsem(self):
        s = self.sems[self.si]
        self.si += 1
        return s

    def sb(self, name, shape, dt):
        return self.es.enter_context(self.nc.sbuf_tensor(name, list(shape), dt))

    def blk(self, **fns):
        m = {"pe": "tensor", "act": "scalar", "dve": "vector", "pool": "gpsimd", "sp": "sync"}
        if self.rec is not None:
            self.rec.append(dict(fns))
            return
        self.nblk += 1
        with self.nc.Block() as block:
            for name, fn in fns.items():
                if fn is None:
                    continue
                ew = self.ew[name]

                def run(e, ew=ew, fn=fn):
                    ew.e = e
                    fn(ew)
                    ew.fin()

                getattr(block, m[name])(run)

    def pipeline(self, tasks):
        nsteps = max((k + len(t) for k, t in enumerate(tasks)), default=0)
        for t in range(nsteps):
            fns = {}
            for k in range(max(0, t - 24), min(len(tasks), t + 1)):
                j = t - k
                if j < len(tasks[k]):
                    for name, fn in tasks[k][j].items():
                        fns.setdefault(name, []).append(fn)
            if not fns:
                continue
            self.blk(**{n: (lambda ew, l=l: [f(ew) for f in reversed(l)]) for n, l in fns.items()})


_UC = [0]


def uniq(n):
    _UC[0] += 1
    return f"{n}_{_UC[0]}"


def mkap(tile, off, dims):
    full = tile[:]
    return bass.AP(full.tensor, full.offset + off, [list(full.ap[0])] + [list(d) for d in dims])


def bc(ap, shape):
    return ap.to_broadcast(list(shape))


def build_program():
    nc = bass.Bass("TRN2", target_bir_lowering=False)
    b = Bld(nc)

    def din(name, shape, dt=F32):
        return nc.dram_tensor(name, list(shape), dt, kind="ExternalInput").ap()

    def dscr(name, shape, dt=F32):
        kind = "ExternalOutput"
        return nc.dram_tensor(name, list(shape), dt, kind=kind).ap()

    xs = din("xs", [DM, NT])
    cvec = din("cvec", [128, 8, 2])
    w_ada = din("w_ada", [NL, DM, 6 * DM])
    b_ada = din("b_ada", [128, NL, 48])
    gains = din("gains", [128, NL, 4, 8])
    w_in = din("w_in", [NL, DM, 1536])
    w_out = din("w_out", [NL, DM, DM])
    w_ffi = din("w_ffi", [NL, DM, 2 * DFF])
    w_ffo = din("w_ffo", [NL, DFF, DM])
    w_glu = din("w_glu", [NL, 512, 512])
    b_glu = din("b_glu", [128, NL, 4])
    sa_re = din("sa_re", [128, NL, 4, 16])
    sa_im = din("sa_im", [128, NL, 4, 16])
    sldt = din("sldt", [128, NL, 4, 16])
    sBP = din("sBP", [128, NL, 4, 16, 16])
    sBQ = din("sBQ", [128, NL, 4, 16, 16])
    sCP = din("sCP", [128, NL, 4, 16, 16])
    sCQ = din("sCQ", [128, NL, 4, 16, 16])
    sD = din("sD", [128, NL, 4, 8])
    cw = din("cw", [128, NL, 4, 4])
    cb = din("cb", [128, NL, 4])
    lruW = din("lruW", [NL, 2, 2, 4, 128, 128])
    lb_rg = din("lb_rg", [128, NL, 2, 4])
    lb_ig = din("lb_ig", [128, NL, 2, 4])
    llam = din("llam", [128, NL, 2, 4])
    cI = din("cI", [128, 128])
    cJ = din("cJ", [128, 128])
    cSW = din("cSW", [128, 128])
    cML = din("cML", [128, 128])
    cSG = din("cSG", [128, 4])
    yout = nc.dram_tensor("yout", [DM, NLAT], F32, kind="ExternalOutput").ap()

    xres = dscr("xres", [DM, NT])
    ud = dscr("ud", [8, 512, NTI], BF)
    yd = dscr("yd", [2, 8, 512, NTI], BF)
    xr_d = dscr("xr_d", [512, NT])
    gg_d = dscr("gg_d", [512, NT], BF)
    yl_d = dscr("yl_d", [512, NT], BF)
    yg_d = dscr("yg_d", [512, NT], BF)

    xres_v = xres.rearrange("(kc p) n -> p kc n", p=128)
    xs_v = xs.rearrange("(kc p) n -> p kc n", p=128)

    ident = b.sb("ident", [128, 128], F32)
    identb = b.sb("identb", [128, 128], BF)
    Jm = b.sb("Jm", [128, 128], F32)
    SWm = b.sb("SWm", [128, 128], F32)
    MLm = b.sb("MLm", [128, 128], F32)
    sg = b.sb("sg", [128, 4], F32)
    onesb = b.sb("onesb", [128, 128], BF)
    epsb = b.sb("epsb", [128, 1], F32)
    MOD = b.sb("MOD", [128, NL, 6, 8, 2], F32)
    GA = b.sb("GA", [128, NL, 4, 8], F32)
    DER = b.sb("DER", [128, NL, 4, 8, 2], F32)
    BADA = b.sb("BADA", [128, NL, 48], F32)
    ps = [b.es.enter_context(nc.psum_tensor(f"ps{i}", [128, 512], F32)) for i in range(8)]

    def c0(ew):
        ew.dma(ident[:], cI[:, :])
        ew.dma(Jm[:], cJ[:, :])
        ew.dma(SWm[:], cSW[:, :])
        ew.dma(MLm[:], cML[:, :])
        ew.dma(sg[:], cSG[:, :])
        ew.dma(GA[:], gains[:, :, :, :])
        ew.dma(BADA[:], b_ada[:, :, :])
    b.blk(sp=c0)

    def c1(ew):
        ew.c(ew.e.tensor_copy(identb[:], ident[:]))
        ew.c(ew.e.memset(onesb[:], 1.0 / 1024.0))
        ew.c(ew.e.memset(epsb[:], EPS))
    b.blk(dve=c1)

    def c2(ew):
        for kc in range(8):
            ew.dma(xres[kc * 128:(kc + 1) * 128, :], xs[kc * 128:(kc + 1) * 128, :])
    b.blk(sp=c2)

    with ExitStack() as st:
        cv = st.enter_context(nc.sbuf_tensor(uniq("cv"), [128, 8, 2], F32))
        scb = st.enter_context(nc.sbuf_tensor("scb", [128, 8, 2], BF))
        wad = [st.enter_context(nc.sbuf_tensor(f"wad{i}", [128, 8, 1024], BF)) for i in range(2)]

        b.blk(sp=lambda ew: ew.dma(cv[:], cvec[:, :, :]))
        b.blk(act=lambda ew: ew.c(ew.e.activation(out=scb[:], in_=cv[:], func=AF.Silu)))
        tasks = []
        for l in range(NL):
            for j in range(6):
                k = l * 6 + j
                wb = wad[k % 2]
                src = w_ada[l].rearrange("(kc p) n -> p kc n", p=128)[:, :, j * 1024:(j + 1) * 1024]

                def s_load(ew, wb=wb, src=src):
                    ew.dma(wb[:], src)

                def s_mm(ew, wb=wb, k=k):
                    for oc in range(8):
                        for kc in range(8):
                            ew.mm(ps[k % 2][:, oc * 2:oc * 2 + 2], wb[:, kc, oc * 128:(oc + 1) * 128],
                                  scb[:, kc, :], start=(kc == 0), stop=(kc == 7))

                def s_ev(ew, l=l, j=j, k=k):
                    ew.c(ew.e.tensor_tensor(
                        out=MOD[:, l, j, :, :],
                        in0=ps[k % 2][:, 0:16].rearrange("p (o w) -> p o w", w=2),
                        in1=bc(BADA[:, l, j * 8:(j + 1) * 8].unsqueeze(2), [128, 8, 2]),
                        op=ALU.add))
                tasks.append([{"pool": s_load}, {"pe": s_mm}, {"dve": s_ev}])
        b.pipeline(tasks)

        def derive(ew):
            for l in range(NL):
                for (dk, gk, mk, addone) in ((0, 0, 1, True), (1, 1, 2, False), (2, 2, 4, True), (3, 3, 5, False)):
                    g = bc(GA[:, l, gk, :].unsqueeze(2), [128, 8, 2])
                    if addone:
                        ew.c(ew.e.scalar_tensor_tensor(out=DER[:, l, dk, :, :], in0=MOD[:, l, mk, :, :],
                                                       scalar=1.0, in1=g, op0=ALU.add, op1=ALU.mult))
                    else:
                        ew.c(ew.e.tensor_tensor(out=DER[:, l, dk, :, :], in0=MOD[:, l, mk, :, :], in1=g,
                                                op=ALU.mult))
        b.blk(dve=derive)

    def norm_stages(l, ti, xt, sq, psS, rstd, tmp, h, which, tn, gap=0):
        w = 1 if ti * tn < NCTX else 0
        cols = slice(ti * tn, (ti + 1) * tn)
        dk = 0 if which == 0 else 2
        mk = 0 if which == 0 else 3

        tk = {}

        def s0(ew):
            if gap:
                tk["x"] = ew.dma_async(xt[:], xres_v[:, :, cols], track=False)
            else:
                ew.dma(xt[:], xres_v[:, :, cols])

        def s1(ew):
            if gap:
                ew.wait_tok(tk["x"])
            ew.c(ew.e.activation(out=sq[:], in_=xt[:], func=AF.Square))

        def s2(ew):
            for kc in range(8):
                ew.mm(psS, onesb[:], sq[:, kc, :], start=(kc == 0), stop=(kc == 7))

        def s2b(ew):
            ew.c(ew.e.activation(out=rstd[:], in_=psS, func=AF.Sqrt, bias=epsb[:, 0:1]))

        def s3(ew):
            ew.c(ew.e.reciprocal(rstd[:], rstd[:]))
            ew.c(ew.e.tensor_tensor(out=tmp[:], in0=xt[:], in1=bc(rstd[:].unsqueeze(1), [128, 8, tn]),
                                    op=ALU.mult))

        def s4(ew):
            for kc in range(8):
                ew.c(ew.e.activation(out=h[:, kc, :], in_=tmp[:, kc, :], func=AF.Identity,
                                     scale=DER[:, l, dk, kc, w:w + 1], bias=MOD[:, l, mk, kc, w:w + 1]), wait=(kc == 7))
        return [{"sp": s0}] + [{}] * gap + [{"act": s1}, {"pe": s2}, {"act": s2b}, {"dve": s3}, {"act": s4}]

    def post_stages(l, ti, banks, xt, osb, sq, psS, rstd, which, last_layer, tn, load_stage=0):
        w = 1 if ti * tn < NCTX else 0
        cols = slice(ti * tn, (ti + 1) * tn)
        dk = 1 if which == 0 else 3
        cpb = 512 // tn

        def s1(ew):
            for bi, bank in enumerate(banks):
                ew.c(ew.e.activation(out=osb[:, bi * cpb:(bi + 1) * cpb, :],
                                     in_=bank[:, 0:512].rearrange("p (c n) -> p c n", c=cpb), func=AF.Copy),
                     wait=(bi == len(banks) - 1))

        tk = {}

        def s1l(ew):
            tk["x"] = ew.dma_async(xt[:], xres_v[:, :, cols], track=False)

        def s1b(ew):
            ew.c(ew.e.activation(out=sq[:], in_=osb[:], func=AF.Square))

        def s2(ew):
            for kc in range(8):
                ew.mm(psS, onesb[:], sq[:, kc, :], start=(kc == 0), stop=(kc == 7))

        def s2b(ew):
            ew.c(ew.e.activation(out=rstd[:], in_=psS, func=AF.Sqrt, bias=epsb[:, 0:1]))

        def s3(ew):
            ew.c(ew.e.reciprocal(rstd[:], rstd[:]))
            for mc in range(8):
                ew.c(ew.e.scalar_tensor_tensor(out=osb[:, mc, :], in0=osb[:, mc, :],
                                               scalar=DER[:, l, dk, mc, w:w + 1], in1=rstd[:],
                                               op0=ALU.mult, op1=ALU.mult), wait=(mc == 7))

        def s4(ew):
            ew.wait_tok(tk["x"])
            ew.c(ew.e.tensor_tensor(out=xt[:], in0=xt[:], in1=osb[:], op=ALU.add))

        def s5(ew):
            ew.dma_async(xres_v[:, :, cols], xt[:])
            if last_layer and which == 1 and ti * tn >= NCTX:
                ew.dma_async(yout.rearrange("(kc p) n -> p kc n", p=128)[:, :, ti * tn - NCTX:(ti + 1) * tn - NCTX], xt[:])
        stg = [{"act": s1}, {"act": s1b}, {"pe": s2}, {"act": s2b}, {"dve": s3}, {"pool": s4}, {"sp": s5}]
        stg[load_stage] = dict(stg[load_stage])
        stg[load_stage]["sp"] = s1l
        return stg

    for l in range(NL_RUN):
        last = (l == NL - 1)
        with ExitStack() as st:
            sbt = lambda n, s, d: st.enter_context(nc.sbuf_tensor(uniq(n), list(s), d))
            T1 = 256
            NT1 = NT // T1
            win = sbt("win", [128, 8, 1536], BF)
            ut_all = sbt("ut_all", [128, 4, 8, NTI], BF)
            NB = 3
            xt = [sbt(f"xt{i}", [128, 8, T1], F32) for i in range(5)]
            sq = [sbt(f"sq{i}", [128, 8, T1], BF) for i in range(NB)]
            rstd = [sbt(f"rstd{i}", [128, T1], F32) for i in range(NB)]
            hb = [sbt(f"hb{i}", [128, 8, T1], BF) for i in range(NB)]
            xro = [sbt(f"xro{i}", [128, 4, T1], F32) for i in range(NB)]
            ggo = [sbt(f"ggo{i}", [128, 4, T1], BF) for i in range(NB)]
            b.blk(pool=lambda ew: ew.dma(win[:], w_in[l].rearrange("(kc p) n -> p kc n", p=128)))
            tasks = []
            for ti in range(NT1):
                k = ti % NB
                psS = ps[6 + ti % 2][:, 0:T1]
                stg = norm_stages(l, ti, xt[ti % 5], sq[k], psS, rstd[k], xt[ti % 5], hb[k], 0, T1, gap=2)
                cols = slice(ti * T1, (ti + 1) * T1)

                def mmh(ew, k=k, half=0):
                    for mc in range(half * 6, half * 6 + 6):
                        bank = ps[half * 3 + (mc % 6) // 2]
                        for kc in range(8):
                            ew.mm(bank[:, (mc % 2) * T1:(mc % 2 + 1) * T1],
                                  win[:, kc, mc * 128:(mc + 1) * 128], hb[k][:, kc, :],
                                  start=(kc == 0), stop=(kc == 7))

                def evA_d(ew, ti=ti):
                    for bk in range(2):
                        ew.c(ew.e.tensor_copy(
                            out=ut_all[:, bk * 2:bk * 2 + 2, :, ti * 32:(ti + 1) * 32],
                            in_=ps[bk][:, 0:512].rearrange("p (c i s) -> p c s i", c=2, s=8)), wait=False)

                def evA_a(ew, k=k):
                    ew.c(ew.e.activation(out=xro[k][:, 0:2, :], in_=ps[2][:, 0:512].rearrange("p (c n) -> p c n", c=2),
                                         func=AF.Copy))

                def evB_a(ew, k=k):
                    ew.c(ew.e.activation(out=xro[k][:, 2:4, :], in_=ps[3][:, 0:512].rearrange("p (c n) -> p c n", c=2),
                                         func=AF.Copy), wait=False)
                    for bk in range(2):
                        ew.c(ew.e.activation(out=ggo[k][:, bk * 2:bk * 2 + 2, :],
                                             in_=ps[4 + bk][:, 0:512].rearrange("p (c n) -> p c n", c=2),
                                             func=AF.Gelu_apprx_tanh), wait=False)

                def stB(ew, k=k, cols=cols):
                    ew.dma(xr_d.rearrange("(c p) n -> p c n", p=128)[:, :, cols], xro[k][:])
                    ew.dma(gg_d.rearrange("(c p) n -> p c n", p=128)[:, :, cols], ggo[k][:])
                tasks.append(stg + [{"pe": lambda ew, f=mmh: f(ew, half=0)}, {"dve": evA_d, "act": evA_a}])
                tasks.append([{}] * 8 + [{"pe": lambda ew, f=mmh: f(ew, half=1)}, {"act": evB_a}, {"sp": stB}])
            b.pipeline(tasks)

            def uout(ew):
                for s in range(8):
                    ew.dma(ud[s].rearrange("(c p) i -> p c i", p=128), ut_all[:, :, s, :])
            b.blk(sp=uout)

        if STOP_AFTER < 2:
            continue
        s5_phase(nc, b, l, ps, ud, yd, ident, Jm, SWm, MLm, sg,
                 sa_re, sa_im, sldt, sBP, sBQ, sCP, sCQ, sD)

        if STOP_AFTER < 3:
            continue
        lru_phase(nc, b, l, ps, xr_d, gg_d, yl_d, cw, cb, lruW, lb_rg, lb_ig, llam)

        if STOP_AFTER < 4:
            continue
        cm_wfi = nc.sbuf_tensor(uniq("wfi"), [128, 8, 2 * DFF], BF)
        wfi = cm_wfi.__enter__()
        cm_wo = nc.sbuf_tensor(uniq("wo"), [128, 8, 1024], BF)
        wo = cm_wo.__enter__()
        cm_wg = nc.sbuf_tensor(uniq("wg"), [128, 4, 512], BF)
        wg = cm_wg.__enter__()
        wtok = {}

        def pre4(ew):
            wtok["wo"] = ew.dma_async(wo[:], w_out[l].rearrange("(kc p) n -> p kc n", p=128), track=False)
            wtok["wg"] = ew.dma_async(wg[:], w_glu[l].rearrange("(kc p) n -> p kc n", p=128), track=False)
            wtok["wfi"] = ew.dma_async(wfi[:], w_ffi[l].rearrange("(kc p) n -> p kc n", p=128), track=False)
        b.blk(pool=pre4)

        with ExitStack() as st:
            sbt = lambda n, s, d: st.enter_context(nc.sbuf_tensor(uniq(n), list(s), d))
            yf_ = [sbt(f"yf{i}", [128, 2, 8, NTI], BF) for i in range(2)]
            ys_ = [sbt(f"ys{i}", [128, NT], F32) for i in range(2)]
            ygo = [sbt(f"ygo{i}", [128, NT], BF) for i in range(2)]
            tasks = []
            for cc in range(4):
                k = cc % 2
                rows = slice(cc * 128, (cc + 1) * 128)

                def q0(ew, k=k, rows=rows):
                    for d in range(2):
                        ew.dma(yf_[k][:, d, :, :], yd[d, :, rows, :].rearrange("t p i -> p t i"))

                def q1(ew, k=k):
                    ew.c(ew.e.tensor_tensor(out=ys_[k][:].rearrange("p (i t) -> p t i", t=8),
                                            in0=yf_[k][:, 0, :, :], in1=yf_[k][:, 1, :, :], op=ALU.add))

                def q2(ew, k=k):
                    ew.c(ew.e.activation(out=ygo[k][:], in_=ys_[k][:], func=AF.Gelu_apprx_tanh))

                def q3(ew, k=k, rows=rows):
                    ew.dma(yg_d[rows, :], ygo[k][:])
                tasks.append([{"sp": q0}, {"dve": q1}, {"act": q2}, {"sp": q3}])
            b.pipeline(tasks)

        with ExitStack() as st:
            sbt = lambda n, s, d: st.enter_context(nc.sbuf_tensor(uniq(n), list(s), d))
            bg = sbt("bg", [128, 4], F32)
            xt = [sbt(f"xt{i}", [128, 8, TN], F32) for i in range(4)]
            osb = [sbt(f"osb{i}", [128, 8, TN], F32) for i in range(6)]
            sq = [sbt(f"sq{i}", [128, 8, TN], BF) for i in range(5)]
            rstd = [sbt(f"rstd{i}", [128, TN], F32) for i in range(5)]
            yg = [sbt(f"yg{i}", [128, 4, TN], BF) for i in range(5)]
            sig = [sbt(f"sig{i}", [128, 4, TN], F32) for i in range(5)]
            ymix = [sbt(f"ymix{i}", [128, 8, TN], BF) for i in range(5)]
            b.blk(sp=lambda ew: ew.dma(bg[:], b_glu[:, l, :]))
            tasks = []
            tlist = range(NTT) if not last else range(2, NTT)
            for n_, ti in enumerate(tlist):
                k = n_ % 5
                cols = slice(ti * TN, (ti + 1) * TN)
                psG = ps[n_ % 2]
                psO2 = [ps[2 + (n_ % 2) * 2], ps[3 + (n_ % 2) * 2]]
                psS = ps[6 + n_ % 2][:, 0:TN]

                def a0(ew, k=k, cols=cols):
                    ew.dma(yg[k][:], yg_d.rearrange("(c p) n -> p c n", p=128)[:, :, cols])
                    ew.dma(ymix[k][:, 4:8, :], yl_d.rearrange("(c p) n -> p c n", p=128)[:, :, cols])

                def a3(ew, k=k, psG=psG, first=(n_ == 0)):
                    if first:
                        ew.wait_tok(wtok["wg"])
                        ew.wait_tok(wtok["wo"])
                    for mc in range(4):
                        for kc in range(4):
                            ew.mm(psG[:, mc * TN:(mc + 1) * TN], wg[:, kc, mc * 128:(mc + 1) * 128],
                                  yg[k][:, kc, :], start=(kc == 0), stop=(kc == 3))

                def a4(ew, k=k, psG=psG):
                    for mc in range(4):
                        ew.c(ew.e.activation(out=sig[k][:, mc, :], in_=psG[:, mc * TN:(mc + 1) * TN],
                                             func=AF.Sigmoid, bias=bg[:, mc:mc + 1]), wait=(mc == 3))

                def a5(ew, k=k):
                    ew.c(ew.e.tensor_tensor(out=ymix[k][:, 0:4, :], in0=yg[k][:], in1=sig[k][:], op=ALU.mult))

                def a6(ew, k=k, psO2=psO2):
                    for mc in range(8):
                        for kc in range(8):
                            ew.mm(psO2[mc // 4][:, (mc % 4) * TN:(mc % 4 + 1) * TN],
                                  wo[:, kc, mc * 128:(mc + 1) * 128], ymix[k][:, kc, :],
                                  start=(kc == 0), stop=(kc == 7))
                stg = [{"sp": a0}, {"pe": a3}, {"act": a4}, {"dve": a5}, {"pe": a6}]
                stg += post_stages(l, ti, psO2, xt[n_ % 4], osb[n_ % 6], sq[k], psS, rstd[k], 0, last, TN, load_stage=3)
                tasks.append(stg)
            b.pipeline(tasks)
            b.drain()
        cm_wg.__exit__(None, None, None)
        cm_wo.__exit__(None, None, None)

        if STOP_AFTER < 5:
            cm_wfi.__exit__(None, None, None)
            continue
        with ExitStack() as st:
            sbt = lambda n, s, d: st.enter_context(nc.sbuf_tensor(uniq(n), list(s), d))
            T5 = 256
            wfo = sbt("wfo", [128, 22, 1024], BF)
            NB = 2
            xt = [sbt(f"xt{i}", [128, 8, T5], F32) for i in range(NB)]
            sq = [sbt(f"sq{i}", [128, 8, T5], BF) for i in range(NB)]
            rstd = [sbt(f"rstd{i}", [128, T5], F32) for i in range(NB)]
            hb = [sbt(f"hb{i}", [128, 8, T5], BF) for i in range(NB)]
            xt2 = sbt("xtb", [128, 8, T5], F32)
            osb = sbt("osb", [128, 8, T5], F32)
            sq2 = sbt("sqb", [128, 8, T5], BF)
            rstd2 = sbt("rstdb", [128, T5], F32)
            act_ = sbt("act", [128, 22, T5], BF)
            sgt = [sbt(f"sgt{i}", [128, T5], F32) for i in range(3)]

            b.blk(pool=lambda ew: wtok.__setitem__("wfo", ew.dma_async(wfo[:], w_ffo[l].rearrange("(kc p) n -> p kc n", p=128), track=False)))
            tlist = list(range(NT // T5) if not last else range(1, NT // T5))
            tasks = []
            gcount = [0]
            psO = [ps[3], ps[4], ps[5], ps[6]]

            def norm_task(n_):
                ti = tlist[n_]
                k = n_ % NB
                return norm_stages(l, ti, xt[k], sq[k], ps[7][:, 0:T5], rstd[k], xt[k], hb[k], 1, T5, gap=2)

            def gu_task(n_, r):
                k = n_ % NB
                gi = gcount[0]
                gcount[0] += 1
                pg = ps[gi % 3]
                sgb = sgt[gi % 3]

                def g0(ew):
                    if n_ == 0 and r == 0:
                        ew.wait_tok(wtok["wfi"])
                    for half in range(2):
                        co = half * DFF + r * 128
                        for kc in range(8):
                            ew.mm(pg[:, half * T5:(half + 1) * T5], wfi[:, kc, co:co + 128], hb[k][:, kc, :],
                                  start=(kc == 0), stop=(kc == 7))

                def g1a(ew):
                    ew.c(ew.e.activation(out=sgb[:], in_=pg[:, 0:T5], func=AF.Silu))

                def g1d(ew):
                    ew.c(ew.e.tensor_tensor(out=act_[:, r, :], in0=sgb[:], in1=pg[:, T5:2 * T5], op=ALU.mult))

                def o_r(ew):
                    if n_ == 0 and r == 0:
                        ew.wait_tok(wtok["wfo"])
                    for mc in range(8):
                        ew.mm(psO[mc // 2][:, (mc % 2) * T5:(mc % 2 + 1) * T5],
                              wfo[:, r, mc * 128:(mc + 1) * 128], act_[:, r, :],
                              start=(r == 0 and mc % 2 == 0), stop=(r == 21 and mc % 2 == 1))
                return [{"pe": g0}, {"act": g1a}, {"dve": g1d}, {"pe": o_r}]

            def out_task(n_):
                ti = tlist[n_]
                return post_stages(l, ti, psO, xt2, osb, sq2, ps[7][:, T5:2 * T5], rstd2, 1, last, T5)

            ntl = len(tlist)
            tasks.append(norm_task(0))
            for _ in range(8):
                tasks.append([])
            for n_ in range(ntl):
                for r in range(22):
                    tasks.append(gu_task(n_, r))
                    if r == 1 and n_ >= 1:
                        tasks.append(out_task(n_ - 1))
                    if r == 10 and n_ + 1 < ntl:
                        tasks.append(norm_task(n_ + 1))
                tasks.append([])
            for _ in range(3):
                tasks.append([])
            tasks.append(out_task(ntl - 1))
            b.pipeline(tasks)
            b.drain()
        cm_wfi.__exit__(None, None, None)

    b.es.close()
    return nc


def s5_phase(nc, b, l, ps, ud, yd, ident, Jm, SWm, MLm, sg,
             sa_re, sa_im, sldt, sBP, sBQ, sCP, sCQ, sD):
    NP = 16
    with ExitStack() as st:
        sbt = lambda n, s, d: st.enter_context(nc.sbuf_tensor(uniq(n), list(s), d))
        are = sbt("are", [128, NP], F32)
        aim = sbt("aim", [128, NP], F32)
        ldt = sbt("ldt", [128, NP], F32)
        BP = sbt("BP", [128, NP, 16], F32)
        BQ = sbt("BQ", [128, NP, 16], F32)
        CP = sbt("CP", [128, NP, 16], F32)
        CQ = sbt("CQ", [128, NP, 16], F32)
        Dp = sbt("Dp", [128, 8], F32)
        U2_s = [sbt("U2%d" % i_, [128, NP, NTI], BF) for i_ in range(2)]
        T = {}
        for nm in ("dt", "ar", "ai", "mg", "c0", "s0", "t1", "t2", "t3", "den", "fre", "fim",
                   "fres", "fims", "are64", "aim64", "a8re", "a8ims"):
            T[nm] = sbt("T" + nm, [128, NP], F32)
        PW = sbt("PW", [128, 2, 9, NP], F32)
        PI = sbt("PI", [128, 2, 8, NP], F32)
        fBP = sbt("fBP", [128, NP, 16], F32)
        fBQ = sbt("fBQ", [128, NP, 16], F32)
        AL = sbt("AL", [128, NP, 8], F32)
        BE = sbt("BE", [128, NP, 8], F32)
        Zst = sbt("Zst", [128, NP, 8, 16], F32)
        W1_s = [sbt("W1%d" % i_, [128, NP, 128], BF) for i_ in range(2)]
        W2_s = [sbt("W2%d" % i_, [128, NP, 8, 16], BF) for i_ in range(2)]
        E0 = sbt("E0", [128, NP, 8, 16], F32)
        CT0 = sbt("CT0", [128, NP, 8, 16], F32)
        TP_s = [sbt("TP%d" % i_, [128, NP, 128], BF) for i_ in range(2)]
        R8_s = [sbt("R8%d" % i_, [128, NP, 128], BF) for i_ in range(2)]
        tA = sbt("tA", [128, NP, 8, 16], F32)
        tB = sbt("tB", [128, NP, 8, 16], F32)
        Sa = sbt("Sa", [128, NP, NCH], BF)
        Sb = sbt("Sb", [128, NP, NCH], BF)
        Fm = sbt("Fm", [128, NP, NCH], F32)
        Fs = sbt("Fs", [128, NP, NCH], F32)
        St = sbt("St", [128, NP, NTI], BF)
        Yo = sbt("Yo", [128, NP, NTI], BF)
        l2t = [sbt(f"l2t{i}", [128, 4, 8], F32) for i in range(2)]
        V2 = sbt("V2", [128, NCH, 32], F32)
        F2 = sbt("F2", [128, NCH, 32], F32)
        A2_s = [sbt("A2%d" % i_, [128, 32], F32) for i_ in range(2)]
        B2_s = [sbt("B2%d" % i_, [128, 32], F32) for i_ in range(2)]


        ytok = []

        def prep(qb, par):
            U2, W1, W2, TP, R8, A2, B2 = U2_s[par], W1_s[par], W2_s[par], TP_s[par], R8_s[par], A2_s[par], B2_s[par]
            def ld(ew):
                for d in range(2):
                    sl = slice(d * 8, (d + 1) * 8)
                    hs = slice((qb % 2) * 8, (qb % 2) * 8 + 8)
                    hb_ = d * 2 + qb // 2
                    ew.dma(are[:, sl], sa_re[:, l, hb_, hs])
                    ew.dma(aim[:, sl], sa_im[:, l, hb_, hs])
                    ew.dma(ldt[:, sl], sldt[:, l, hb_, hs])
                    ew.dma(BP[:, sl, :], sBP[:, l, hb_, hs, :])
                    ew.dma(BQ[:, sl, :], sBQ[:, l, hb_, hs, :])
                    ew.dma(CP[:, sl, :], sCP[:, l, hb_, hs, :])
                    ew.dma(CQ[:, sl, :], sCQ[:, l, hb_, hs, :])
                ew.dma(Dp[:], sD[:, l, qb, :])
                for s in range(8):
                    src = ud[s, qb * 128:(qb + 1) * 128, :].rearrange("(g p) i -> p g i", p=16)
                    ew.dma(U2[s * 16:(s + 1) * 16, 0:8, :], src)
                    ew.dma(U2[(7 - s) * 16:(8 - s) * 16, 8:16, :], src)
            ldtok = []
            b.blk(sp=lambda ew: ld(_AsyncDma(ew, ldtok)))
            for _ in range(3):
                b.blk()
            b.blk(sp=lambda ew: [ew.wait_tok(t_) for t_ in ldtok])

            def sc1(ew):
                ew.c(ew.e.activation(out=T["dt"][:], in_=ldt[:], func=AF.Exp))
            b.blk(act=sc1)

            def sc2(ew):
                ew.c(ew.e.tensor_tensor(out=T["ar"][:], in0=are[:], in1=T["dt"][:], op=ALU.mult))
                ew.c(ew.e.tensor_tensor(out=T["ai"][:], in0=aim[:], in1=T["dt"][:], op=ALU.mult))
                ew.c(ew.e.tensor_scalar(out=T["t1"][:], in0=T["ai"][:], scalar1=1.0 / 16.0, scalar2=math.pi / 2,
                                        op0=ALU.mult, op1=ALU.add))
            b.blk(dve=sc2)

            def sc3(ew):
                ew.c(ew.e.activation(out=T["mg"][:], in_=T["ar"][:], func=AF.Exp, scale=1.0 / 16.0))
                ew.c(ew.e.activation(out=T["s0"][:], in_=T["ai"][:], func=AF.Sin, scale=1.0 / 16.0))
                ew.c(ew.e.activation(out=T["c0"][:], in_=T["t1"][:], func=AF.Sin))
                ew.c(ew.e.activation(out=T["t2"][:], in_=T["ar"][:], func=AF.Exp, scale=-1.0 / 16.0))
            b.blk(act=sc3)

            def cmul(ew, o_re, o_im, a_re, a_im, b_re, b_im, t1, t2):
                ew.c(ew.e.tensor_tensor(out=t1, in0=a_re, in1=b_re, op=ALU.mult))
                ew.c(ew.e.tensor_tensor(out=t2, in0=a_im, in1=b_im, op=ALU.mult))
                ew.c(ew.e.tensor_tensor(out=t2, in0=t1, in1=t2, op=ALU.subtract))
                ew.c(ew.e.tensor_tensor(out=t1, in0=a_re, in1=b_im, op=ALU.mult))
                ew.c(ew.e.tensor_tensor(out=o_im, in0=a_im, in1=b_re, op=ALU.mult))
                ew.c(ew.e.tensor_tensor(out=o_im, in0=o_im, in1=t1, op=ALU.add))
                ew.c(ew.e.tensor_copy(o_re, t2))

            def sc4(ew):
                t1, t2, t3 = T["t1"][:], T["t3"][:], T["den"][:]
                ew.c(ew.e.tensor_tensor(out=PW[:, 0, 1, :], in0=T["mg"][:], in1=T["c0"][:], op=ALU.mult))
                ew.c(ew.e.tensor_tensor(out=PW[:, 1, 1, :], in0=T["mg"][:], in1=T["s0"][:], op=ALU.mult))
                ew.c(ew.e.tensor_tensor(out=PI[:, 0, 1, :], in0=T["t2"][:], in1=T["c0"][:], op=ALU.mult))
                ew.c(ew.e.scalar_tensor_tensor(out=PI[:, 1, 1, :], in0=T["t2"][:], scalar=-1.0, in1=T["s0"][:],
                                               op0=ALU.mult, op1=ALU.mult))
                for _ in range(4):
                    cmul(ew, PW[:, 0, 1, :], PW[:, 1, 1, :], PW[:, 0, 1, :], PW[:, 1, 1, :],
                         PW[:, 0, 1, :], PW[:, 1, 1, :], t1, t2)
                    cmul(ew, PI[:, 0, 1, :], PI[:, 1, 1, :], PI[:, 0, 1, :], PI[:, 1, 1, :],
                         PI[:, 0, 1, :], PI[:, 1, 1, :], t1, t2)
                ew.c(ew.e.memset(PW[:, 0, 0, :], 1.0))
                ew.c(ew.e.memset(PW[:, 1, 0, :], 0.0))
                ew.c(ew.e.memset(PI[:, 0, 0, :], 1.0))
                ew.c(ew.e.memset(PI[:, 1, 0, :], 0.0))
                for k in range(2, 9):
                    cmul(ew, PW[:, 0, k, :], PW[:, 1, k, :], PW[:, 0, k - 1, :], PW[:, 1, k - 1, :],
                         PW[:, 0, 1, :], PW[:, 1, 1, :], t1, t2)
                for k in range(2, 8):
                    cmul(ew, PI[:, 0, k, :], PI[:, 1, k, :], PI[:, 0, k - 1, :], PI[:, 1, k - 1, :],
                         PI[:, 0, 1, :], PI[:, 1, 1, :], t1, t2)
                ew.c(ew.e.tensor_copy(T["are64"][:], PW[:, 0, 8, :]))
                ew.c(ew.e.tensor_copy(T["aim64"][:], PW[:, 1, 8, :]))
                for _ in range(3):
                    cmul(ew, T["are64"][:], T["aim64"][:], T["are64"][:], T["aim64"][:],
                         T["are64"][:], T["aim64"][:], t1, t2)
                nr = T["c0"][:]
                ew.c(ew.e.tensor_scalar_add(nr, PW[:, 0, 1, :], -1.0))
                ew.c(ew.e.tensor_tensor(out=t1, in0=are[:], in1=are[:], op=ALU.mult))
                ew.c(ew.e.tensor_tensor(out=t2, in0=aim[:], in1=aim[:], op=ALU.mult))
                ew.c(ew.e.tensor_tensor(out=t3, in0=t1, in1=t2, op=ALU.add))
                ew.c(ew.e.reciprocal(t3, t3))
                ew.c(ew.e.tensor_tensor(out=t1, in0=nr, in1=are[:], op=ALU.mult))
                ew.c(ew.e.tensor_tensor(out=t2, in0=PW[:, 1, 1, :], in1=aim[:], op=ALU.mult))
                ew.c(ew.e.tensor_tensor(out=t1, in0=t1, in1=t2, op=ALU.add))
                ew.c(ew.e.tensor_tensor(out=T["fre"][:], in0=t1, in1=t3, op=ALU.mult))
                ew.c(ew.e.tensor_tensor(out=t1, in0=PW[:, 1, 1, :], in1=are[:], op=ALU.mult))
                ew.c(ew.e.tensor_tensor(out=t2, in0=nr, in1=aim[:], op=ALU.mult))
                ew.c(ew.e.tensor_tensor(out=t1, in0=t1, in1=t2, op=ALU.subtract))
                ew.c(ew.e.tensor_tensor(out=T["fim"][:], in0=t1, in1=t3, op=ALU.mult))
                ew.c(ew.e.tensor_scalar(out=T["fims"][:], in0=T["fim"][:], scalar1=sg[:, 0:1], scalar2=None,
                                        op0=ALU.mult))
                f_re = bc(T["fre"][:].unsqueeze(2), [128, NP, 16])
                f_ims = bc(T["fims"][:].unsqueeze(2), [128, NP, 16])
                ta = tA[:, :, 0, :]
                ew.c(ew.e.tensor_tensor(out=fBP[:], in0=BP[:], in1=f_re, op=ALU.mult))
                ew.c(ew.e.tensor_tensor(out=ta, in0=BQ[:], in1=f_ims, op=ALU.mult))
                ew.c(ew.e.tensor_tensor(out=fBP[:], in0=fBP[:], in1=ta, op=ALU.add))
                ew.c(ew.e.tensor_tensor(out=fBQ[:], in0=BQ[:], in1=f_re, op=ALU.mult))
                ew.c(ew.e.tensor_tensor(out=ta, in0=BP[:], in1=f_ims, op=ALU.mult))
                ew.c(ew.e.tensor_tensor(out=fBQ[:], in0=fBQ[:], in1=ta, op=ALU.subtract))

                def ctab(out, P_, Q_, pw_re_fn, pw_im_fn, a_sgn_col, b_sgn_col, b_neg=False):
                    for j in range(8):
                        if a_sgn_col is None:
                            ew.c(ew.e.tensor_copy(AL[:, :, j], pw_re_fn(j)), wait=False)
                        else:
                            ew.c(ew.e.tensor_scalar(out=AL[:, :, j], in0=pw_re_fn(j), scalar1=sg[:, a_sgn_col:a_sgn_col + 1],
                                                    scalar2=None, op0=ALU.mult), wait=False)
                        if b_sgn_col is None:
                            ew.c(ew.e.tensor_scalar(out=BE[:, :, j], in0=pw_im_fn(j), scalar1=(-1.0 if b_neg else 1.0),
                                                    scalar2=None, op0=ALU.mult), wait=(j == 7))
                        else:
                            ew.c(ew.e.tensor_scalar(out=BE[:, :, j], in0=pw_im_fn(j), scalar1=sg[:, b_sgn_col:b_sgn_col + 1],
                                                    scalar2=None, op0=ALU.mult), wait=(j == 7))
                    sh = [128, NP, 8, 16]
                    ew.c(ew.e.tensor_tensor(out=tA[:], in0=bc(P_.unsqueeze(2), sh), in1=bc(AL[:].unsqueeze(3), sh), op=ALU.mult), wait=False)
                    ew.c(ew.e.tensor_tensor(out=tB[:], in0=bc(Q_.unsqueeze(2), sh), in1=bc(BE[:].unsqueeze(3), sh), op=ALU.mult))
                    ew.c(ew.e.tensor_tensor(out=out, in0=tA[:], in1=tB[:], op=ALU.add))
                ctab(Zst[:], fBP[:], fBQ[:], lambda j: PW[:, 0, 7 - j, :], lambda j: PW[:, 1, 7 - j, :], None, 0)
                ctab(W2[:], CP[:], CQ[:], lambda j: PW[:, 0, j + 1, :], lambda j: PW[:, 1, j + 1, :], 1, None, True)
                ctab(E0[:], fBP[:], fBQ[:], lambda j: PI[:, 0, j, :], lambda j: PI[:, 1, j, :], 1, None, True)
                ctab(CT0[:], CP[:], CQ[:], lambda j: PW[:, 0, j, :], lambda j: PW[:, 1, j, :], None, 0)
                ew.c(ew.e.tensor_copy(T["a8re"][:], PW[:, 0, 8, :]))
                ew.c(ew.e.tensor_scalar(out=T["a8ims"][:], in0=PW[:, 1, 8, :], scalar1=sg[:, 1:2], scalar2=None,
                                        op0=ALU.mult))
                for p_ in range(NP):
                    ew.c(ew.e.tensor_scalar(out=tA[:, 0, :, :].rearrange("p a b -> p (a b)"), in0=Jm[:],
                                            scalar1=T["a8ims"][:, p_:p_ + 1], scalar2=None, op0=ALU.mult))
                    ew.c(ew.e.scalar_tensor_tensor(out=R8[:, p_, :], in0=ident[:], scalar=T["a8re"][:, p_:p_ + 1],
                                                   in1=tA[:, 0, :, :].rearrange("p a b -> p (a b)"),
                                                   op0=ALU.mult, op1=ALU.add))
                for d in range(2):
                    for hf in range(2):
                        o = A2[:, d * 16 + hf * 8:d * 16 + hf * 8 + 8]
                        ew.c(ew.e.tensor_copy(o, T["are64"][:, d * 8:(d + 1) * 8]))
                        ew.c(ew.e.tensor_scalar(out=B2[:, d * 16 + hf * 8:d * 16 + hf * 8 + 8],
                                                in0=T["aim64"][:, d * 8:(d + 1) * 8],
                                                scalar1=(1.0 if hf == 0 else -1.0), scalar2=None, op0=ALU.mult))
            pew = _ProxyEW()
            sc4(pew)
            CH = 10
            for c0_ in range(0, len(pew.ops), CH):
                chunk = pew.ops[c0_:c0_ + CH]
                b.blk(dve=lambda ew, chunk=chunk: [ew.c(getattr(ew.e, n_)(*a_, **k_), wait=w_) for ((n_, a_, k_), w_) in chunk])

            for half in range(4):
                prs = range(half * 4, half * 4 + 4)

                def tp(ew, prs=prs):
                    for q, p_ in enumerate(prs):
                        ew.tr(ps[6][:, q * 128:(q + 1) * 128], Zst[:, p_, :, :].rearrange("p a b -> p (a b)"), ident[:])
                        ew.mm(ps[7][:, q * 128:(q + 1) * 128], E0[:, p_, :, :].rearrange("p a b -> p (a b)"),
                              CT0[:, p_, :, :].rearrange("p a b -> p (a b)"), start=True, stop=True)
                b.blk(pe=tp)

                def tpe(ew, prs=prs):
                    for q, p_ in enumerate(prs):
                        ew.c(ew.e.tensor_copy(W1[:, p_, :], ps[6][:, q * 128:(q + 1) * 128]), wait=False)
                        ew.c(ew.e.tensor_tensor(out=tA[:, 0, :, :].rearrange("p a b -> p (a b)"),
                                                in0=ps[7][:, q * 128:(q + 1) * 128], in1=MLm[:], op=ALU.mult))
                        if p_ < 8:
                            ew.c(ew.e.scalar_tensor_tensor(out=TP[:, p_, :], in0=ident[:], scalar=Dp[:, p_:p_ + 1],
                                                           in1=tA[:, 0, :, :].rearrange("p a b -> p (a b)"),
                                                           op0=ALU.mult, op1=ALU.add))
                        else:
                            ew.c(ew.e.tensor_copy(TP[:, p_, :], tA[:, 0, :, :].rearrange("p a b -> p (a b)")))
                b.blk(dve=tpe)


        def main(qb, par):
            U2, W1, W2, TP, R8, A2, B2 = U2_s[par], W1_s[par], W2_s[par], TP_s[par], R8_s[par], A2_s[par], B2_s[par]
            def tiles(p_, r):
                off = r if p_ < 8 else 7 - r
                return slice(off, NTI, 8)

            def sweep(down):
                cur, nxt = Sa, Sb
                for r in range(8 if not down else 7):
                    def pe_(ew, r=r, cur=cur):
                        for p_ in range(NP):
                            o = ps[p_ // 7][:, (p_ % 7) * NCH:(p_ % 7 + 1) * NCH]
                            first = True
                            if down:
                                ew.mm(o, R8[:, p_, :], St[:, p_, tiles(p_, r)], start=True, stop=False)
                                first = False
                            elif r > 0:
                                ew.mm(o, R8[:, p_, :], cur[:, p_, :], start=True, stop=False)
                                first = False
                            ew.mm(o, W1[:, p_, :], U2[:, p_, tiles(p_, r)], start=first, stop=True)
                    b.blk(pe=pe_)

                    def ev(ew, r=r, nxt=nxt):
                        for bk in range(3):
                            n_p = 7 if bk < 2 else 2
                            prs = slice(bk * 7, bk * 7 + n_p)
                            src = ps[bk][:, 0:n_p * NCH].rearrange("p (a m) -> p a m", m=NCH)
                            if down:
                                lo, hi = bk * 7, bk * 7 + n_p
                                for (a, e_) in ((lo, min(hi, 8)), (max(lo, 8), hi)):
                                    if e_ > a:
                                        ew.c(ew.e.activation(
                                            out=St[:, a:e_, tiles(a, r + 1)],
                                            in_=ps[bk][:, (a - lo) * NCH:(e_ - lo) * NCH].rearrange("p (a m) -> p a m", m=NCH), func=AF.Copy), wait=False)
                            elif r < 7:
                                ew.c(ew.e.activation(out=nxt[:, prs, :], in_=src, func=AF.Copy), wait=False)
                            else:
                                ew.c(ew.e.activation(out=Fm[:, prs, :], in_=src, func=AF.Copy), wait=False)
                    b.blk(act=ev)
                    cur, nxt = nxt, cur

            sweep(False)

            def swp(ew):
                for bk in range(3):
                    n_p = 7 if bk < 2 else 2
                    ew.mm(ps[bk][:, 0:n_p * NCH], SWm[:], Fm[:, bk * 7:bk * 7 + n_p, :].rearrange("p a m -> p (a m)"),
                          start=True, stop=True)
            b.blk(pe=swp)

            def swe(ew):
                for bk in range(3):
                    n_p = 7 if bk < 2 else 2
                    ew.c(ew.e.tensor_copy(Fs[:, bk * 7:bk * 7 + n_p, :],
                                          ps[bk][:, 0:n_p * NCH].rearrange("p (a m) -> p a m", m=NCH)))
                ew.c(ew.e.memset(V2[:], 0.0))
                for (src, slot) in ((Fm, 0), (Fs, 1)):
                    ew.c(ew.e.tensor_copy(F2[:, :, slot * 8:(slot + 1) * 8], src[:, 0:8, :].rearrange("p j m -> p m j")))
                    ew.c(ew.e.tensor_copy(F2[:, 0:4, 16 + slot * 8:24 + slot * 8],
                                          mkap(src, 8 * NCH + 3, [[-1, 4], [NCH, 8]])))
                    ew.c(ew.e.tensor_copy(F2[:, 4:NCH, 16 + slot * 8:24 + slot * 8],
                                          mkap(src, 8 * NCH + NCH - 1, [[-1, NCH - 4], [NCH, 8]])))
            b.blk(dve=swe)

            def l2(ew):
                t = ew.e.tensor_tensor
                ta = l2t[0][:].rearrange("p a b -> p (a b)")
                tb = l2t[1][:].rearrange("p a b -> p (a b)")
                for k in range(NCH - 1):
                    vsw = mkap(V2, k * 32 + 8, [[16, 2], [-8, 2], [1, 8]])
                    t(out=ta, in0=A2[:], in1=V2[:, k, :], op=ALU.mult)
                    ew.c(t(out=tb.rearrange("p (d h j) -> p d h j", d=2, h=2), in0=B2[:].rearrange("p (d h j) -> p d h j", d=2, h=2),
                           in1=vsw, op=ALU.mult))
                    ew.c(t(out=ta, in0=ta, in1=tb, op=ALU.add))
                    ew.c(t(out=V2[:, k + 1, :], in0=ta, in1=F2[:, k, :], op=ALU.add))
            b.blk(dve=l2)

            def dinit(ew):
                ew.c(ew.e.tensor_copy(St[:, 0:8, slice(0, NTI, 8)], V2[:, :, 0:8].rearrange("p k j -> p j k")))
                ew.c(ew.e.tensor_copy(St[:, 8:16, slice(7, 7 + 8 * 4, 8)], mkap(V2, 3 * 32 + 16, [[1, 8], [-32, 4]])))
                ew.c(ew.e.tensor_copy(St[:, 8:16, slice(7 + 8 * 4, NTI, 8)],
                                      mkap(V2, (NCH - 1) * 32 + 16, [[1, 8], [-32, NCH - 4]])))
            b.blk(dve=dinit)
            sweep(True)

            ytasks = []
            for p_ in range(NP):
                def ype(ew, p_=p_):
                    for cb_, (c0_, c1_) in enumerate(((0, 512), (512, NTI))):
                        o = ps[(p_ % 2) * 2 + cb_][:, 0:c1_ - c0_]
                        ew.mm(o, TP[:, p_, :], U2[:, p_, c0_:c1_], start=True, stop=False)
                        ew.mm(o, W2[:, p_, :, :].rearrange("p a b -> p (a b)"), St[:, p_, c0_:c1_], start=False, stop=True)

                def yev(ew, p_=p_):
                    ew.c(ew.e.activation(out=Yo[:, p_, 0:512], in_=ps[(p_ % 2) * 2][:, 0:512], func=AF.Copy), wait=False)
                    ew.c(ew.e.activation(out=Yo[:, p_, 512:NTI], in_=ps[(p_ % 2) * 2 + 1][:, 0:NTI - 512], func=AF.Copy))
                ytasks.append([{"pe": ype}, {"act": yev}])
            prev = list(ytok)
            del ytok[:]
            if prev:
                b.blk(sp=lambda ew: [ew.wait_tok(t_) for t_ in prev])
            b.pipeline(ytasks)

            def yout_(ew):
                for t in range(8):
                    ew.dma(yd[0, t, qb * 128:(qb + 1) * 128, :].rearrange("(g q) i -> q g i", q=16),
                           Yo[t * 16:(t + 1) * 16, 0:8, :])
                    ew.dma(yd[1, 7 - t, qb * 128:(qb + 1) * 128, :].rearrange("(g q) i -> q g i", q=16),
                           Yo[t * 16:(t + 1) * 16, 8:16, :])
            b.blk(sp=lambda ew: yout_(_AsyncDma(ew, ytok)))


        b.rec = []
        prep(0, 0)
        steps = b.rec
        b.rec = None
        b.emit_zipped(steps, [])
        for qb in range(4):
            b.rec = []
            main(qb, qb % 2)
            sm = b.rec
            b.rec = []
            if qb < 3:
                prep(qb + 1, (qb + 1) % 2)
            sp_ = b.rec
            b.rec = None
            b.emit_zipped(sm, sp_)
        b.blk(sp=lambda ew: [ew.wait_tok(t_) for t_ in ytok])


def lru_phase(nc, b, l, ps, xr_d, gg_d, yl_d, cw, cb, lruW, lb_rg, lb_ig, llam):
    with ExitStack() as st:
        sbt = lambda n, s, d: st.enter_context(nc.sbuf_tensor(uniq(n), list(s), d))
        cwt = sbt("cwt", [128, 4, 4], F32)
        cbt = sbt("cbt", [128, 4], F32)
        brg = sbt("brg", [128, 2, 4], F32)
        big = sbt("big", [128, 2, 4], F32)
        lam = sbt("lam", [128, 2, 4], F32)
        nsp = sbt("nsp", [128, 2, 4], F32)
        nsp2 = sbt("nsp2", [128, 2, 4], F32)
        Wg = sbt("Wg", [128, 2, 2, 4, 128], BF)
        xp = sbt("xp", [128, 3 + NCTX + 3 + NLAT + 3], F32)
        xc = sbt("xc", [128, NT], F32)
        xcb = sbt("xcb", [128, NT], BF)
        R = [xp, sbt("R1", [128, NT], F32)]
        I = [sbt(f"I{d}", [128, NT], F32) for d in range(2)]
        H = [sbt(f"H{d}", [128, NT], F32) for d in range(2)]
        xraw = H[0]
        ggt = sbt("ggt", [128, NT], BF)
        ylo = xcb
        hc = sbt("hc", [128, 2], F32)
        CO = 3
        LO = 3 + NCTX + 3

        def ld(ew):
            ew.dma(cwt[:], cw[:, l, :, :])
            ew.dma(cbt[:], cb[:, l, :])
            ew.dma(brg[:], lb_rg[:, l, :, :])
            ew.dma(big[:], lb_ig[:, l, :, :])
            ew.dma(lam[:], llam[:, l, :, :])

        def ldw(ew):
            for d in range(2):
                for g in range(2):
                    ew.dma(Wg[:, d, g, :, :], lruW[l, d, g].rearrange("c i o -> i c o"))
        b.blk(sp=ld, pool=ldw)

        b.blk(act=lambda ew: ew.c(ew.e.activation(out=nsp[:], in_=lam[:], func=AF.Exp, scale=-1.0)))
        b.blk(act=lambda ew: ew.c(ew.e.activation(out=nsp[:], in_=nsp[:], func=AF.Ln, bias=1.0)))

        def nsp_(ew):
            ew.c(ew.e.tensor_scalar(out=nsp2[:], in0=nsp[:], scalar1=-16.0, scalar2=None, op0=ALU.mult))
            ew.c(ew.e.tensor_scalar(out=nsp[:], in0=nsp[:], scalar1=-8.0, scalar2=None, op0=ALU.mult))
        b.blk(dve=nsp_)

        for cc in range(4):
            rows = slice(cc * 128, (cc + 1) * 128)

            def l0(ew, rows=rows):
                ew.dma(xraw[:], xr_d[rows, :])
                ew.dma(ggt[:], gg_d[rows, :])
            b.blk(sp=l0)

            def l1(ew):
                ew.c(ew.e.memset(xp[:], 0.0))
                ew.c(ew.e.tensor_copy(xp[:, CO:CO + NCTX], xraw[:, 0:NCTX]))
                ew.c(ew.e.tensor_copy(xp[:, LO:LO + NLAT].rearrange("p (c r) -> p c r", r=64),
                                      xraw[:, NCTX:NT].rearrange("p (r c) -> p c r", c=64)))
            b.blk(dve=l1)

            def l2_(ew, cc=cc):
                for (o0, n, base) in ((0, NCTX, CO), (NCTX, NLAT, LO)):
                    ew.c(ew.e.tensor_scalar(out=xc[:, o0:o0 + n], in0=xp[:, base - 2:base - 2 + n],
                                            scalar1=cwt[:, 0, cc:cc + 1], scalar2=cbt[:, cc:cc + 1],
                                            op0=ALU.mult, op1=ALU.add))
                    for k in range(1, 4):
                        ew.c(ew.e.scalar_tensor_tensor(out=xc[:, o0:o0 + n], in0=xp[:, base - 2 + k:base - 2 + k + n],
                                                       scalar=cwt[:, k, cc:cc + 1], in1=xc[:, o0:o0 + n],
                                                       op0=ALU.mult, op1=ALU.add))
                ew.c(ew.e.tensor_copy(xcb[:], xc[:]))
            b.blk(dve=l2_)

            blocks = [(i * 512, min(512, NT - i * 512)) for i in range(9)]
            gtasks = []
            for bi, (c0_, n) in enumerate(blocks):
                pb = (bi % 2) * 4

                def gpe(ew, c0_=c0_, n=n, cc=cc, pb=pb):
                    for d in range(2):
                        for g in range(2):
                            ew.mm(ps[pb + d * 2 + g][:, 0:n], Wg[:, d, g, cc, :], xcb[:, c0_:c0_ + n], start=True, stop=True)

                def gev(ew, c0_=c0_, n=n, cc=cc, pb=pb):
                    for d in range(2):
                        ew.c(ew.e.activation(out=R[d][:, c0_:c0_ + n], in_=ps[pb + d * 2][:, 0:n], func=AF.Sigmoid,
                                             bias=brg[:, d, cc:cc + 1]), wait=False)
                        ew.c(ew.e.activation(out=I[d][:, c0_:c0_ + n], in_=ps[pb + d * 2 + 1][:, 0:n], func=AF.Sigmoid,
                                             bias=big[:, d, cc:cc + 1]), wait=(d == 1))
                gtasks.append([{"pe": gpe}, {"act": gev}])
            b.pipeline(gtasks)

            def e1(ew, cc=cc):
                for d in range(2):
                    ew.c(ew.e.activation(out=H[d][:], in_=R[d][:, 0:NT], func=AF.Exp, scale=nsp2[:, d, cc:cc + 1]))
                    ew.c(ew.e.activation(out=R[d][:, 0:NT], in_=R[d][:, 0:NT], func=AF.Exp, scale=nsp[:, d, cc:cc + 1]))

            def e1d(ew):
                for d in range(2):
                    ew.c(ew.e.tensor_tensor(out=I[d][:], in0=I[d][:], in1=xc[:], op=ALU.mult), wait=False)
            b.blk(act=e1, dve=e1d)

            def e2(ew):
                for d in range(2):
                    ew.c(ew.e.activation(out=H[d][:], in_=H[d][:], func=AF.Sqrt, scale=-1.0, bias=1.0), wait=False)
            b.blk(act=e2)

            def e3(ew):
                for d in range(2):
                    ew.c(ew.e.tensor_tensor(out=I[d][:], in0=I[d][:], in1=H[d][:], op=ALU.mult), wait=False)
            b.blk(dve=e3)

            def rv(ap2d, o0, n):
                full = ap2d[:, o0:o0 + n]
                return bass.AP(full.tensor, full.offset + (n - 1), [list(full.ap[0]), [-1, n]])

            def sc(ew):
                ew.c(ew.e.tensor_tensor_scan(out=H[0][:, 0:NCTX], data0=R[0][:, 0:NCTX], data1=I[0][:, 0:NCTX],
                                             initial=0.0, op0=ALU.mult, op1=ALU.add))
                ew.c(ew.e.tensor_copy(hc[:, 0:1], H[0][:, NCTX - 1:NCTX]))
                ew.c(ew.e.tensor_tensor_scan(out=H[0][:, NCTX:NT], data0=R[0][:, NCTX:NT], data1=I[0][:, NCTX:NT],
                                             initial=hc[:, 0:1], op0=ALU.mult, op1=ALU.add))
                ew.c(ew.e.tensor_tensor_scan(out=rv(H[1], 0, NCTX), data0=rv(R[1], 0, NCTX), data1=rv(I[1], 0, NCTX),
                                             initial=0.0, op0=ALU.mult, op1=ALU.add))
                ew.c(ew.e.tensor_copy(hc[:, 1:2], H[1][:, 0:1]))
                ew.c(ew.e.tensor_tensor_scan(out=rv(H[1], NCTX, NLAT), data0=rv(R[1], NCTX, NLAT),
                                             data1=rv(I[1], NCTX, NLAT), initial=hc[:, 1:2], op0=ALU.mult, op1=ALU.add))
                ew.c(ew.e.tensor_tensor(out=H[0][:], in0=H[0][:], in1=H[1][:], op=ALU.add))
                ew.c(ew.e.tensor_tensor(out=ylo[:, 0:NCTX], in0=H[0][:, 0:NCTX], in1=ggt[:, 0:NCTX], op=ALU.mult))
                ew.c(ew.e.tensor_tensor(out=ylo[:, NCTX:NT].rearrange("p (r c) -> p r c", c=64),
                                        in0=H[0][:, NCTX:NT].rearrange("p (c r) -> p r c", r=64),
                                        in1=ggt[:, NCTX:NT].rearrange("p (r c) -> p r c", c=64), op=ALU.mult))
            b.blk(dve=sc)
            b.blk(sp=lambda ew, rows=rows: ew.dma(yl_d[rows, :], ylo[:]))


_NC_CACHE = {}


def _host_inputs(inp, bidx):
    f = np.float32
    x = np.asarray(inp["x"], f)
    ctx = np.asarray(inp["ctx"], f)
    d = {}
    d["xs"] = np.ascontiguousarray(np.concatenate([ctx[bidx].T, x[bidx].T], axis=1))
    cv = np.stack([np.asarray(inp["c"], f)[bidx], np.asarray(inp["c_ctx"], f)], axis=-1)
    d["cvec"] = np.ascontiguousarray(cv.reshape(8, 128, 2).transpose(1, 0, 2))
    return d


def _shared_inputs(inp):
    f = np.float32
    g = lambda k: np.asarray(inp[k], f)
    d = {}
    d["w_ada"] = g("w_ada")
    d["b_ada"] = np.ascontiguousarray(g("b_ada").reshape(NL, 48, 128).transpose(2, 0, 1))
    d["gains"] = np.ascontiguousarray(g("norm_gains").reshape(NL, 4, 8, 128).transpose(3, 0, 1, 2))
    d["w_in"] = g("w_in")
    d["w_out"] = g("w_out")
    d["w_ffi"] = g("w_ffn_in")
    d["w_ffo"] = g("w_ffn_out")
    d["w_glu"] = g("s5_w_glu")
    d["b_glu"] = np.ascontiguousarray(g("s5_b_glu").reshape(NL, 4, 128).transpose(2, 0, 1))

    def st2(a):
        a = np.concatenate([a, a], axis=-1)
        a = a.reshape(NL, 2, 2, 16, 128).reshape(NL, 4, 16, 128)
        return np.ascontiguousarray(a.transpose(3, 0, 1, 2))
    d["sa_re"] = st2(g("s5_a_re"))
    d["sa_im"] = st2(g("s5_a_im"))
    d["sldt"] = st2(np.broadcast_to(g("s5_log_dt")[..., None], (NL, 2, 32, 64)))

    def st3(top, bot):
        a = np.concatenate([top, bot], axis=3)
        a = a.reshape(NL, 4, 16, 128, a.shape[-1])
        return np.ascontiguousarray(a.transpose(3, 0, 1, 2, 4))
    bre, bim = g("s5_b_re"), g("s5_b_im")
    d["sBP"] = st3(bre, bim)
    d["sBQ"] = st3(bim, bre)
    cre = g("s5_c_re").transpose(0, 1, 2, 4, 3)
    cim = g("s5_c_im").transpose(0, 1, 2, 4, 3)
    d["sCP"] = st3(cre, cim)
    d["sCQ"] = st3(cim, cre)
    sd = g("s5_d").reshape(NL, 4, 8, 16)
    sd = np.broadcast_to(sd[:, :, :, None, :], (NL, 4, 8, 8, 16))
    d["sD"] = np.ascontiguousarray(sd.transpose(3, 4, 0, 1, 2).reshape(128, NL, 4, 8))
    d["cw"] = np.ascontiguousarray(g("lru_conv_w").reshape(NL, 4, 4, 128).transpose(3, 0, 1, 2))
    d["cb"] = np.ascontiguousarray(g("lru_conv_b").reshape(NL, 4, 128).transpose(2, 0, 1))
    W = np.zeros((NL, 2, 2, 4, 128, 128), f)
    for gi, key in enumerate(("lru_w_rg", "lru_w_ig")):
        w = g(key)
        for cc in range(4):
            W[:, :, gi, cc, 0:64, 0:64] = w[:, :, 2 * cc]
            W[:, :, gi, cc, 64:128, 64:128] = w[:, :, 2 * cc + 1]
    d["lruW"] = W
    v = lambda k: np.ascontiguousarray(g(k).reshape(NL, 2, 4, 128).transpose(3, 0, 1, 2))
    d["lb_rg"] = v("lru_b_rg")
    d["lb_ig"] = v("lru_b_ig")
    d["llam"] = v("lru_lambda")
    I = np.eye(128, dtype=f)
    d["cI"] = I
    J = np.zeros((128, 128), f)
    SW = np.zeros((128, 128), f)
    for n in range(64):
        J[n, 64 + n] = 1.0
        J[64 + n, n] = 1.0
        SW[64 + n, n] = -1.0
        SW[n, 64 + n] = 1.0
    d["cJ"] = J
    d["cSW"] = SW
    sidx = np.arange(128) // 16
    d["cML"] = (sidx[None, :] >= sidx[:, None]).astype(f)
    sgn = np.where(np.arange(128) < 64, -1.0, 1.0).astype(f)
    d["cSG"] = np.stack([sgn, -sgn, np.ones(128, f), np.full(128, 1.0 / 1024, f)], axis=1)
    return d


def kernel(**inputs):
    if "nc" not in _NC_CACHE:
        _NC_CACHE["nc"] = build_program()
    nc = _NC_CACHE["nc"]
    shared = _shared_inputs(inputs)
    in_maps = []
    for bidx in range(8):
        m = dict(shared)
        m.update(_host_inputs(inputs, bidx))
        in_maps.append(m)
    res = run_bass_kernel_spmd(nc, in_maps, core_ids=list(range(8)))
    out = np.stack([np.asarray(r["yout"], np.float32).T for r in res.results], axis=0)
    return np.ascontiguousarray(out)
```
